# Optimizing a Trainium2 kernel written in Bass

```python
import jax, jax.numpy as jnp
from jax import lax
import numpy as np

D_MODEL = 1024
BATCH = 4
SEQ = 4096
DEPTH = 1
DEC_BATCH = 32
DEC_SEQ = 1
PAST_LEN = 8192
PAGE_SIZE = 128

N_META = 16
D_MIX = D_MODEL
D_ATTN = D_MIX // 2
H_A = 8
DH = D_ATTN // H_A
D_POOL = D_MIX - D_ATTN
POOL_WINDOWS = (2, 4, 8, 16)
POOL_GROUPS = len(POOL_WINDOWS)
POOL_CH = D_POOL // POOL_GROUPS
POOL_STATE = max(POOL_WINDOWS) - 1
D_FF = 2816
CONV_W = 3
Q_BLOCK = 128
EPS = 1e-6
ATTN_SCALE = DH ** -0.5
SB_BIAS_INIT = -7.0

kernel_name = "hymba_stickbreak_pool_convffn_step"


def rmsnorm(x, g):
    xf = x.astype(jnp.float32)
    r = lax.rsqrt(jnp.mean(xf * xf, axis=-1, keepdims=True) + EPS)
    return (xf * r * g.astype(jnp.float32)).astype(x.dtype)


def stick_breaking(q, k, v, q_pos, k_pos, sb_bias):
    z = jnp.einsum('bqhd,bkhd->bhqk', q.astype(jnp.float32), k.astype(jnp.float32)) * ATTN_SCALE
    z = z + sb_bias.astype(jnp.float32)[None, :, None, None]
    valid = k_pos[None, :] < q_pos[:, None]
    neg = jnp.where(valid, -jax.nn.softplus(z), 0.0)
    suffix = lax.cumsum(neg, axis=3, reverse=True) - neg
    a = jnp.where(valid, jnp.exp(jax.nn.log_sigmoid(z) + suffix), 0.0)
    o = jnp.einsum('bhqk,bkhd->bqhd', a, v.astype(jnp.float32))
    return o.astype(q.dtype)


def multiscale_pool(u, p0, n_keep, pool_w, pool_scale):
    B, L, _ = u.shape
    uf = u.astype(jnp.float32).reshape(B, L, POOL_GROUPS, POOL_CH)
    csum = jnp.cumsum(uf, axis=1)
    pos = p0 + jnp.arange(L)
    diffs = []
    for g, w in enumerate(POOL_WINDOWS):
        c = csum[:, :, g]
        lagged = jnp.pad(c, ((0, 0), (w, 0), (0, 0)))[:, :L]
        cnt = jnp.minimum(w, pos + 1).astype(jnp.float32)
        diffs.append((c - lagged) / cnt[None, :, None] - uf[:, :, g])
    d = jnp.stack(diffs, axis=2)[:, L - n_keep:]
    out = jnp.einsum('blgc,gce->blge', d, pool_w.astype(jnp.float32)) * pool_scale.astype(jnp.float32)
    return out.reshape(B, n_keep, D_POOL).astype(u.dtype)


def project_mix(h, w_in):
    B, L, _ = h.shape
    p = h @ w_in
    q = p[..., :D_ATTN].reshape(B, L, H_A, DH)
    k = p[..., D_ATTN:2 * D_ATTN].reshape(B, L, H_A, DH)
    v = p[..., 2 * D_ATTN:3 * D_ATTN].reshape(B, L, H_A, DH)
    u = p[..., 3 * D_ATTN:]
    return q, k, v, u


def conv_ffn(h, prefix, w_up, conv_w, conv_b, w_down):
    up = h @ w_up
    L = up.shape[1]
    ext = jnp.concatenate([prefix.astype(up.dtype), up], axis=1)
    c = conv_b + sum(conv_w[i] * ext[:, i:i + L] for i in range(CONV_W))
    gate, val = c[..., :D_FF], c[..., D_FF:]
    out = (jax.nn.silu(gate) * val) @ w_down
    return out, ext[:, L + 2 - (CONV_W - 1):]


def setup_inputs(seed: int = 0) -> dict:
    key = jax.random.key(seed)
    ks = jax.random.split(key, 24)
    n_pages = PAST_LEN // PAGE_SIZE
    n_used = DEC_BATCH * n_pages
    n_pool = n_used + n_used // 4
    f32 = jnp.float32
    nrm = lambda k, shape, s: jax.random.normal(k, shape, f32) * s
    perm = jax.random.permutation(ks[0], n_pool)[:n_used]
    page_table = perm.reshape(DEC_BATCH, n_pages).astype(jnp.int32)
    return {
        "x_prompt": nrm(ks[1], (BATCH, SEQ, D_MODEL), 1.0),
        "x_sample": nrm(ks[2], (DEC_BATCH, DEC_SEQ, D_MODEL), 1.0),
        "cache_k": nrm(ks[3], (n_pool, PAGE_SIZE, H_A, DH), 1.0),
        "cache_v": nrm(ks[4], (n_pool, PAGE_SIZE, H_A, DH), 1.0),
        "state_pool": nrm(ks[5], (DEC_BATCH, POOL_STATE, D_POOL), 1.0),
        "state_conv": nrm(ks[6], (DEC_BATCH, CONV_W - 1, 2 * D_FF), 1.0),
        "page_table": page_table,
        "meta_tokens": nrm(ks[7], (N_META, D_MODEL), 1.0),
        "norm_mix_g": 1.0 + nrm(ks[8], (D_MODEL,), 0.05),
        "w_in": nrm(ks[9], (D_MODEL, 3 * D_ATTN + D_POOL), D_MODEL ** -0.5),
        "sb_bias": SB_BIAS_INIT + nrm(ks[19], (H_A,), 0.1),
        "pool_w": nrm(ks[10], (POOL_GROUPS, POOL_CH, POOL_CH), POOL_CH ** -0.5),
        "pool_scale": 1.0 + nrm(ks[11], (POOL_GROUPS, POOL_CH), 0.1),
        "w_out": nrm(ks[12], (D_MIX, D_MODEL), D_MIX ** -0.5),
        "norm_ffn_g": 1.0 + nrm(ks[13], (D_MODEL,), 0.05),
        "w_up": nrm(ks[14], (D_MODEL, 2 * D_FF), D_MODEL ** -0.5),
        "conv_w": nrm(ks[15], (CONV_W, 2 * D_FF), CONV_W ** -0.5),
        "conv_b": nrm(ks[16], (2 * D_FF,), 0.02),
        "w_down": nrm(ks[17], (D_FF, D_MODEL), D_FF ** -0.5),
        "norm_final_g": 1.0 + nrm(ks[18], (D_MODEL,), 0.05),
    }


def reference(x_prompt, x_sample, cache_k, cache_v, state_pool, state_conv, page_table,
              meta_tokens, norm_mix_g, w_in, sb_bias, pool_w, pool_scale, w_out, norm_ffn_g,
              w_up, conv_w, conv_b, w_down, norm_final_g):
    B = x_prompt.shape[0]
    T = N_META + SEQ
    n_blocks = SEQ // Q_BLOCK

    meta = jnp.broadcast_to(meta_tokens.astype(x_prompt.dtype)[None], (B, N_META, D_MODEL))
    xp = jnp.concatenate([meta, x_prompt], axis=1)
    for _ in range(DEPTH):
        h = rmsnorm(xp, norm_mix_g)
        q, k, v, u = project_mix(h, w_in)
        k_prompt, v_prompt = k, v
        k_pos = jnp.arange(T)
        o_meta = stick_breaking(q[:, :N_META], k[:, :N_META], v[:, :N_META],
                                jnp.arange(N_META), jnp.arange(N_META), sb_bias)
        q_blk = q[:, N_META:].reshape(B, n_blocks, Q_BLOCK, H_A, DH).transpose(1, 0, 2, 3, 4)
        qpos_blk = (N_META + jnp.arange(SEQ)).reshape(n_blocks, Q_BLOCK)
        o_real = lax.map(lambda a: stick_breaking(a[0], k, v, a[1], k_pos, sb_bias), (q_blk, qpos_blk))
        o_real = o_real.transpose(1, 0, 2, 3, 4).reshape(B, SEQ, H_A, DH)
        o_attn = jnp.concatenate([o_meta, o_real], axis=1).reshape(B, T, D_ATTN)
        o_pool = multiscale_pool(u, 0, T, pool_w, pool_scale)
        pool_prompt = u[:, T - POOL_STATE:]
        xp = xp + jnp.concatenate([o_attn, o_pool], axis=-1) @ w_out
        h2 = rmsnorm(xp, norm_ffn_g)
        zero_prefix = jnp.zeros((B, CONV_W - 1, 2 * D_FF), xp.dtype)
        f, conv_prompt = conv_ffn(h2, zero_prefix, w_up, conv_w, conv_b, w_down)
        xp = xp + f
    y_prompt = rmsnorm(xp, norm_final_g)[:, N_META:]

    DB, L = x_sample.shape[0], x_sample.shape[1]
    xs = x_sample
    for _ in range(DEPTH):
        h = rmsnorm(xs, norm_mix_g)
        q, k_new, v_new, u_new = project_mix(h, w_in)
        k_sample, v_sample = k_new, v_new
        k_past = cache_k[page_table].reshape(DB, PAST_LEN, H_A, DH)
        v_past = cache_v[page_table].reshape(DB, PAST_LEN, H_A, DH)
        k_all = jnp.concatenate([k_past.astype(k_new.dtype), k_new], axis=1)
        v_all = jnp.concatenate([v_past.astype(v_new.dtype), v_new], axis=1)
        o_attn = stick_breaking(q, k_all, v_all, PAST_LEN + jnp.arange(L),
                                jnp.arange(PAST_LEN + L), sb_bias).reshape(DB, L, D_ATTN)
        u_ext = jnp.concatenate([state_pool.astype(u_new.dtype), u_new], axis=1)
        o_pool = multiscale_pool(u_ext, PAST_LEN - POOL_STATE, L, pool_w, pool_scale)
        pool_sample = u_ext[:, u_ext.shape[1] - POOL_STATE:]
        xs = xs + jnp.concatenate([o_attn, o_pool], axis=-1) @ w_out
        h2 = rmsnorm(xs, norm_ffn_g)
        f, conv_sample = conv_ffn(h2, state_conv, w_up, conv_w, conv_b, w_down)
        xs = xs + f
    y_sample = rmsnorm(xs, norm_final_g)

    return (y_prompt, y_sample, k_prompt, v_prompt, pool_prompt, conv_prompt,
            k_sample, v_sample, pool_sample, conv_sample)
```

```python
import numpy as np
from contextlib import ExitStack
import concourse.bass as bass
import concourse.mybir as mybir
from concourse.bass_utils import run_bass_kernel_spmd

F32 = mybir.dt.float32
BF16 = mybir.dt.bfloat16
I32 = mybir.dt.int32
AF = mybir.ActivationFunctionType
ALU = mybir.AluOpType
AX = mybir.AxisListType

D = 1024
T = 4112
NBLK = 33
TP = NBLK * 128
N_META = 16
NS = 5
W = 412
HALO = 16
WR = W + HALO
OWN = 410
DFF = 2816
NFC = DFF // 128
EPS = 1e-6
NEG = -30000.0
QT = [(0, 128), (128, 128), (256, 128), (384, 28)]
N_SEQ = 4
NPG = 64
SAME_ENGINE_SYNC = True


def slot_start(g, k):
    return 14 + 2048 * g + OWN * k


def n_kb_for_slot(k):
    last = slot_start(1, k) + W - 1
    return min(NBLK, last // 128 + 1)


def kb_needs_mask(k, kb):
    return not (128 * kb + 127 < slot_start(0, k))


class SemObj:
    def __init__(self, sem):
        self.sem = sem
        self.cnt = 0


class Eng(SemObj):
    def __init__(self, sem, h, name):
        super().__init__(sem)
        self.h = h
        self.name = name
        self.waited = {}


class Res:
    __slots__ = ("w", "r", "excl")

    def __init__(self, excl=False):
        self.w = None
        self.r = {}
        self.excl = excl


class Prog:
    def __init__(self, nc, es):
        self.nc = nc
        self.es = es
        mk = lambda n: es.enter_context(nc.semaphore(n))
        self.pe = Eng(mk("s_pe"), nc.tensor, "pe")
        self.act = Eng(mk("s_act"), nc.scalar, "act")
        self.dve = Eng(mk("s_dve"), nc.vector, "dve")
        self.pool = Eng(mk("s_pool"), nc.gpsimd, "pool")
        self.sp = Eng(mk("s_sp"), nc.sync, "sp")
        self.dsems = {}
        for q in ("sp", "pool", "act"):
            self.dsems[q] = [SemObj(mk(f"d_{q}{i}")) for i in range(12)]
        self.dnext = {"sp": 0, "pool": 0, "act": 0}

    def _wait(self, eng, toks):
        for (so, v) in toks:
            if so is eng and (not SAME_ENGINE_SYNC or eng is self.pe):
                continue
            if eng.waited.get(so, 0) < v:
                eng.h.wait_ge(so.sem, v)
                eng.waited[so] = v

    @staticmethod
    def _deps(reads, writes):
        toks = []
        for r in reads:
            if r.w is not None:
                toks.append(r.w)
            if r.excl:
                toks.extend(r.r.items())
        for w in writes:
            if w.w is not None:
                toks.append(w.w)
            toks.extend(w.r.items())
        return toks

    @staticmethod
    def _record(tok, reads, writes):
        so, v = tok
        for r in reads:
            if r.r.get(so, 0) < v:
                r.r[so] = v
        for w in writes:
            w.w = tok
            w.r = {}

    def op(self, eng, fn, reads=(), writes=(), inc=True):
        self._wait(eng, self._deps(reads, writes))
        ins = fn()
        if inc:
            ins.then_inc(eng.sem, 1)
            eng.cnt += 1
            tok = (eng, eng.cnt)
        else:
            tok = (eng, eng.cnt + 1)
        self._record(tok, reads, writes)
        return ins

    def dma(self, q, out, in_, reads=(), writes=(), fn=None, **kw):
        eng = {"sp": self.sp, "pool": self.pool, "act": self.act}[q]
        lst = self.dsems[q]
        d = lst[self.dnext[q] % len(lst)]
        self.dnext[q] += 1
        toks = self._deps(reads, writes)
        if d.cnt > 0:
            toks.append((d, d.cnt))
        self._wait(eng, toks)
        if fn is not None:
            fn().then_inc(d.sem, 16)
        else:
            eng.h.dma_start(out=out, in_=in_, **kw).then_inc(d.sem, 16)
        d.cnt += 16
        self._record((d, d.cnt), reads, writes)

    def final_wait(self):
        toks = []
        for q in self.dsems:
            for d in self.dsems[q]:
                if d.cnt:
                    toks.append((d, d.cnt))
        for e in (self.pe, self.act, self.dve, self.pool):
            if e.cnt:
                toks.append((e, e.cnt))
        self._wait(self.sp, toks)


class Tl:
    def __init__(self, t, excl=False):
        self.t = t
        self.res = Res(excl)

    def __getitem__(self, k):
        return self.t[k]


WX = W + 4
NPOOL = [2560]
import os
DBG = int(os.environ.get('KDBG', '99'))


def build(with_sample=True, stop_after=None):
    nc = bass.Bass("TRN2", target_bir_lowering=False)
    dr = lambda name, shape, dt=F32, kind="ExternalInput": nc.dram_tensor(name, shape, dt, kind=kind).ap()
    xall = dr("xall", [TP, D])
    xslot = dr("xslot", [NS, WR, D])
    qpos_d = dr("qpos", [1, NS * W])
    w_in_d = dr("w_in", [D, 2048])
    w_out_d = dr("w_out", [D, D])
    w_up_d = dr("w_up", [D, 2 * DFF])
    w_down_d = dr("w_down", [DFF, D])
    g_mix_d = dr("norm_mix_g", [D, 1])
    g_ffn_d = dr("norm_ffn_g", [D, 1])
    g_fin_d = dr("norm_final_g", [1, D])
    sbb_d = dr("sb_bias", [1, 8])
    pool_w_d = dr("pool_w", [4, 128, 128])
    pool_s_d = dr("pool_scale", [4, 128])
    conv_w_d = dr("conv_w", [3, 2 * DFF])
    conv_b_d = dr("conv_b", [1, 2 * DFF])
    y_o = dr("y_slot", [NS, W, D], kind="ExternalOutput")
    k_o = dr("k_all", [TP, 512], kind="ExternalOutput")
    v_o = dr("v_all", [TP, 512], kind="ExternalOutput")
    pp_o = dr("pool_last", [15, 512], kind="ExternalOutput")
    cp_o = dr("conv_last", [2, 2 * DFF], kind="ExternalOutput")
    o_scr = nc.dram_tensor("o_scr", [NS, 128, 12, WX], BF16).ap()
    if with_sample:
        xsam = dr("x_sample", [N_SEQ, D])
        ck_d = dr("cache_k", [NPOOL[0] * 128, 512])
        cv_d = dr("cache_v", [NPOOL[0] * 128, 512])
        stp_d = dr("state_pool", [N_SEQ * 15, 512])
        stc_d = dr("state_conv", [N_SEQ * 2, 2 * DFF])
        pt_d = dr("page_table", [1, N_SEQ * NPG], I32)
        ys_o = dr("y_sample", [N_SEQ, D], kind="ExternalOutput")
        ks_o = dr("k_sample", [N_SEQ, 512], kind="ExternalOutput")
        vs_o = dr("v_sample", [N_SEQ, 512], kind="ExternalOutput")
        ps_o = dr("pool_sample", [N_SEQ, 15, 512], kind="ExternalOutput")
        cs_o = dr("conv_sample", [N_SEQ, 2, 2 * DFF], kind="ExternalOutput")
        q_scr = nc.dram_tensor("q_scr", [N_SEQ, 512], F32).ap()

    es = ExitStack()
    with es:
        P = Prog(nc, es)
        pe, act, dve, pool = P.pe, P.act, P.dve, P.pool

        def sb(name, shape, dt=F32, st=es):
            return Tl(st.enter_context(nc.sbuf_tensor(name, shape, dt)))

        def ps(name, shape, dt=F32):
            return Tl(es.enter_context(nc.psum_tensor(name, shape, dt)), excl=True)

        pT = ps("pT", [128, 1024], BF16)
        pG = [ps(f"pG{i}", [128, 512], F32) for i in range(5)]
        pO = [ps(f"pO{i}", [128, 512], F32) for i in range(2)]
        gi = [0]

        def next_pg():
            t = pG[gi[0] % len(pG)]
            gi[0] += 1
            return t

        ident = sb("ident", [128, 128], BF16)
        identf = sb("identf", [128, 128], F32)
        tri = sb("tri", [128, 128], BF16)
        negones = sb("negones", [128, 128], BF16)
        onesf = sb("onesf", [128, 128], F32)
        negf = sb("negf", [128, 128], F32)
        kpos = sb("kpos", [128, NBLK], F32)
        mhalf = sb("mhalf", [128, 1], F32)
        P.op(pool, lambda: nc.gpsimd.memset(onesf[:], 1.0), writes=[onesf.res])
        P.op(pool, lambda: nc.gpsimd.memset(negf[:], -1.0), writes=[negf.res])
        P.op(pool, lambda: nc.gpsimd.memset(mhalf[:], -0.5), writes=[mhalf.res])
        P.op(pool, lambda: nc.gpsimd.memset(negones[:], -1.0), writes=[negones.res])
        P.op(pool, lambda: nc.gpsimd.affine_select(out=identf[:], in_=onesf[:], pattern=[[-1, 128]],
                                                    compare_op=ALU.is_equal, fill=0.0, base=0, channel_multiplier=1),
             reads=[onesf.res], writes=[identf.res])
        P.op(pool, lambda: nc.gpsimd.tensor_copy(out=ident[:], in_=identf[:]), reads=[identf.res], writes=[ident.res])
        P.op(pool, lambda: nc.gpsimd.affine_select(out=tri[:], in_=negf[:], pattern=[[-1, 128]],
                                                    compare_op=ALU.is_ge, fill=0.0, base=0, channel_multiplier=1),
             reads=[negf.res], writes=[tri.res])
        P.op(pool, lambda: nc.gpsimd.iota(kpos[:], pattern=[[128, NBLK]], base=0, channel_multiplier=1,
                                          allow_small_or_imprecise_dtypes=True), writes=[kpos.res])

        bias_t = sb("bias_t", [128, 8])
        P.dma("sp", bias_t[:], sbb_d.partition_broadcast(128), writes=[bias_t.res])
        gfin = sb("gfin", [128, D])
        P.dma("sp", gfin[:], g_fin_d.partition_broadcast(128), writes=[gfin.res])
        gmix = sb("gmix", [128, 8])
        P.dma("sp", gmix[:], g_mix_d.rearrange("(c p) o -> p (c o)", p=128), writes=[gmix.res],
              allow_slow_non_contiguous=True)
        gffn = sb("gffn", [128, 8])
        P.dma("sp", gffn[:], g_ffn_d.rearrange("(c p) o -> p (c o)", p=128), writes=[gffn.res],
              allow_slow_non_contiguous=True)
        gmix_bc = sb("gmix_bc", [128, 8, 128])
        gffn_bc = sb("gffn_bc", [128, 8, 128])
        for dc in range(8):
            P.op(pool, lambda dc=dc: nc.gpsimd.tensor_scalar(out=gmix_bc[:, dc, :], in0=onesf[:], scalar1=gmix[:, dc:dc + 1],
                                                             scalar2=None, op0=ALU.mult),
                 reads=[onesf.res, gmix.res], writes=[gmix_bc.res])
            P.op(pool, lambda dc=dc: nc.gpsimd.tensor_scalar(out=gffn_bc[:, dc, :], in0=onesf[:], scalar1=gffn[:, dc:dc + 1],
                                                             scalar2=None, op0=ALU.mult),
                 reads=[onesf.res, gffn.res], writes=[gffn_bc.res])
        qpos_bc = sb("qpos_bc", [128, NS * W])
        P.dma("sp", qpos_bc[:], qpos_d.partition_broadcast(128), writes=[qpos_bc.res])
        pscale = sb("pscale", [128, 4])
        P.dma("sp", pscale[:], pool_s_d.rearrange("g c -> c g"), writes=[pscale.res], allow_slow_non_contiguous=True)
        cw = sb("cw", [128, 3, 44])
        cb = sb("cb", [128, 44])
        for q4 in range(4):
            cs_ = slice(q4 * 1408, (q4 + 1) * 1408)
            for i3 in range(3):
                P.dma("sp", cw[:, i3, q4 * 11:(q4 + 1) * 11], conv_w_d[i3:i3 + 1, cs_].rearrange("o (c p) -> p (o c)", p=128),
                      writes=[cw.res], allow_slow_non_contiguous=True)
            P.dma("sp", cb[:, q4 * 11:(q4 + 1) * 11], conv_b_d[:, cs_].rearrange("o (c p) -> p (o c)", p=128),
                  writes=[cb.res], allow_slow_non_contiguous=True)
        pw_bf = sb("pw_bf", [128, 4, 128], BF16)
        P.dma("pool", pw_bf[:], pool_w_d.rearrange("g c e -> c g e"), writes=[pw_bf.res])

        junk = [sb(f"junk{i}", [128, D], BF16) for i in range(2)]
        ssq = [sb(f"ssq{i}", [128, 1]) for i in range(4)]
        xn = [sb(f"xn{i}", [128, D], BF16) for i in range(2)]
        cnt = {"x": 0, "ss": 0, "xn": 0, "hT": 0, "j": 0}

        def rms_rows(x_ap_fn, x_res, rows, out_bf_tl):
            jk = junk[cnt["j"] % 2]; cnt["j"] += 1
            ss = ssq[cnt["ss"] % 4]; cnt["ss"] += 1
            P.op(act, lambda: nc.scalar.activation(out=jk[0:rows, :], in_=x_ap_fn(), func=AF.Square,
                                                   accum_out=ss[0:rows, :]),
                 reads=[x_res], writes=[jk.res, ss.res])
            P.op(dve, lambda: nc.vector.tensor_scalar(out=ss[0:rows, :], in0=ss[0:rows, :], scalar1=1.0 / D,
                                                      scalar2=EPS, op0=ALU.mult, op1=ALU.add),
                 reads=[ss.res], writes=[ss.res])
            P.op(pool, lambda: nc.gpsimd.tensor_tensor(out=ss[0:rows, :], in0=ss[0:rows, :], in1=mhalf[0:rows, :],
                                                       op=ALU.pow),
                 reads=[ss.res, mhalf.res], writes=[ss.res])
            if out_bf_tl is not None:
                P.op(dve, lambda: nc.vector.tensor_scalar(out=out_bf_tl[0:rows, :], in0=x_ap_fn(),
                                                          scalar1=ss[0:rows, 0:1], scalar2=None, op0=ALU.mult),
                     reads=[x_res, ss.res], writes=[out_bf_tl.res])
            return ss

        def transpose_rows(xn_tl, rows, dst_ap_fn, dst_res, g_bc):
            for dc in range(8):
                P.op(pe, lambda dc=dc: nc.tensor.transpose(out=pT[:, dc * 128:dc * 128 + rows],
                                                           in_=xn_tl[0:rows, dc * 128:(dc + 1) * 128],
                                                           identity=ident[0:rows, 0:rows]),
                     reads=[xn_tl.res, ident.res], writes=[pT.res], inc=(dc == 7))
            P.op(dve, lambda: nc.vector.tensor_tensor(
                out=dst_ap_fn(), in0=pT[:].rearrange("p (c t) -> p c t", c=8)[:, :, 0:rows],
                in1=g_bc[:, :, 0:rows], op=ALU.mult),
                reads=[pT.res, g_bc.res], writes=[dst_res])

        def barrier():
            toks = [(e, e.cnt) for e in (pe, act, dve, pool) if e.cnt]
            for q in P.dsems:
                toks += [(d, d.cnt) for d in P.dsems[q] if d.cnt]
            for e in (pe, act, dve, pool, P.sp):
                P._wait(e, toks)

        hT_sam = sb("hT_sam", [128, 8, N_SEQ], BF16) if with_sample else None
        if stop_after == "consts":
            P.final_wait()
            return nc

        esAB = ExitStack()
        KT = sb("KT", [128, 4, TP], BF16, esAB)
        Vb = sb("Vb", [128, NBLK, 512], BF16, esAB)
        kt_res = [Res() for _ in range(NBLK)]
        v_res = [Res() for _ in range(NBLK)]
        with ExitStack() as esA:
            w_kv = sb("w_kv", [128, 8, 1024], BF16, esA)
            P.dma("pool", w_kv[:], w_in_d[:, 512:1536].rearrange("(c p) n -> p c n", p=128), writes=[w_kv.res])
            xt = [sb(f"xt{i}", [128, D], F32, esA) for i in range(2)]
            hT = [sb(f"hT{i}", [128, 8, 128], BF16, esA) for i in range(2)]
            ko = [sb(f"ko{i}", [128, 512], F32, esA) for i in range(2)]
            vo = [sb(f"vo{i}", [128, 512], F32, esA) for i in range(2)]
            nblocks = NBLK + (1 if with_sample else 0)
            if stop_after and stop_after.startswith('A') and len(stop_after) > 1:
                nblocks = int(stop_after[1:])
            for tb in range(nblocks):
                sam = tb == NBLK
                rows = N_SEQ if sam else 128
                x_tl = xt[tb % 2]
                if sam:
                    P.dma("sp", x_tl[0:rows, :], xsam[:, :], writes=[x_tl.res])
                else:
                    P.dma("sp", x_tl[:], xall[tb * 128:(tb + 1) * 128, :], writes=[x_tl.res])
                if DBG < 2:
                    continue
                xn_tl = xn[cnt["xn"] % 2]; cnt["xn"] += 1
                rms_rows(lambda x_tl=x_tl, rows=rows: x_tl[0:rows, :], x_tl.res, rows, xn_tl)
                if DBG < 3:
                    continue
                if sam:
                    h_ap = lambda: hT_sam[:]
                    h_res = hT_sam.res
                    h_rd = lambda dc: hT_sam[:, dc, :]
                else:
                    h_tl = hT[tb % 2]
                    h_ap = lambda h_tl=h_tl: h_tl[:]
                    h_res = h_tl.res
                    h_rd = lambda dc, h_tl=h_tl: h_tl[:, dc, :]
                transpose_rows(xn_tl, rows, h_ap, h_res, gmix_bc)
                if DBG < 4:
                    continue
                if not sam:
                    pk = next_pg()
                    for j in range(4):
                        for dc in range(8):
                            P.op(pe, lambda j=j, dc=dc, pk=pk: nc.tensor.matmul(
                                pk[:, j * 128:(j + 1) * 128], lhsT=w_kv[:, dc, j * 128:(j + 1) * 128],
                                rhs=h_rd(dc), start=(dc == 0), stop=(dc == 7)),
                                reads=[w_kv.res, h_res], writes=[pk.res], inc=(j == 3 and dc == 7))
                    P.op(act, lambda pk=pk, tb=tb: nc.scalar.copy(
                        out=KT[:, :, tb * 128:(tb + 1) * 128], in_=pk[:].rearrange("p (j t) -> p j t", j=4)),
                        reads=[pk.res], writes=[kt_res[tb]])
                if DBG < 5:
                    continue
                pk2 = next_pg()
                for dc in range(8):
                    P.op(pe, lambda dc=dc, pk2=pk2: nc.tensor.matmul(
                        pk2[0:rows, :], lhsT=h_rd(dc), rhs=w_kv[:, dc, 0:512], start=(dc == 0), stop=(dc == 7)),
                        reads=[w_kv.res, h_res], writes=[pk2.res], inc=(dc == 7))
                ko_tl = ko[tb % 2]
                P.op(act, lambda pk2=pk2, ko_tl=ko_tl: nc.scalar.copy(out=ko_tl[0:rows, :], in_=pk2[0:rows, :]),
                     reads=[pk2.res], writes=[ko_tl.res])
                P.dma("sp", (ks_o[:, :] if sam else k_o[tb * 128:(tb + 1) * 128, :]), ko_tl[0:rows, :], reads=[ko_tl.res])
                if DBG < 6:
                    continue
                pv = next_pg()
                for dc in range(8):
                    P.op(pe, lambda dc=dc, pv=pv: nc.tensor.matmul(
                        pv[0:rows, :], lhsT=h_rd(dc), rhs=w_kv[:, dc, 512:1024], start=(dc == 0), stop=(dc == 7)),
                        reads=[w_kv.res, h_res], writes=[pv.res], inc=(dc == 7))
                vo_tl = vo[tb % 2]
                P.op(act, lambda pv=pv, vo_tl=vo_tl: nc.scalar.copy(out=vo_tl[0:rows, :], in_=pv[0:rows, :]),
                     reads=[pv.res], writes=[vo_tl.res])
                if not sam:
                    P.op(dve, lambda vo_tl=vo_tl, tb=tb: nc.vector.tensor_copy(out=Vb[:, tb, :], in_=vo_tl[:]),
                         reads=[vo_tl.res], writes=[v_res[tb]])
                P.dma("sp", (vs_o[:, :] if sam else v_o[tb * 128:(tb + 1) * 128, :]), vo_tl[0:rows, :], reads=[vo_tl.res])
            barrier()
        if stop_after and stop_after.startswith("A"):
            P.final_wait()
            esAB.close()
            return nc

        with ExitStack() as esB:
            w_qu = sb("w_qu", [128, 8, 1024], BF16, esB)
            P.dma("pool", w_qu[:, :, 0:512], w_in_d[:, 0:512].rearrange("(c p) n -> p c n", p=128), writes=[w_qu.res])
            P.dma("pool", w_qu[:, :, 512:1024], w_in_d[:, 1536:2048].rearrange("(c p) n -> p c n", p=128), writes=[w_qu.res])
            o_at = sb("o_at", [64, 8, WX], BF16, esB)
            o_pl = sb("o_pl", [128, 4, WX], BF16, esB)
            P.op(pool, lambda: nc.gpsimd.memset(o_at[:], 0.0), writes=[o_at.res])
            P.op(pool, lambda: nc.gpsimd.memset(o_pl[:], 0.0), writes=[o_pl.res])
            if with_sample:
                with ExitStack() as esS:
                    esB_save = esB
                    L_ = dict(locals())
                    L_["esB"] = esS
                    sample_mixer(nc, P, L_)
                    barrier()
            xs1 = [sb(f"xs1_{i}", [128, D], F32, esB) for i in range(2)]
            hTs = sb("hTs", [128, 8, WR], BF16, esB)
            qT = sb("qT", [128, 4, W], BF16, esB)
            uT = sb("uT", [128, 4, WR], F32, esB)
            lv = [sb(f"lv{i}", [128, WR], F32, esB) for i in range(2)]
            icnt = sb("icnt", [128, W], F32, esB)
            dpool = [sb(f"dpool{i}", [128, W], BF16, esB) for i in range(2)]
            NMK = 21
            masks = sb("masks", [128, NMK, W], BF16, esB)
            mask_res = [Res() for _ in range(NMK)]
            e_t = [sb(f"e_t{i}", [128, W], F32, esB) for i in range(3)]
            sp_t = [sb(f"sp_t{i}", [128, W], BF16, esB) for i in range(3)]
            a_t = [sb(f"a_t{i}", [128, W], BF16, esB) for i in range(3)]
            w_t = [sb(f"w_t{i}", [128, W], F32, esB) for i in range(3)]
            acc32 = sb("acc32", [128, W], F32, esB)
            accbf = [sb(f"accbf{i}", [128, W], BF16, esB) for i in range(3)]


            for k in range(NS):
                for (r0, n) in [(0, HALO)] + [(HALO + c0, n) for (c0, n) in QT]:
                    xt_ = xs1[cnt["x"] % 2]; cnt["x"] += 1
                    P.dma("sp", xt_[0:n, :], xslot[k, r0:r0 + n, :], writes=[xt_.res])
                    xn_tl = xn[cnt["xn"] % 2]; cnt["xn"] += 1
                    rms_rows(lambda xt_=xt_, n=n: xt_[0:n, :], xt_.res, n, xn_tl)
                    transpose_rows(xn_tl, n, lambda r0=r0, n=n: hTs[:, :, r0:r0 + n], hTs.res, gmix_bc)
                for j in range(4):
                    pq = next_pg()
                    for dc in range(8):
                        P.op(pe, lambda j=j, dc=dc, pq=pq: nc.tensor.matmul(
                            pq[:, 0:W], lhsT=w_qu[:, dc, j * 128:(j + 1) * 128], rhs=hTs[:, dc, HALO:WR],
                            start=(dc == 0), stop=(dc == 7)),
                            reads=[w_qu.res, hTs.res], writes=[pq.res], inc=(dc == 7))
                    P.op(dve, lambda j=j, pq=pq: nc.vector.tensor_scalar(
                        out=qT[:, j, :], in0=pq[:, 0:W], scalar1=0.125, scalar2=None, op0=ALU.mult),
                        reads=[pq.res], writes=[qT.res])
                for g in range(4):
                    pu = next_pg()
                    for dc in range(8):
                        P.op(pe, lambda g=g, dc=dc, pu=pu: nc.tensor.matmul(
                            pu[:, 0:WR], lhsT=w_qu[:, dc, 512 + g * 128:512 + (g + 1) * 128], rhs=hTs[:, dc, :],
                            start=(dc == 0), stop=(dc == 7)),
                            reads=[w_qu.res, hTs.res], writes=[pu.res], inc=(dc == 7))
                    P.op(act, lambda g=g, pu=pu: nc.scalar.copy(out=uT[:, g, :], in_=pu[:, 0:WR]),
                         reads=[pu.res], writes=[uT.res])
                if k == NS - 1:
                    for g in range(4):
                        P.dma("sp", pp_o[:, g * 128:(g + 1) * 128].rearrange("r c -> c r"), uT[:, g, 411:426],
                              reads=[uT.res], allow_slow_non_contiguous=True)
                for g in range(4):
                    wg = 2 << g
                    cur, cur_res = (lambda g=g: uT[:, g, :]), uT.res
                    lo = 0
                    for lvl in range(g + 1):
                        sh = 1 << lvl
                        dst = lv[lvl % 2]
                        P.op(dve, lambda cur=cur, dst=dst, sh=sh, lo=lo: nc.vector.tensor_tensor(
                            out=dst[:, lo + sh:WR], in0=cur()[:, lo + sh:WR], in1=cur()[:, lo:WR - sh], op=ALU.add),
                            reads=[cur_res], writes=[dst.res])
                        cur, cur_res = (lambda dst=dst: dst[:]), dst.res
                        lo += sh
                    P.op(dve, lambda wg=wg, k=k: nc.vector.tensor_scalar(
                        out=icnt[:], in0=qpos_bc[:, k * W:(k + 1) * W], scalar1=1.0, scalar2=float(wg),
                        op0=ALU.add, op1=ALU.min), reads=[qpos_bc.res], writes=[icnt.res])
                    P.op(dve, lambda: nc.vector.reciprocal(out=icnt[:], in_=icnt[:]), reads=[icnt.res], writes=[icnt.res])
                    P.op(dve, lambda cur=cur: nc.vector.tensor_tensor(
                        out=icnt[:], in0=cur()[:, HALO:WR], in1=icnt[:], op=ALU.mult),
                        reads=[cur_res, icnt.res], writes=[icnt.res])
                    dp = dpool[g % 2]
                    P.op(dve, lambda g=g, dp=dp: nc.vector.tensor_tensor(
                        out=dp[:], in0=icnt[:], in1=uT[:, g, HALO:WR], op=ALU.subtract),
                        reads=[icnt.res, uT.res], writes=[dp.res])
                    pp = next_pg()
                    P.op(pe, lambda g=g, dp=dp, pp=pp: nc.tensor.matmul(
                        pp[:, 0:W], lhsT=pw_bf[:, g, :], rhs=dp[:], start=True, stop=True),
                        reads=[pw_bf.res, dp.res], writes=[pp.res])
                    P.op(dve, lambda g=g, pp=pp: nc.vector.tensor_scalar(
                        out=o_pl[:, g, 0:W], in0=pp[:, 0:W], scalar1=pscale[:, g:g + 1], scalar2=None, op0=ALU.mult),
                        reads=[pp.res, pscale.res], writes=[o_pl.res])
                nkb = n_kb_for_slot(k)
                mkb = [kb for kb in range(nkb) if kb_needs_mask(k, kb)]
                assert len(mkb) <= NMK, len(mkb)
                midx = {kb: i for i, kb in enumerate(mkb)}
                for kb in mkb:
                    i = midx[kb]
                    P.op(dve, lambda kb=kb, i=i, k=k: nc.vector.tensor_scalar(
                        out=masks[:, i, :], in0=qpos_bc[:, k * W:(k + 1) * W], scalar1=kpos[:, kb:kb + 1],
                        scalar2=NEG, op0=ALU.is_le, op1=ALU.mult),
                        reads=[qpos_bc.res, kpos.res], writes=[mask_res[i]])
                ucount = [0]
                for h in range(8):
                    attn_head(nc, P, h, nkb, midx, ucount, locals())
                P.dma("sp", o_scr[k, 0:64, 0:8, :], o_at[:], reads=[o_at.res])
                P.dma("sp", o_scr[k, :, 8:12, :], o_pl[:], reads=[o_pl.res])
            barrier()
        esAB.close()
        if stop_after == "B1":
            P.final_wait()
            return nc

        with ExitStack() as esC:
            w_oa = sb("w_oa", [64, 8, D], BF16, esC)
            P.dma("pool", w_oa[:], w_out_d[0:512, :].rearrange("(h p) n -> p h n", p=64), writes=[w_oa.res])
            w_op = sb("w_op", [128, 4, D], BF16, esC)
            P.dma("pool", w_op[:], w_out_d[512:1024, :].rearrange("(g p) n -> p g n", p=128), writes=[w_op.res])
            wdn = sb("wdn", [128, NFC, D], BF16, esC)
            for i in range(NFC):
                P.dma("pool", wdn[:, i, :], w_down_d[i * 128:(i + 1) * 128, :], writes=[wdn.res])
            xs = sb("xs2", [128, 5, D], F32, esC)
            o_at = sb("o_at2", [64, 8, WX], BF16, esC)
            o_pl = sb("o_pl2", [128, 4, WX], BF16, esC)
            h2T = sb("h2T", [128, 8, WX], BF16, esC)
            upsb = [sb(f"upsb{i}", [128, WX], F32, esC) for i in range(4)]
            cv_t = [sb(f"cv_t{i}", [128, WX], F32, esC) for i in range(4)]
            sg_t = [sb(f"sg_t{i}", [128, WX], F32, esC) for i in range(2)]
            actT = sb("actT", [128, NFC, WX], BF16, esC)
            upl = sb("upl", [128, 2, 44], F32, esC)
            wup_t = [sb(f"wup{i}", [128, 8, 256], BF16, esC) for i in range(3)]
            P.op(pool, lambda: nc.gpsimd.memset(actT[:], 0.0), writes=[actT.res])
            if with_sample:
                upl_s = sb("upl_s", [128, N_SEQ, 44], F32, esC)
                sc8 = [sb(f"sc8_{i}", [8, 128], F32, esC) for i in range(2)]
                pF = pO[1]

            for k in range(NS):
                wk = WX if (k == 0 and with_sample) else W
                tiles = list(QT) + ([(W, N_SEQ)] if (k == 0 and with_sample) else [])
                P.dma("sp", o_at[:], o_scr[k, 0:64, 0:8, :], writes=[o_at.res])
                P.dma("sp", o_pl[:], o_scr[k, :, 8:12, :], writes=[o_pl.res])
                for ti, (c0, n) in enumerate(tiles):
                    if ti < 4:
                        P.dma("sp", xs[0:n, ti, :], xslot[k, HALO + c0:HALO + c0 + n, :], writes=[xs.res])
                    else:
                        P.dma("sp", xs[0:n, ti, :], xsam[:, :], writes=[xs.res])
                for ti, (c0, n) in enumerate(tiles):
                    for half in range(2):
                        pw = next_pg()
                        for h in range(8):
                            P.op(pe, lambda h=h, pw=pw: nc.tensor.matmul(
                                pw[0:n, :], lhsT=o_at[:, h, c0:c0 + n], rhs=w_oa[:, h, half * 512:(half + 1) * 512],
                                start=(h == 0), stop=False),
                                reads=[o_at.res, w_oa.res], writes=[pw.res], inc=False)
                        for g in range(4):
                            P.op(pe, lambda g=g, pw=pw: nc.tensor.matmul(
                                pw[0:n, :], lhsT=o_pl[:, g, c0:c0 + n], rhs=w_op[:, g, half * 512:(half + 1) * 512],
                                start=False, stop=(g == 3)),
                                reads=[o_pl.res, w_op.res], writes=[pw.res], inc=(g == 3))
                        P.op(dve, lambda pw=pw, half=half, ti=ti, n=n: nc.vector.tensor_tensor(
                            out=xs[0:n, ti, half * 512:(half + 1) * 512], in0=pw[0:n, :],
                            in1=xs[0:n, ti, half * 512:(half + 1) * 512], op=ALU.add),
                            reads=[pw.res, xs.res], writes=[xs.res])
                    xn_tl = xn[cnt["xn"] % 2]; cnt["xn"] += 1
                    rms_rows(lambda ti=ti, n=n: xs[0:n, ti, :], xs.res, n, xn_tl)
                    transpose_rows(xn_tl, n, lambda c0=c0, n=n: h2T[:, :, c0:c0 + n], h2T.res, gffn_bc)
                for i in range(NFC):
                    wt = wup_t[i % 3]
                    P.dma("pool", wt[:, :, 0:128], w_up_d[:, i * 128:(i + 1) * 128].rearrange("(c p) f -> p c f", p=128),
                          writes=[wt.res])
                    P.dma("pool", wt[:, :, 128:256],
                          w_up_d[:, DFF + i * 128:DFF + (i + 1) * 128].rearrange("(c p) f -> p c f", p=128), writes=[wt.res])
                    cvs = []
                    for a in range(2):
                        fc = i + a * NFC
                        pu = next_pg()
                        for dc in range(8):
                            P.op(pe, lambda a=a, dc=dc, pu=pu, wt=wt: nc.tensor.matmul(
                                pu[:, 0:wk], lhsT=wt[:, dc, a * 128:(a + 1) * 128], rhs=h2T[:, dc, 0:wk],
                                start=(dc == 0), stop=(dc == 7)),
                                reads=[wt.res, h2T.res], writes=[pu.res], inc=(dc == 7))
                        us = upsb[(2 * i + a) % 4]
                        P.op(act, lambda pu=pu, us=us: nc.scalar.copy(out=us[:, 0:wk], in_=pu[:, 0:wk]),
                             reads=[pu.res], writes=[us.res])
                        if k == NS - 1:
                            P.op(pool, lambda us=us, fc=fc: nc.gpsimd.tensor_copy(out=upl[:, :, fc], in_=us[:, 408:410]),
                                 reads=[us.res], writes=[upl.res])
                        cv = cv_t[(2 * i + a) % 4]
                        eng = dve if a == 0 else pool
                        P.op(eng, lambda eng=eng, us=us, cv=cv, fc=fc: eng.h.tensor_scalar(
                            out=cv[:, 2:W], in0=us[:, 0:W - 2], scalar1=cw[:, 0, fc:fc + 1], scalar2=cb[:, fc:fc + 1],
                            op0=ALU.mult, op1=ALU.add), reads=[us.res, cw.res, cb.res], writes=[cv.res])
                        for tap in (1, 2):
                            P.op(dve, lambda us=us, cv=cv, fc=fc, tap=tap: nc.vector.scalar_tensor_tensor(
                                out=cv[:, 2:W], in0=us[:, tap:W - 2 + tap], scalar=cw[:, tap, fc:fc + 1], in1=cv[:, 2:W],
                                op0=ALU.mult, op1=ALU.add), reads=[us.res, cw.res, cv.res], writes=[cv.res])
                        if k == 0 and with_sample:
                            s8 = sc8[(2 * i + a) % 2]
                            P.dma("sp", s8[:], stc_d[:, fc * 128:(fc + 1) * 128], writes=[s8.res])
                            P.op(pe, lambda s8=s8: nc.tensor.transpose(out=pF[:, 0:8], in_=s8[0:8, :],
                                                                       identity=identf[0:8, 0:8]),
                                 reads=[s8.res, identf.res], writes=[pF.res])
                            P.op(pool, lambda us=us, fc=fc: nc.gpsimd.tensor_copy(out=upl_s[:, :, fc], in_=us[:, W:WX]),
                                 reads=[us.res], writes=[upl_s.res])
                            P.op(dve, lambda us=us, cv=cv, fc=fc: nc.vector.tensor_scalar(
                                out=cv[:, W:WX], in0=us[:, W:WX], scalar1=cw[:, 2, fc:fc + 1], scalar2=cb[:, fc:fc + 1],
                                op0=ALU.mult, op1=ALU.add), reads=[us.res, cw.res, cb.res], writes=[cv.res])
                            for r in (0, 1):
                                P.op(dve, lambda cv=cv, fc=fc, r=r: nc.vector.scalar_tensor_tensor(
                                    out=cv[:, W:WX], in0=pF[:, 0:8].rearrange("p (s r) -> p r s", r=2)[:, r, :],
                                    scalar=cw[:, r, fc:fc + 1], in1=cv[:, W:WX], op0=ALU.mult, op1=ALU.add),
                                    reads=[pF.res, cw.res, cv.res], writes=[cv.res])
                        cvs.append(cv)
                    sg = sg_t[i % 2]
                    P.op(act, lambda sg=sg, cv=cvs[0]: nc.scalar.activation(out=sg[:, 2:wk], in_=cv[:, 2:wk], func=AF.Silu),
                         reads=[cvs[0].res], writes=[sg.res])
                    P.op(dve, lambda sg=sg, cv=cvs[1], i=i: nc.vector.tensor_tensor(
                        out=actT[:, i, 2:wk], in0=sg[:, 2:wk], in1=cv[:, 2:wk], op=ALU.mult),
                        reads=[sg.res, cvs[1].res], writes=[actT.res])
                for ti, (c0, n) in enumerate(tiles):
                    for half in range(2):
                        pd = next_pg()
                        for i in range(NFC):
                            P.op(pe, lambda i=i, pd=pd: nc.tensor.matmul(
                                pd[0:n, :], lhsT=actT[:, i, c0:c0 + n], rhs=wdn[:, i, half * 512:(half + 1) * 512],
                                start=(i == 0), stop=(i == NFC - 1)),
                                reads=[actT.res, wdn.res], writes=[pd.res], inc=(i == NFC - 1))
                        P.op(dve, lambda pd=pd, half=half, ti=ti, n=n: nc.vector.tensor_tensor(
                            out=xs[0:n, ti, half * 512:(half + 1) * 512], in0=pd[0:n, :],
                            in1=xs[0:n, ti, half * 512:(half + 1) * 512], op=ALU.add),
                            reads=[pd.res, xs.res], writes=[xs.res])
                    ss = rms_rows(lambda ti=ti, n=n: xs[0:n, ti, :], xs.res, n, None)
                    P.op(dve, lambda ss=ss, ti=ti, n=n: nc.vector.scalar_tensor_tensor(
                        out=xs[0:n, ti, :], in0=xs[0:n, ti, :], scalar=ss[0:n, 0:1], in1=gfin[0:n, :],
                        op0=ALU.mult, op1=ALU.mult),
                        reads=[xs.res, ss.res, gfin.res], writes=[xs.res])
                    if ti < 4:
                        P.dma("sp", y_o[k, c0:c0 + n, :], xs[0:n, ti, :], reads=[xs.res])
                    else:
                        P.dma("sp", ys_o[:, :], xs[0:n, ti, :], reads=[xs.res])
            for r in range(2):
                for q4 in range(4):
                    P.dma("sp", cp_o[r:r + 1, q4 * 1408:(q4 + 1) * 1408].rearrange("r (c p) -> p (r c)", p=128),
                          upl[:, r, q4 * 11:(q4 + 1) * 11], reads=[upl.res], allow_slow_non_contiguous=True)
            if with_sample:
                for s in range(N_SEQ):
                    for q4 in range(4):
                        P.dma("sp", cs_o[s, 1:2, q4 * 1408:(q4 + 1) * 1408].rearrange("r (c p) -> p (r c)", p=128),
                              upl_s[:, s, q4 * 11:(q4 + 1) * 11], reads=[upl_s.res], allow_slow_non_contiguous=True)
                    P.dma("sp", cs_o[s, 0:1, :], stc_d[2 * s + 1:2 * s + 2, :])
            P.final_wait()
    return nc


def attn_head(nc, P, h, nkb, midx, ucount, L):
    pe, act, dve, pool = P.pe, P.act, P.dve, P.pool
    KT, Vb, qT, kt_res, v_res = L["KT"], L["Vb"], L["qT"], L["kt_res"], L["v_res"]
    masks, mask_res, ident, tri, negones = L["masks"], L["mask_res"], L["ident"], L["tri"], L["negones"]
    e_t, sp_t, a_t, acc32, accbf, w_t = L["e_t"], L["sp_t"], L["a_t"], L["acc32"], L["accbf"], L["w_t"]
    bias_t, o_at, pO, next_pg = L["bias_t"], L["o_at"], L["pO"], L["next_pg"]
    j, hb = h // 2, (h % 2) * 64
    po = pO[h % 2]
    kbs = list(range(nkb - 1, -1, -1))
    n = len(kbs)
    st_ = {}
    u0 = ucount[0]

    def s1(kb):
        p = next_pg()
        need_m = kb in midx
        P.op(pe, lambda: nc.tensor.matmul(
            p[:, 0:W], lhsT=KT[hb:hb + 64, j, kb * 128:(kb + 1) * 128], rhs=qT[hb:hb + 64, j, :],
            start=True, stop=not need_m),
            reads=[kt_res[kb], qT.res], writes=[p.res], inc=not need_m)
        if need_m:
            P.op(pe, lambda: nc.tensor.matmul(
                p[:, 0:W], lhsT=ident[:], rhs=masks[:, midx[kb], :], start=False, stop=True),
                reads=[ident.res, mask_res[midx[kb]]], writes=[p.res])
        st_[kb] = {"p": p}

    def s2(kb, u):
        p = st_[kb]["p"]
        e = e_t[u % 3]; s = sp_t[u % 3]
        P.op(act, lambda: nc.scalar.activation(out=e[:], in_=p[:, 0:W], func=AF.Exp,
                                               bias=bias_t[:, h:h + 1], scale=1.0),
             reads=[p.res, bias_t.res], writes=[e.res])
        P.op(act, lambda: nc.scalar.activation(out=s[:], in_=e[:], func=AF.Ln, bias=1.0, scale=1.0),
             reads=[e.res], writes=[s.res])
        st_[kb]["s"] = s
        st_[kb]["e"] = e

    def s3(kb, u, first):
        s = st_[kb]["s"]
        pc = next_pg()
        st_[kb]["pc"] = pc
        P.op(pe, lambda: nc.tensor.matmul(pc[:, 0:W], lhsT=tri[:], rhs=s[:], start=True, stop=first),
             reads=[tri.res, s.res], writes=[pc.res], inc=first)
        if not first:
            ab = accbf[(u - 1) % 3]
            P.op(pe, lambda: nc.tensor.matmul(pc[:, 0:W], lhsT=negones[:], rhs=ab[:], start=False, stop=True),
                 reads=[negones.res, ab.res], writes=[pc.res])
        abn = accbf[u % 3]
        if first:
            P.op(pool, lambda: nc.gpsimd.tensor_copy(out=acc32[:], in_=s[:]), reads=[s.res], writes=[acc32.res])
        else:
            P.op(pool, lambda: nc.gpsimd.tensor_tensor(out=acc32[:], in0=acc32[:], in1=s[:], op=ALU.add),
                 reads=[acc32.res, s.res], writes=[acc32.res])
        P.op(pool, lambda: nc.gpsimd.tensor_copy(out=abn[:], in_=acc32[:]), reads=[acc32.res], writes=[abn.res])

    def s4(kb, u):
        pc = st_[kb]["pc"]; e = st_[kb]["e"]
        w = w_t[u % 3]
        a = a_t[u % 3]
        P.op(act, lambda: nc.scalar.activation(out=w[:], in_=pc[:, 0:W], func=AF.Exp),
             reads=[pc.res], writes=[w.res])
        P.op(dve, lambda: nc.vector.tensor_tensor(out=a[:], in0=e[:], in1=w[:], op=ALU.mult),
             reads=[e.res, w.res], writes=[a.res])
        st_[kb]["a"] = a

    def s5(kb, first, last):
        a = st_[kb]["a"]
        P.op(pe, lambda: nc.tensor.matmul(po[0:64, 0:W], lhsT=Vb[:, kb, h * 64:(h + 1) * 64], rhs=a[:],
                                          start=first, stop=last),
             reads=[v_res[kb], a.res], writes=[po.res], inc=last)
        del st_[kb]

    s1(kbs[0])
    for i in range(n + 1):
        if i + 1 < n:
            s1(kbs[i + 1])
        if i < n:
            s2(kbs[i], u0 + i)
            s3(kbs[i], u0 + i, i == 0)
        if i >= 1:
            s4(kbs[i - 1], u0 + i - 1)
            s5(kbs[i - 1], i - 1 == 0, i - 1 == n - 1)
    ucount[0] += n
    P.op(dve, lambda: nc.vector.tensor_copy(out=o_at[:, h, 0:W], in_=po[0:64, 0:W]),
         reads=[po.res], writes=[o_at.res])


def sample_mixer(nc, P, L):
    pe, act, dve, pool = P.pe, P.act, P.dve, P.pool
    sb, esB, next_pg = L["sb"], L["esB"], L["next_pg"]
    hT_sam, w_qu, pw_bf, pscale, bias_t = L["hT_sam"], L["w_qu"], L["pw_bf"], L["pscale"], L["bias_t"]
    tri, negones, identf, o_at, o_pl = L["tri"], L["negones"], L["identf"], L["o_at"], L["o_pl"]
    ck_d, cv_d, stp_d, pt_d, ps_o, q_scr = L["ck_d"], L["cv_d"], L["stp_d"], L["pt_d"], L["ps_o"], L["q_scr"]
    NP = N_SEQ * NPG
    pti = sb("pti", [128, NP], I32, esB)
    ptf = sb("ptf", [128, NP], F32, esB)
    iop = sb("iop", [128, 1], F32, esB)
    idxi = sb("idxi", [128, NP], I32, esB)
    P.dma("sp", pti[:], pt_d.partition_broadcast(128), writes=[pti.res])
    P.op(pool, lambda: nc.gpsimd.iota(iop[:], pattern=[[0, 1]], base=0, channel_multiplier=1,
                                      allow_small_or_imprecise_dtypes=True), writes=[iop.res])
    P.op(pool, lambda: nc.gpsimd.tensor_copy(out=ptf[:], in_=pti[:]), reads=[pti.res], writes=[ptf.res])
    P.op(pool, lambda: nc.gpsimd.tensor_scalar(out=ptf[:], in0=ptf[:], scalar1=128.0, scalar2=iop[:, 0:1],
                                               op0=ALU.mult, op1=ALU.add), reads=[ptf.res, iop.res], writes=[ptf.res])
    P.op(pool, lambda: nc.gpsimd.tensor_copy(out=idxi[:], in_=ptf[:]), reads=[ptf.res], writes=[idxi.res])
    bd0 = sb("bd0", [8, 512], F32, esB)
    bd1 = sb("bd1", [8, 512], F32, esB)
    bdiag = sb("bdiag", [8, 512], F32, esB)
    ones8 = sb("ones8", [8, 1], BF16, esB)
    P.op(pool, lambda: nc.gpsimd.memset(bd0[:], 1.0), writes=[bd0.res])
    P.op(pool, lambda: nc.gpsimd.memset(ones8[:], 1.0), writes=[ones8.res])
    P.op(pool, lambda: nc.gpsimd.affine_select(out=bd1[:], in_=bd0[:], pattern=[[1, 512]], compare_op=ALU.is_ge,
                                                fill=0.0, base=0, channel_multiplier=-64),
         reads=[bd0.res], writes=[bd1.res])
    P.op(pool, lambda: nc.gpsimd.affine_select(out=bdiag[:], in_=bd1[:], pattern=[[-1, 512]], compare_op=ALU.is_ge,
                                                fill=0.0, base=63, channel_multiplier=64),
         reads=[bd1.res], writes=[bdiag.res])
    qscr_res = Res()
    pq = next_pg()
    for dc in range(8):
        P.op(pe, lambda dc=dc: nc.tensor.matmul(pq[0:N_SEQ, :], lhsT=hT_sam[:, dc, :], rhs=w_qu[:, dc, 0:512],
                                                start=(dc == 0), stop=(dc == 7)),
             reads=[hT_sam.res, w_qu.res], writes=[pq.res], inc=(dc == 7))
    q_tok = sb("q_tok", [N_SEQ, 512], F32, esB)
    P.op(dve, lambda: nc.vector.tensor_scalar(out=q_tok[:], in0=pq[0:N_SEQ, :], scalar1=0.125, scalar2=None, op0=ALU.mult),
         reads=[pq.res], writes=[q_tok.res])
    P.dma("sp", q_scr[:, :], q_tok[:], reads=[q_tok.res], writes=[qscr_res])
    pu = next_pg()
    for dc in range(8):
        P.op(pe, lambda dc=dc: nc.tensor.matmul(pu[0:N_SEQ, :], lhsT=hT_sam[:, dc, :], rhs=w_qu[:, dc, 512:1024],
                                                start=(dc == 0), stop=(dc == 7)),
             reads=[hT_sam.res, w_qu.res], writes=[pu.res], inc=(dc == 7))
    u_tok = sb("u_tok", [N_SEQ, 512], F32, esB)
    P.op(act, lambda: nc.scalar.copy(out=u_tok[:], in_=pu[0:N_SEQ, :]), reads=[pu.res], writes=[u_tok.res])
    P.dma("sp", ps_o[:, 14, :], u_tok[:], reads=[u_tok.res])
    for s in range(N_SEQ):
        P.dma("sp", ps_o[s, 0:14, :], stp_d[s * 15 + 1:s * 15 + 15, :])
    uTs = sb("uTs", [128, 4, N_SEQ], F32, esB)
    for g in range(4):
        pu2 = next_pg()
        for dc in range(8):
            P.op(pe, lambda g=g, dc=dc, pu2=pu2: nc.tensor.matmul(
                pu2[:, 0:N_SEQ], lhsT=w_qu[:, dc, 512 + g * 128:512 + (g + 1) * 128], rhs=hT_sam[:, dc, :],
                start=(dc == 0), stop=(dc == 7)),
                reads=[hT_sam.res, w_qu.res], writes=[pu2.res], inc=(dc == 7))
        P.op(act, lambda g=g, pu2=pu2: nc.scalar.copy(out=uTs[:, g, :], in_=pu2[:, 0:N_SEQ]),
             reads=[pu2.res], writes=[uTs.res])
    st60 = sb("st60", [N_SEQ * 15, 512], F32, esB)
    P.dma("sp", st60[:], stp_d[:, :], writes=[st60.res])
    stT = sb("stT", [128, 4, N_SEQ * 15], F32, esB)
    pst = next_pg()
    for g in range(4):
        P.op(pe, lambda g=g: nc.tensor.transpose(out=pst[:, g * 60:(g + 1) * 60], in_=st60[0:60, g * 128:(g + 1) * 128],
                                                 identity=identf[0:60, 0:60]),
             reads=[st60.res, identf.res], writes=[pst.res], inc=(g == 3))
    P.op(act, lambda: nc.scalar.copy(out=stT[:], in_=pst[:, 0:240].rearrange("p (g c) -> p g c", g=4)),
         reads=[pst.res], writes=[stT.res])
    ssum = sb("ssum", [128, N_SEQ], F32, esB)
    d_bf = [sb(f"d_bf{i}", [128, N_SEQ], BF16, esB) for i in range(2)]
    for g in range(4):
        wg = 2 << g
        nr = wg - 1
        P.op(dve, lambda g=g, nr=nr: nc.vector.tensor_reduce(
            out=ssum[:], in_=stT[:, g, :].rearrange("p (s r) -> p s r", r=15)[:, :, 15 - nr:15], axis=AX.X, op=ALU.add),
            reads=[stT.res], writes=[ssum.res])
        P.op(dve, lambda g=g: nc.vector.tensor_tensor(out=ssum[:], in0=ssum[:], in1=uTs[:, g, :], op=ALU.add),
             reads=[ssum.res, uTs.res], writes=[ssum.res])
        db = d_bf[g % 2]
        P.op(dve, lambda g=g, wg=wg, db=db: nc.vector.scalar_tensor_tensor(
            out=db[:], in0=ssum[:], scalar=1.0 / wg, in1=uTs[:, g, :], op0=ALU.mult, op1=ALU.subtract),
            reads=[ssum.res, uTs.res], writes=[db.res])
        pp = next_pg()
        P.op(pe, lambda g=g, db=db, pp=pp: nc.tensor.matmul(pp[:, 0:N_SEQ], lhsT=pw_bf[:, g, :], rhs=db[:],
                                                            start=True, stop=True),
             reads=[pw_bf.res, db.res], writes=[pp.res])
        P.op(dve, lambda g=g, pp=pp: nc.vector.tensor_scalar(
            out=o_pl[:, g, W:WX], in0=pp[:, 0:N_SEQ], scalar1=pscale[:, g:g + 1], scalar2=None, op0=ALU.mult),
            reads=[pp.res, pscale.res], writes=[o_pl.res])
    qb = [sb(f"qb{i}", [128, 512], F32, esB) for i in range(2)]
    Kp = [sb(f"Kp{i}", [128, 512], F32, esB) for i in range(4)]
    Vp = [sb(f"Vp{i}", [128, 512], BF16, esB) for i in range(4)]
    tmp = [sb(f"ktmp{i}", [128, 512], F32, esB) for i in range(2)]
    Z = sb("Zs", [128, 512], F32, esB)
    e8 = sb("e8", [128, 512], F32, esB)
    sp8 = sb("sp8", [128, 512], BF16, esB)
    S0 = sb("S0", [128, 512], F32, esB)
    Sa = sb("Sa", [128, 512], F32, esB)
    Sb_ = sb("Sb", [128, 512], F32, esB)
    A8 = sb("A8", [128, 512], BF16, esB)
    m8 = sb("m8", [8, 512], BF16, esB)
    hv = lambda t: t[:].rearrange("p (g h) -> p h g", h=8)
    for s in range(N_SEQ):
        q_t = qb[s % 2]
        P.dma("sp", q_t[:], q_scr[s:s + 1, :].partition_broadcast(128), reads=[qscr_res], writes=[q_t.res])
        for pg in range(NPG):
            col = s * NPG + pg
            kp = Kp[pg % 4]
            P.dma("pool", None, None, reads=[idxi.res], writes=[kp.res],
                  fn=lambda kp=kp, col=col: nc.gpsimd.indirect_dma_start(
                      out=kp[:], out_offset=None, in_=ck_d,
                      in_offset=bass.IndirectOffsetOnAxis(ap=idxi[:, col:col + 1], axis=0)))
            tm = tmp[pg % 2]
            eng = dve if pg % 2 == 0 else pool
            P.op(eng, lambda eng=eng, tm=tm, kp=kp: eng.h.tensor_tensor(out=tm[:], in0=kp[:], in1=q_t[:], op=ALU.mult),
                 reads=[kp.res, q_t.res], writes=[tm.res])
            P.op(dve, lambda tm=tm, pg=pg: nc.vector.tensor_reduce(
                out=Z[:, pg * 8:(pg + 1) * 8], in_=tm[:].rearrange("p (h d) -> p h d", h=8), axis=AX.X, op=ALU.add),
                reads=[tm.res], writes=[Z.res])
        for h in range(8):
            P.op(act, lambda h=h: nc.scalar.activation(out=hv(e8)[:, h, :], in_=hv(Z)[:, h, :], func=AF.Exp,
                                                       bias=bias_t[:, h:h + 1], scale=1.0),
                 reads=[Z.res, bias_t.res], writes=[e8.res])
        P.op(act, lambda: nc.scalar.activation(out=sp8[:], in_=e8[:], func=AF.Ln, bias=1.0, scale=1.0),
             reads=[e8.res], writes=[sp8.res])
        pc = next_pg()
        P.op(pe, lambda: nc.tensor.matmul(pc[:, :], lhsT=tri[:], rhs=sp8[:], start=True, stop=True),
             reads=[tri.res, sp8.res], writes=[pc.res])
        pt_ = next_pg()
        P.op(pe, lambda: nc.tensor.matmul(pt_[:, :], lhsT=negones[:], rhs=sp8[:], start=True, stop=True),
             reads=[negones.res, sp8.res], writes=[pt_.res])
        P.op(act, lambda: nc.scalar.copy(out=S0[:], in_=pt_[:, :]), reads=[pt_.res], writes=[S0.res])
        cur = S0
        for i, dd in enumerate((1, 2, 4, 8, 16, 32)):
            nxt = Sa if i % 2 == 0 else Sb_
            n0 = 512 - 8 * dd
            P.op(dve, lambda cur=cur, nxt=nxt, n0=n0, dd=dd: nc.vector.tensor_tensor(
                out=nxt[:, 0:n0], in0=cur[:, 0:n0], in1=cur[:, 8 * dd:512], op=ALU.add),
                reads=[cur.res], writes=[nxt.res])
            P.op(dve, lambda cur=cur, nxt=nxt, n0=n0: nc.vector.tensor_copy(out=nxt[:, n0:512], in_=cur[:, n0:512]),
                 reads=[cur.res, nxt.res], writes=[nxt.res])
            cur = nxt
        P.op(dve, lambda cur=cur: nc.vector.tensor_tensor(out=cur[:], in0=cur[:], in1=S0[:], op=ALU.subtract),
             reads=[cur.res, S0.res], writes=[cur.res])
        P.op(dve, lambda cur=cur: nc.vector.tensor_tensor(out=cur[:], in0=cur[:], in1=Z[:], op=ALU.add),
             reads=[cur.res, Z.res], writes=[cur.res])
        P.op(dve, lambda cur=cur: nc.vector.tensor_tensor(out=cur[:], in0=cur[:], in1=pc[:, :], op=ALU.add),
             reads=[cur.res, pc.res], writes=[cur.res])
        for h in range(8):
            P.op(act, lambda h=h, cur=cur: nc.scalar.activation(out=hv(A8)[:, h, :], in_=hv(cur)[:, h, :], func=AF.Exp,
                                                                bias=bias_t[:, h:h + 1], scale=1.0),
                 reads=[cur.res, bias_t.res], writes=[A8.res])
        pov = next_pg()
        for pg in range(NPG):
            col = s * NPG + pg
            vp = Vp[pg % 4]
            P.dma("pool", None, None, reads=[idxi.res], writes=[vp.res],
                  fn=lambda vp=vp, col=col: nc.gpsimd.indirect_dma_start(
                      out=vp[:], out_offset=None, in_=cv_d,
                      in_offset=bass.IndirectOffsetOnAxis(ap=idxi[:, col:col + 1], axis=0)))
            P.op(pe, lambda vp=vp, pg=pg: nc.tensor.matmul(pov[0:8, :], lhsT=A8[:, pg * 8:(pg + 1) * 8], rhs=vp[:],
                                                           start=(pg == 0), stop=(pg == NPG - 1)),
                 reads=[A8.res, vp.res], writes=[pov.res], inc=True)
        P.op(dve, lambda: nc.vector.tensor_tensor(out=m8[:], in0=pov[0:8, :], in1=bdiag[:], op=ALU.mult),
             reads=[pov.res, bdiag.res], writes=[m8.res])
        pox = next_pg()
        for h in range(8):
            P.op(pe, lambda h=h: nc.tensor.matmul(pox[0:64, h:h + 1], lhsT=m8[0:8, h * 64:(h + 1) * 64], rhs=ones8[0:8, 0:1],
                                                  start=True, stop=True),
                 reads=[m8.res, ones8.res], writes=[pox.res], inc=(h == 7))
        P.op(dve, lambda s=s: nc.vector.tensor_copy(out=o_at[:, :, W + s], in_=pox[0:64, 0:8]),
             reads=[pox.res], writes=[o_at.res])


_NC_CACHE = {}
_RUNNER = [None]
_REMAP = [None, None]


_STOP = [None]


def _get_nc(with_sample):
    key = (with_sample, _STOP[0])
    if key not in _NC_CACHE:
        _NC_CACHE[key] = build(with_sample, _STOP[0])
    return _NC_CACHE[key]


def kernel(x_prompt, x_sample, cache_k, cache_v, state_pool, state_conv, page_table,
           meta_tokens, norm_mix_g, w_in, sb_bias, pool_w, pool_scale, w_out, norm_ffn_g,
           w_up, conv_w, conv_b, w_down, norm_final_g, _with_sample=True):
    f32 = np.float32
    B = x_prompt.shape[0]
    nc = _get_nc(_with_sample)
    in_maps = []
    ck2 = cv2 = None
    if _with_sample:
        ck2 = np.ascontiguousarray(cache_k, dtype=f32).reshape(-1, 512)
        cv2 = np.ascontiguousarray(cache_v, dtype=f32).reshape(-1, 512)
    for c in range(8):
        b, g = c // 2, c % 2
        xa = np.zeros((TP, D), f32)
        xa[:N_META] = meta_tokens
        xa[N_META:T] = x_prompt[b]
        xsl = np.zeros((NS, WR, D), f32)
        qp = np.zeros((1, NS * W), f32)
        for k in range(NS):
            s = slot_start(g, k)
            lo, hi = s - HALO, s + W
            a, e = max(lo, 0), min(hi, T)
            xsl[k, a - lo:e - lo] = xa[a:e]
            qp[0, k * W:(k + 1) * W] = np.arange(s, s + W, dtype=f32)
        m = {
            "xall": xa, "xslot": xsl, "qpos": qp,
            "w_in": np.asarray(w_in, f32), "w_out": np.asarray(w_out, f32), "w_up": np.asarray(w_up, f32),
            "w_down": np.asarray(w_down, f32),
            "norm_mix_g": np.asarray(norm_mix_g, f32).reshape(D, 1),
            "norm_ffn_g": np.asarray(norm_ffn_g, f32).reshape(D, 1),
            "norm_final_g": np.asarray(norm_final_g, f32).reshape(1, D),
            "sb_bias": np.asarray(sb_bias, f32).reshape(1, 8),
            "pool_w": np.asarray(pool_w, f32), "pool_scale": np.asarray(pool_scale, f32),
            "conv_w": np.asarray(conv_w, f32), "conv_b": np.asarray(conv_b, f32).reshape(1, 2 * DFF),
        }
        if _with_sample:
            sl = slice(N_SEQ * c, N_SEQ * (c + 1))
            m.update({
                "x_sample": np.asarray(x_sample[sl], f32).reshape(N_SEQ, D),
                "cache_k": (_REMAP[0](c, ck2) if _REMAP[0] else ck2), "cache_v": (_REMAP[0](c, cv2) if _REMAP[0] else cv2),
                "state_pool": np.asarray(state_pool[sl], f32).reshape(N_SEQ * 15, 512),
                "state_conv": np.asarray(state_conv[sl], f32).reshape(N_SEQ * 2, 2 * DFF),
                "page_table": (_REMAP[1](c) if _REMAP[1] else np.asarray(page_table[sl], np.int32).reshape(1, N_SEQ * NPG)),
            })
        in_maps.append(m)
    if _RUNNER[0] is not None:
        res = _RUNNER[0](nc, in_maps)
    else:
        res = run_bass_kernel_spmd(nc, in_maps, core_ids=list(range(8))).results

    y_prompt = np.zeros((B, 4096, D), f32)
    k_prompt = np.zeros((B, T, 8, 64), f32)
    v_prompt = np.zeros((B, T, 8, 64), f32)
    pool_prompt = np.zeros((B, 15, 512), f32)
    conv_prompt = np.zeros((B, 2, 2 * DFF), f32)
    for c in range(8):
        b, g = c // 2, c % 2
        r = res[c]
        for k in range(NS):
            r0 = 2048 * g + OWN * k
            nv = min(OWN, 2048 * (g + 1) - r0)
            y_prompt[b, r0:r0 + nv] = r["y_slot"][k, 2:2 + nv]
        if g == 0:
            k_prompt[b] = r["k_all"][:T].reshape(T, 8, 64)
            v_prompt[b] = r["v_all"][:T].reshape(T, 8, 64)
        else:
            pool_prompt[b] = r["pool_last"]
            conv_prompt[b] = r["conv_last"]
    if not _with_sample:
        return (y_prompt, None, k_prompt, v_prompt, pool_prompt, conv_prompt, None, None, None, None)
    DB = x_sample.shape[0]
    y_sample = np.zeros((DB, 1, D), f32)
    k_sample = np.zeros((DB, 1, 8, 64), f32)
    v_sample = np.zeros((DB, 1, 8, 64), f32)
    pool_sample = np.zeros((DB, 15, 512), f32)
    conv_sample = np.zeros((DB, 2, 2 * DFF), f32)
    for c in range(8):
        r = res[c]
        sl = slice(N_SEQ * c, N_SEQ * (c + 1))
        y_sample[sl, 0] = r["y_sample"]
        k_sample[sl, 0] = r["k_sample"].reshape(N_SEQ, 8, 64)
        v_sample[sl, 0] = r["v_sample"].reshape(N_SEQ, 8, 64)
        pool_sample[sl] = r["pool_sample"]
        conv_sample[sl] = r["conv_sample"]
    return (y_prompt, y_sample, k_prompt, v_prompt, pool_prompt, conv_prompt,
            k_sample, v_sample, pool_sample, conv_sample)
```

```python
import numpy as np
from contextlib import ExitStack
import concourse.bass as bass
import concourse.mybir as mybir
from concourse.bass_utils import run_bass_kernel_spmd

F32 = mybir.dt.float32
BF16 = mybir.dt.bfloat16
I32 = mybir.dt.int32
AF = mybir.ActivationFunctionType
ALU = mybir.AluOpType
AX = mybir.AxisListType

D = 1024
T = 4112
NBLK = 33
TP = NBLK * 128
N_META = 16
NS = 5
W = 412
HALO = 16
WR = W + HALO
OWN = 410
DFF = 2816
NFC = DFF // 128
EPS = 1e-6
NEG = -30000.0
QT = [(0, 128), (128, 128), (256, 128), (384, 28)]
N_SEQ = 4
NPG = 64
SAME_ENGINE_SYNC = True


TILES = ((0, 3, 4, 7, 8), (1, 2, 5, 6, 9))
LAST_S = 14 + OWN * 9
PP0 = 4097 - LAST_S + HALO
CP0 = 4110 - LAST_S


def slot_start(g, k):
    return 14 + OWN * TILES[g][k]


def n_kb_for_slot(k):
    last = 14 + OWN * (2 * k + 1) + W - 1
    return min(NBLK, last // 128 + 1)


def kb_needs_mask(k, kb):
    return not (128 * kb + 127 < 14 + OWN * 2 * k)


class SemObj:
    def __init__(self, sem):
        self.sem = sem
        self.cnt = 0


class Eng(SemObj):
    def __init__(self, sem, h, name):
        super().__init__(sem)
        self.h = h
        self.name = name
        self.waited = {}


class Res:
    __slots__ = ("w", "r", "excl")

    def __init__(self, excl=False):
        self.w = None
        self.r = {}
        self.excl = excl


class Prog:
    def __init__(self, nc, es):
        self.nc = nc
        self.es = es
        mk = lambda n: es.enter_context(nc.semaphore(n))
        self.pe = Eng(mk("s_pe"), nc.tensor, "pe")
        self.act = Eng(mk("s_act"), nc.scalar, "act")
        self.dve = Eng(mk("s_dve"), nc.vector, "dve")
        self.pool = Eng(mk("s_pool"), nc.gpsimd, "pool")
        self.sp = Eng(mk("s_sp"), nc.sync, "sp")
        self.dsems = {}
        for q in ("sp", "pool", "act"):
            self.dsems[q] = [SemObj(mk(f"d_{q}{i}")) for i in range(12)]
        self.dnext = {"sp": 0, "pool": 0, "act": 0}

    def _wait(self, eng, toks):
        for (so, v) in toks:
            if so is eng and (not SAME_ENGINE_SYNC or eng is self.pe):
                continue
            if eng.waited.get(so, 0) < v:
                eng.h.wait_ge(so.sem, v)
                eng.waited[so] = v

    @staticmethod
    def _deps(reads, writes):
        toks = []
        for r in reads:
            if r.w is not None:
                toks.append(r.w)
            if r.excl:
                toks.extend(r.r.items())
        for w in writes:
            if w.w is not None:
                toks.append(w.w)
            toks.extend(w.r.items())
        return toks

    @staticmethod
    def _record(tok, reads, writes):
        so, v = tok
        for r in reads:
            if r.r.get(so, 0) < v:
                r.r[so] = v
        for w in writes:
            w.w = tok
            w.r = {}

    def op(self, eng, fn, reads=(), writes=(), inc=True):
        self._wait(eng, self._deps(reads, writes))
        ins = fn()
        if inc:
            ins.then_inc(eng.sem, 1)
            eng.cnt += 1
            tok = (eng, eng.cnt)
        else:
            tok = (eng, eng.cnt + 1)
        self._record(tok, reads, writes)
        return ins

    def dma(self, q, out, in_, reads=(), writes=(), fn=None, **kw):
        eng = {"sp": self.sp, "pool": self.pool, "act": self.act}[q]
        lst = self.dsems[q]
        d = lst[self.dnext[q] % len(lst)]
        self.dnext[q] += 1
        toks = self._deps(reads, writes)
        if d.cnt > 0:
            toks.append((d, d.cnt))
        self._wait(eng, toks)
        if fn is not None:
            fn().then_inc(d.sem, 16)
        else:
            eng.h.dma_start(out=out, in_=in_, **kw).then_inc(d.sem, 16)
        d.cnt += 16
        self._record((d, d.cnt), reads, writes)

    def final_wait(self):
        toks = []
        for q in self.dsems:
            for d in self.dsems[q]:
                if d.cnt:
                    toks.append((d, d.cnt))
        for e in (self.pe, self.act, self.dve, self.pool):
            if e.cnt:
                toks.append((e, e.cnt))
        self._wait(self.sp, toks)


class Tl:
    def __init__(self, t, excl=False):
        self.t = t
        self.res = Res(excl)

    def __getitem__(self, k):
        return self.t[k]


WX = W + 4
NPOOL = [2560]
import os
DBG = int(os.environ.get('KDBG', '99'))


def build(with_sample=True, stop_after=None):
    nc = bass.Bass("TRN2", target_bir_lowering=False)
    dr = lambda name, shape, dt=F32, kind="ExternalInput": nc.dram_tensor(name, shape, dt, kind=kind).ap()
    xall = dr("xall", [TP, D])
    xslot = dr("xslot", [NS, WR, D])
    qpos_d = dr("qpos", [1, NS * W])
    w_in_d = dr("w_in", [D, 2048])
    w_out_d = dr("w_out", [D, D])
    w_up_d = dr("w_up", [D, 2 * DFF])
    w_down_d = dr("w_down", [DFF, D])
    g_mix_d = dr("norm_mix_g", [D, 1])
    g_ffn_d = dr("norm_ffn_g", [D, 1])
    g_fin_d = dr("norm_final_g", [1, D])
    sbb_d = dr("sb_bias", [1, 8])
    pool_w_d = dr("pool_w", [4, 128, 128])
    pool_s_d = dr("pool_scale", [4, 128])
    conv_w_d = dr("conv_w", [3, 2 * DFF])
    conv_b_d = dr("conv_b", [1, 2 * DFF])
    y_o = dr("y_slot", [NS, W, D], kind="ExternalOutput")
    k_o = dr("k_all", [TP, 512], kind="ExternalOutput")
    v_o = dr("v_all", [TP, 512], kind="ExternalOutput")
    pp_o = dr("pool_last", [15, 512], kind="ExternalOutput")
    cp_o = dr("conv_last", [2, 2 * DFF], kind="ExternalOutput")
    o_scr = nc.dram_tensor("o_scr", [NS, 128, 12, WX], BF16).ap()
    if with_sample:
        xsam = dr("x_sample", [N_SEQ, D])
        ck_d = dr("cache_k", [NPOOL[0] * 128, 512])
        cv_d = dr("cache_v", [NPOOL[0] * 128, 512])
        stp_d = dr("state_pool", [N_SEQ * 15, 512])
        stc_d = dr("state_conv", [N_SEQ * 2, 2 * DFF])
        pt_d = dr("page_table", [1, N_SEQ * NPG], I32)
        ys_o = dr("y_sample", [N_SEQ, D], kind="ExternalOutput")
        ks_o = dr("k_sample", [N_SEQ, 512], kind="ExternalOutput")
        vs_o = dr("v_sample", [N_SEQ, 512], kind="ExternalOutput")
        ps_o = dr("pool_sample", [N_SEQ, 15, 512], kind="ExternalOutput")
        cs_o = dr("conv_sample", [N_SEQ, 2, 2 * DFF], kind="ExternalOutput")
        q_scr = nc.dram_tensor("q_scr", [N_SEQ, 512], F32).ap()

    es = ExitStack()
    with es:
        P = Prog(nc, es)
        pe, act, dve, pool = P.pe, P.act, P.dve, P.pool

        def sb(name, shape, dt=F32, st=es):
            return Tl(st.enter_context(nc.sbuf_tensor(name, shape, dt)))

        def ps(name, shape, dt=F32):
            return Tl(es.enter_context(nc.psum_tensor(name, shape, dt)), excl=True)

        pT = ps("pT", [128, 1024], BF16)
        pG = [ps(f"pG{i}", [128, 512], F32) for i in range(5)]
        pO = [ps(f"pO{i}", [128, 512], F32) for i in range(2)]
        gi = [0]

        def next_pg():
            t = pG[gi[0] % len(pG)]
            gi[0] += 1
            return t

        ident = sb("ident", [128, 128], BF16)
        identf = sb("identf", [128, 128], F32)
        tri = sb("tri", [128, 128], BF16)
        negones = sb("negones", [128, 128], BF16)
        onesf = sb("onesf", [128, 128], F32)
        negf = sb("negf", [128, 128], F32)
        kpos = sb("kpos", [128, NBLK], F32)
        mhalf = sb("mhalf", [128, 1], F32)
        P.op(pool, lambda: nc.gpsimd.memset(onesf[:], 1.0), writes=[onesf.res])
        P.op(pool, lambda: nc.gpsimd.memset(negf[:], -1.0), writes=[negf.res])
        P.op(pool, lambda: nc.gpsimd.memset(mhalf[:], -0.5), writes=[mhalf.res])
        P.op(pool, lambda: nc.gpsimd.memset(negones[:], -1.0), writes=[negones.res])
        P.op(pool, lambda: nc.gpsimd.affine_select(out=identf[:], in_=onesf[:], pattern=[[-1, 128]],
                                                    compare_op=ALU.is_equal, fill=0.0, base=0, channel_multiplier=1),
             reads=[onesf.res], writes=[identf.res])
        P.op(pool, lambda: nc.gpsimd.tensor_copy(out=ident[:], in_=identf[:]), reads=[identf.res], writes=[ident.res])
        P.op(pool, lambda: nc.gpsimd.affine_select(out=tri[:], in_=negf[:], pattern=[[-1, 128]],
                                                    compare_op=ALU.is_ge, fill=0.0, base=0, channel_multiplier=1),
             reads=[negf.res], writes=[tri.res])
        P.op(pool, lambda: nc.gpsimd.iota(kpos[:], pattern=[[128, NBLK]], base=0, channel_multiplier=1,
                                          allow_small_or_imprecise_dtypes=True), writes=[kpos.res])

        bias_t = sb("bias_t", [128, 8])
        P.dma("sp", bias_t[:], sbb_d.partition_broadcast(128), writes=[bias_t.res])
        gfin = sb("gfin", [128, D])
        P.dma("sp", gfin[:], g_fin_d.partition_broadcast(128), writes=[gfin.res])
        gmix = sb("gmix", [128, 8])
        P.dma("sp", gmix[:], g_mix_d.rearrange("(c p) o -> p (c o)", p=128), writes=[gmix.res],
              allow_slow_non_contiguous=True)
        gffn = sb("gffn", [128, 8])
        P.dma("sp", gffn[:], g_ffn_d.rearrange("(c p) o -> p (c o)", p=128), writes=[gffn.res],
              allow_slow_non_contiguous=True)
        gmix_bc = sb("gmix_bc", [128, 8, 128])
        gffn_bc = sb("gffn_bc", [128, 8, 128])
        for dc in range(8):
            P.op(pool, lambda dc=dc: nc.gpsimd.tensor_scalar(out=gmix_bc[:, dc, :], in0=onesf[:], scalar1=gmix[:, dc:dc + 1],
                                                             scalar2=None, op0=ALU.mult),
                 reads=[onesf.res, gmix.res], writes=[gmix_bc.res])
            P.op(pool, lambda dc=dc: nc.gpsimd.tensor_scalar(out=gffn_bc[:, dc, :], in0=onesf[:], scalar1=gffn[:, dc:dc + 1],
                                                             scalar2=None, op0=ALU.mult),
                 reads=[onesf.res, gffn.res], writes=[gffn_bc.res])
        qpos_bc = sb("qpos_bc", [128, NS * W])
        P.dma("sp", qpos_bc[:], qpos_d.partition_broadcast(128), writes=[qpos_bc.res])
        pscale = sb("pscale", [128, 4])
        P.dma("sp", pscale[:], pool_s_d.rearrange("g c -> c g"), writes=[pscale.res], allow_slow_non_contiguous=True)
        cw = sb("cw", [128, 3, 44])
        cb = sb("cb", [128, 44])
        for q4 in range(4):
            cs_ = slice(q4 * 1408, (q4 + 1) * 1408)
            for i3 in range(3):
                P.dma("sp", cw[:, i3, q4 * 11:(q4 + 1) * 11], conv_w_d[i3:i3 + 1, cs_].rearrange("o (c p) -> p (o c)", p=128),
                      writes=[cw.res], allow_slow_non_contiguous=True)
            P.dma("sp", cb[:, q4 * 11:(q4 + 1) * 11], conv_b_d[:, cs_].rearrange("o (c p) -> p (o c)", p=128),
                  writes=[cb.res], allow_slow_non_contiguous=True)
        pw_bf = sb("pw_bf", [128, 4, 128], BF16)
        P.dma("pool", pw_bf[:], pool_w_d.rearrange("g c e -> c g e"), writes=[pw_bf.res])

        junk = [sb(f"junk{i}", [128, D], BF16) for i in range(2)]
        ssq = [sb(f"ssq{i}", [128, 1]) for i in range(4)]
        xn = [sb(f"xn{i}", [128, D], BF16) for i in range(2)]
        cnt = {"x": 0, "ss": 0, "xn": 0, "hT": 0, "j": 0}

        def rms_rows(x_ap_fn, x_res, rows, out_bf_tl):
            jk = junk[cnt["j"] % 2]; cnt["j"] += 1
            ss = ssq[cnt["ss"] % 4]; cnt["ss"] += 1
            P.op(act, lambda: nc.scalar.activation(out=jk[0:rows, :], in_=x_ap_fn(), func=AF.Square,
                                                   accum_out=ss[0:rows, :]),
                 reads=[x_res], writes=[jk.res, ss.res])
            P.op(dve, lambda: nc.vector.tensor_scalar(out=ss[0:rows, :], in0=ss[0:rows, :], scalar1=1.0 / D,
                                                      scalar2=EPS, op0=ALU.mult, op1=ALU.add),
                 reads=[ss.res], writes=[ss.res])
            P.op(pool, lambda: nc.gpsimd.tensor_tensor(out=ss[0:rows, :], in0=ss[0:rows, :], in1=mhalf[0:rows, :],
                                                       op=ALU.pow),
                 reads=[ss.res, mhalf.res], writes=[ss.res])
            if out_bf_tl is not None:
                P.op(dve, lambda: nc.vector.tensor_scalar(out=out_bf_tl[0:rows, :], in0=x_ap_fn(),
                                                          scalar1=ss[0:rows, 0:1], scalar2=None, op0=ALU.mult),
                     reads=[x_res, ss.res], writes=[out_bf_tl.res])
            return ss

        def transpose_rows(xn_tl, rows, dst_ap_fn, dst_res, g_bc):
            for dc in range(8):
                P.op(pe, lambda dc=dc: nc.tensor.transpose(out=pT[:, dc * 128:dc * 128 + rows],
                                                           in_=xn_tl[0:rows, dc * 128:(dc + 1) * 128],
                                                           identity=ident[0:rows, 0:rows]),
                     reads=[xn_tl.res, ident.res], writes=[pT.res], inc=(dc == 7))
            P.op(dve, lambda: nc.vector.tensor_tensor(
                out=dst_ap_fn(), in0=pT[:].rearrange("p (c t) -> p c t", c=8)[:, :, 0:rows],
                in1=g_bc[:, :, 0:rows], op=ALU.mult),
                reads=[pT.res, g_bc.res], writes=[dst_res])

        def barrier():
            toks = [(e, e.cnt) for e in (pe, act, dve, pool) if e.cnt]
            for q in P.dsems:
                toks += [(d, d.cnt) for d in P.dsems[q] if d.cnt]
            for e in (pe, act, dve, pool, P.sp):
                P._wait(e, toks)

        hT_sam = sb("hT_sam", [128, 8, N_SEQ], BF16) if with_sample else None
        if stop_after == "consts":
            P.final_wait()
            return nc

        esAB = ExitStack()
        KT = sb("KT", [128, 4, TP], BF16, esAB)
        Vb = sb("Vb", [128, NBLK, 512], BF16, esAB)
        kt_res = [Res() for _ in range(NBLK)]
        v_res = [Res() for _ in range(NBLK)]
        with ExitStack() as esA:
            w_kv = sb("w_kv", [128, 8, 1024], BF16, esA)
            P.dma("pool", w_kv[:], w_in_d[:, 512:1536].rearrange("(c p) n -> p c n", p=128), writes=[w_kv.res])
            xt = [sb(f"xt{i}", [128, D], F32, esA) for i in range(3)]
            hT = [sb(f"hT{i}", [128, 8, 128], BF16, esA) for i in range(3)]
            ko = [sb(f"ko{i}", [128, 512], F32, esA) for i in range(2)]
            vo = [sb(f"vo{i}", [128, 512], F32, esA) for i in range(2)]
            nblocks = NBLK + (1 if with_sample else 0)
            if stop_after and stop_after.startswith('A') and len(stop_after) > 1:
                nblocks = int(stop_after[1:])
            for tb in range(nblocks):
                sam = tb == NBLK
                rows = N_SEQ if sam else 128
                x_tl = xt[tb % 3]
                if sam:
                    P.dma("sp", x_tl[0:rows, :], xsam[:, :], writes=[x_tl.res])
                else:
                    P.dma("sp", x_tl[:], xall[tb * 128:(tb + 1) * 128, :], writes=[x_tl.res])
                if DBG < 2:
                    continue
                xn_tl = xn[cnt["xn"] % 2]; cnt["xn"] += 1
                rms_rows(lambda x_tl=x_tl, rows=rows: x_tl[0:rows, :], x_tl.res, rows, xn_tl)
                if DBG < 3:
                    continue
                if sam:
                    h_ap = lambda: hT_sam[:]
                    h_res = hT_sam.res
                    h_rd = lambda dc: hT_sam[:, dc, :]
                else:
                    h_tl = hT[tb % 3]
                    h_ap = lambda h_tl=h_tl: h_tl[:]
                    h_res = h_tl.res
                    h_rd = lambda dc, h_tl=h_tl: h_tl[:, dc, :]
                transpose_rows(xn_tl, rows, h_ap, h_res, gmix_bc)
                if DBG < 4:
                    continue
                if not sam:
                    pk = next_pg()
                    for j in range(4):
                        for dc in range(8):
                            P.op(pe, lambda j=j, dc=dc, pk=pk: nc.tensor.matmul(
                                pk[:, j * 128:(j + 1) * 128], lhsT=w_kv[:, dc, j * 128:(j + 1) * 128],
                                rhs=h_rd(dc), start=(dc == 0), stop=(dc == 7)),
                                reads=[w_kv.res, h_res], writes=[pk.res], inc=(j == 3 and dc == 7))
                    P.op(act, lambda pk=pk, tb=tb: nc.scalar.copy(
                        out=KT[:, :, tb * 128:(tb + 1) * 128], in_=pk[:].rearrange("p (j t) -> p j t", j=4)),
                        reads=[pk.res], writes=[kt_res[tb]])
                if DBG < 5:
                    continue
                pk2 = next_pg()
                for dc in range(8):
                    P.op(pe, lambda dc=dc, pk2=pk2: nc.tensor.matmul(
                        pk2[0:rows, :], lhsT=h_rd(dc), rhs=w_kv[:, dc, 0:512], start=(dc == 0), stop=(dc == 7)),
                        reads=[w_kv.res, h_res], writes=[pk2.res], inc=(dc == 7))
                ko_tl = ko[tb % 2]
                P.op(act, lambda pk2=pk2, ko_tl=ko_tl: nc.scalar.copy(out=ko_tl[0:rows, :], in_=pk2[0:rows, :]),
                     reads=[pk2.res], writes=[ko_tl.res])
                P.dma("sp", (ks_o[:, :] if sam else k_o[tb * 128:(tb + 1) * 128, :]), ko_tl[0:rows, :], reads=[ko_tl.res])
                if DBG < 6:
                    continue
                pv = next_pg()
                for dc in range(8):
                    P.op(pe, lambda dc=dc, pv=pv: nc.tensor.matmul(
                        pv[0:rows, :], lhsT=h_rd(dc), rhs=w_kv[:, dc, 512:1024], start=(dc == 0), stop=(dc == 7)),
                        reads=[w_kv.res, h_res], writes=[pv.res], inc=(dc == 7))
                vo_tl = vo[tb % 2]
                P.op(act, lambda pv=pv, vo_tl=vo_tl: nc.scalar.copy(out=vo_tl[0:rows, :], in_=pv[0:rows, :]),
                     reads=[pv.res], writes=[vo_tl.res])
                if not sam:
                    P.op(dve, lambda vo_tl=vo_tl, tb=tb: nc.vector.tensor_copy(out=Vb[:, tb, :], in_=vo_tl[:]),
                         reads=[vo_tl.res], writes=[v_res[tb]])
                P.dma("sp", (vs_o[:, :] if sam else v_o[tb * 128:(tb + 1) * 128, :]), vo_tl[0:rows, :], reads=[vo_tl.res])
            barrier()
        if stop_after and stop_after.startswith("A"):
            P.final_wait()
            esAB.close()
            return nc

        with ExitStack() as esB:
            w_qu = sb("w_qu", [128, 8, 1024], BF16, esB)
            P.dma("pool", w_qu[:, :, 0:512], w_in_d[:, 0:512].rearrange("(c p) n -> p c n", p=128), writes=[w_qu.res])
            P.dma("pool", w_qu[:, :, 512:1024], w_in_d[:, 1536:2048].rearrange("(c p) n -> p c n", p=128), writes=[w_qu.res])
            o_at = sb("o_at", [64, 8, WX], BF16, esB)
            o_pl = sb("o_pl", [128, 4, WX], BF16, esB)
            P.op(pool, lambda: nc.gpsimd.memset(o_at[:], 0.0), writes=[o_at.res])
            P.op(pool, lambda: nc.gpsimd.memset(o_pl[:], 0.0), writes=[o_pl.res])
            if with_sample:
                with ExitStack() as esS:
                    esB_save = esB
                    L_ = dict(locals())
                    L_["esB"] = esS
                    sample_mixer(nc, P, L_)
                    barrier()
            xs1 = [sb(f"xs1_{i}", [128, D], F32, esB) for i in range(2)]
            hTs = sb("hTs", [128, 8, WR], BF16, esB)
            qT = sb("qT", [128, 4, W], BF16, esB)
            uT = sb("uT", [128, 4, WR], F32, esB)
            lv = [sb(f"lv{i}", [128, WR], F32, esB) for i in range(2)]
            icnt = sb("icnt", [128, W], F32, esB)
            dpool = [sb(f"dpool{i}", [128, W], BF16, esB) for i in range(2)]
            NMK = 9
            masks = sb("masks", [128, NMK, W], BF16, esB)
            mask_res = [Res() for _ in range(NMK)]
            e_t = [sb(f"e_t{i}", [128, W], F32, esB) for i in range(3)]
            sp_t = [sb(f"sp_t{i}", [128, W], BF16, esB) for i in range(3)]
            a_t = [sb(f"a_t{i}", [128, W], BF16, esB) for i in range(3)]
            w_t = [sb(f"w_t{i}", [128, W], F32, esB) for i in range(3)]
            acc32 = sb("acc32", [128, W], F32, esB)
            accbf = [sb(f"accbf{i}", [128, W], BF16, esB) for i in range(3)]


            for k in range(NS):
                for (r0, n) in [(0, HALO)] + [(HALO + c0, n) for (c0, n) in QT]:
                    xt_ = xs1[cnt["x"] % 2]; cnt["x"] += 1
                    P.dma("sp", xt_[0:n, :], xslot[k, r0:r0 + n, :], writes=[xt_.res])
                    xn_tl = xn[cnt["xn"] % 2]; cnt["xn"] += 1
                    rms_rows(lambda xt_=xt_, n=n: xt_[0:n, :], xt_.res, n, xn_tl)
                    transpose_rows(xn_tl, n, lambda r0=r0, n=n: hTs[:, :, r0:r0 + n], hTs.res, gmix_bc)
                for j in range(4):
                    pq = next_pg()
                    for dc in range(8):
                        P.op(pe, lambda j=j, dc=dc, pq=pq: nc.tensor.matmul(
                            pq[:, 0:W], lhsT=w_qu[:, dc, j * 128:(j + 1) * 128], rhs=hTs[:, dc, HALO:WR],
                            start=(dc == 0), stop=(dc == 7)),
                            reads=[w_qu.res, hTs.res], writes=[pq.res], inc=(dc == 7))
                    P.op(dve, lambda j=j, pq=pq: nc.vector.tensor_scalar(
                        out=qT[:, j, :], in0=pq[:, 0:W], scalar1=0.125, scalar2=None, op0=ALU.mult),
                        reads=[pq.res], writes=[qT.res])
                for g in range(4):
                    pu = next_pg()
                    for dc in range(8):
                        P.op(pe, lambda g=g, dc=dc, pu=pu: nc.tensor.matmul(
                            pu[:, 0:WR], lhsT=w_qu[:, dc, 512 + g * 128:512 + (g + 1) * 128], rhs=hTs[:, dc, :],
                            start=(dc == 0), stop=(dc == 7)),
                            reads=[w_qu.res, hTs.res], writes=[pu.res], inc=(dc == 7))
                    P.op(act, lambda g=g, pu=pu: nc.scalar.copy(out=uT[:, g, :], in_=pu[:, 0:WR]),
                         reads=[pu.res], writes=[uT.res])
                if k == NS - 1:
                    for g in range(4):
                        P.dma("sp", pp_o[:, g * 128:(g + 1) * 128].rearrange("r c -> c r"), uT[:, g, PP0:PP0 + 15],
                              reads=[uT.res], allow_slow_non_contiguous=True)
                for g in range(4):
                    wg = 2 << g
                    cur, cur_res = (lambda g=g: uT[:, g, :]), uT.res
                    lo = 0
                    for lvl in range(g + 1):
                        sh = 1 << lvl
                        dst = lv[lvl % 2]
                        P.op(dve, lambda cur=cur, dst=dst, sh=sh, lo=lo: nc.vector.tensor_tensor(
                            out=dst[:, lo + sh:WR], in0=cur()[:, lo + sh:WR], in1=cur()[:, lo:WR - sh], op=ALU.add),
                            reads=[cur_res], writes=[dst.res])
                        cur, cur_res = (lambda dst=dst: dst[:]), dst.res
                        lo += sh
                    P.op(dve, lambda wg=wg, k=k: nc.vector.tensor_scalar(
                        out=icnt[:], in0=qpos_bc[:, k * W:(k + 1) * W], scalar1=1.0, scalar2=float(wg),
                        op0=ALU.add, op1=ALU.min), reads=[qpos_bc.res], writes=[icnt.res])
                    P.op(dve, lambda: nc.vector.reciprocal(out=icnt[:], in_=icnt[:]), reads=[icnt.res], writes=[icnt.res])
                    P.op(dve, lambda cur=cur: nc.vector.tensor_tensor(
                        out=icnt[:], in0=cur()[:, HALO:WR], in1=icnt[:], op=ALU.mult),
                        reads=[cur_res, icnt.res], writes=[icnt.res])
                    dp = dpool[g % 2]
                    P.op(dve, lambda g=g, dp=dp: nc.vector.tensor_tensor(
                        out=dp[:], in0=icnt[:], in1=uT[:, g, HALO:WR], op=ALU.subtract),
                        reads=[icnt.res, uT.res], writes=[dp.res])
                    pp = next_pg()
                    P.op(pe, lambda g=g, dp=dp, pp=pp: nc.tensor.matmul(
                        pp[:, 0:W], lhsT=pw_bf[:, g, :], rhs=dp[:], start=True, stop=True),
                        reads=[pw_bf.res, dp.res], writes=[pp.res])
                    P.op(dve, lambda g=g, pp=pp: nc.vector.tensor_scalar(
                        out=o_pl[:, g, 0:W], in0=pp[:, 0:W], scalar1=pscale[:, g:g + 1], scalar2=None, op0=ALU.mult),
                        reads=[pp.res, pscale.res], writes=[o_pl.res])
                nkb = n_kb_for_slot(k)
                mkb = [kb for kb in range(nkb) if kb_needs_mask(k, kb)]
                assert len(mkb) <= NMK, len(mkb)
                midx = {kb: i for i, kb in enumerate(mkb)}
                for kb in mkb:
                    i = midx[kb]
                    P.op(dve, lambda kb=kb, i=i, k=k: nc.vector.tensor_scalar(
                        out=masks[:, i, :], in0=qpos_bc[:, k * W:(k + 1) * W], scalar1=kpos[:, kb:kb + 1],
                        scalar2=NEG, op0=ALU.is_le, op1=ALU.mult),
                        reads=[qpos_bc.res, kpos.res], writes=[mask_res[i]])
                ucount = [0]
                for h in range(8):
                    attn_head(nc, P, h, nkb, midx, ucount, locals())
                P.dma("sp", o_scr[k, 0:64, 0:8, :], o_at[:], reads=[o_at.res])
                P.dma("sp", o_scr[k, :, 8:12, :], o_pl[:], reads=[o_pl.res])
            barrier()
        esAB.close()
        if stop_after == "B1":
            P.final_wait()
            return nc

        with ExitStack() as esC:
            w_oa = sb("w_oa", [64, 8, D], BF16, esC)
            P.dma("pool", w_oa[:], w_out_d[0:512, :].rearrange("(h p) n -> p h n", p=64), writes=[w_oa.res])
            w_op = sb("w_op", [128, 4, D], BF16, esC)
            P.dma("pool", w_op[:], w_out_d[512:1024, :].rearrange("(g p) n -> p g n", p=128), writes=[w_op.res])
            wdn = sb("wdn", [128, NFC, D], BF16, esC)
            for i in range(NFC):
                P.dma("pool", wdn[:, i, :], w_down_d[i * 128:(i + 1) * 128, :], writes=[wdn.res])
            xs = sb("xs2", [128, 5, D], F32, esC)
            o_at = sb("o_at2", [64, 8, WX], BF16, esC)
            o_pl = sb("o_pl2", [128, 4, WX], BF16, esC)
            h2T = sb("h2T", [128, 8, WX], BF16, esC)
            cv_t = [sb(f"cv_t{i}", [128, WX], F32, esC) for i in range(6)]
            sg_t = [sb(f"sg_t{i}", [128, WX], F32, esC) for i in range(3)]
            actT = sb("actT", [128, NFC, WX], BF16, esC)
            upl = sb("upl", [128, 2, 44], F32, esC)
            wup_t = [sb(f"wup{i}", [128, 8, 256], BF16, esC) for i in range(4)]
            P.op(dve, lambda: nc.vector.memset(actT[:], 0.0), writes=[actT.res])
            if with_sample:
                upl_s = sb("upl_s", [128, N_SEQ, 44], F32, esC)
                sc8 = [sb(f"sc8_{i}", [8, 128], F32, esC) for i in range(2)]
                pF = pO[1]

            for k in range(NS):
                wk = WX if (k == 0 and with_sample) else W
                tiles = list(QT) + ([(W, N_SEQ)] if (k == 0 and with_sample) else [])
                P.dma("sp", o_at[:], o_scr[k, 0:64, 0:8, :], writes=[o_at.res])
                P.dma("sp", o_pl[:], o_scr[k, :, 8:12, :], writes=[o_pl.res])
                for ti, (c0, n) in enumerate(tiles):
                    if ti < 4:
                        P.dma("sp", xs[0:n, ti, :], xslot[k, HALO + c0:HALO + c0 + n, :], writes=[xs.res])
                    else:
                        P.dma("sp", xs[0:n, ti, :], xsam[:, :], writes=[xs.res])
                for ti, (c0, n) in enumerate(tiles):
                    for half in range(2):
                        pw = next_pg()
                        for h in range(8):
                            P.op(pe, lambda h=h, pw=pw: nc.tensor.matmul(
                                pw[0:n, :], lhsT=o_at[:, h, c0:c0 + n], rhs=w_oa[:, h, half * 512:(half + 1) * 512],
                                start=(h == 0), stop=False),
                                reads=[o_at.res, w_oa.res], writes=[pw.res], inc=False)
                        for g in range(4):
                            P.op(pe, lambda g=g, pw=pw: nc.tensor.matmul(
                                pw[0:n, :], lhsT=o_pl[:, g, c0:c0 + n], rhs=w_op[:, g, half * 512:(half + 1) * 512],
                                start=False, stop=(g == 3)),
                                reads=[o_pl.res, w_op.res], writes=[pw.res], inc=(g == 3))
                        P.op(dve, lambda pw=pw, half=half, ti=ti, n=n: nc.vector.tensor_tensor(
                            out=xs[0:n, ti, half * 512:(half + 1) * 512], in0=pw[0:n, :],
                            in1=xs[0:n, ti, half * 512:(half + 1) * 512], op=ALU.add),
                            reads=[pw.res, xs.res], writes=[xs.res])
                    xn_tl = xn[cnt["xn"] % 2]; cnt["xn"] += 1
                    rms_rows(lambda ti=ti, n=n: xs[0:n, ti, :], xs.res, n, xn_tl)
                    transpose_rows(xn_tl, n, lambda c0=c0, n=n: h2T[:, :, c0:c0 + n], h2T.res, gffn_bc)
                for i in range(NFC):
                    wt = wup_t[i % 4]
                    P.dma("pool", wt[:, :, 0:128], w_up_d[:, i * 128:(i + 1) * 128].rearrange("(c p) f -> p c f", p=128),
                          writes=[wt.res])
                    P.dma("pool", wt[:, :, 128:256],
                          w_up_d[:, DFF + i * 128:DFF + (i + 1) * 128].rearrange("(c p) f -> p c f", p=128), writes=[wt.res])
                    cvs = []
                    for a in range(2):
                        fc = i + a * NFC
                        pu = next_pg()
                        for dc in range(8):
                            P.op(pe, lambda a=a, dc=dc, pu=pu, wt=wt: nc.tensor.matmul(
                                pu[:, 0:wk], lhsT=wt[:, dc, a * 128:(a + 1) * 128], rhs=h2T[:, dc, 0:wk],
                                start=(dc == 0), stop=(dc == 7)),
                                reads=[wt.res, h2T.res], writes=[pu.res], inc=(dc == 7))
                        cv = cv_t[(2 * i + a) % 6]
                        P.op(act, lambda pu=pu, cv=cv, fc=fc: nc.scalar.activation(
                            out=cv[:, 2:W], in_=pu[:, 0:W - 2], func=AF.Identity, scale=cw[:, 0, fc:fc + 1],
                            bias=cb[:, fc:fc + 1]), reads=[pu.res, cw.res, cb.res], writes=[cv.res])
                        for tap in (1, 2):
                            P.op(dve, lambda pu=pu, cv=cv, fc=fc, tap=tap: nc.vector.scalar_tensor_tensor(
                                out=cv[:, 2:W], in0=pu[:, tap:W - 2 + tap], scalar=cw[:, tap, fc:fc + 1], in1=cv[:, 2:W],
                                op0=ALU.mult, op1=ALU.add), reads=[pu.res, cw.res, cv.res], writes=[cv.res])
                        if k == NS - 1:
                            P.op(dve, lambda pu=pu, fc=fc: nc.vector.tensor_copy(out=upl[:, :, fc], in_=pu[:, CP0:CP0 + 2]),
                                 reads=[pu.res], writes=[upl.res])
                        if k == 0 and with_sample:
                            s8 = sc8[(2 * i + a) % 2]
                            P.dma("sp", s8[:], stc_d[:, fc * 128:(fc + 1) * 128], writes=[s8.res])
                            P.op(pe, lambda s8=s8: nc.tensor.transpose(out=pF[:, 0:8], in_=s8[0:8, :],
                                                                       identity=identf[0:8, 0:8]),
                                 reads=[s8.res, identf.res], writes=[pF.res])
                            P.op(dve, lambda pu=pu, fc=fc: nc.vector.tensor_copy(out=upl_s[:, :, fc], in_=pu[:, W:WX]),
                                 reads=[pu.res], writes=[upl_s.res])
                            P.op(dve, lambda pu=pu, cv=cv, fc=fc: nc.vector.tensor_scalar(
                                out=cv[:, W:WX], in0=pu[:, W:WX], scalar1=cw[:, 2, fc:fc + 1], scalar2=cb[:, fc:fc + 1],
                                op0=ALU.mult, op1=ALU.add), reads=[pu.res, cw.res, cb.res], writes=[cv.res])
                            for r in (0, 1):
                                P.op(dve, lambda cv=cv, fc=fc, r=r: nc.vector.scalar_tensor_tensor(
                                    out=cv[:, W:WX], in0=pF[:, 0:8].rearrange("p (s r) -> p r s", r=2)[:, r, :],
                                    scalar=cw[:, r, fc:fc + 1], in1=cv[:, W:WX], op0=ALU.mult, op1=ALU.add),
                                    reads=[pF.res, cw.res, cv.res], writes=[cv.res])
                        cvs.append(cv)
                    sg = sg_t[i % 3]
                    P.op(act, lambda sg=sg, cv=cvs[0]: nc.scalar.activation(out=sg[:, 2:wk], in_=cv[:, 2:wk], func=AF.Silu),
                         reads=[cvs[0].res], writes=[sg.res])
                    P.op(dve, lambda sg=sg, cv=cvs[1], i=i: nc.vector.tensor_tensor(
                        out=actT[:, i, 2:wk], in0=sg[:, 2:wk], in1=cv[:, 2:wk], op=ALU.mult),
                        reads=[sg.res, cvs[1].res], writes=[actT.res])
                for ti, (c0, n) in enumerate(tiles):
                    for half in range(2):
                        pd = next_pg()
                        for i in range(NFC):
                            P.op(pe, lambda i=i, pd=pd: nc.tensor.matmul(
                                pd[0:n, :], lhsT=actT[:, i, c0:c0 + n], rhs=wdn[:, i, half * 512:(half + 1) * 512],
                                start=(i == 0), stop=(i == NFC - 1)),
                                reads=[actT.res, wdn.res], writes=[pd.res], inc=(i == NFC - 1))
                        P.op(dve, lambda pd=pd, half=half, ti=ti, n=n: nc.vector.tensor_tensor(
                            out=xs[0:n, ti, half * 512:(half + 1) * 512], in0=pd[0:n, :],
                            in1=xs[0:n, ti, half * 512:(half + 1) * 512], op=ALU.add),
                            reads=[pd.res, xs.res], writes=[xs.res])
                    ss = rms_rows(lambda ti=ti, n=n: xs[0:n, ti, :], xs.res, n, None)
                    P.op(dve, lambda ss=ss, ti=ti, n=n: nc.vector.scalar_tensor_tensor(
                        out=xs[0:n, ti, :], in0=xs[0:n, ti, :], scalar=ss[0:n, 0:1], in1=gfin[0:n, :],
                        op0=ALU.mult, op1=ALU.mult),
                        reads=[xs.res, ss.res, gfin.res], writes=[xs.res])
                    if ti < 4:
                        P.dma("sp", y_o[k, c0:c0 + n, :], xs[0:n, ti, :], reads=[xs.res])
                    else:
                        P.dma("sp", ys_o[:, :], xs[0:n, ti, :], reads=[xs.res])
            for r in range(2):
                for q4 in range(4):
                    P.dma("sp", cp_o[r:r + 1, q4 * 1408:(q4 + 1) * 1408].rearrange("r (c p) -> p (r c)", p=128),
                          upl[:, r, q4 * 11:(q4 + 1) * 11], reads=[upl.res], allow_slow_non_contiguous=True)
            if with_sample:
                for s in range(N_SEQ):
                    for q4 in range(4):
                        P.dma("sp", cs_o[s, 1:2, q4 * 1408:(q4 + 1) * 1408].rearrange("r (c p) -> p (r c)", p=128),
                              upl_s[:, s, q4 * 11:(q4 + 1) * 11], reads=[upl_s.res], allow_slow_non_contiguous=True)
                    P.dma("sp", cs_o[s, 0:1, :], stc_d[2 * s + 1:2 * s + 2, :])
            P.final_wait()
    return nc


def attn_head(nc, P, h, nkb, midx, ucount, L):
    pe, act, dve, pool = P.pe, P.act, P.dve, P.pool
    KT, Vb, qT, kt_res, v_res = L["KT"], L["Vb"], L["qT"], L["kt_res"], L["v_res"]
    masks, mask_res, ident, tri, negones = L["masks"], L["mask_res"], L["ident"], L["tri"], L["negones"]
    e_t, sp_t, a_t, acc32, accbf, w_t = L["e_t"], L["sp_t"], L["a_t"], L["acc32"], L["accbf"], L["w_t"]
    bias_t, o_at, pO, next_pg = L["bias_t"], L["o_at"], L["pO"], L["next_pg"]
    j, hb = h // 2, (h % 2) * 64
    po = pO[h % 2]
    kbs = list(range(nkb - 1, -1, -1))
    n = len(kbs)
    st_ = {}
    u0 = ucount[0]

    def s1(kb):
        p = next_pg()
        need_m = kb in midx
        P.op(pe, lambda: nc.tensor.matmul(
            p[:, 0:W], lhsT=KT[hb:hb + 64, j, kb * 128:(kb + 1) * 128], rhs=qT[hb:hb + 64, j, :],
            start=True, stop=not need_m),
            reads=[kt_res[kb], qT.res], writes=[p.res], inc=not need_m)
        if need_m:
            P.op(pe, lambda: nc.tensor.matmul(
                p[:, 0:W], lhsT=ident[:], rhs=masks[:, midx[kb], :], start=False, stop=True),
                reads=[ident.res, mask_res[midx[kb]]], writes=[p.res])
        st_[kb] = {"p": p}

    def s2(kb, u):
        p = st_[kb]["p"]
        e = e_t[u % 3]; s = sp_t[u % 3]
        P.op(act, lambda: nc.scalar.activation(out=e[:], in_=p[:, 0:W], func=AF.Exp,
                                               bias=bias_t[:, h:h + 1], scale=1.0),
             reads=[p.res, bias_t.res], writes=[e.res])
        P.op(act, lambda: nc.scalar.activation(out=s[:], in_=e[:], func=AF.Ln, bias=1.0, scale=1.0),
             reads=[e.res], writes=[s.res])
        st_[kb]["s"] = s
        st_[kb]["e"] = e

    def s3(kb, u, first):
        s = st_[kb]["s"]
        pc = next_pg()
        st_[kb]["pc"] = pc
        P.op(pe, lambda: nc.tensor.matmul(pc[:, 0:W], lhsT=tri[:], rhs=s[:], start=True, stop=first),
             reads=[tri.res, s.res], writes=[pc.res], inc=first)
        if not first:
            ab = accbf[(u - 1) % 3]
            P.op(pe, lambda: nc.tensor.matmul(pc[:, 0:W], lhsT=negones[:], rhs=ab[:], start=False, stop=True),
                 reads=[negones.res, ab.res], writes=[pc.res])
        abn = accbf[u % 3]
        if first:
            P.op(dve, lambda: nc.vector.tensor_copy(out=abn[:], in_=s[:]), reads=[s.res], writes=[abn.res])
        else:
            abo = accbf[(u - 1) % 3]
            P.op(dve, lambda: nc.vector.tensor_tensor(out=abn[:], in0=abo[:], in1=s[:], op=ALU.add),
                 reads=[abo.res, s.res], writes=[abn.res])

    def s4(kb, u):
        pc = st_[kb]["pc"]; e = st_[kb]["e"]
        w = w_t[u % 3]
        a = a_t[u % 3]
        P.op(act, lambda: nc.scalar.activation(out=w[:], in_=pc[:, 0:W], func=AF.Exp),
             reads=[pc.res], writes=[w.res])
        P.op(dve, lambda: nc.vector.tensor_tensor(out=a[:], in0=e[:], in1=w[:], op=ALU.mult),
             reads=[e.res, w.res], writes=[a.res])
        st_[kb]["a"] = a

    def s5(kb, first, last):
        a = st_[kb]["a"]
        P.op(pe, lambda: nc.tensor.matmul(po[0:64, 0:W], lhsT=Vb[:, kb, h * 64:(h + 1) * 64], rhs=a[:],
                                          start=first, stop=last),
             reads=[v_res[kb], a.res], writes=[po.res], inc=last)
        del st_[kb]

    s1(kbs[0])
    for i in range(n + 1):
        if i + 1 < n:
            s1(kbs[i + 1])
        if i < n:
            s2(kbs[i], u0 + i)
            s3(kbs[i], u0 + i, i == 0)
        if i >= 1:
            s4(kbs[i - 1], u0 + i - 1)
            s5(kbs[i - 1], i - 1 == 0, i - 1 == n - 1)
    ucount[0] += n
    P.op(dve, lambda: nc.vector.tensor_copy(out=o_at[:, h, 0:W], in_=po[0:64, 0:W]),
         reads=[po.res], writes=[o_at.res])


def sample_mixer(nc, P, L):
    pe, act, dve, pool = P.pe, P.act, P.dve, P.pool
    sb, esB, next_pg = L["sb"], L["esB"], L["next_pg"]
    hT_sam, w_qu, pw_bf, pscale, bias_t = L["hT_sam"], L["w_qu"], L["pw_bf"], L["pscale"], L["bias_t"]
    tri, negones, identf, o_at, o_pl = L["tri"], L["negones"], L["identf"], L["o_at"], L["o_pl"]
    ck_d, cv_d, stp_d, pt_d, ps_o, q_scr = L["ck_d"], L["cv_d"], L["stp_d"], L["pt_d"], L["ps_o"], L["q_scr"]
    NP = N_SEQ * NPG
    pti = sb("pti", [128, NP], I32, esB)
    ptf = sb("ptf", [128, NP], F32, esB)
    iop = sb("iop", [128, 1], F32, esB)
    idxi = sb("idxi", [128, NP], I32, esB)
    P.dma("sp", pti[:], pt_d.partition_broadcast(128), writes=[pti.res])
    P.op(pool, lambda: nc.gpsimd.iota(iop[:], pattern=[[0, 1]], base=0, channel_multiplier=1,
                                      allow_small_or_imprecise_dtypes=True), writes=[iop.res])
    P.op(pool, lambda: nc.gpsimd.tensor_copy(out=ptf[:], in_=pti[:]), reads=[pti.res], writes=[ptf.res])
    P.op(pool, lambda: nc.gpsimd.tensor_scalar(out=ptf[:], in0=ptf[:], scalar1=128.0, scalar2=iop[:, 0:1],
                                               op0=ALU.mult, op1=ALU.add), reads=[ptf.res, iop.res], writes=[ptf.res])
    P.op(pool, lambda: nc.gpsimd.tensor_copy(out=idxi[:], in_=ptf[:]), reads=[ptf.res], writes=[idxi.res])
    bd0 = sb("bd0", [8, 512], F32, esB)
    bd1 = sb("bd1", [8, 512], F32, esB)
    bdiag = sb("bdiag", [8, 512], F32, esB)
    ones8 = sb("ones8", [8, 1], BF16, esB)
    P.op(pool, lambda: nc.gpsimd.memset(bd0[:], 1.0), writes=[bd0.res])
    P.op(pool, lambda: nc.gpsimd.memset(ones8[:], 1.0), writes=[ones8.res])
    P.op(pool, lambda: nc.gpsimd.affine_select(out=bd1[:], in_=bd0[:], pattern=[[1, 512]], compare_op=ALU.is_ge,
                                                fill=0.0, base=0, channel_multiplier=-64),
         reads=[bd0.res], writes=[bd1.res])
    P.op(pool, lambda: nc.gpsimd.affine_select(out=bdiag[:], in_=bd1[:], pattern=[[-1, 512]], compare_op=ALU.is_ge,
                                                fill=0.0, base=63, channel_multiplier=64),
         reads=[bd1.res], writes=[bdiag.res])
    qscr_res = Res()
    pq = next_pg()
    for dc in range(8):
        P.op(pe, lambda dc=dc: nc.tensor.matmul(pq[0:N_SEQ, :], lhsT=hT_sam[:, dc, :], rhs=w_qu[:, dc, 0:512],
                                                start=(dc == 0), stop=(dc == 7)),
             reads=[hT_sam.res, w_qu.res], writes=[pq.res], inc=(dc == 7))
    q_tok = sb("q_tok", [N_SEQ, 512], F32, esB)
    P.op(dve, lambda: nc.vector.tensor_scalar(out=q_tok[:], in0=pq[0:N_SEQ, :], scalar1=0.125, scalar2=None, op0=ALU.mult),
         reads=[pq.res], writes=[q_tok.res])
    P.dma("sp", q_scr[:, :], q_tok[:], reads=[q_tok.res], writes=[qscr_res])
    pu = next_pg()
    for dc in range(8):
        P.op(pe, lambda dc=dc: nc.tensor.matmul(pu[0:N_SEQ, :], lhsT=hT_sam[:, dc, :], rhs=w_qu[:, dc, 512:1024],
                                                start=(dc == 0), stop=(dc == 7)),
             reads=[hT_sam.res, w_qu.res], writes=[pu.res], inc=(dc == 7))
    u_tok = sb("u_tok", [N_SEQ, 512], F32, esB)
    P.op(act, lambda: nc.scalar.copy(out=u_tok[:], in_=pu[0:N_SEQ, :]), reads=[pu.res], writes=[u_tok.res])
    P.dma("sp", ps_o[:, 14, :], u_tok[:], reads=[u_tok.res])
    for s in range(N_SEQ):
        P.dma("sp", ps_o[s, 0:14, :], stp_d[s * 15 + 1:s * 15 + 15, :])
    uTs = sb("uTs", [128, 4, N_SEQ], F32, esB)
    for g in range(4):
        pu2 = next_pg()
        for dc in range(8):
            P.op(pe, lambda g=g, dc=dc, pu2=pu2: nc.tensor.matmul(
                pu2[:, 0:N_SEQ], lhsT=w_qu[:, dc, 512 + g * 128:512 + (g + 1) * 128], rhs=hT_sam[:, dc, :],
                start=(dc == 0), stop=(dc == 7)),
                reads=[hT_sam.res, w_qu.res], writes=[pu2.res], inc=(dc == 7))
        P.op(act, lambda g=g, pu2=pu2: nc.scalar.copy(out=uTs[:, g, :], in_=pu2[:, 0:N_SEQ]),
             reads=[pu2.res], writes=[uTs.res])
    st60 = sb("st60", [N_SEQ * 15, 512], F32, esB)
    P.dma("sp", st60[:], stp_d[:, :], writes=[st60.res])
    stT = sb("stT", [128, 4, N_SEQ * 15], F32, esB)
    pst = next_pg()
    for g in range(4):
        P.op(pe, lambda g=g: nc.tensor.transpose(out=pst[:, g * 60:(g + 1) * 60], in_=st60[0:60, g * 128:(g + 1) * 128],
                                                 identity=identf[0:60, 0:60]),
             reads=[st60.res, identf.res], writes=[pst.res], inc=(g == 3))
    P.op(act, lambda: nc.scalar.copy(out=stT[:], in_=pst[:, 0:240].rearrange("p (g c) -> p g c", g=4)),
         reads=[pst.res], writes=[stT.res])
    ssum = sb("ssum", [128, N_SEQ], F32, esB)
    d_bf = [sb(f"d_bf{i}", [128, N_SEQ], BF16, esB) for i in range(2)]
    for g in range(4):
        wg = 2 << g
        nr = wg - 1
        P.op(dve, lambda g=g, nr=nr: nc.vector.tensor_reduce(
            out=ssum[:], in_=stT[:, g, :].rearrange("p (s r) -> p s r", r=15)[:, :, 15 - nr:15], axis=AX.X, op=ALU.add),
            reads=[stT.res], writes=[ssum.res])
        P.op(dve, lambda g=g: nc.vector.tensor_tensor(out=ssum[:], in0=ssum[:], in1=uTs[:, g, :], op=ALU.add),
             reads=[ssum.res, uTs.res], writes=[ssum.res])
        db = d_bf[g % 2]
        P.op(dve, lambda g=g, wg=wg, db=db: nc.vector.scalar_tensor_tensor(
            out=db[:], in0=ssum[:], scalar=1.0 / wg, in1=uTs[:, g, :], op0=ALU.mult, op1=ALU.subtract),
            reads=[ssum.res, uTs.res], writes=[db.res])
        pp = next_pg()
        P.op(pe, lambda g=g, db=db, pp=pp: nc.tensor.matmul(pp[:, 0:N_SEQ], lhsT=pw_bf[:, g, :], rhs=db[:],
                                                            start=True, stop=True),
             reads=[pw_bf.res, db.res], writes=[pp.res])
        P.op(dve, lambda g=g, pp=pp: nc.vector.tensor_scalar(
            out=o_pl[:, g, W:WX], in0=pp[:, 0:N_SEQ], scalar1=pscale[:, g:g + 1], scalar2=None, op0=ALU.mult),
            reads=[pp.res, pscale.res], writes=[o_pl.res])
    qb = [sb(f"qb{i}", [128, 512], F32, esB) for i in range(2)]
    Kp = [sb(f"Kp{i}", [128, 512], F32, esB) for i in range(6)]
    Vp = [sb(f"Vp{i}", [128, 512], BF16, esB) for i in range(6)]
    tmp = [sb(f"ktmp{i}", [128, 512], F32, esB) for i in range(2)]
    Z = sb("Zs", [128, 512], F32, esB)
    e8 = sb("e8", [128, 512], F32, esB)
    sp8 = sb("sp8", [128, 512], BF16, esB)
    S0 = sb("S0", [128, 512], F32, esB)
    Sa = sb("Sa", [128, 512], F32, esB)
    Sb_ = sb("Sb", [128, 512], F32, esB)
    A8 = sb("A8", [128, 512], BF16, esB)
    m8 = sb("m8", [8, 512], BF16, esB)
    hv = lambda t: t[:].rearrange("p (g h) -> p h g", h=8)
    for s in range(N_SEQ):
        q_t = qb[s % 2]
        P.dma("sp", q_t[:], q_scr[s:s + 1, :].partition_broadcast(128), reads=[qscr_res], writes=[q_t.res])
        for pg in range(NPG):
            col = s * NPG + pg
            kp = Kp[pg % 6]
            P.dma("pool", None, None, reads=[idxi.res], writes=[kp.res],
                  fn=lambda kp=kp, col=col: nc.gpsimd.indirect_dma_start(
                      out=kp[:], out_offset=None, in_=ck_d,
                      in_offset=bass.IndirectOffsetOnAxis(ap=idxi[:, col:col + 1], axis=0)))
            tm = tmp[pg % 2]
            eng = dve
            P.op(eng, lambda eng=eng, tm=tm, kp=kp: eng.h.tensor_tensor(out=tm[:], in0=kp[:], in1=q_t[:], op=ALU.mult),
                 reads=[kp.res, q_t.res], writes=[tm.res])
            P.op(dve, lambda tm=tm, pg=pg: nc.vector.tensor_reduce(
                out=Z[:, pg * 8:(pg + 1) * 8], in_=tm[:].rearrange("p (h d) -> p h d", h=8), axis=AX.X, op=ALU.add),
                reads=[tm.res], writes=[Z.res])
        for h in range(8):
            P.op(act, lambda h=h: nc.scalar.activation(out=hv(e8)[:, h, :], in_=hv(Z)[:, h, :], func=AF.Exp,
                                                       bias=bias_t[:, h:h + 1], scale=1.0),
                 reads=[Z.res, bias_t.res], writes=[e8.res])
        P.op(act, lambda: nc.scalar.activation(out=sp8[:], in_=e8[:], func=AF.Ln, bias=1.0, scale=1.0),
             reads=[e8.res], writes=[sp8.res])
        pc = next_pg()
        P.op(pe, lambda: nc.tensor.matmul(pc[:, :], lhsT=tri[:], rhs=sp8[:], start=True, stop=True),
             reads=[tri.res, sp8.res], writes=[pc.res])
        pt_ = next_pg()
        P.op(pe, lambda: nc.tensor.matmul(pt_[:, :], lhsT=negones[:], rhs=sp8[:], start=True, stop=True),
             reads=[negones.res, sp8.res], writes=[pt_.res])
        P.op(act, lambda: nc.scalar.copy(out=S0[:], in_=pt_[:, :]), reads=[pt_.res], writes=[S0.res])
        cur = S0
        for i, dd in enumerate((1, 2, 4, 8, 16, 32)):
            nxt = Sa if i % 2 == 0 else Sb_
            n0 = 512 - 8 * dd
            P.op(dve, lambda cur=cur, nxt=nxt, n0=n0, dd=dd: nc.vector.tensor_tensor(
                out=nxt[:, 0:n0], in0=cur[:, 0:n0], in1=cur[:, 8 * dd:512], op=ALU.add),
                reads=[cur.res], writes=[nxt.res])
            P.op(dve, lambda cur=cur, nxt=nxt, n0=n0: nc.vector.tensor_copy(out=nxt[:, n0:512], in_=cur[:, n0:512]),
                 reads=[cur.res, nxt.res], writes=[nxt.res])
            cur = nxt
        P.op(dve, lambda cur=cur: nc.vector.tensor_tensor(out=cur[:], in0=cur[:], in1=S0[:], op=ALU.subtract),
             reads=[cur.res, S0.res], writes=[cur.res])
        P.op(dve, lambda cur=cur: nc.vector.tensor_tensor(out=cur[:], in0=cur[:], in1=Z[:], op=ALU.add),
             reads=[cur.res, Z.res], writes=[cur.res])
        P.op(dve, lambda cur=cur: nc.vector.tensor_tensor(out=cur[:], in0=cur[:], in1=pc[:, :], op=ALU.add),
             reads=[cur.res, pc.res], writes=[cur.res])
        for h in range(8):
            P.op(act, lambda h=h, cur=cur: nc.scalar.activation(out=hv(A8)[:, h, :], in_=hv(cur)[:, h, :], func=AF.Exp,
                                                                bias=bias_t[:, h:h + 1], scale=1.0),
                 reads=[cur.res, bias_t.res], writes=[A8.res])
        pov = next_pg()
        for pg in range(NPG):
            col = s * NPG + pg
            vp = Vp[pg % 6]
            P.dma("pool", None, None, reads=[idxi.res], writes=[vp.res],
                  fn=lambda vp=vp, col=col: nc.gpsimd.indirect_dma_start(
                      out=vp[:], out_offset=None, in_=cv_d,
                      in_offset=bass.IndirectOffsetOnAxis(ap=idxi[:, col:col + 1], axis=0)))
            P.op(pe, lambda vp=vp, pg=pg: nc.tensor.matmul(pov[0:8, :], lhsT=A8[:, pg * 8:(pg + 1) * 8], rhs=vp[:],
                                                           start=(pg == 0), stop=(pg == NPG - 1)),
                 reads=[A8.res, vp.res], writes=[pov.res], inc=True)
        P.op(dve, lambda: nc.vector.tensor_tensor(out=m8[:], in0=pov[0:8, :], in1=bdiag[:], op=ALU.mult),
             reads=[pov.res, bdiag.res], writes=[m8.res])
        pox = next_pg()
        for h in range(8):
            P.op(pe, lambda h=h: nc.tensor.matmul(pox[0:64, h:h + 1], lhsT=m8[0:8, h * 64:(h + 1) * 64], rhs=ones8[0:8, 0:1],
                                                  start=True, stop=True),
                 reads=[m8.res, ones8.res], writes=[pox.res], inc=(h == 7))
        P.op(dve, lambda s=s: nc.vector.tensor_copy(out=o_at[:, :, W + s], in_=pox[0:64, 0:8]),
             reads=[pox.res], writes=[o_at.res])


_NC_CACHE = {}
_RUNNER = [None]
_REMAP = [None, None]


_STOP = [None]


def _get_nc(with_sample):
    key = (with_sample, _STOP[0])
    if key not in _NC_CACHE:
        _NC_CACHE[key] = build(with_sample, _STOP[0])
    return _NC_CACHE[key]


def kernel(x_prompt, x_sample, cache_k, cache_v, state_pool, state_conv, page_table,
           meta_tokens, norm_mix_g, w_in, sb_bias, pool_w, pool_scale, w_out, norm_ffn_g,
           w_up, conv_w, conv_b, w_down, norm_final_g, _with_sample=True):
    f32 = np.float32
    B = x_prompt.shape[0]
    nc = _get_nc(_with_sample)
    in_maps = []
    ck2 = cv2 = None
    if _with_sample:
        ck2 = np.ascontiguousarray(cache_k, dtype=f32).reshape(-1, 512)
        cv2 = np.ascontiguousarray(cache_v, dtype=f32).reshape(-1, 512)
    for c in range(8):
        b, g = c // 2, c % 2
        xa = np.zeros((TP, D), f32)
        xa[:N_META] = meta_tokens
        xa[N_META:T] = x_prompt[b]
        xsl = np.zeros((NS, WR, D), f32)
        qp = np.zeros((1, NS * W), f32)
        for k in range(NS):
            s = slot_start(g, k)
            lo, hi = s - HALO, s + W
            a, e = max(lo, 0), min(hi, T)
            xsl[k, a - lo:e - lo] = xa[a:e]
            qp[0, k * W:(k + 1) * W] = np.arange(s, s + W, dtype=f32)
        m = {
            "xall": xa, "xslot": xsl, "qpos": qp,
            "w_in": np.asarray(w_in, f32), "w_out": np.asarray(w_out, f32), "w_up": np.asarray(w_up, f32),
            "w_down": np.asarray(w_down, f32),
            "norm_mix_g": np.asarray(norm_mix_g, f32).reshape(D, 1),
            "norm_ffn_g": np.asarray(norm_ffn_g, f32).reshape(D, 1),
            "norm_final_g": np.asarray(norm_final_g, f32).reshape(1, D),
            "sb_bias": np.asarray(sb_bias, f32).reshape(1, 8),
            "pool_w": np.asarray(pool_w, f32), "pool_scale": np.asarray(pool_scale, f32),
            "conv_w": np.asarray(conv_w, f32), "conv_b": np.asarray(conv_b, f32).reshape(1, 2 * DFF),
        }
        if _with_sample:
            sl = slice(N_SEQ * c, N_SEQ * (c + 1))
            m.update({
                "x_sample": np.asarray(x_sample[sl], f32).reshape(N_SEQ, D),
                "cache_k": (_REMAP[0](c, ck2) if _REMAP[0] else ck2), "cache_v": (_REMAP[0](c, cv2) if _REMAP[0] else cv2),
                "state_pool": np.asarray(state_pool[sl], f32).reshape(N_SEQ * 15, 512),
                "state_conv": np.asarray(state_conv[sl], f32).reshape(N_SEQ * 2, 2 * DFF),
                "page_table": (_REMAP[1](c) if _REMAP[1] else np.asarray(page_table[sl], np.int32).reshape(1, N_SEQ * NPG)),
            })
        in_maps.append(m)
    if _RUNNER[0] is not None:
        res = _RUNNER[0](nc, in_maps)
    else:
        res = run_bass_kernel_spmd(nc, in_maps, core_ids=list(range(8))).results

    y_prompt = np.zeros((B, 4096, D), f32)
    k_prompt = np.zeros((B, T, 8, 64), f32)
    v_prompt = np.zeros((B, T, 8, 64), f32)
    pool_prompt = np.zeros((B, 15, 512), f32)
    conv_prompt = np.zeros((B, 2, 2 * DFF), f32)
    for c in range(8):
        b, g = c // 2, c % 2
        r = res[c]
        for k in range(NS):
            r0 = OWN * TILES[g][k]
            nv = min(OWN, 4096 - r0)
            y_prompt[b, r0:r0 + nv] = r["y_slot"][k, 2:2 + nv]
        if g == 0:
            k_prompt[b] = r["k_all"][:T].reshape(T, 8, 64)
            v_prompt[b] = r["v_all"][:T].reshape(T, 8, 64)
        else:
            pool_prompt[b] = r["pool_last"]
            conv_prompt[b] = r["conv_last"]
    if not _with_sample:
        return (y_prompt, None, k_prompt, v_prompt, pool_prompt, conv_prompt, None, None, None, None)
    DB = x_sample.shape[0]
    y_sample = np.zeros((DB, 1, D), f32)
    k_sample = np.zeros((DB, 1, 8, 64), f32)
    v_sample = np.zeros((DB, 1, 8, 64), f32)
    pool_sample = np.zeros((DB, 15, 512), f32)
    conv_sample = np.zeros((DB, 2, 2 * DFF), f32)
    for c in range(8):
        r = res[c]
        sl = slice(N_SEQ * c, N_SEQ * (c + 1))
        y_sample[sl, 0] = r["y_sample"]
        k_sample[sl, 0] = r["k_sample"].reshape(N_SEQ, 8, 64)
        v_sample[sl, 0] = r["v_sample"].reshape(N_SEQ, 8, 64)
        pool_sample[sl] = r["pool_sample"]
        conv_sample[sl] = r["conv_sample"]
    return (y_prompt, y_sample, k_prompt, v_prompt, pool_prompt, conv_prompt,
            k_sample, v_sample, pool_sample, conv_sample)
```

```python
import numpy as np
from contextlib import ExitStack
import concourse.bass as bass
import concourse.mybir as mybir
from concourse.bass_utils import run_bass_kernel_spmd

F32 = mybir.dt.float32
BF16 = mybir.dt.bfloat16
I32 = mybir.dt.int32
AF = mybir.ActivationFunctionType
ALU = mybir.AluOpType
AX = mybir.AxisListType

D = 1024
T = 4112
NBLK = 33
TP = NBLK * 128
N_META = 16
NS = 5
W = 412
HALO = 16
WR = W + HALO
OWN = 410
DFF = 2816
NFC = DFF // 128
EPS = 1e-6
NEG = -30000.0
QT = [(0, 128), (128, 128), (256, 128), (384, 28)]
N_SEQ = 4
NPG = 64
SAME_ENGINE_SYNC = True


TILES = ((0, 3, 4, 7, 8), (1, 2, 5, 6, 9))
LAST_S = 14 + OWN * 9
PP0 = 4097 - LAST_S + HALO
CP0 = 4110 - LAST_S


def slot_start(g, k):
    return 14 + OWN * TILES[g][k]


def n_kb_for_slot(k):
    last = 14 + OWN * (2 * k + 1) + W - 1
    return min(NBLK, last // 128 + 1)


def kb_needs_mask(k, kb):
    return not (128 * kb + 127 < 14 + OWN * 2 * k)


class SemObj:
    def __init__(self, sem):
        self.sem = sem
        self.cnt = 0


class Eng(SemObj):
    def __init__(self, sem, h, name):
        super().__init__(sem)
        self.h = h
        self.name = name
        self.waited = {}


class Res:
    __slots__ = ("w", "r", "excl")

    def __init__(self, excl=False):
        self.w = None
        self.r = {}
        self.excl = excl


class Prog:
    def __init__(self, nc, es):
        self.nc = nc
        self.es = es
        mk = lambda n: es.enter_context(nc.semaphore(n))
        self.pe = Eng(mk("s_pe"), nc.tensor, "pe")
        self.act = Eng(mk("s_act"), nc.scalar, "act")
        self.dve = Eng(mk("s_dve"), nc.vector, "dve")
        self.pool = Eng(mk("s_pool"), nc.gpsimd, "pool")
        self.sp = Eng(mk("s_sp"), nc.sync, "sp")
        self.dsems = {}
        for q in ("sp", "pool", "act"):
            self.dsems[q] = [SemObj(mk(f"d_{q}{i}")) for i in range(12)]
        self.dnext = {"sp": 0, "pool": 0, "act": 0}

    def _wait(self, eng, toks):
        for tk in toks:
            so, v = tk[0], tk[1]
            if so is eng and (not SAME_ENGINE_SYNC or eng is self.pe or len(tk) == 3):
                continue
            if eng.waited.get(so, 0) < v:
                eng.h.wait_ge(so.sem, v)
                eng.waited[so] = v

    @staticmethod
    def _deps(reads, writes):
        toks = []
        for r in reads:
            if r.w is not None:
                toks.append(r.w)
            if r.excl:
                toks.extend(r.r.items())
        for w in writes:
            if w.w is not None:
                toks.append(w.w)
            toks.extend((so, v, "war") for so, v in w.r.items())
        return toks

    @staticmethod
    def _record(tok, reads, writes):
        so, v = tok
        for r in reads:
            if r.r.get(so, 0) < v:
                r.r[so] = v
        for w in writes:
            w.w = tok
            w.r = {}

    def op(self, eng, fn, reads=(), writes=(), inc=True):
        self._wait(eng, self._deps(reads, writes))
        ins = fn()
        if inc:
            ins.then_inc(eng.sem, 1)
            eng.cnt += 1
            tok = (eng, eng.cnt)
        else:
            tok = (eng, eng.cnt + 1)
        self._record(tok, reads, writes)
        return ins

    def dma(self, q, out, in_, reads=(), writes=(), fn=None, **kw):
        eng = {"sp": self.sp, "pool": self.pool, "act": self.act}[q]
        lst = self.dsems[q]
        d = lst[self.dnext[q] % len(lst)]
        self.dnext[q] += 1
        toks = self._deps(reads, writes)
        if d.cnt > 0:
            toks.append((d, d.cnt))
        self._wait(eng, toks)
        if fn is not None:
            fn().then_inc(d.sem, 16)
        else:
            eng.h.dma_start(out=out, in_=in_, **kw).then_inc(d.sem, 16)
        d.cnt += 16
        self._record((d, d.cnt), reads, writes)

    def final_wait(self):
        toks = []
        for q in self.dsems:
            for d in self.dsems[q]:
                if d.cnt:
                    toks.append((d, d.cnt))
        for e in (self.pe, self.act, self.dve, self.pool):
            if e.cnt:
                toks.append((e, e.cnt))
        self._wait(self.sp, toks)


class Tl:
    def __init__(self, t, excl=False):
        self.t = t
        self.res = Res(excl)

    def __getitem__(self, k):
        return self.t[k]


WX = W + 4
NPOOL = [2560]
SAMPLE_STEPS_PER_BLOCK = 18
import os
DBG = int(os.environ.get('KDBG', '99'))


def build(with_sample=True, stop_after=None):
    nc = bass.Bass("TRN2", target_bir_lowering=False)
    dr = lambda name, shape, dt=F32, kind="ExternalInput": nc.dram_tensor(name, shape, dt, kind=kind).ap()
    xall = dr("xall", [TP, D])
    xslot = dr("xslot", [NS, WR, D])
    qpos_d = dr("qpos", [1, NS * W])
    w_in_d = dr("w_in", [D, 2048])
    w_out_d = dr("w_out", [D, D])
    w_up_d = dr("w_up", [D, 2 * DFF])
    w_down_d = dr("w_down", [DFF, D])
    g_mix_d = dr("norm_mix_g", [D, 1])
    g_ffn_d = dr("norm_ffn_g", [D, 1])
    g_fin_d = dr("norm_final_g", [1, D])
    sbb_d = dr("sb_bias", [1, 8])
    pool_w_d = dr("pool_w", [4, 128, 128])
    pool_s_d = dr("pool_scale", [4, 128])
    conv_w_d = dr("conv_w", [3, 2 * DFF])
    conv_b_d = dr("conv_b", [1, 2 * DFF])
    y_o = dr("y_slot", [NS, W, D], kind="ExternalOutput")
    k_o = dr("k_all", [TP, 512], kind="ExternalOutput")
    v_o = dr("v_all", [TP, 512], kind="ExternalOutput")
    pp_o = dr("pool_last", [15, 512], kind="ExternalOutput")
    cp_o = dr("conv_last", [2, 2 * DFF], kind="ExternalOutput")
    o_scr = nc.dram_tensor("o_scr", [NS, 128, 12, WX], BF16).ap()
    if with_sample:
        xsam = dr("x_sample", [N_SEQ, D])
        ck_d = dr("cache_k", [NPOOL[0] * 128, 512])
        cv_d = dr("cache_v", [NPOOL[0] * 128, 512])
        stp_d = dr("state_pool", [N_SEQ * 15, 512])
        stc_d = dr("state_conv", [N_SEQ * 2, 2 * DFF])
        pt_d = dr("page_table", [1, N_SEQ * NPG], I32)
        ys_o = dr("y_sample", [N_SEQ, D], kind="ExternalOutput")
        ks_o = dr("k_sample", [N_SEQ, 512], kind="ExternalOutput")
        vs_o = dr("v_sample", [N_SEQ, 512], kind="ExternalOutput")
        ps_o = dr("pool_sample", [N_SEQ, 15, 512], kind="ExternalOutput")
        cs_o = dr("conv_sample", [N_SEQ, 2, 2 * DFF], kind="ExternalOutput")
        q_scr = nc.dram_tensor("q_scr", [N_SEQ, 512], F32).ap()

    es = ExitStack()
    with es:
        P = Prog(nc, es)
        pe, act, dve, pool = P.pe, P.act, P.dve, P.pool

        def sb(name, shape, dt=F32, st=es):
            return Tl(st.enter_context(nc.sbuf_tensor(name, shape, dt)))

        def ps(name, shape, dt=F32):
            return Tl(es.enter_context(nc.psum_tensor(name, shape, dt)), excl=True)

        pT = ps("pT", [128, 1024], BF16)
        pG = [ps(f"pG{i}", [128, 512], F32) for i in range(5)]
        pO = [ps(f"pO{i}", [128, 512], F32) for i in range(2)]
        gi = [0]

        def next_pg():
            t = pG[gi[0] % len(pG)]
            gi[0] += 1
            return t

        ident = sb("ident", [128, 128], BF16)
        identf = sb("identf", [128, 128], F32)
        tri = sb("tri", [128, 128], BF16)
        negones = sb("negones", [128, 128], BF16)
        onesf = sb("onesf", [128, 128], F32)
        negf = sb("negf", [128, 128], F32)
        kpos = sb("kpos", [128, NBLK], F32)
        mhalf = sb("mhalf", [128, 1], F32)
        P.op(pool, lambda: nc.gpsimd.memset(onesf[:], 1.0), writes=[onesf.res])
        P.op(pool, lambda: nc.gpsimd.memset(negf[:], -1.0), writes=[negf.res])
        P.op(pool, lambda: nc.gpsimd.memset(mhalf[:], -0.5), writes=[mhalf.res])
        P.op(pool, lambda: nc.gpsimd.memset(negones[:], -1.0), writes=[negones.res])
        P.op(pool, lambda: nc.gpsimd.affine_select(out=identf[:], in_=onesf[:], pattern=[[-1, 128]],
                                                    compare_op=ALU.is_equal, fill=0.0, base=0, channel_multiplier=1),
             reads=[onesf.res], writes=[identf.res])
        P.op(pool, lambda: nc.gpsimd.tensor_copy(out=ident[:], in_=identf[:]), reads=[identf.res], writes=[ident.res])
        P.op(pool, lambda: nc.gpsimd.affine_select(out=tri[:], in_=negf[:], pattern=[[-1, 128]],
                                                    compare_op=ALU.is_ge, fill=0.0, base=0, channel_multiplier=1),
             reads=[negf.res], writes=[tri.res])
        P.op(pool, lambda: nc.gpsimd.iota(kpos[:], pattern=[[128, NBLK]], base=0, channel_multiplier=1,
                                          allow_small_or_imprecise_dtypes=True), writes=[kpos.res])

        bias_t = sb("bias_t", [128, 8])
        P.dma("sp", bias_t[:], sbb_d.partition_broadcast(128), writes=[bias_t.res])
        gfin = sb("gfin", [128, D])
        P.dma("sp", gfin[:], g_fin_d.partition_broadcast(128), writes=[gfin.res])
        gmix = sb("gmix", [128, 8])
        P.dma("sp", gmix[:], g_mix_d.rearrange("(c p) o -> p (c o)", p=128), writes=[gmix.res],
              allow_slow_non_contiguous=True)
        gffn = sb("gffn", [128, 8])
        P.dma("sp", gffn[:], g_ffn_d.rearrange("(c p) o -> p (c o)", p=128), writes=[gffn.res],
              allow_slow_non_contiguous=True)
        gmix_bc = sb("gmix_bc", [128, 8, 128])
        gffn_bc = sb("gffn_bc", [128, 8, 128])
        for dc in range(8):
            P.op(pool, lambda dc=dc: nc.gpsimd.tensor_scalar(out=gmix_bc[:, dc, :], in0=onesf[:], scalar1=gmix[:, dc:dc + 1],
                                                             scalar2=None, op0=ALU.mult),
                 reads=[onesf.res, gmix.res], writes=[gmix_bc.res])
            P.op(pool, lambda dc=dc: nc.gpsimd.tensor_scalar(out=gffn_bc[:, dc, :], in0=onesf[:], scalar1=gffn[:, dc:dc + 1],
                                                             scalar2=None, op0=ALU.mult),
                 reads=[onesf.res, gffn.res], writes=[gffn_bc.res])
        qpos_bc = sb("qpos_bc", [128, NS * W])
        P.dma("sp", qpos_bc[:], qpos_d.partition_broadcast(128), writes=[qpos_bc.res])
        pscale = sb("pscale", [128, 4])
        P.dma("sp", pscale[:], pool_s_d.rearrange("g c -> c g"), writes=[pscale.res], allow_slow_non_contiguous=True)
        cw = sb("cw", [128, 3, 44])
        cb = sb("cb", [128, 44])
        for q4 in range(4):
            cs_ = slice(q4 * 1408, (q4 + 1) * 1408)
            for i3 in range(3):
                P.dma("sp", cw[:, i3, q4 * 11:(q4 + 1) * 11], conv_w_d[i3:i3 + 1, cs_].rearrange("o (c p) -> p (o c)", p=128),
                      writes=[cw.res], allow_slow_non_contiguous=True)
            P.dma("sp", cb[:, q4 * 11:(q4 + 1) * 11], conv_b_d[:, cs_].rearrange("o (c p) -> p (o c)", p=128),
                  writes=[cb.res], allow_slow_non_contiguous=True)
        pw_bf = sb("pw_bf", [128, 4, 128], BF16)
        P.dma("pool", pw_bf[:], pool_w_d.rearrange("g c e -> c g e"), writes=[pw_bf.res])

        junk = [sb(f"junk{i}", [128, D], BF16) for i in range(2)]
        ssq = [sb(f"ssq{i}", [128, 1]) for i in range(4)]
        xn = [sb(f"xn{i}", [128, D], BF16) for i in range(2)]
        cnt = {"x": 0, "ss": 0, "xn": 0, "hT": 0, "j": 0}

        def rms_rows(x_ap_fn, x_res, rows, out_bf_tl):
            jk = junk[cnt["j"] % 2]; cnt["j"] += 1
            ss = ssq[cnt["ss"] % 4]; cnt["ss"] += 1
            P.op(act, lambda: nc.scalar.activation(out=jk[0:rows, :], in_=x_ap_fn(), func=AF.Square,
                                                   accum_out=ss[0:rows, :]),
                 reads=[x_res], writes=[jk.res, ss.res])
            P.op(dve, lambda: nc.vector.tensor_scalar(out=ss[0:rows, :], in0=ss[0:rows, :], scalar1=1.0 / D,
                                                      scalar2=EPS, op0=ALU.mult, op1=ALU.add),
                 reads=[ss.res], writes=[ss.res])
            P.op(pool, lambda: nc.gpsimd.tensor_tensor(out=ss[0:rows, :], in0=ss[0:rows, :], in1=mhalf[0:rows, :],
                                                       op=ALU.pow),
                 reads=[ss.res, mhalf.res], writes=[ss.res])
            if out_bf_tl is not None:
                P.op(dve, lambda: nc.vector.tensor_scalar(out=out_bf_tl[0:rows, :], in0=x_ap_fn(),
                                                          scalar1=ss[0:rows, 0:1], scalar2=None, op0=ALU.mult),
                     reads=[x_res, ss.res], writes=[out_bf_tl.res])
            return ss

        def transpose_rows(xn_tl, rows, dst_ap_fn, dst_res, g_bc):
            for dc in range(8):
                P.op(pe, lambda dc=dc: nc.tensor.transpose(out=pT[:, dc * 128:dc * 128 + rows],
                                                           in_=xn_tl[0:rows, dc * 128:(dc + 1) * 128],
                                                           identity=ident[0:rows, 0:rows]),
                     reads=[xn_tl.res, ident.res], writes=[pT.res], inc=(dc == 7))
            P.op(dve, lambda: nc.vector.tensor_tensor(
                out=dst_ap_fn(), in0=pT[:].rearrange("p (c t) -> p c t", c=8)[:, :, 0:rows],
                in1=g_bc[:, :, 0:rows], op=ALU.mult),
                reads=[pT.res, g_bc.res], writes=[dst_res])

        def barrier():
            toks = [(e, e.cnt) for e in (pe, act, dve, pool) if e.cnt]
            for q in P.dsems:
                toks += [(d, d.cnt) for d in P.dsems[q] if d.cnt]
            for e in (pe, act, dve, pool, P.sp):
                P._wait(e, toks)

        hT_sam = sb("hT_sam", [128, 8, N_SEQ], BF16) if with_sample else None
        if stop_after == "consts":
            P.final_wait()
            return nc

        esAB = ExitStack()
        KT = sb("KT", [128, 4, TP], BF16, esAB)
        Vb = sb("Vb", [128, NBLK, 512], BF16, esAB)
        kt_res = [Res() for _ in range(NBLK)]
        v_res = [Res() for _ in range(NBLK)]
        w_qu = sb("w_qu", [128, 8, 1024], BF16, esAB)
        sam_state = {}
        if with_sample:
            sam_state.update(
                idxi=sb("idxi", [128, N_SEQ * NPG], I32, esAB), bdiag=sb("bdiag", [8, 512], BF16, esAB),
                ones8=sb("ones8", [8, 1], BF16, esAB), A8all=sb("A8all", [128, N_SEQ, 512], BF16, esAB))
        with ExitStack() as esA:
            w_kv = sb("w_kv", [128, 8, 1024], BF16, esA)
            P.dma("pool", w_kv[:], w_in_d[:, 512:1536].rearrange("(c p) n -> p c n", p=128), writes=[w_kv.res])
            P.dma("pool", w_qu[:, :, 0:512], w_in_d[:, 0:512].rearrange("(c p) n -> p c n", p=128), writes=[w_qu.res])
            P.dma("pool", w_qu[:, :, 512:1024], w_in_d[:, 1536:2048].rearrange("(c p) n -> p c n", p=128), writes=[w_qu.res])
            xt = [sb(f"xt{i}", [128, D], F32, esA) for i in range(2)]
            hT = [sb(f"hT{i}", [128, 8, 128], BF16, esA) for i in range(3)]
            ko = [sb(f"ko{i}", [128, 512], F32, esA) for i in range(2)]
            vo = [sb(f"vo{i}", [128, 512], F32, esA) for i in range(2)]
            nblocks = NBLK + (1 if with_sample else 0)
            if stop_after and stop_after.startswith('A') and len(stop_after) > 1:
                nblocks = int(stop_after[1:])
            order = list(range(nblocks))
            if with_sample and nblocks == NBLK + 1:
                order = [NBLK] + list(range(NBLK))
            esS = ExitStack()
            sgen = None
            for tb in order:
                if sgen is not None:
                    for _ in range(SAMPLE_STEPS_PER_BLOCK):
                        if next(sgen, "done") == "done":
                            break
                sam = tb == NBLK
                rows = N_SEQ if sam else 128
                x_tl = xt[tb % 2]
                if sam:
                    P.dma("sp", x_tl[0:rows, :], xsam[:, :], writes=[x_tl.res])
                else:
                    P.dma("sp", x_tl[:], xall[tb * 128:(tb + 1) * 128, :], writes=[x_tl.res])
                if DBG < 2:
                    continue
                xn_tl = xn[cnt["xn"] % 2]; cnt["xn"] += 1
                rms_rows(lambda x_tl=x_tl, rows=rows: x_tl[0:rows, :], x_tl.res, rows, xn_tl)
                if DBG < 3:
                    continue
                if sam:
                    h_ap = lambda: hT_sam[:]
                    h_res = hT_sam.res
                    h_rd = lambda dc: hT_sam[:, dc, :]
                else:
                    h_tl = hT[tb % 3]
                    h_ap = lambda h_tl=h_tl: h_tl[:]
                    h_res = h_tl.res
                    h_rd = lambda dc, h_tl=h_tl: h_tl[:, dc, :]
                transpose_rows(xn_tl, rows, h_ap, h_res, gmix_bc)
                if DBG < 4:
                    continue
                if not sam:
                    pk = next_pg()
                    for j in range(4):
                        for dc in range(8):
                            P.op(pe, lambda j=j, dc=dc, pk=pk: nc.tensor.matmul(
                                pk[:, j * 128:(j + 1) * 128], lhsT=w_kv[:, dc, j * 128:(j + 1) * 128],
                                rhs=h_rd(dc), start=(dc == 0), stop=(dc == 7)),
                                reads=[w_kv.res, h_res], writes=[pk.res], inc=(j == 3 and dc == 7))
                    P.op(act, lambda pk=pk, tb=tb: nc.scalar.copy(
                        out=KT[:, :, tb * 128:(tb + 1) * 128], in_=pk[:].rearrange("p (j t) -> p j t", j=4)),
                        reads=[pk.res], writes=[kt_res[tb]])
                if DBG < 5:
                    continue
                pk2 = next_pg()
                for dc in range(8):
                    P.op(pe, lambda dc=dc, pk2=pk2: nc.tensor.matmul(
                        pk2[0:rows, :], lhsT=h_rd(dc), rhs=w_kv[:, dc, 0:512], start=(dc == 0), stop=(dc == 7)),
                        reads=[w_kv.res, h_res], writes=[pk2.res], inc=(dc == 7))
                ko_tl = ko[tb % 2]
                P.op(act, lambda pk2=pk2, ko_tl=ko_tl: nc.scalar.copy(out=ko_tl[0:rows, :], in_=pk2[0:rows, :]),
                     reads=[pk2.res], writes=[ko_tl.res])
                P.dma("sp", (ks_o[:, :] if sam else k_o[tb * 128:(tb + 1) * 128, :]), ko_tl[0:rows, :], reads=[ko_tl.res])
                if DBG < 6:
                    continue
                pv = next_pg()
                for dc in range(8):
                    P.op(pe, lambda dc=dc, pv=pv: nc.tensor.matmul(
                        pv[0:rows, :], lhsT=h_rd(dc), rhs=w_kv[:, dc, 512:1024], start=(dc == 0), stop=(dc == 7)),
                        reads=[w_kv.res, h_res], writes=[pv.res], inc=(dc == 7))
                vo_tl = vo[tb % 2]
                P.op(act, lambda pv=pv, vo_tl=vo_tl: nc.scalar.copy(out=vo_tl[0:rows, :], in_=pv[0:rows, :]),
                     reads=[pv.res], writes=[vo_tl.res])
                if not sam:
                    P.op(dve, lambda vo_tl=vo_tl, tb=tb: nc.vector.tensor_copy(out=Vb[:, tb, :], in_=vo_tl[:]),
                         reads=[vo_tl.res], writes=[v_res[tb]])
                P.dma("sp", (vs_o[:, :] if sam else v_o[tb * 128:(tb + 1) * 128, :]), vo_tl[0:rows, :], reads=[vo_tl.res])
                if sam:
                    L_ = dict(locals())
                    L_["esB"] = esS
                    L_["sam_state"] = sam_state
                    sgen = sample_mixer(nc, P, L_)
            if sgen is not None:
                for _ in sgen:
                    pass
            barrier()
            esS.close()
        if stop_after and stop_after.startswith("A"):
            P.final_wait()
            esAB.close()
            return nc

        with ExitStack() as esB:
            o_at = sb("o_at", [64, 8, WX], BF16, esB)
            o_pl = sb("o_pl", [128, 4, WX], BF16, esB)
            P.op(pool, lambda: nc.gpsimd.memset(o_at[:], 0.0), writes=[o_at.res])
            P.op(pool, lambda: nc.gpsimd.memset(o_pl[:], 0.0), writes=[o_pl.res])
            xs1 = [sb(f"xs1_{i}", [128, D], F32, esB) for i in range(2)]
            hTs = sb("hTs", [128, 8, WR], BF16, esB)
            qT = sb("qT", [128, 4, W], BF16, esB)
            uT = sb("uT", [128, 4, WR], F32, esB)
            lv = [sb(f"lv{i}", [128, WR], F32, esB) for i in range(2)]
            icnt = sb("icnt", [128, W], F32, esB)
            dpool = [sb(f"dpool{i}", [128, W], BF16, esB) for i in range(2)]
            NMK = 9
            masks = sb("masks", [128, NMK, W], BF16, esB)
            mask_res = [Res() for _ in range(NMK)]
            e_t = [sb(f"e_t{i}", [128, W], F32, esB) for i in range(3)]
            sp_t = [sb(f"sp_t{i}", [128, W], BF16, esB) for i in range(3)]
            a_t = [sb(f"a_t{i}", [128, W], BF16, esB) for i in range(3)]
            w_t = [sb(f"w_t{i}", [128, W], F32, esB) for i in range(3)]
            acc32 = sb("acc32", [128, W], F32, esB)
            accbf = [sb(f"accbf{i}", [128, W], BF16, esB) for i in range(3)]


            vgen = None
            if with_sample:
                L_ = dict(locals())
                vgen = sample_vpass(nc, P, L_)
            for k in range(NS):
                for (r0, n) in [(0, HALO)] + [(HALO + c0, n) for (c0, n) in QT]:
                    xt_ = xs1[cnt["x"] % 2]; cnt["x"] += 1
                    P.dma("sp", xt_[0:n, :], xslot[k, r0:r0 + n, :], writes=[xt_.res])
                    xn_tl = xn[cnt["xn"] % 2]; cnt["xn"] += 1
                    rms_rows(lambda xt_=xt_, n=n: xt_[0:n, :], xt_.res, n, xn_tl)
                    transpose_rows(xn_tl, n, lambda r0=r0, n=n: hTs[:, :, r0:r0 + n], hTs.res, gmix_bc)
                for j in range(4):
                    pq = next_pg()
                    for dc in range(8):
                        P.op(pe, lambda j=j, dc=dc, pq=pq: nc.tensor.matmul(
                            pq[:, 0:W], lhsT=w_qu[:, dc, j * 128:(j + 1) * 128], rhs=hTs[:, dc, HALO:WR],
                            start=(dc == 0), stop=(dc == 7)),
                            reads=[w_qu.res, hTs.res], writes=[pq.res], inc=(dc == 7))
                    P.op(dve, lambda j=j, pq=pq: nc.vector.tensor_scalar(
                        out=qT[:, j, :], in0=pq[:, 0:W], scalar1=0.125, scalar2=None, op0=ALU.mult),
                        reads=[pq.res], writes=[qT.res])
                for g in range(4):
                    pu = next_pg()
                    for dc in range(8):
                        P.op(pe, lambda g=g, dc=dc, pu=pu: nc.tensor.matmul(
                            pu[:, 0:WR], lhsT=w_qu[:, dc, 512 + g * 128:512 + (g + 1) * 128], rhs=hTs[:, dc, :],
                            start=(dc == 0), stop=(dc == 7)),
                            reads=[w_qu.res, hTs.res], writes=[pu.res], inc=(dc == 7))
                    P.op(act, lambda g=g, pu=pu: nc.scalar.copy(out=uT[:, g, :], in_=pu[:, 0:WR]),
                         reads=[pu.res], writes=[uT.res])
                if k == NS - 1:
                    for g in range(4):
                        P.dma("sp", pp_o[:, g * 128:(g + 1) * 128].rearrange("r c -> c r"), uT[:, g, PP0:PP0 + 15],
                              reads=[uT.res], allow_slow_non_contiguous=True)
                for g in range(4):
                    wg = 2 << g
                    cur, cur_res = (lambda g=g: uT[:, g, :]), uT.res
                    lo = 0
                    for lvl in range(g + 1):
                        sh = 1 << lvl
                        dst = lv[lvl % 2]
                        P.op(dve, lambda cur=cur, dst=dst, sh=sh, lo=lo: nc.vector.tensor_tensor(
                            out=dst[:, lo + sh:WR], in0=cur()[:, lo + sh:WR], in1=cur()[:, lo:WR - sh], op=ALU.add),
                            reads=[cur_res], writes=[dst.res])
                        cur, cur_res = (lambda dst=dst: dst[:]), dst.res
                        lo += sh
                    P.op(dve, lambda wg=wg, k=k: nc.vector.tensor_scalar(
                        out=icnt[:], in0=qpos_bc[:, k * W:(k + 1) * W], scalar1=1.0, scalar2=float(wg),
                        op0=ALU.add, op1=ALU.min), reads=[qpos_bc.res], writes=[icnt.res])
                    P.op(dve, lambda: nc.vector.reciprocal(out=icnt[:], in_=icnt[:]), reads=[icnt.res], writes=[icnt.res])
                    P.op(dve, lambda cur=cur: nc.vector.tensor_tensor(
                        out=icnt[:], in0=cur()[:, HALO:WR], in1=icnt[:], op=ALU.mult),
                        reads=[cur_res, icnt.res], writes=[icnt.res])
                    dp = dpool[g % 2]
                    P.op(dve, lambda g=g, dp=dp: nc.vector.tensor_tensor(
                        out=dp[:], in0=icnt[:], in1=uT[:, g, HALO:WR], op=ALU.subtract),
                        reads=[icnt.res, uT.res], writes=[dp.res])
                    pp = next_pg()
                    P.op(pe, lambda g=g, dp=dp, pp=pp: nc.tensor.matmul(
                        pp[:, 0:W], lhsT=pw_bf[:, g, :], rhs=dp[:], start=True, stop=True),
                        reads=[pw_bf.res, dp.res], writes=[pp.res])
                    P.op(dve, lambda g=g, pp=pp: nc.vector.tensor_scalar(
                        out=o_pl[:, g, 0:W], in0=pp[:, 0:W], scalar1=pscale[:, g:g + 1], scalar2=None, op0=ALU.mult),
                        reads=[pp.res, pscale.res], writes=[o_pl.res])
                nkb = n_kb_for_slot(k)
                mkb = [kb for kb in range(nkb) if kb_needs_mask(k, kb)]
                assert len(mkb) <= NMK, len(mkb)
                midx = {kb: i for i, kb in enumerate(mkb)}
                for kb in mkb:
                    i = midx[kb]
                    P.op(dve, lambda kb=kb, i=i, k=k: nc.vector.tensor_scalar(
                        out=masks[:, i, :], in0=qpos_bc[:, k * W:(k + 1) * W], scalar1=kpos[:, kb:kb + 1],
                        scalar2=NEG, op0=ALU.is_le, op1=ALU.mult),
                        reads=[qpos_bc.res, kpos.res], writes=[mask_res[i]])
                ucount = [0]
                for h in range(8):
                    attn_head(nc, P, h, nkb, midx, ucount, locals())
                P.dma("sp", o_scr[k, 0:64, 0:8, 0:W], o_at[:, :, 0:W], reads=[o_at.res])
                P.dma("sp", o_scr[k, :, 8:12, 0:W], o_pl[:, :, 0:W], reads=[o_pl.res])
            if vgen is not None:
                for _ in vgen:
                    pass
            barrier()
        esAB.close()
        if stop_after == "B1":
            P.final_wait()
            return nc

        with ExitStack() as esC:
            w_oa = sb("w_oa", [64, 8, D], BF16, esC)
            P.dma("pool", w_oa[:], w_out_d[0:512, :].rearrange("(h p) n -> p h n", p=64), writes=[w_oa.res])
            w_op = sb("w_op", [128, 4, D], BF16, esC)
            P.dma("pool", w_op[:], w_out_d[512:1024, :].rearrange("(g p) n -> p g n", p=128), writes=[w_op.res])
            wdn = sb("wdn", [128, NFC, D], BF16, esC)
            for i in range(NFC):
                P.dma("pool", wdn[:, i, :], w_down_d[i * 128:(i + 1) * 128, :], writes=[wdn.res])
            xs = sb("xs2", [128, 5, D], F32, esC)
            o_at = sb("o_at2", [64, 8, WX], BF16, esC)
            o_pl = sb("o_pl2", [128, 4, WX], BF16, esC)
            h2T = sb("h2T", [128, 8, WX], BF16, esC)
            cv_t = [sb(f"cv_t{i}", [128, WX], F32, esC) for i in range(6)]
            sg_t = [sb(f"sg_t{i}", [128, WX], F32, esC) for i in range(3)]
            actT = sb("actT", [128, NFC, WX], BF16, esC)
            upl = sb("upl", [128, 2, 44], F32, esC)
            wup_t = [sb(f"wup{i}", [128, 8, 256], BF16, esC) for i in range(4)]
            P.op(dve, lambda: nc.vector.memset(actT[:], 0.0), writes=[actT.res])
            if with_sample:
                upl_s = sb("upl_s", [128, N_SEQ, 44], F32, esC)
                sc8 = [sb(f"sc8_{i}", [8, 128], F32, esC) for i in range(2)]
                pF = pO[1]

            for k in range(NS):
                wk = WX if (k == 0 and with_sample) else W
                tiles = list(QT) + ([(W, N_SEQ)] if (k == 0 and with_sample) else [])
                P.dma("sp", o_at[:], o_scr[k, 0:64, 0:8, :], writes=[o_at.res])
                P.dma("sp", o_pl[:], o_scr[k, :, 8:12, :], writes=[o_pl.res])
                for ti, (c0, n) in enumerate(tiles):
                    if ti < 4:
                        P.dma("sp", xs[0:n, ti, :], xslot[k, HALO + c0:HALO + c0 + n, :], writes=[xs.res])
                    else:
                        P.dma("sp", xs[0:n, ti, :], xsam[:, :], writes=[xs.res])
                for ti, (c0, n) in enumerate(tiles):
                    for half in range(2):
                        pw = next_pg()
                        for h in range(8):
                            P.op(pe, lambda h=h, pw=pw: nc.tensor.matmul(
                                pw[0:n, :], lhsT=o_at[:, h, c0:c0 + n], rhs=w_oa[:, h, half * 512:(half + 1) * 512],
                                start=(h == 0), stop=False),
                                reads=[o_at.res, w_oa.res], writes=[pw.res], inc=False)
                        for g in range(4):
                            P.op(pe, lambda g=g, pw=pw: nc.tensor.matmul(
                                pw[0:n, :], lhsT=o_pl[:, g, c0:c0 + n], rhs=w_op[:, g, half * 512:(half + 1) * 512],
                                start=False, stop=(g == 3)),
                                reads=[o_pl.res, w_op.res], writes=[pw.res], inc=(g == 3))
                        P.op(dve, lambda pw=pw, half=half, ti=ti, n=n: nc.vector.tensor_tensor(
                            out=xs[0:n, ti, half * 512:(half + 1) * 512], in0=pw[0:n, :],
                            in1=xs[0:n, ti, half * 512:(half + 1) * 512], op=ALU.add),
                            reads=[pw.res, xs.res], writes=[xs.res])
                    xn_tl = xn[cnt["xn"] % 2]; cnt["xn"] += 1
                    rms_rows(lambda ti=ti, n=n: xs[0:n, ti, :], xs.res, n, xn_tl)
                    transpose_rows(xn_tl, n, lambda c0=c0, n=n: h2T[:, :, c0:c0 + n], h2T.res, gffn_bc)
                for i in range(NFC):
                    wt = wup_t[i % 4]
                    P.dma("pool", wt[:, :, 0:128], w_up_d[:, i * 128:(i + 1) * 128].rearrange("(c p) f -> p c f", p=128),
                          writes=[wt.res])
                    P.dma("pool", wt[:, :, 128:256],
                          w_up_d[:, DFF + i * 128:DFF + (i + 1) * 128].rearrange("(c p) f -> p c f", p=128), writes=[wt.res])
                    cvs = []
                    for a in range(2):
                        fc = i + a * NFC
                        pu = next_pg()
                        for dc in range(8):
                            P.op(pe, lambda a=a, dc=dc, pu=pu, wt=wt: nc.tensor.matmul(
                                pu[:, 0:wk], lhsT=wt[:, dc, a * 128:(a + 1) * 128], rhs=h2T[:, dc, 0:wk],
                                start=(dc == 0), stop=(dc == 7)),
                                reads=[wt.res, h2T.res], writes=[pu.res], inc=(dc == 7))
                        cv = cv_t[(2 * i + a) % 6]
                        P.op(act, lambda pu=pu, cv=cv, fc=fc: nc.scalar.activation(
                            out=cv[:, 2:W], in_=pu[:, 0:W - 2], func=AF.Identity, scale=cw[:, 0, fc:fc + 1],
                            bias=cb[:, fc:fc + 1]), reads=[pu.res, cw.res, cb.res], writes=[cv.res])
                        for tap in (1, 2):
                            P.op(dve, lambda pu=pu, cv=cv, fc=fc, tap=tap: nc.vector.scalar_tensor_tensor(
                                out=cv[:, 2:W], in0=pu[:, tap:W - 2 + tap], scalar=cw[:, tap, fc:fc + 1], in1=cv[:, 2:W],
                                op0=ALU.mult, op1=ALU.add), reads=[pu.res, cw.res, cv.res], writes=[cv.res])
                        if k == NS - 1:
                            P.op(dve, lambda pu=pu, fc=fc: nc.vector.tensor_copy(out=upl[:, :, fc], in_=pu[:, CP0:CP0 + 2]),
                                 reads=[pu.res], writes=[upl.res])
                        if k == 0 and with_sample:
                            s8 = sc8[(2 * i + a) % 2]
                            P.dma("sp", s8[:], stc_d[:, fc * 128:(fc + 1) * 128], writes=[s8.res])
                            P.op(pe, lambda s8=s8: nc.tensor.transpose(out=pF[:, 0:8], in_=s8[0:8, :],
                                                                       identity=identf[0:8, 0:8]),
                                 reads=[s8.res, identf.res], writes=[pF.res])
                            P.op(dve, lambda pu=pu, fc=fc: nc.vector.tensor_copy(out=upl_s[:, :, fc], in_=pu[:, W:WX]),
                                 reads=[pu.res], writes=[upl_s.res])
                            P.op(dve, lambda pu=pu, cv=cv, fc=fc: nc.vector.tensor_scalar(
                                out=cv[:, W:WX], in0=pu[:, W:WX], scalar1=cw[:, 2, fc:fc + 1], scalar2=cb[:, fc:fc + 1],
                                op0=ALU.mult, op1=ALU.add), reads=[pu.res, cw.res, cb.res], writes=[cv.res])
                            for r in (0, 1):
                                P.op(dve, lambda cv=cv, fc=fc, r=r: nc.vector.scalar_tensor_tensor(
                                    out=cv[:, W:WX], in0=pF[:, 0:8].rearrange("p (s r) -> p r s", r=2)[:, r, :],
                                    scalar=cw[:, r, fc:fc + 1], in1=cv[:, W:WX], op0=ALU.mult, op1=ALU.add),
                                    reads=[pF.res, cw.res, cv.res], writes=[cv.res])
                        cvs.append(cv)
                    sg = sg_t[i % 3]
                    P.op(act, lambda sg=sg, cv=cvs[0]: nc.scalar.activation(out=sg[:, 2:wk], in_=cv[:, 2:wk], func=AF.Silu),
                         reads=[cvs[0].res], writes=[sg.res])
                    P.op(dve, lambda sg=sg, cv=cvs[1], i=i: nc.vector.tensor_tensor(
                        out=actT[:, i, 2:wk], in0=sg[:, 2:wk], in1=cv[:, 2:wk], op=ALU.mult),
                        reads=[sg.res, cvs[1].res], writes=[actT.res])
                for ti, (c0, n) in enumerate(tiles):
                    for half in range(2):
                        pd = next_pg()
                        for i in range(NFC):
                            P.op(pe, lambda i=i, pd=pd: nc.tensor.matmul(
                                pd[0:n, :], lhsT=actT[:, i, c0:c0 + n], rhs=wdn[:, i, half * 512:(half + 1) * 512],
                                start=(i == 0), stop=(i == NFC - 1)),
                                reads=[actT.res, wdn.res], writes=[pd.res], inc=(i == NFC - 1))
                        P.op(dve, lambda pd=pd, half=half, ti=ti, n=n: nc.vector.tensor_tensor(
                            out=xs[0:n, ti, half * 512:(half + 1) * 512], in0=pd[0:n, :],
                            in1=xs[0:n, ti, half * 512:(half + 1) * 512], op=ALU.add),
                            reads=[pd.res, xs.res], writes=[xs.res])
                    ss = rms_rows(lambda ti=ti, n=n: xs[0:n, ti, :], xs.res, n, None)
                    P.op(dve, lambda ss=ss, ti=ti, n=n: nc.vector.scalar_tensor_tensor(
                        out=xs[0:n, ti, :], in0=xs[0:n, ti, :], scalar=ss[0:n, 0:1], in1=gfin[0:n, :],
                        op0=ALU.mult, op1=ALU.mult),
                        reads=[xs.res, ss.res, gfin.res], writes=[xs.res])
                    if ti < 4:
                        P.dma("sp", y_o[k, c0:c0 + n, :], xs[0:n, ti, :], reads=[xs.res])
                    else:
                        P.dma("sp", ys_o[:, :], xs[0:n, ti, :], reads=[xs.res])
            for r in range(2):
                for q4 in range(4):
                    P.dma("sp", cp_o[r:r + 1, q4 * 1408:(q4 + 1) * 1408].rearrange("r (c p) -> p (r c)", p=128),
                          upl[:, r, q4 * 11:(q4 + 1) * 11], reads=[upl.res], allow_slow_non_contiguous=True)
            if with_sample:
                for s in range(N_SEQ):
                    for q4 in range(4):
                        P.dma("sp", cs_o[s, 1:2, q4 * 1408:(q4 + 1) * 1408].rearrange("r (c p) -> p (r c)", p=128),
                              upl_s[:, s, q4 * 11:(q4 + 1) * 11], reads=[upl_s.res], allow_slow_non_contiguous=True)
                    P.dma("sp", cs_o[s, 0:1, :], stc_d[2 * s + 1:2 * s + 2, :])
            P.final_wait()
    return nc


def attn_head(nc, P, h, nkb, midx, ucount, L):
    pe, act, dve, pool = P.pe, P.act, P.dve, P.pool
    KT, Vb, qT, kt_res, v_res = L["KT"], L["Vb"], L["qT"], L["kt_res"], L["v_res"]
    masks, mask_res, ident, tri, negones = L["masks"], L["mask_res"], L["ident"], L["tri"], L["negones"]
    e_t, sp_t, a_t, acc32, accbf, w_t = L["e_t"], L["sp_t"], L["a_t"], L["acc32"], L["accbf"], L["w_t"]
    bias_t, o_at, pO, next_pg = L["bias_t"], L["o_at"], L["pO"], L["next_pg"]
    j, hb = h // 2, (h % 2) * 64
    po = pO[0]
    kbs = list(range(nkb - 1, -1, -1))
    n = len(kbs)
    st_ = {}
    u0 = ucount[0]

    def s1(kb):
        p = next_pg()
        need_m = kb in midx
        P.op(pe, lambda: nc.tensor.matmul(
            p[:, 0:W], lhsT=KT[hb:hb + 64, j, kb * 128:(kb + 1) * 128], rhs=qT[hb:hb + 64, j, :],
            start=True, stop=not need_m),
            reads=[kt_res[kb], qT.res], writes=[p.res], inc=not need_m)
        if need_m:
            P.op(pe, lambda: nc.tensor.matmul(
                p[:, 0:W], lhsT=ident[:], rhs=masks[:, midx[kb], :], start=False, stop=True),
                reads=[ident.res, mask_res[midx[kb]]], writes=[p.res])
        st_[kb] = {"p": p}

    def s2a(kb, u):
        p = st_[kb]["p"]
        e = e_t[u % 3]
        P.op(act, lambda: nc.scalar.activation(out=e[:], in_=p[:, 0:W], func=AF.Exp,
                                               bias=bias_t[:, h:h + 1], scale=1.0),
             reads=[p.res, bias_t.res], writes=[e.res])
        st_[kb]["e"] = e

    def s2b(kb, u):
        e = st_[kb]["e"]
        s = sp_t[u % 3]
        P.op(act, lambda: nc.scalar.activation(out=s[:], in_=e[:], func=AF.Ln, bias=1.0, scale=1.0),
             reads=[e.res], writes=[s.res])
        st_[kb]["s"] = s

    def s3(kb, u, first):
        s = st_[kb]["s"]
        pc = next_pg()
        st_[kb]["pc"] = pc
        P.op(pe, lambda: nc.tensor.matmul(pc[:, 0:W], lhsT=tri[:], rhs=s[:], start=True, stop=first),
             reads=[tri.res, s.res], writes=[pc.res], inc=first)
        if not first:
            ab = accbf[(u - 1) % 3]
            P.op(pe, lambda: nc.tensor.matmul(pc[:, 0:W], lhsT=negones[:], rhs=ab[:], start=False, stop=True),
                 reads=[negones.res, ab.res], writes=[pc.res])
        abn = accbf[u % 3]
        if first:
            P.op(dve, lambda: nc.vector.tensor_copy(out=abn[:], in_=s[:]), reads=[s.res], writes=[abn.res])
        else:
            abo = accbf[(u - 1) % 3]
            P.op(dve, lambda: nc.vector.tensor_tensor(out=abn[:], in0=abo[:], in1=s[:], op=ALU.add),
                 reads=[abo.res, s.res], writes=[abn.res])

    def s4(kb, u):
        pc = st_[kb]["pc"]; e = st_[kb]["e"]
        w = w_t[u % 3]
        a = a_t[u % 3]
        P.op(act, lambda: nc.scalar.activation(out=w[:], in_=pc[:, 0:W], func=AF.Exp),
             reads=[pc.res], writes=[w.res])
        P.op(dve, lambda: nc.vector.tensor_tensor(out=a[:], in0=e[:], in1=w[:], op=ALU.mult),
             reads=[e.res, w.res], writes=[a.res])
        st_[kb]["a"] = a

    def s5(kb, first, last):
        a = st_[kb]["a"]
        P.op(pe, lambda: nc.tensor.matmul(po[0:64, 0:W], lhsT=Vb[:, kb, h * 64:(h + 1) * 64], rhs=a[:],
                                          start=first, stop=last),
             reads=[v_res[kb], a.res], writes=[po.res], inc=last)
        del st_[kb]

    vgen = L.get("vgen")
    s1(kbs[0])
    for i in range(n + 1):
        if i + 1 < n:
            s1(kbs[i + 1])
        if i < n:
            s2a(kbs[i], u0 + i)
        if i >= 1:
            s4(kbs[i - 1], u0 + i - 1)
        if i < n:
            s2b(kbs[i], u0 + i)
            s3(kbs[i], u0 + i, i == 0)
        if i >= 1:
            s5(kbs[i - 1], i - 1 == 0, i - 1 == n - 1)
        if vgen is not None and i % 2 == 1:
            next(vgen, None)
    ucount[0] += n
    P.op(dve, lambda: nc.vector.tensor_copy(out=o_at[:, h, 0:W], in_=po[0:64, 0:W]),
         reads=[po.res], writes=[o_at.res])


def sample_mixer(nc, P, L):
    pe, act, dve, pool = P.pe, P.act, P.dve, P.pool
    sb, esB, next_pg, sam_state = L["sb"], L["esB"], L["next_pg"], L["sam_state"]
    hT_sam, w_qu, pw_bf, pscale, bias_t = L["hT_sam"], L["w_qu"], L["pw_bf"], L["pscale"], L["bias_t"]
    tri, negones, identf, pO, o_scr = L["tri"], L["negones"], L["identf"], L["pO"], L["o_scr"]
    o_pl = sb("o_spl", [128, 4, N_SEQ], BF16, esB)
    ck_d, cv_d, stp_d, pt_d, ps_o, q_scr = L["ck_d"], L["cv_d"], L["stp_d"], L["pt_d"], L["ps_o"], L["q_scr"]
    NP = N_SEQ * NPG
    pti = sb("pti", [128, NP], I32, esB)
    ptf = sb("ptf", [128, NP], F32, esB)
    iop = sb("iop", [128, 1], F32, esB)
    idxi = sam_state["idxi"]
    P.dma("sp", pti[:], pt_d.partition_broadcast(128), writes=[pti.res])
    P.op(pool, lambda: nc.gpsimd.iota(iop[:], pattern=[[0, 1]], base=0, channel_multiplier=1,
                                      allow_small_or_imprecise_dtypes=True), writes=[iop.res])
    P.op(pool, lambda: nc.gpsimd.tensor_copy(out=ptf[:], in_=pti[:]), reads=[pti.res], writes=[ptf.res])
    P.op(pool, lambda: nc.gpsimd.tensor_scalar(out=ptf[:], in0=ptf[:], scalar1=128.0, scalar2=iop[:, 0:1],
                                               op0=ALU.mult, op1=ALU.add), reads=[ptf.res, iop.res], writes=[ptf.res])
    P.op(pool, lambda: nc.gpsimd.tensor_copy(out=idxi[:], in_=ptf[:]), reads=[ptf.res], writes=[idxi.res])
    bd0 = sb("bd0", [8, 512], F32, esB)
    bd1 = sb("bd1", [8, 512], F32, esB)
    bdiag = sam_state["bdiag"]
    ones8 = sam_state["ones8"]
    P.op(pool, lambda: nc.gpsimd.memset(bd0[:], 1.0), writes=[bd0.res])
    P.op(pool, lambda: nc.gpsimd.memset(ones8[:], 1.0), writes=[ones8.res])
    P.op(pool, lambda: nc.gpsimd.affine_select(out=bd1[:], in_=bd0[:], pattern=[[1, 512]], compare_op=ALU.is_ge,
                                                fill=0.0, base=0, channel_multiplier=-64),
         reads=[bd0.res], writes=[bd1.res])
    P.op(pool, lambda: nc.gpsimd.affine_select(out=bd0[:], in_=bd1[:], pattern=[[-1, 512]], compare_op=ALU.is_ge,
                                                fill=0.0, base=63, channel_multiplier=64),
         reads=[bd1.res], writes=[bd0.res])
    P.op(pool, lambda: nc.gpsimd.tensor_copy(out=bdiag[:], in_=bd0[:]), reads=[bd0.res], writes=[bdiag.res])
    qscr_res = Res()
    pq = next_pg()
    for dc in range(8):
        P.op(pe, lambda dc=dc: nc.tensor.matmul(pq[0:N_SEQ, :], lhsT=hT_sam[:, dc, :], rhs=w_qu[:, dc, 0:512],
                                                start=(dc == 0), stop=(dc == 7)),
             reads=[hT_sam.res, w_qu.res], writes=[pq.res], inc=(dc == 7))
    q_tok = sb("q_tok", [N_SEQ, 512], F32, esB)
    P.op(dve, lambda: nc.vector.tensor_scalar(out=q_tok[:], in0=pq[0:N_SEQ, :], scalar1=0.125, scalar2=None, op0=ALU.mult),
         reads=[pq.res], writes=[q_tok.res])
    P.dma("sp", q_scr[:, :], q_tok[:], reads=[q_tok.res], writes=[qscr_res])
    pu = next_pg()
    for dc in range(8):
        P.op(pe, lambda dc=dc: nc.tensor.matmul(pu[0:N_SEQ, :], lhsT=hT_sam[:, dc, :], rhs=w_qu[:, dc, 512:1024],
                                                start=(dc == 0), stop=(dc == 7)),
             reads=[hT_sam.res, w_qu.res], writes=[pu.res], inc=(dc == 7))
    u_tok = sb("u_tok", [N_SEQ, 512], F32, esB)
    P.op(act, lambda: nc.scalar.copy(out=u_tok[:], in_=pu[0:N_SEQ, :]), reads=[pu.res], writes=[u_tok.res])
    P.dma("sp", ps_o[:, 14, :], u_tok[:], reads=[u_tok.res])
    for s in range(N_SEQ):
        P.dma("sp", ps_o[s, 0:14, :], stp_d[s * 15 + 1:s * 15 + 15, :])
    uTs = sb("uTs", [128, 4, N_SEQ], F32, esB)
    for g in range(4):
        pu2 = next_pg()
        for dc in range(8):
            P.op(pe, lambda g=g, dc=dc, pu2=pu2: nc.tensor.matmul(
                pu2[:, 0:N_SEQ], lhsT=w_qu[:, dc, 512 + g * 128:512 + (g + 1) * 128], rhs=hT_sam[:, dc, :],
                start=(dc == 0), stop=(dc == 7)),
                reads=[hT_sam.res, w_qu.res], writes=[pu2.res], inc=(dc == 7))
        P.op(act, lambda g=g, pu2=pu2: nc.scalar.copy(out=uTs[:, g, :], in_=pu2[:, 0:N_SEQ]),
             reads=[pu2.res], writes=[uTs.res])
    st60 = sb("st60", [N_SEQ * 15, 512], F32, esB)
    P.dma("sp", st60[:], stp_d[:, :], writes=[st60.res])
    stT = sb("stT", [128, 4, N_SEQ * 15], F32, esB)
    pst = next_pg()
    for g in range(4):
        P.op(pe, lambda g=g: nc.tensor.transpose(out=pst[:, g * 60:(g + 1) * 60], in_=st60[0:60, g * 128:(g + 1) * 128],
                                                 identity=identf[0:60, 0:60]),
             reads=[st60.res, identf.res], writes=[pst.res], inc=(g == 3))
    P.op(act, lambda: nc.scalar.copy(out=stT[:], in_=pst[:, 0:240].rearrange("p (g c) -> p g c", g=4)),
         reads=[pst.res], writes=[stT.res])
    ssum = sb("ssum", [128, N_SEQ], F32, esB)
    d_bf = [sb(f"d_bf{i}", [128, N_SEQ], BF16, esB) for i in range(2)]
    for g in range(4):
        wg = 2 << g
        nr = wg - 1
        P.op(dve, lambda g=g, nr=nr: nc.vector.tensor_reduce(
            out=ssum[:], in_=stT[:, g, :].rearrange("p (s r) -> p s r", r=15)[:, :, 15 - nr:15], axis=AX.X, op=ALU.add),
            reads=[stT.res], writes=[ssum.res])
        P.op(dve, lambda g=g: nc.vector.tensor_tensor(out=ssum[:], in0=ssum[:], in1=uTs[:, g, :], op=ALU.add),
             reads=[ssum.res, uTs.res], writes=[ssum.res])
        db = d_bf[g % 2]
        P.op(dve, lambda g=g, wg=wg, db=db: nc.vector.scalar_tensor_tensor(
            out=db[:], in0=ssum[:], scalar=1.0 / wg, in1=uTs[:, g, :], op0=ALU.mult, op1=ALU.subtract),
            reads=[ssum.res, uTs.res], writes=[db.res])
        pp = next_pg()
        P.op(pe, lambda g=g, db=db, pp=pp: nc.tensor.matmul(pp[:, 0:N_SEQ], lhsT=pw_bf[:, g, :], rhs=db[:],
                                                            start=True, stop=True),
             reads=[pw_bf.res, db.res], writes=[pp.res])
        P.op(dve, lambda g=g, pp=pp: nc.vector.tensor_scalar(
            out=o_pl[:, g, :], in0=pp[:, 0:N_SEQ], scalar1=pscale[:, g:g + 1], scalar2=None, op0=ALU.mult),
            reads=[pp.res, pscale.res], writes=[o_pl.res])
    P.dma("sp", o_scr[0, :, 8:12, W:WX], o_pl[:], reads=[o_pl.res])
    yield
    LAG = 3
    qb = [sb(f"qb{i}", [128, 512], F32, esB) for i in range(2)]
    Kp = [sb(f"Kp{i}", [128, 512], F32, esB) for i in range(5)]
    tmp = [sb(f"ktmp{i}", [128, 512], F32, esB) for i in range(4)]
    Z = sb("Zs", [128, 512], F32, esB)
    e8 = sb("e8", [128, 512], F32, esB)
    sp8 = sb("sp8", [128, 512], BF16, esB)
    S0 = sb("S0", [128, 512], F32, esB)
    Sa = sb("Sa", [128, 512], F32, esB)
    Sb_ = sb("Sb", [128, 512], F32, esB)
    A8all = sam_state["A8all"]
    hv = lambda t: t[:].rearrange("p (g h) -> p h g", h=8)
    for s in range(N_SEQ):
        q_t = qb[s % 2]
        P.dma("sp", q_t[:], q_scr[s:s + 1, :].partition_broadcast(128), reads=[qscr_res], writes=[q_t.res])
        zres = [Res() for _ in range(NPG)]
        for zr in zres:
            zr.w = Z.res.w
            zr.r = dict(Z.res.r)
        for step in range(NPG + LAG + 1):
            if step < NPG:
                col = s * NPG + step
                kp = Kp[step % 5]
                P.dma("pool", None, None, reads=[idxi.res], writes=[kp.res],
                      fn=lambda kp=kp, col=col: nc.gpsimd.indirect_dma_start(
                          out=kp[:], out_offset=None, in_=ck_d,
                          in_offset=bass.IndirectOffsetOnAxis(ap=idxi[:, col:col + 1], axis=0)))
            if LAG <= step < NPG + LAG:
                pg = step - LAG
                kp = Kp[pg % 5]
                tm = tmp[pg % 4]
                P.op(dve, lambda tm=tm, kp=kp: nc.vector.tensor_tensor(out=tm[:], in0=kp[:], in1=q_t[:], op=ALU.mult),
                     reads=[kp.res, q_t.res], writes=[tm.res])
            if step >= LAG + 1:
                pg = step - LAG - 1
                tm = tmp[pg % 4]
                P.op(dve, lambda tm=tm, pg=pg: nc.vector.tensor_reduce(
                    out=Z[:, pg * 8:(pg + 1) * 8], in_=tm[:].rearrange("p (h d) -> p h d", h=8), axis=AX.X, op=ALU.add),
                    reads=[tm.res], writes=[zres[pg]])
            yield
        Z.res.w = zres[NPG - 1].w
        Z.res.r = {}
        for h in range(8):
            P.op(act, lambda h=h: nc.scalar.activation(out=hv(e8)[:, h, :], in_=hv(Z)[:, h, :], func=AF.Exp,
                                                       bias=bias_t[:, h:h + 1], scale=1.0),
                 reads=[Z.res, bias_t.res], writes=[e8.res])
        P.op(act, lambda: nc.scalar.activation(out=sp8[:], in_=e8[:], func=AF.Ln, bias=1.0, scale=1.0),
             reads=[e8.res], writes=[sp8.res])
        pc = pO[0]
        P.op(pe, lambda: nc.tensor.matmul(pc[:, :], lhsT=tri[:], rhs=sp8[:], start=True, stop=True),
             reads=[tri.res, sp8.res], writes=[pc.res])
        pt_ = pO[1]
        P.op(pe, lambda: nc.tensor.matmul(pt_[:, :], lhsT=negones[:], rhs=sp8[:], start=True, stop=True),
             reads=[negones.res, sp8.res], writes=[pt_.res])
        P.op(act, lambda: nc.scalar.copy(out=S0[:], in_=pt_[:, :]), reads=[pt_.res], writes=[S0.res])
        cur = S0
        for i, dd in enumerate((1, 2, 4, 8, 16, 32)):
            nxt = Sa if i % 2 == 0 else Sb_
            n0 = 512 - 8 * dd
            P.op(dve, lambda cur=cur, nxt=nxt, n0=n0, dd=dd: nc.vector.tensor_tensor(
                out=nxt[:, 0:n0], in0=cur[:, 0:n0], in1=cur[:, 8 * dd:512], op=ALU.add),
                reads=[cur.res], writes=[nxt.res])
            P.op(dve, lambda cur=cur, nxt=nxt, n0=n0: nc.vector.tensor_copy(out=nxt[:, n0:512], in_=cur[:, n0:512]),
                 reads=[cur.res, nxt.res], writes=[nxt.res])
            cur = nxt
        P.op(dve, lambda cur=cur: nc.vector.tensor_tensor(out=cur[:], in0=cur[:], in1=S0[:], op=ALU.subtract),
             reads=[cur.res, S0.res], writes=[cur.res])
        P.op(dve, lambda cur=cur: nc.vector.tensor_tensor(out=cur[:], in0=cur[:], in1=Z[:], op=ALU.add),
             reads=[cur.res, Z.res], writes=[cur.res])
        P.op(dve, lambda cur=cur: nc.vector.tensor_tensor(out=cur[:], in0=cur[:], in1=pc[:, :], op=ALU.add),
             reads=[cur.res, pc.res], writes=[cur.res])
        for h in range(8):
            P.op(act, lambda h=h, cur=cur: nc.scalar.activation(out=A8all[:, s, :].rearrange("p (g h) -> p h g", h=8)[:, h, :], in_=hv(cur)[:, h, :], func=AF.Exp,
                                                                bias=bias_t[:, h:h + 1], scale=1.0),
                 reads=[cur.res, bias_t.res], writes=[A8all.res])
        yield


def sample_vpass(nc, P, L):
    pe, act, dve, pool = P.pe, P.act, P.dve, P.pool
    sb, esB, pO, o_scr, cv_d = L["sb"], L["esB"], L["pO"], L["o_scr"], L["cv_d"]
    st = L["sam_state"]
    A8all, idxi, bdiag, ones8 = st["A8all"], st["idxi"], st["bdiag"], st["ones8"]
    LAG = 2
    NB = 4
    Vp = [sb(f"Vp{i}", [128, 512], BF16, esB) for i in range(NB)]
    m8 = sb("m8", [8, 512], BF16, esB)
    o_at = sb("o_sat", [64, 8, N_SEQ], BF16, esB)
    pov = pO[1]
    for s in range(N_SEQ):
        for step in range(NPG + LAG):
            if step < NPG:
                col = s * NPG + step
                vp = Vp[step % NB]
                P.dma("pool", None, None, reads=[idxi.res], writes=[vp.res],
                      fn=lambda vp=vp, col=col: nc.gpsimd.indirect_dma_start(
                          out=vp[:], out_offset=None, in_=cv_d,
                          in_offset=bass.IndirectOffsetOnAxis(ap=idxi[:, col:col + 1], axis=0)))
            if step >= LAG:
                pg = step - LAG
                vp = Vp[pg % NB]
                P.op(pe, lambda vp=vp, pg=pg, s=s: nc.tensor.matmul(pov[0:8, :], lhsT=A8all[:, s, pg * 8:(pg + 1) * 8], rhs=vp[:],
                                                                    start=(pg == 0), stop=(pg == NPG - 1)),
                     reads=[A8all.res, vp.res], writes=[pov.res], inc=True)
            yield
        P.op(dve, lambda: nc.vector.tensor_tensor(out=m8[:], in0=pov[0:8, :], in1=bdiag[:], op=ALU.mult),
             reads=[pov.res, bdiag.res], writes=[m8.res])
        for h in range(8):
            P.op(pe, lambda h=h: nc.tensor.matmul(pov[0:64, h:h + 1], lhsT=m8[0:8, h * 64:(h + 1) * 64], rhs=ones8[0:8, 0:1],
                                                  start=True, stop=True),
                 reads=[m8.res, ones8.res], writes=[pov.res], inc=(h == 7))
        P.op(dve, lambda s=s: nc.vector.tensor_copy(out=o_at[:, :, s], in_=pov[0:64, 0:8]),
             reads=[pov.res], writes=[o_at.res])
        yield
    P.dma("sp", o_scr[0, 0:64, 0:8, W:WX], o_at[:], reads=[o_at.res])


_NC_CACHE = {}
_RUNNER = [None]
_REMAP = [None, None]


_STOP = [None]


def _get_nc(with_sample):
    key = (with_sample, _STOP[0])
    if key not in _NC_CACHE:
        _NC_CACHE[key] = build(with_sample, _STOP[0])
    return _NC_CACHE[key]


def kernel(x_prompt, x_sample, cache_k, cache_v, state_pool, state_conv, page_table,
           meta_tokens, norm_mix_g, w_in, sb_bias, pool_w, pool_scale, w_out, norm_ffn_g,
           w_up, conv_w, conv_b, w_down, norm_final_g, _with_sample=True):
    f32 = np.float32
    B = x_prompt.shape[0]
    nc = _get_nc(_with_sample)
    in_maps = []
    ck2 = cv2 = None
    if _with_sample:
        ck2 = np.ascontiguousarray(cache_k, dtype=f32).reshape(-1, 512)
        cv2 = np.ascontiguousarray(cache_v, dtype=f32).reshape(-1, 512)
    for c in range(8):
        b, g = c // 2, c % 2
        xa = np.zeros((TP, D), f32)
        xa[:N_META] = meta_tokens
        xa[N_META:T] = x_prompt[b]
        xsl = np.zeros((NS, WR, D), f32)
        qp = np.zeros((1, NS * W), f32)
        for k in range(NS):
            s = slot_start(g, k)
            lo, hi = s - HALO, s + W
            a, e = max(lo, 0), min(hi, T)
            xsl[k, a - lo:e - lo] = xa[a:e]
            qp[0, k * W:(k + 1) * W] = np.arange(s, s + W, dtype=f32)
        m = {
            "xall": xa, "xslot": xsl, "qpos": qp,
            "w_in": np.asarray(w_in, f32), "w_out": np.asarray(w_out, f32), "w_up": np.asarray(w_up, f32),
            "w_down": np.asarray(w_down, f32),
            "norm_mix_g": np.asarray(norm_mix_g, f32).reshape(D, 1),
            "norm_ffn_g": np.asarray(norm_ffn_g, f32).reshape(D, 1),
            "norm_final_g": np.asarray(norm_final_g, f32).reshape(1, D),
            "sb_bias": np.asarray(sb_bias, f32).reshape(1, 8),
            "pool_w": np.asarray(pool_w, f32), "pool_scale": np.asarray(pool_scale, f32),
            "conv_w": np.asarray(conv_w, f32), "conv_b": np.asarray(conv_b, f32).reshape(1, 2 * DFF),
        }
        if _with_sample:
            sl = slice(N_SEQ * c, N_SEQ * (c + 1))
            m.update({
                "x_sample": np.asarray(x_sample[sl], f32).reshape(N_SEQ, D),
                "cache_k": (_REMAP[0](c, ck2) if _REMAP[0] else ck2), "cache_v": (_REMAP[0](c, cv2) if _REMAP[0] else cv2),
                "state_pool": np.asarray(state_pool[sl], f32).reshape(N_SEQ * 15, 512),
                "state_conv": np.asarray(state_conv[sl], f32).reshape(N_SEQ * 2, 2 * DFF),
                "page_table": (_REMAP[1](c) if _REMAP[1] else np.asarray(page_table[sl], np.int32).reshape(1, N_SEQ * NPG)),
            })
        in_maps.append(m)
    if _RUNNER[0] is not None:
        res = _RUNNER[0](nc, in_maps)
    else:
        res = run_bass_kernel_spmd(nc, in_maps, core_ids=list(range(8))).results

    y_prompt = np.zeros((B, 4096, D), f32)
    k_prompt = np.zeros((B, T, 8, 64), f32)
    v_prompt = np.zeros((B, T, 8, 64), f32)
    pool_prompt = np.zeros((B, 15, 512), f32)
    conv_prompt = np.zeros((B, 2, 2 * DFF), f32)
    for c in range(8):
        b, g = c // 2, c % 2
        r = res[c]
        for k in range(NS):
            r0 = OWN * TILES[g][k]
            nv = min(OWN, 4096 - r0)
            y_prompt[b, r0:r0 + nv] = r["y_slot"][k, 2:2 + nv]
        if g == 0:
            k_prompt[b] = r["k_all"][:T].reshape(T, 8, 64)
            v_prompt[b] = r["v_all"][:T].reshape(T, 8, 64)
        else:
            pool_prompt[b] = r["pool_last"]
            conv_prompt[b] = r["conv_last"]
    if not _with_sample:
        return (y_prompt, None, k_prompt, v_prompt, pool_prompt, conv_prompt, None, None, None, None)
    DB = x_sample.shape[0]
    y_sample = np.zeros((DB, 1, D), f32)
    k_sample = np.zeros((DB, 1, 8, 64), f32)
    v_sample = np.zeros((DB, 1, 8, 64), f32)
    pool_sample = np.zeros((DB, 15, 512), f32)
    conv_sample = np.zeros((DB, 2, 2 * DFF), f32)
    for c in range(8):
        r = res[c]
        sl = slice(N_SEQ * c, N_SEQ * (c + 1))
        y_sample[sl, 0] = r["y_sample"]
        k_sample[sl, 0] = r["k_sample"].reshape(N_SEQ, 8, 64)
        v_sample[sl, 0] = r["v_sample"].reshape(N_SEQ, 8, 64)
        pool_sample[sl] = r["pool_sample"]
        conv_sample[sl] = r["conv_sample"]
    return (y_prompt, y_sample, k_prompt, v_prompt, pool_prompt, conv_prompt,
            k_sample, v_sample, pool_sample, conv_sample)
```

```python
import numpy as np
from contextlib import ExitStack
import concourse.bass as bass
import concourse.mybir as mybir
from concourse.bass_utils import run_bass_kernel_spmd

F32 = mybir.dt.float32
BF16 = mybir.dt.bfloat16
I32 = mybir.dt.int32
AF = mybir.ActivationFunctionType
ALU = mybir.AluOpType
AX = mybir.AxisListType

D = 1024
T = 4112
NBLK = 33
TP = NBLK * 128
N_META = 16
NS = 5
W = 412
HALO = 16
WR = W + HALO
OWN = 410
DFF = 2816
NFC = DFF // 128
EPS = 1e-6
NEG = -30000.0
QT = [(0, 128), (128, 128), (256, 128), (384, 28)]
N_SEQ = 4
NPG = 64
SAME_ENGINE_SYNC = True


TILES = ((0, 3, 4, 7, 8), (1, 2, 5, 6, 9))
LAST_S = 14 + OWN * 9
PP0 = 4097 - LAST_S + HALO
CP0 = 4110 - LAST_S


def slot_start(g, k):
    return 14 + OWN * TILES[g][k]


def n_kb_for_slot(k):
    last = 14 + OWN * (2 * k + 1) + W - 1
    return min(NBLK, last // 128 + 1)


def kb_needs_mask(k, kb):
    return not (128 * kb + 127 < 14 + OWN * 2 * k)


class SemObj:
    def __init__(self, sem):
        self.sem = sem
        self.cnt = 0


class Eng(SemObj):
    def __init__(self, sem, h, name):
        super().__init__(sem)
        self.h = h
        self.name = name
        self.waited = {}


class Res:
    __slots__ = ("w", "r", "excl")

    def __init__(self, excl=False):
        self.w = None
        self.r = {}
        self.excl = excl


class Prog:
    def __init__(self, nc, es):
        self.nc = nc
        self.es = es
        mk = lambda n: es.enter_context(nc.semaphore(n))
        self.pe = Eng(mk("s_pe"), nc.tensor, "pe")
        self.act = Eng(mk("s_act"), nc.scalar, "act")
        self.dve = Eng(mk("s_dve"), nc.vector, "dve")
        self.pool = Eng(mk("s_pool"), nc.gpsimd, "pool")
        self.sp = Eng(mk("s_sp"), nc.sync, "sp")
        self.dsems = {}
        for q in ("sp", "pool", "act"):
            self.dsems[q] = [SemObj(mk(f"d_{q}{i}")) for i in range(12)]
        self.dnext = {"sp": 0, "pool": 0, "act": 0}

    def _wait(self, eng, toks):
        for tk in toks:
            so, v = tk[0], tk[1]
            if so is eng and (not SAME_ENGINE_SYNC or eng is self.pe or len(tk) == 3):
                continue
            if eng.waited.get(so, 0) < v:
                eng.h.wait_ge(so.sem, v)
                eng.waited[so] = v

    @staticmethod
    def _deps(reads, writes):
        toks = []
        for r in reads:
            if r.w is not None:
                toks.append(r.w)
            if r.excl:
                toks.extend(r.r.items())
        for w in writes:
            if w.w is not None:
                toks.append(w.w)
            toks.extend((so, v, "war") for so, v in w.r.items())
        return toks

    @staticmethod
    def _record(tok, reads, writes):
        so, v = tok
        for r in reads:
            if r.r.get(so, 0) < v:
                r.r[so] = v
        for w in writes:
            w.w = tok
            w.r = {}

    def op(self, eng, fn, reads=(), writes=(), inc=True):
        self._wait(eng, self._deps(reads, writes))
        ins = fn()
        if inc:
            ins.then_inc(eng.sem, 1)
            eng.cnt += 1
            tok = (eng, eng.cnt)
        else:
            tok = (eng, eng.cnt + 1)
        self._record(tok, reads, writes)
        return ins

    def dma(self, q, out, in_, reads=(), writes=(), fn=None, **kw):
        eng = {"sp": self.sp, "pool": self.pool, "act": self.act}[q]
        lst = self.dsems[q]
        d = lst[self.dnext[q] % len(lst)]
        self.dnext[q] += 1
        toks = self._deps(reads, writes)
        if d.cnt > 0:
            toks.append((d, d.cnt))
        self._wait(eng, toks)
        if fn is not None:
            fn().then_inc(d.sem, 16)
        else:
            eng.h.dma_start(out=out, in_=in_, **kw).then_inc(d.sem, 16)
        d.cnt += 16
        self._record((d, d.cnt), reads, writes)

    def final_wait(self):
        toks = []
        for q in self.dsems:
            for d in self.dsems[q]:
                if d.cnt:
                    toks.append((d, d.cnt))
        for e in (self.pe, self.act, self.dve, self.pool):
            if e.cnt:
                toks.append((e, e.cnt))
        self._wait(self.sp, toks)


class Tl:
    def __init__(self, t, excl=False):
        self.t = t
        self.res = Res(excl)

    def __getitem__(self, k):
        return self.t[k]


WX = W + 4
NPOOL = [2560]
SAMPLE_STEPS_PER_BLOCK = 9
import os
DBG = int(os.environ.get('KDBG', '99'))


def build(with_sample=True, stop_after=None):
    nc = bass.Bass("TRN2", target_bir_lowering=False)
    dr = lambda name, shape, dt=F32, kind="ExternalInput": nc.dram_tensor(name, shape, dt, kind=kind).ap()
    xall = dr("xall", [TP, D])
    xslot = dr("xslot", [NS, WR, D])
    qpos_d = dr("qpos", [1, NS * W])
    w_in_d = dr("w_in", [D, 2048])
    w_out_d = dr("w_out", [D, D])
    w_up_d = dr("w_up", [D, 2 * DFF])
    w_down_d = dr("w_down", [DFF, D])
    g_mix_d = dr("norm_mix_g", [D, 1])
    g_ffn_d = dr("norm_ffn_g", [D, 1])
    g_fin_d = dr("norm_final_g", [1, D])
    sbb_d = dr("sb_bias", [1, 8])
    pool_w_d = dr("pool_w", [4, 128, 128])
    pool_s_d = dr("pool_scale", [4, 128])
    conv_w_d = dr("conv_w", [3, 2 * DFF])
    conv_b_d = dr("conv_b", [1, 2 * DFF])
    y_o = dr("y_slot", [NS, W, D], kind="ExternalOutput")
    k_o = dr("k_all", [TP, 512], kind="ExternalOutput")
    v_o = dr("v_all", [TP, 512], kind="ExternalOutput")
    pp_o = dr("pool_last", [15, 512], kind="ExternalOutput")
    cp_o = dr("conv_last", [2, 2 * DFF], kind="ExternalOutput")
    o_scr = nc.dram_tensor("o_scr", [NS, 128, 12, WX], BF16).ap()
    if with_sample:
        xsam = dr("x_sample", [N_SEQ, D])
        ck_d = dr("cache_k", [NPOOL[0] * 128, 512])
        cv_d = dr("cache_v", [NPOOL[0] * 128, 512])
        stp_d = dr("state_pool", [N_SEQ * 15, 512])
        stc_d = dr("state_conv", [N_SEQ * 2, 2 * DFF])
        pt_d = dr("page_table", [1, N_SEQ * NPG], I32)
        ys_o = dr("y_sample", [N_SEQ, D], kind="ExternalOutput")
        ks_o = dr("k_sample", [N_SEQ, 512], kind="ExternalOutput")
        vs_o = dr("v_sample", [N_SEQ, 512], kind="ExternalOutput")
        ps_o = dr("pool_sample", [N_SEQ, 15, 512], kind="ExternalOutput")
        cs_o = dr("conv_sample", [N_SEQ, 2, 2 * DFF], kind="ExternalOutput")
        q_scr = nc.dram_tensor("q_scr", [N_SEQ, 512], F32).ap()

    es = ExitStack()
    with es:
        P = Prog(nc, es)
        pe, act, dve, pool = P.pe, P.act, P.dve, P.pool

        def sb(name, shape, dt=F32, st=es):
            return Tl(st.enter_context(nc.sbuf_tensor(name, shape, dt)))

        def ps(name, shape, dt=F32):
            return Tl(es.enter_context(nc.psum_tensor(name, shape, dt)), excl=True)

        pT = ps("pT", [128, 1024], BF16)
        pG = [ps(f"pG{i}", [128, 512], F32) for i in range(5)]
        pO = [ps(f"pO{i}", [128, 512], F32) for i in range(2)]
        gi = [0]

        def next_pg():
            t = pG[gi[0] % len(pG)]
            gi[0] += 1
            return t

        ident = sb("ident", [128, 128], BF16)
        identf = sb("identf", [128, 128], F32)
        tri = sb("tri", [128, 128], BF16)
        negones = sb("negones", [128, 128], BF16)
        onesf = sb("onesf", [128, 128], F32)
        negf = sb("negf", [128, 128], F32)
        kpos = sb("kpos", [128, NBLK], F32)
        mhalf = sb("mhalf", [128, 1], F32)
        P.op(pool, lambda: nc.gpsimd.memset(onesf[:], 1.0), writes=[onesf.res])
        P.op(pool, lambda: nc.gpsimd.memset(negf[:], -1.0), writes=[negf.res])
        P.op(pool, lambda: nc.gpsimd.memset(mhalf[:], -0.5), writes=[mhalf.res])
        P.op(pool, lambda: nc.gpsimd.memset(negones[:], -1.0), writes=[negones.res])
        P.op(pool, lambda: nc.gpsimd.affine_select(out=identf[:], in_=onesf[:], pattern=[[-1, 128]],
                                                    compare_op=ALU.is_equal, fill=0.0, base=0, channel_multiplier=1),
             reads=[onesf.res], writes=[identf.res])
        P.op(pool, lambda: nc.gpsimd.tensor_copy(out=ident[:], in_=identf[:]), reads=[identf.res], writes=[ident.res])
        P.op(pool, lambda: nc.gpsimd.affine_select(out=tri[:], in_=negf[:], pattern=[[-1, 128]],
                                                    compare_op=ALU.is_ge, fill=0.0, base=0, channel_multiplier=1),
             reads=[negf.res], writes=[tri.res])
        P.op(pool, lambda: nc.gpsimd.iota(kpos[:], pattern=[[128, NBLK]], base=0, channel_multiplier=1,
                                          allow_small_or_imprecise_dtypes=True), writes=[kpos.res])

        bias_t = sb("bias_t", [128, 8])
        P.dma("sp", bias_t[:], sbb_d.partition_broadcast(128), writes=[bias_t.res])
        gfin = sb("gfin", [128, D])
        P.dma("sp", gfin[:], g_fin_d.partition_broadcast(128), writes=[gfin.res])
        gmix = sb("gmix", [128, 8])
        P.dma("sp", gmix[:], g_mix_d.rearrange("(c p) o -> p (c o)", p=128), writes=[gmix.res],
              allow_slow_non_contiguous=True)
        gffn = sb("gffn", [128, 8])
        P.dma("sp", gffn[:], g_ffn_d.rearrange("(c p) o -> p (c o)", p=128), writes=[gffn.res],
              allow_slow_non_contiguous=True)
        gmix_bc = sb("gmix_bc", [128, 8, 128])
        gffn_bc = sb("gffn_bc", [128, 8, 128])
        for dc in range(8):
            P.op(pool, lambda dc=dc: nc.gpsimd.tensor_scalar(out=gmix_bc[:, dc, :], in0=onesf[:], scalar1=gmix[:, dc:dc + 1],
                                                             scalar2=None, op0=ALU.mult),
                 reads=[onesf.res, gmix.res], writes=[gmix_bc.res])
            P.op(pool, lambda dc=dc: nc.gpsimd.tensor_scalar(out=gffn_bc[:, dc, :], in0=onesf[:], scalar1=gffn[:, dc:dc + 1],
                                                             scalar2=None, op0=ALU.mult),
                 reads=[onesf.res, gffn.res], writes=[gffn_bc.res])
        qpos_bc = sb("qpos_bc", [128, NS * W])
        P.dma("sp", qpos_bc[:], qpos_d.partition_broadcast(128), writes=[qpos_bc.res])
        pscale = sb("pscale", [128, 4])
        P.dma("sp", pscale[:], pool_s_d.rearrange("g c -> c g"), writes=[pscale.res], allow_slow_non_contiguous=True)
        cw = sb("cw", [128, 3, 44])
        cb = sb("cb", [128, 44])
        for q4 in range(4):
            cs_ = slice(q4 * 1408, (q4 + 1) * 1408)
            for i3 in range(3):
                P.dma("sp", cw[:, i3, q4 * 11:(q4 + 1) * 11], conv_w_d[i3:i3 + 1, cs_].rearrange("o (c p) -> p (o c)", p=128),
                      writes=[cw.res], allow_slow_non_contiguous=True)
            P.dma("sp", cb[:, q4 * 11:(q4 + 1) * 11], conv_b_d[:, cs_].rearrange("o (c p) -> p (o c)", p=128),
                  writes=[cb.res], allow_slow_non_contiguous=True)
        pw_bf = sb("pw_bf", [128, 4, 128], BF16)
        P.dma("pool", pw_bf[:], pool_w_d.rearrange("g c e -> c g e"), writes=[pw_bf.res])

        junk = [sb(f"junk{i}", [128, D], BF16) for i in range(2)]
        ssq = [sb(f"ssq{i}", [128, 1]) for i in range(4)]
        xn = [sb(f"xn{i}", [128, D], BF16) for i in range(2)]
        cnt = {"x": 0, "ss": 0, "xn": 0, "hT": 0, "j": 0}

        def rms_rows(x_ap_fn, x_res, rows, out_bf_tl):
            jk = junk[cnt["j"] % 2]; cnt["j"] += 1
            ss = ssq[cnt["ss"] % 4]; cnt["ss"] += 1
            P.op(act, lambda: nc.scalar.activation(out=jk[0:rows, :], in_=x_ap_fn(), func=AF.Square,
                                                   accum_out=ss[0:rows, :]),
                 reads=[x_res], writes=[jk.res, ss.res])
            P.op(dve, lambda: nc.vector.tensor_scalar(out=ss[0:rows, :], in0=ss[0:rows, :], scalar1=1.0 / D,
                                                      scalar2=EPS, op0=ALU.mult, op1=ALU.add),
                 reads=[ss.res], writes=[ss.res])
            P.op(pool, lambda: nc.gpsimd.tensor_tensor(out=ss[0:rows, :], in0=ss[0:rows, :], in1=mhalf[0:rows, :],
                                                       op=ALU.pow),
                 reads=[ss.res, mhalf.res], writes=[ss.res])
            if out_bf_tl is not None:
                P.op(dve, lambda: nc.vector.tensor_scalar(out=out_bf_tl[0:rows, :], in0=x_ap_fn(),
                                                          scalar1=ss[0:rows, 0:1], scalar2=None, op0=ALU.mult),
                     reads=[x_res, ss.res], writes=[out_bf_tl.res])
            return ss

        def transpose_rows(xn_tl, rows, dst_ap_fn, dst_res, g_bc):
            for dc in range(8):
                P.op(pe, lambda dc=dc: nc.tensor.transpose(out=pT[:, dc * 128:dc * 128 + rows],
                                                           in_=xn_tl[0:rows, dc * 128:(dc + 1) * 128],
                                                           identity=ident[0:rows, 0:rows]),
                     reads=[xn_tl.res, ident.res], writes=[pT.res], inc=(dc == 7))
            P.op(dve, lambda: nc.vector.tensor_tensor(
                out=dst_ap_fn(), in0=pT[:].rearrange("p (c t) -> p c t", c=8)[:, :, 0:rows],
                in1=g_bc[:, :, 0:rows], op=ALU.mult),
                reads=[pT.res, g_bc.res], writes=[dst_res])

        def barrier():
            toks = [(e, e.cnt) for e in (pe, act, dve, pool) if e.cnt]
            for q in P.dsems:
                toks += [(d, d.cnt) for d in P.dsems[q] if d.cnt]
            for e in (pe, act, dve, pool, P.sp):
                P._wait(e, toks)

        hT_sam = sb("hT_sam", [128, 8, N_SEQ], BF16) if with_sample else None
        if stop_after == "consts":
            P.final_wait()
            return nc

        esAB = ExitStack()
        KT = sb("KT", [128, 4, TP], BF16, esAB)
        Vb = sb("Vb", [128, NBLK, 512], BF16, esAB)
        kt_res = [Res() for _ in range(NBLK)]
        v_res = [Res() for _ in range(NBLK)]
        w_qu = sb("w_qu", [128, 8, 1024], BF16, esAB)
        sam_state = {}
        if with_sample:
            sam_state.update(
                idxi=sb("idxi", [128, N_SEQ * NPG], I32, esAB), bdiag=sb("bdiag", [8, 512], BF16, esAB),
                ones8=sb("ones8", [8, 1], BF16, esAB), A8all=sb("A8all", [128, N_SEQ, 512], BF16, esAB))
        with ExitStack() as esA:
            w_kv = sb("w_kv", [128, 8, 1024], BF16, esA)
            P.dma("pool", w_kv[:], w_in_d[:, 512:1536].rearrange("(c p) n -> p c n", p=128), writes=[w_kv.res])
            P.dma("pool", w_qu[:, :, 0:512], w_in_d[:, 0:512].rearrange("(c p) n -> p c n", p=128), writes=[w_qu.res])
            P.dma("pool", w_qu[:, :, 512:1024], w_in_d[:, 1536:2048].rearrange("(c p) n -> p c n", p=128), writes=[w_qu.res])
            xt = [sb(f"xt{i}", [128, D], F32, esA) for i in range(2)]
            hT = [sb(f"hT{i}", [128, 8, 128], BF16, esA) for i in range(3)]
            ko = [sb(f"ko{i}", [128, 512], F32, esA) for i in range(2)]
            vo = [sb(f"vo{i}", [128, 512], F32, esA) for i in range(2)]
            nblocks = NBLK + (1 if with_sample else 0)
            if stop_after and stop_after.startswith('A') and len(stop_after) > 1:
                nblocks = int(stop_after[1:])
            order = list(range(nblocks))
            if with_sample and nblocks == NBLK + 1:
                order = [NBLK] + list(range(NBLK))
            esS = ExitStack()
            sgen = None
            for tb in order:
                if sgen is not None:
                    for _ in range(SAMPLE_STEPS_PER_BLOCK):
                        if next(sgen, "done") == "done":
                            break
                sam = tb == NBLK
                rows = N_SEQ if sam else 128
                x_tl = xt[tb % 2]
                if sam:
                    P.dma("sp", x_tl[0:rows, :], xsam[:, :], writes=[x_tl.res])
                else:
                    P.dma("sp", x_tl[:], xall[tb * 128:(tb + 1) * 128, :], writes=[x_tl.res])
                if DBG < 2:
                    continue
                xn_tl = xn[cnt["xn"] % 2]; cnt["xn"] += 1
                rms_rows(lambda x_tl=x_tl, rows=rows: x_tl[0:rows, :], x_tl.res, rows, xn_tl)
                if DBG < 3:
                    continue
                if sam:
                    h_ap = lambda: hT_sam[:]
                    h_res = hT_sam.res
                    h_rd = lambda dc: hT_sam[:, dc, :]
                else:
                    h_tl = hT[tb % 3]
                    h_ap = lambda h_tl=h_tl: h_tl[:]
                    h_res = h_tl.res
                    h_rd = lambda dc, h_tl=h_tl: h_tl[:, dc, :]
                transpose_rows(xn_tl, rows, h_ap, h_res, gmix_bc)
                if DBG < 4:
                    continue
                if not sam:
                    pk = next_pg()
                    for j in range(4):
                        for dc in range(8):
                            P.op(pe, lambda j=j, dc=dc, pk=pk: nc.tensor.matmul(
                                pk[:, j * 128:(j + 1) * 128], lhsT=w_kv[:, dc, j * 128:(j + 1) * 128],
                                rhs=h_rd(dc), start=(dc == 0), stop=(dc == 7)),
                                reads=[w_kv.res, h_res], writes=[pk.res], inc=(j == 3 and dc == 7))
                    P.op(act, lambda pk=pk, tb=tb: nc.scalar.copy(
                        out=KT[:, :, tb * 128:(tb + 1) * 128], in_=pk[:].rearrange("p (j t) -> p j t", j=4)),
                        reads=[pk.res], writes=[kt_res[tb]])
                if DBG < 5:
                    continue
                pk2 = next_pg()
                for dc in range(8):
                    P.op(pe, lambda dc=dc, pk2=pk2: nc.tensor.matmul(
                        pk2[0:rows, :], lhsT=h_rd(dc), rhs=w_kv[:, dc, 0:512], start=(dc == 0), stop=(dc == 7)),
                        reads=[w_kv.res, h_res], writes=[pk2.res], inc=(dc == 7))
                ko_tl = ko[tb % 2]
                P.op(act, lambda pk2=pk2, ko_tl=ko_tl: nc.scalar.copy(out=ko_tl[0:rows, :], in_=pk2[0:rows, :]),
                     reads=[pk2.res], writes=[ko_tl.res])
                P.dma("sp", (ks_o[:, :] if sam else k_o[tb * 128:(tb + 1) * 128, :]), ko_tl[0:rows, :], reads=[ko_tl.res])
                if DBG < 6:
                    continue
                pv = next_pg()
                for dc in range(8):
                    P.op(pe, lambda dc=dc, pv=pv: nc.tensor.matmul(
                        pv[0:rows, :], lhsT=h_rd(dc), rhs=w_kv[:, dc, 512:1024], start=(dc == 0), stop=(dc == 7)),
                        reads=[w_kv.res, h_res], writes=[pv.res], inc=(dc == 7))
                vo_tl = vo[tb % 2]
                P.op(act, lambda pv=pv, vo_tl=vo_tl: nc.scalar.copy(out=vo_tl[0:rows, :], in_=pv[0:rows, :]),
                     reads=[pv.res], writes=[vo_tl.res])
                if not sam:
                    P.op(dve, lambda vo_tl=vo_tl, tb=tb: nc.vector.tensor_copy(out=Vb[:, tb, :], in_=vo_tl[:]),
                         reads=[vo_tl.res], writes=[v_res[tb]])
                P.dma("sp", (vs_o[:, :] if sam else v_o[tb * 128:(tb + 1) * 128, :]), vo_tl[0:rows, :], reads=[vo_tl.res])
                if sam:
                    L_ = dict(locals())
                    L_["esB"] = esS
                    L_["sam_state"] = sam_state
                    sgen = sample_mixer(nc, P, L_)
            if sgen is not None:
                for _ in sgen:
                    pass
            barrier()
            esS.close()
        if stop_after and stop_after.startswith("A"):
            P.final_wait()
            esAB.close()
            return nc

        with ExitStack() as esB:
            o_at = sb("o_at", [64, 8, WX], BF16, esB)
            o_pl = sb("o_pl", [128, 4, WX], BF16, esB)
            P.op(pool, lambda: nc.gpsimd.memset(o_at[:], 0.0), writes=[o_at.res])
            P.op(pool, lambda: nc.gpsimd.memset(o_pl[:], 0.0), writes=[o_pl.res])
            xs1 = [sb(f"xs1_{i}", [128, D], F32, esB) for i in range(2)]
            hTs = sb("hTs", [128, 8, WR], BF16, esB)
            qT = sb("qT", [128, 4, W], BF16, esB)
            uT = sb("uT", [128, 4, WR], F32, esB)
            lv = [sb(f"lv{i}", [128, WR], F32, esB) for i in range(2)]
            icnt = sb("icnt", [128, W], F32, esB)
            dpool = [sb(f"dpool{i}", [128, W], BF16, esB) for i in range(2)]
            NMK = 9
            masks = sb("masks", [128, NMK, W], BF16, esB)
            mask_res = [Res() for _ in range(NMK)]
            e_t = [sb(f"e_t{i}", [128, W], F32, esB) for i in range(3)]
            sp_t = [sb(f"sp_t{i}", [128, W], BF16, esB) for i in range(3)]
            a_t = [sb(f"a_t{i}", [128, W], BF16, esB) for i in range(3)]
            w_t = [sb(f"w_t{i}", [128, W], F32, esB) for i in range(3)]
            acc32 = sb("acc32", [128, W], F32, esB)
            accbf = [sb(f"accbf{i}", [128, W], BF16, esB) for i in range(3)]


            vgen = None
            if with_sample:
                L_ = dict(locals())
                vgen = sample_vpass(nc, P, L_)
            for k in range(NS):
                for (r0, n) in [(0, HALO)] + [(HALO + c0, n) for (c0, n) in QT]:
                    xt_ = xs1[cnt["x"] % 2]; cnt["x"] += 1
                    P.dma("sp", xt_[0:n, :], xslot[k, r0:r0 + n, :], writes=[xt_.res])
                    xn_tl = xn[cnt["xn"] % 2]; cnt["xn"] += 1
                    rms_rows(lambda xt_=xt_, n=n: xt_[0:n, :], xt_.res, n, xn_tl)
                    transpose_rows(xn_tl, n, lambda r0=r0, n=n: hTs[:, :, r0:r0 + n], hTs.res, gmix_bc)
                for j in range(4):
                    pq = next_pg()
                    for dc in range(8):
                        P.op(pe, lambda j=j, dc=dc, pq=pq: nc.tensor.matmul(
                            pq[:, 0:W], lhsT=w_qu[:, dc, j * 128:(j + 1) * 128], rhs=hTs[:, dc, HALO:WR],
                            start=(dc == 0), stop=(dc == 7)),
                            reads=[w_qu.res, hTs.res], writes=[pq.res], inc=(dc == 7))
                    P.op(dve, lambda j=j, pq=pq: nc.vector.tensor_scalar(
                        out=qT[:, j, :], in0=pq[:, 0:W], scalar1=0.125, scalar2=None, op0=ALU.mult),
                        reads=[pq.res], writes=[qT.res])
                for g in range(4):
                    pu = next_pg()
                    for dc in range(8):
                        P.op(pe, lambda g=g, dc=dc, pu=pu: nc.tensor.matmul(
                            pu[:, 0:WR], lhsT=w_qu[:, dc, 512 + g * 128:512 + (g + 1) * 128], rhs=hTs[:, dc, :],
                            start=(dc == 0), stop=(dc == 7)),
                            reads=[w_qu.res, hTs.res], writes=[pu.res], inc=(dc == 7))
                    P.op(act, lambda g=g, pu=pu: nc.scalar.copy(out=uT[:, g, :], in_=pu[:, 0:WR]),
                         reads=[pu.res], writes=[uT.res])
                if k == NS - 1:
                    for g in range(4):
                        P.dma("sp", pp_o[:, g * 128:(g + 1) * 128].rearrange("r c -> c r"), uT[:, g, PP0:PP0 + 15],
                              reads=[uT.res], allow_slow_non_contiguous=True)
                for g in range(4):
                    wg = 2 << g
                    cur, cur_res = (lambda g=g: uT[:, g, :]), uT.res
                    lo = 0
                    for lvl in range(g + 1):
                        sh = 1 << lvl
                        dst = lv[lvl % 2]
                        P.op(dve, lambda cur=cur, dst=dst, sh=sh, lo=lo: nc.vector.tensor_tensor(
                            out=dst[:, lo + sh:WR], in0=cur()[:, lo + sh:WR], in1=cur()[:, lo:WR - sh], op=ALU.add),
                            reads=[cur_res], writes=[dst.res])
                        cur, cur_res = (lambda dst=dst: dst[:]), dst.res
                        lo += sh
                    P.op(dve, lambda wg=wg, k=k: nc.vector.tensor_scalar(
                        out=icnt[:], in0=qpos_bc[:, k * W:(k + 1) * W], scalar1=1.0, scalar2=float(wg),
                        op0=ALU.add, op1=ALU.min), reads=[qpos_bc.res], writes=[icnt.res])
                    P.op(dve, lambda: nc.vector.reciprocal(out=icnt[:], in_=icnt[:]), reads=[icnt.res], writes=[icnt.res])
                    P.op(dve, lambda cur=cur: nc.vector.tensor_tensor(
                        out=icnt[:], in0=cur()[:, HALO:WR], in1=icnt[:], op=ALU.mult),
                        reads=[cur_res, icnt.res], writes=[icnt.res])
                    dp = dpool[g % 2]
                    P.op(dve, lambda g=g, dp=dp: nc.vector.tensor_tensor(
                        out=dp[:], in0=icnt[:], in1=uT[:, g, HALO:WR], op=ALU.subtract),
                        reads=[icnt.res, uT.res], writes=[dp.res])
                    pp = next_pg()
                    P.op(pe, lambda g=g, dp=dp, pp=pp: nc.tensor.matmul(
                        pp[:, 0:W], lhsT=pw_bf[:, g, :], rhs=dp[:], start=True, stop=True),
                        reads=[pw_bf.res, dp.res], writes=[pp.res])
                    P.op(dve, lambda g=g, pp=pp: nc.vector.tensor_scalar(
                        out=o_pl[:, g, 0:W], in0=pp[:, 0:W], scalar1=pscale[:, g:g + 1], scalar2=None, op0=ALU.mult),
                        reads=[pp.res, pscale.res], writes=[o_pl.res])
                nkb = n_kb_for_slot(k)
                mkb = [kb for kb in range(nkb) if kb_needs_mask(k, kb)]
                assert len(mkb) <= NMK, len(mkb)
                midx = {kb: i for i, kb in enumerate(mkb)}
                for kb in mkb:
                    i = midx[kb]
                    P.op(dve, lambda kb=kb, i=i, k=k: nc.vector.tensor_scalar(
                        out=masks[:, i, :], in0=qpos_bc[:, k * W:(k + 1) * W], scalar1=kpos[:, kb:kb + 1],
                        scalar2=NEG, op0=ALU.is_le, op1=ALU.mult),
                        reads=[qpos_bc.res, kpos.res], writes=[mask_res[i]])
                ucount = [0]
                for h in range(8):
                    attn_head(nc, P, h, nkb, midx, ucount, locals())
                P.dma("sp", o_scr[k, 0:64, 0:8, 0:W], o_at[:, :, 0:W], reads=[o_at.res])
                P.dma("sp", o_scr[k, :, 8:12, 0:W], o_pl[:, :, 0:W], reads=[o_pl.res])
            if vgen is not None:
                for _ in vgen:
                    pass
            barrier()
        esAB.close()
        if stop_after == "B1":
            P.final_wait()
            return nc

        with ExitStack() as esC:
            w_oa = sb("w_oa", [64, 8, D], BF16, esC)
            P.dma("pool", w_oa[:], w_out_d[0:512, :].rearrange("(h p) n -> p h n", p=64), writes=[w_oa.res])
            w_op = sb("w_op", [128, 4, D], BF16, esC)
            P.dma("pool", w_op[:], w_out_d[512:1024, :].rearrange("(g p) n -> p g n", p=128), writes=[w_op.res])
            wdn = sb("wdn", [128, NFC, D], BF16, esC)
            for i in range(NFC):
                P.dma("pool", wdn[:, i, :], w_down_d[i * 128:(i + 1) * 128, :], writes=[wdn.res])
            xs = sb("xs2", [128, 5, D], F32, esC)
            o_at = sb("o_at2", [64, 8, WX], BF16, esC)
            o_pl = sb("o_pl2", [128, 4, WX], BF16, esC)
            h2T = sb("h2T", [128, 8, WX], BF16, esC)
            cv_t = [sb(f"cv_t{i}", [128, WX], F32, esC) for i in range(6)]
            sg_t = [sb(f"sg_t{i}", [128, WX], F32, esC) for i in range(3)]
            actT = sb("actT", [128, NFC, WX], BF16, esC)
            upl = sb("upl", [128, 2, 44], F32, esC)
            wup_t = [sb(f"wup{i}", [128, 8, 256], BF16, esC) for i in range(4)]
            P.op(dve, lambda: nc.vector.memset(actT[:], 0.0), writes=[actT.res])
            if with_sample:
                upl_s = sb("upl_s", [128, N_SEQ, 44], F32, esC)
                sc8 = [sb(f"sc8_{i}", [8, 128], F32, esC) for i in range(2)]
                pF = pO[1]

            for k in range(NS):
                wk = WX if (k == 0 and with_sample) else W
                tiles = list(QT) + ([(W, N_SEQ)] if (k == 0 and with_sample) else [])
                P.dma("sp", o_at[:], o_scr[k, 0:64, 0:8, :], writes=[o_at.res])
                P.dma("sp", o_pl[:], o_scr[k, :, 8:12, :], writes=[o_pl.res])
                for ti, (c0, n) in enumerate(tiles):
                    if ti < 4:
                        P.dma("sp", xs[0:n, ti, :], xslot[k, HALO + c0:HALO + c0 + n, :], writes=[xs.res])
                    else:
                        P.dma("sp", xs[0:n, ti, :], xsam[:, :], writes=[xs.res])
                for ti, (c0, n) in enumerate(tiles):
                    for half in range(2):
                        pw = next_pg()
                        for h in range(8):
                            P.op(pe, lambda h=h, pw=pw: nc.tensor.matmul(
                                pw[0:n, :], lhsT=o_at[:, h, c0:c0 + n], rhs=w_oa[:, h, half * 512:(half + 1) * 512],
                                start=(h == 0), stop=False),
                                reads=[o_at.res, w_oa.res], writes=[pw.res], inc=False)
                        for g in range(4):
                            P.op(pe, lambda g=g, pw=pw: nc.tensor.matmul(
                                pw[0:n, :], lhsT=o_pl[:, g, c0:c0 + n], rhs=w_op[:, g, half * 512:(half + 1) * 512],
                                start=False, stop=(g == 3)),
                                reads=[o_pl.res, w_op.res], writes=[pw.res], inc=(g == 3))
                        P.op(dve, lambda pw=pw, half=half, ti=ti, n=n: nc.vector.tensor_tensor(
                            out=xs[0:n, ti, half * 512:(half + 1) * 512], in0=pw[0:n, :],
                            in1=xs[0:n, ti, half * 512:(half + 1) * 512], op=ALU.add),
                            reads=[pw.res, xs.res], writes=[xs.res])
                    xn_tl = xn[cnt["xn"] % 2]; cnt["xn"] += 1
                    rms_rows(lambda ti=ti, n=n: xs[0:n, ti, :], xs.res, n, xn_tl)
                    transpose_rows(xn_tl, n, lambda c0=c0, n=n: h2T[:, :, c0:c0 + n], h2T.res, gffn_bc)
                pend = None

                def emit_mult(sg, cv, i, wk):
                    P.op(dve, lambda: nc.vector.tensor_tensor(
                        out=actT[:, i, 2:wk], in0=sg[:, 2:wk], in1=cv[:, 2:wk], op=ALU.mult),
                        reads=[sg.res, cv.res], writes=[actT.res])

                for i in range(NFC):
                    wt = wup_t[i % 4]
                    P.dma("pool", wt[:, :, 0:128], w_up_d[:, i * 128:(i + 1) * 128].rearrange("(c p) f -> p c f", p=128),
                          writes=[wt.res])
                    P.dma("pool", wt[:, :, 128:256],
                          w_up_d[:, DFF + i * 128:DFF + (i + 1) * 128].rearrange("(c p) f -> p c f", p=128), writes=[wt.res])
                    pus = []
                    for a in range(2):
                        pu = next_pg()
                        for dc in range(8):
                            P.op(pe, lambda a=a, dc=dc, pu=pu, wt=wt: nc.tensor.matmul(
                                pu[:, 0:wk], lhsT=wt[:, dc, a * 128:(a + 1) * 128], rhs=h2T[:, dc, 0:wk],
                                start=(dc == 0), stop=(dc == 7)),
                                reads=[wt.res, h2T.res], writes=[pu.res], inc=(dc == 7))
                        pus.append(pu)
                    cvs = [cv_t[(2 * i + a) % 6] for a in range(2)]
                    for a in range(2):
                        fc = i + a * NFC
                        P.op(act, lambda pu=pus[a], cv=cvs[a], fc=fc: nc.scalar.activation(
                            out=cv[:, 2:W], in_=pu[:, 0:W - 2], func=AF.Identity, scale=cw[:, 0, fc:fc + 1],
                            bias=cb[:, fc:fc + 1]), reads=[pus[a].res, cw.res, cb.res], writes=[cvs[a].res])
                    if pend is not None:
                        emit_mult(*pend)
                        pend = None
                    for tap in (1, 2):
                        for a in range(2):
                            fc = i + a * NFC
                            P.op(dve, lambda pu=pus[a], cv=cvs[a], fc=fc, tap=tap: nc.vector.scalar_tensor_tensor(
                                out=cv[:, 2:W], in0=pu[:, tap:W - 2 + tap], scalar=cw[:, tap, fc:fc + 1], in1=cv[:, 2:W],
                                op0=ALU.mult, op1=ALU.add), reads=[pus[a].res, cw.res, cvs[a].res], writes=[cvs[a].res])
                    for a in range(2):
                        fc = i + a * NFC
                        pu, cv = pus[a], cvs[a]
                        if k == NS - 1:
                            P.op(dve, lambda pu=pu, fc=fc: nc.vector.tensor_copy(out=upl[:, :, fc], in_=pu[:, CP0:CP0 + 2]),
                                 reads=[pu.res], writes=[upl.res])
                        if k == 0 and with_sample:
                            s8 = sc8[(2 * i + a) % 2]
                            P.dma("sp", s8[:], stc_d[:, fc * 128:(fc + 1) * 128], writes=[s8.res])
                            P.op(pe, lambda s8=s8: nc.tensor.transpose(out=pF[:, 0:8], in_=s8[0:8, :],
                                                                       identity=identf[0:8, 0:8]),
                                 reads=[s8.res, identf.res], writes=[pF.res])
                            P.op(dve, lambda pu=pu, fc=fc: nc.vector.tensor_copy(out=upl_s[:, :, fc], in_=pu[:, W:WX]),
                                 reads=[pu.res], writes=[upl_s.res])
                            P.op(dve, lambda pu=pu, cv=cv, fc=fc: nc.vector.tensor_scalar(
                                out=cv[:, W:WX], in0=pu[:, W:WX], scalar1=cw[:, 2, fc:fc + 1], scalar2=cb[:, fc:fc + 1],
                                op0=ALU.mult, op1=ALU.add), reads=[pu.res, cw.res, cb.res], writes=[cv.res])
                            for r in (0, 1):
                                P.op(dve, lambda cv=cv, fc=fc, r=r: nc.vector.scalar_tensor_tensor(
                                    out=cv[:, W:WX], in0=pF[:, 0:8].rearrange("p (s r) -> p r s", r=2)[:, r, :],
                                    scalar=cw[:, r, fc:fc + 1], in1=cv[:, W:WX], op0=ALU.mult, op1=ALU.add),
                                    reads=[pF.res, cw.res, cv.res], writes=[cv.res])
                    sg = sg_t[i % 3]
                    P.op(act, lambda sg=sg, cv=cvs[0]: nc.scalar.activation(out=sg[:, 2:wk], in_=cv[:, 2:wk], func=AF.Silu),
                         reads=[cvs[0].res], writes=[sg.res])
                    pend = (sg, cvs[1], i, wk)
                if pend is not None:
                    emit_mult(*pend)
                    pend = None
                for ti, (c0, n) in enumerate(tiles):
                    for half in range(2):
                        pd = next_pg()
                        for i in range(NFC):
                            P.op(pe, lambda i=i, pd=pd: nc.tensor.matmul(
                                pd[0:n, :], lhsT=actT[:, i, c0:c0 + n], rhs=wdn[:, i, half * 512:(half + 1) * 512],
                                start=(i == 0), stop=(i == NFC - 1)),
                                reads=[actT.res, wdn.res], writes=[pd.res], inc=(i == NFC - 1))
                        P.op(dve, lambda pd=pd, half=half, ti=ti, n=n: nc.vector.tensor_tensor(
                            out=xs[0:n, ti, half * 512:(half + 1) * 512], in0=pd[0:n, :],
                            in1=xs[0:n, ti, half * 512:(half + 1) * 512], op=ALU.add),
                            reads=[pd.res, xs.res], writes=[xs.res])
                    ss = rms_rows(lambda ti=ti, n=n: xs[0:n, ti, :], xs.res, n, None)
                    P.op(dve, lambda ss=ss, ti=ti, n=n: nc.vector.scalar_tensor_tensor(
                        out=xs[0:n, ti, :], in0=xs[0:n, ti, :], scalar=ss[0:n, 0:1], in1=gfin[0:n, :],
                        op0=ALU.mult, op1=ALU.mult),
                        reads=[xs.res, ss.res, gfin.res], writes=[xs.res])
                    if ti < 4:
                        P.dma("sp", y_o[k, c0:c0 + n, :], xs[0:n, ti, :], reads=[xs.res])
                    else:
                        P.dma("sp", ys_o[:, :], xs[0:n, ti, :], reads=[xs.res])
            for r in range(2):
                for q4 in range(4):
                    P.dma("sp", cp_o[r:r + 1, q4 * 1408:(q4 + 1) * 1408].rearrange("r (c p) -> p (r c)", p=128),
                          upl[:, r, q4 * 11:(q4 + 1) * 11], reads=[upl.res], allow_slow_non_contiguous=True)
            if with_sample:
                for s in range(N_SEQ):
                    for q4 in range(4):
                        P.dma("sp", cs_o[s, 1:2, q4 * 1408:(q4 + 1) * 1408].rearrange("r (c p) -> p (r c)", p=128),
                              upl_s[:, s, q4 * 11:(q4 + 1) * 11], reads=[upl_s.res], allow_slow_non_contiguous=True)
                    P.dma("sp", cs_o[s, 0:1, :], stc_d[2 * s + 1:2 * s + 2, :])
            P.final_wait()
    return nc


def attn_head(nc, P, h, nkb, midx, ucount, L):
    pe, act, dve, pool = P.pe, P.act, P.dve, P.pool
    KT, Vb, qT, kt_res, v_res = L["KT"], L["Vb"], L["qT"], L["kt_res"], L["v_res"]
    masks, mask_res, ident, tri, negones = L["masks"], L["mask_res"], L["ident"], L["tri"], L["negones"]
    e_t, sp_t, a_t, acc32, accbf, w_t = L["e_t"], L["sp_t"], L["a_t"], L["acc32"], L["accbf"], L["w_t"]
    bias_t, o_at, pO, next_pg = L["bias_t"], L["o_at"], L["pO"], L["next_pg"]
    j, hb = h // 2, (h % 2) * 64
    po = pO[0]
    kbs = list(range(nkb - 1, -1, -1))
    n = len(kbs)
    st_ = {}
    u0 = ucount[0]

    def s1(kb):
        p = next_pg()
        need_m = kb in midx
        P.op(pe, lambda: nc.tensor.matmul(
            p[:, 0:W], lhsT=KT[hb:hb + 64, j, kb * 128:(kb + 1) * 128], rhs=qT[hb:hb + 64, j, :],
            start=True, stop=not need_m),
            reads=[kt_res[kb], qT.res], writes=[p.res], inc=not need_m)
        if need_m:
            P.op(pe, lambda: nc.tensor.matmul(
                p[:, 0:W], lhsT=ident[:], rhs=masks[:, midx[kb], :], start=False, stop=True),
                reads=[ident.res, mask_res[midx[kb]]], writes=[p.res])
        st_[kb] = {"p": p}

    def s2a(kb, u):
        p = st_[kb]["p"]
        e = e_t[u % 3]
        P.op(act, lambda: nc.scalar.activation(out=e[:], in_=p[:, 0:W], func=AF.Exp,
                                               bias=bias_t[:, h:h + 1], scale=1.0),
             reads=[p.res, bias_t.res], writes=[e.res])
        st_[kb]["e"] = e

    def s2b(kb, u):
        e = st_[kb]["e"]
        s = sp_t[u % 3]
        P.op(act, lambda: nc.scalar.activation(out=s[:], in_=e[:], func=AF.Ln, bias=1.0, scale=1.0),
             reads=[e.res], writes=[s.res])
        st_[kb]["s"] = s

    def s3(kb, u, first):
        s = st_[kb]["s"]
        pc = next_pg()
        st_[kb]["pc"] = pc
        P.op(pe, lambda: nc.tensor.matmul(pc[:, 0:W], lhsT=tri[:], rhs=s[:], start=True, stop=first),
             reads=[tri.res, s.res], writes=[pc.res], inc=first)
        if not first:
            ab = accbf[(u - 1) % 3]
            P.op(pe, lambda: nc.tensor.matmul(pc[:, 0:W], lhsT=negones[:], rhs=ab[:], start=False, stop=True),
                 reads=[negones.res, ab.res], writes=[pc.res])
        abn = accbf[u % 3]
        if first:
            P.op(dve, lambda: nc.vector.tensor_copy(out=abn[:], in_=s[:]), reads=[s.res], writes=[abn.res])
        else:
            abo = accbf[(u - 1) % 3]
            P.op(dve, lambda: nc.vector.tensor_tensor(out=abn[:], in0=abo[:], in1=s[:], op=ALU.add),
                 reads=[abo.res, s.res], writes=[abn.res])

    def s4(kb, u):
        pc = st_[kb]["pc"]; e = st_[kb]["e"]
        w = w_t[u % 3]
        a = a_t[u % 3]
        P.op(act, lambda: nc.scalar.activation(out=w[:], in_=pc[:, 0:W], func=AF.Exp),
             reads=[pc.res], writes=[w.res])
        P.op(dve, lambda: nc.vector.tensor_tensor(out=a[:], in0=e[:], in1=w[:], op=ALU.mult),
             reads=[e.res, w.res], writes=[a.res])
        st_[kb]["a"] = a

    def s5(kb, first, last):
        a = st_[kb]["a"]
        P.op(pe, lambda: nc.tensor.matmul(po[0:64, 0:W], lhsT=Vb[:, kb, h * 64:(h + 1) * 64], rhs=a[:],
                                          start=first, stop=last),
             reads=[v_res[kb], a.res], writes=[po.res], inc=last)
        del st_[kb]

    vgen = L.get("vgen")
    s1(kbs[0])
    for i in range(n + 1):
        if i + 1 < n:
            s1(kbs[i + 1])
        if i < n:
            s2a(kbs[i], u0 + i)
        if i >= 1:
            s4(kbs[i - 1], u0 + i - 1)
        if i < n:
            s2b(kbs[i], u0 + i)
            s3(kbs[i], u0 + i, i == 0)
        if i >= 1:
            s5(kbs[i - 1], i - 1 == 0, i - 1 == n - 1)
        if vgen is not None and i % 2 == 1:
            next(vgen, None)
    ucount[0] += n
    P.op(dve, lambda: nc.vector.tensor_copy(out=o_at[:, h, 0:W], in_=po[0:64, 0:W]),
         reads=[po.res], writes=[o_at.res])


def sample_mixer(nc, P, L):
    pe, act, dve, pool = P.pe, P.act, P.dve, P.pool
    sb, esB, next_pg, sam_state = L["sb"], L["esB"], L["next_pg"], L["sam_state"]
    hT_sam, w_qu, pw_bf, pscale, bias_t = L["hT_sam"], L["w_qu"], L["pw_bf"], L["pscale"], L["bias_t"]
    tri, negones, identf, pO, o_scr = L["tri"], L["negones"], L["identf"], L["pO"], L["o_scr"]
    o_pl = sb("o_spl", [128, 4, N_SEQ], BF16, esB)
    ck_d, cv_d, stp_d, pt_d, ps_o, q_scr = L["ck_d"], L["cv_d"], L["stp_d"], L["pt_d"], L["ps_o"], L["q_scr"]
    NP = N_SEQ * NPG
    pti = sb("pti", [128, NP], I32, esB)
    ptf = sb("ptf", [128, NP], F32, esB)
    iop = sb("iop", [128, 1], F32, esB)
    idxi = sam_state["idxi"]
    P.dma("sp", pti[:], pt_d.partition_broadcast(128), writes=[pti.res])
    P.op(pool, lambda: nc.gpsimd.iota(iop[:], pattern=[[0, 1]], base=0, channel_multiplier=1,
                                      allow_small_or_imprecise_dtypes=True), writes=[iop.res])
    P.op(pool, lambda: nc.gpsimd.tensor_copy(out=ptf[:], in_=pti[:]), reads=[pti.res], writes=[ptf.res])
    P.op(pool, lambda: nc.gpsimd.tensor_scalar(out=ptf[:], in0=ptf[:], scalar1=128.0, scalar2=iop[:, 0:1],
                                               op0=ALU.mult, op1=ALU.add), reads=[ptf.res, iop.res], writes=[ptf.res])
    P.op(pool, lambda: nc.gpsimd.tensor_copy(out=idxi[:], in_=ptf[:]), reads=[ptf.res], writes=[idxi.res])
    bd0 = sb("bd0", [8, 512], F32, esB)
    bd1 = sb("bd1", [8, 512], F32, esB)
    bdiag = sam_state["bdiag"]
    ones8 = sam_state["ones8"]
    P.op(pool, lambda: nc.gpsimd.memset(bd0[:], 1.0), writes=[bd0.res])
    P.op(pool, lambda: nc.gpsimd.memset(ones8[:], 1.0), writes=[ones8.res])
    P.op(pool, lambda: nc.gpsimd.affine_select(out=bd1[:], in_=bd0[:], pattern=[[1, 512]], compare_op=ALU.is_ge,
                                                fill=0.0, base=0, channel_multiplier=-64),
         reads=[bd0.res], writes=[bd1.res])
    P.op(pool, lambda: nc.gpsimd.affine_select(out=bd0[:], in_=bd1[:], pattern=[[-1, 512]], compare_op=ALU.is_ge,
                                                fill=0.0, base=63, channel_multiplier=64),
         reads=[bd1.res], writes=[bd0.res])
    P.op(pool, lambda: nc.gpsimd.tensor_copy(out=bdiag[:], in_=bd0[:]), reads=[bd0.res], writes=[bdiag.res])
    qscr_res = Res()
    pq = next_pg()
    for dc in range(8):
        P.op(pe, lambda dc=dc: nc.tensor.matmul(pq[0:N_SEQ, :], lhsT=hT_sam[:, dc, :], rhs=w_qu[:, dc, 0:512],
                                                start=(dc == 0), stop=(dc == 7)),
             reads=[hT_sam.res, w_qu.res], writes=[pq.res], inc=(dc == 7))
    q_tok = sb("q_tok", [N_SEQ, 512], F32, esB)
    P.op(dve, lambda: nc.vector.tensor_scalar(out=q_tok[:], in0=pq[0:N_SEQ, :], scalar1=0.125, scalar2=None, op0=ALU.mult),
         reads=[pq.res], writes=[q_tok.res])
    P.dma("sp", q_scr[:, :], q_tok[:], reads=[q_tok.res], writes=[qscr_res])
    pu = next_pg()
    for dc in range(8):
        P.op(pe, lambda dc=dc: nc.tensor.matmul(pu[0:N_SEQ, :], lhsT=hT_sam[:, dc, :], rhs=w_qu[:, dc, 512:1024],
                                                start=(dc == 0), stop=(dc == 7)),
             reads=[hT_sam.res, w_qu.res], writes=[pu.res], inc=(dc == 7))
    u_tok = sb("u_tok", [N_SEQ, 512], F32, esB)
    P.op(act, lambda: nc.scalar.copy(out=u_tok[:], in_=pu[0:N_SEQ, :]), reads=[pu.res], writes=[u_tok.res])
    P.dma("sp", ps_o[:, 14, :], u_tok[:], reads=[u_tok.res])
    for s in range(N_SEQ):
        P.dma("sp", ps_o[s, 0:14, :], stp_d[s * 15 + 1:s * 15 + 15, :])
    uTs = sb("uTs", [128, 4, N_SEQ], F32, esB)
    for g in range(4):
        pu2 = next_pg()
        for dc in range(8):
            P.op(pe, lambda g=g, dc=dc, pu2=pu2: nc.tensor.matmul(
                pu2[:, 0:N_SEQ], lhsT=w_qu[:, dc, 512 + g * 128:512 + (g + 1) * 128], rhs=hT_sam[:, dc, :],
                start=(dc == 0), stop=(dc == 7)),
                reads=[hT_sam.res, w_qu.res], writes=[pu2.res], inc=(dc == 7))
        P.op(act, lambda g=g, pu2=pu2: nc.scalar.copy(out=uTs[:, g, :], in_=pu2[:, 0:N_SEQ]),
             reads=[pu2.res], writes=[uTs.res])
    st60 = sb("st60", [N_SEQ * 15, 512], F32, esB)
    P.dma("sp", st60[:], stp_d[:, :], writes=[st60.res])
    stT = sb("stT", [128, 4, N_SEQ * 15], F32, esB)
    pst = next_pg()
    for g in range(4):
        P.op(pe, lambda g=g: nc.tensor.transpose(out=pst[:, g * 60:(g + 1) * 60], in_=st60[0:60, g * 128:(g + 1) * 128],
                                                 identity=identf[0:60, 0:60]),
             reads=[st60.res, identf.res], writes=[pst.res], inc=(g == 3))
    P.op(act, lambda: nc.scalar.copy(out=stT[:], in_=pst[:, 0:240].rearrange("p (g c) -> p g c", g=4)),
         reads=[pst.res], writes=[stT.res])
    ssum = sb("ssum", [128, N_SEQ], F32, esB)
    d_bf = [sb(f"d_bf{i}", [128, N_SEQ], BF16, esB) for i in range(2)]
    for g in range(4):
        wg = 2 << g
        nr = wg - 1
        P.op(dve, lambda g=g, nr=nr: nc.vector.tensor_reduce(
            out=ssum[:], in_=stT[:, g, :].rearrange("p (s r) -> p s r", r=15)[:, :, 15 - nr:15], axis=AX.X, op=ALU.add),
            reads=[stT.res], writes=[ssum.res])
        P.op(dve, lambda g=g: nc.vector.tensor_tensor(out=ssum[:], in0=ssum[:], in1=uTs[:, g, :], op=ALU.add),
             reads=[ssum.res, uTs.res], writes=[ssum.res])
        db = d_bf[g % 2]
        P.op(dve, lambda g=g, wg=wg, db=db: nc.vector.scalar_tensor_tensor(
            out=db[:], in0=ssum[:], scalar=1.0 / wg, in1=uTs[:, g, :], op0=ALU.mult, op1=ALU.subtract),
            reads=[ssum.res, uTs.res], writes=[db.res])
        pp = next_pg()
        P.op(pe, lambda g=g, db=db, pp=pp: nc.tensor.matmul(pp[:, 0:N_SEQ], lhsT=pw_bf[:, g, :], rhs=db[:],
                                                            start=True, stop=True),
             reads=[pw_bf.res, db.res], writes=[pp.res])
        P.op(dve, lambda g=g, pp=pp: nc.vector.tensor_scalar(
            out=o_pl[:, g, :], in0=pp[:, 0:N_SEQ], scalar1=pscale[:, g:g + 1], scalar2=None, op0=ALU.mult),
            reads=[pp.res, pscale.res], writes=[o_pl.res])
    P.dma("sp", o_scr[0, :, 8:12, W:WX], o_pl[:], reads=[o_pl.res])
    yield
    LAG = 3
    qb = [sb(f"qb{i}", [128, 512], F32, esB) for i in range(2)]
    Kp = [sb(f"Kp{i}", [128, 512], F32, esB) for i in range(5)]
    tmp = [sb(f"ktmp{i}", [128, 512], F32, esB) for i in range(4)]
    Z = sb("Zs", [128, 512], F32, esB)
    e8 = sb("e8", [128, 512], F32, esB)
    sp8 = sb("sp8", [128, 512], BF16, esB)
    S0 = sb("S0", [128, 512], F32, esB)
    Sa = sb("Sa", [128, 512], F32, esB)
    Sb_ = sb("Sb", [128, 512], F32, esB)
    A8all = sam_state["A8all"]
    hv = lambda t: t[:].rearrange("p (g h) -> p h g", h=8)
    for s in range(N_SEQ):
        q_t = qb[s % 2]
        P.dma("sp", q_t[:], q_scr[s:s + 1, :].partition_broadcast(128), reads=[qscr_res], writes=[q_t.res])
        zres = [Res() for _ in range(NPG)]
        for zr in zres:
            zr.w = Z.res.w
            zr.r = dict(Z.res.r)
        for step in range(NPG + LAG + 1):
            if step < NPG:
                col = s * NPG + step
                kp = Kp[step % 5]
                P.dma("pool", None, None, reads=[idxi.res], writes=[kp.res],
                      fn=lambda kp=kp, col=col: nc.gpsimd.indirect_dma_start(
                          out=kp[:], out_offset=None, in_=ck_d,
                          in_offset=bass.IndirectOffsetOnAxis(ap=idxi[:, col:col + 1], axis=0)))
            if LAG <= step < NPG + LAG:
                pg = step - LAG
                kp = Kp[pg % 5]
                tm = tmp[pg % 4]
                P.op(dve, lambda tm=tm, kp=kp: nc.vector.tensor_tensor(out=tm[:], in0=kp[:], in1=q_t[:], op=ALU.mult),
                     reads=[kp.res, q_t.res], writes=[tm.res])
            if step >= LAG + 1:
                pg = step - LAG - 1
                tm = tmp[pg % 4]
                P.op(dve, lambda tm=tm, pg=pg: nc.vector.tensor_reduce(
                    out=Z[:, pg * 8:(pg + 1) * 8], in_=tm[:].rearrange("p (h d) -> p h d", h=8), axis=AX.X, op=ALU.add),
                    reads=[tm.res], writes=[zres[pg]])
            yield
        Z.res.w = zres[NPG - 1].w
        Z.res.r = {}
        for h in range(8):
            P.op(act, lambda h=h: nc.scalar.activation(out=hv(e8)[:, h, :], in_=hv(Z)[:, h, :], func=AF.Exp,
                                                       bias=bias_t[:, h:h + 1], scale=1.0),
                 reads=[Z.res, bias_t.res], writes=[e8.res])
        P.op(act, lambda: nc.scalar.activation(out=sp8[:], in_=e8[:], func=AF.Ln, bias=1.0, scale=1.0),
             reads=[e8.res], writes=[sp8.res])
        pc = pO[0]
        P.op(pe, lambda: nc.tensor.matmul(pc[:, :], lhsT=tri[:], rhs=sp8[:], start=True, stop=True),
             reads=[tri.res, sp8.res], writes=[pc.res])
        pt_ = pO[1]
        P.op(pe, lambda: nc.tensor.matmul(pt_[:, :], lhsT=negones[:], rhs=sp8[:], start=True, stop=True),
             reads=[negones.res, sp8.res], writes=[pt_.res])
        P.op(act, lambda: nc.scalar.copy(out=S0[:], in_=pt_[:, :]), reads=[pt_.res], writes=[S0.res])
        cur = S0
        for i, dd in enumerate((1, 2, 4, 8, 16, 32)):
            nxt = Sa if i % 2 == 0 else Sb_
            n0 = 512 - 8 * dd
            P.op(dve, lambda cur=cur, nxt=nxt, n0=n0, dd=dd: nc.vector.tensor_tensor(
                out=nxt[:, 0:n0], in0=cur[:, 0:n0], in1=cur[:, 8 * dd:512], op=ALU.add),
                reads=[cur.res], writes=[nxt.res])
            P.op(dve, lambda cur=cur, nxt=nxt, n0=n0: nc.vector.tensor_copy(out=nxt[:, n0:512], in_=cur[:, n0:512]),
                 reads=[cur.res, nxt.res], writes=[nxt.res])
            cur = nxt
        P.op(dve, lambda cur=cur: nc.vector.tensor_tensor(out=cur[:], in0=cur[:], in1=S0[:], op=ALU.subtract),
             reads=[cur.res, S0.res], writes=[cur.res])
        P.op(dve, lambda cur=cur: nc.vector.tensor_tensor(out=cur[:], in0=cur[:], in1=Z[:], op=ALU.add),
             reads=[cur.res, Z.res], writes=[cur.res])
        P.op(dve, lambda cur=cur: nc.vector.tensor_tensor(out=cur[:], in0=cur[:], in1=pc[:, :], op=ALU.add),
             reads=[cur.res, pc.res], writes=[cur.res])
        for h in range(8):
            P.op(act, lambda h=h, cur=cur: nc.scalar.activation(out=A8all[:, s, :].rearrange("p (g h) -> p h g", h=8)[:, h, :], in_=hv(cur)[:, h, :], func=AF.Exp,
                                                                bias=bias_t[:, h:h + 1], scale=1.0),
                 reads=[cur.res, bias_t.res], writes=[A8all.res])
        yield


def sample_vpass(nc, P, L):
    pe, act, dve, pool = P.pe, P.act, P.dve, P.pool
    sb, esB, pO, o_scr, cv_d = L["sb"], L["esB"], L["pO"], L["o_scr"], L["cv_d"]
    st = L["sam_state"]
    A8all, idxi, bdiag, ones8 = st["A8all"], st["idxi"], st["bdiag"], st["ones8"]
    LAG = 2
    NB = 4
    Vp = [sb(f"Vp{i}", [128, 512], BF16, esB) for i in range(NB)]
    m8 = sb("m8", [8, 512], BF16, esB)
    o_at = sb("o_sat", [64, 8, N_SEQ], BF16, esB)
    pov = pO[1]
    for s in range(N_SEQ):
        for step in range(NPG + LAG):
            if step < NPG:
                col = s * NPG + step
                vp = Vp[step % NB]
                P.dma("pool", None, None, reads=[idxi.res], writes=[vp.res],
                      fn=lambda vp=vp, col=col: nc.gpsimd.indirect_dma_start(
                          out=vp[:], out_offset=None, in_=cv_d,
                          in_offset=bass.IndirectOffsetOnAxis(ap=idxi[:, col:col + 1], axis=0)))
            if step >= LAG:
                pg = step - LAG
                vp = Vp[pg % NB]
                P.op(pe, lambda vp=vp, pg=pg, s=s: nc.tensor.matmul(pov[0:8, :], lhsT=A8all[:, s, pg * 8:(pg + 1) * 8], rhs=vp[:],
                                                                    start=(pg == 0), stop=(pg == NPG - 1)),
                     reads=[A8all.res, vp.res], writes=[pov.res], inc=True)
            yield
        P.op(dve, lambda: nc.vector.tensor_tensor(out=m8[:], in0=pov[0:8, :], in1=bdiag[:], op=ALU.mult),
             reads=[pov.res, bdiag.res], writes=[m8.res])
        for h in range(8):
            P.op(pe, lambda h=h: nc.tensor.matmul(pov[0:64, h:h + 1], lhsT=m8[0:8, h * 64:(h + 1) * 64], rhs=ones8[0:8, 0:1],
                                                  start=True, stop=True),
                 reads=[m8.res, ones8.res], writes=[pov.res], inc=(h == 7))
        P.op(dve, lambda s=s: nc.vector.tensor_copy(out=o_at[:, :, s], in_=pov[0:64, 0:8]),
             reads=[pov.res], writes=[o_at.res])
        yield
    P.dma("sp", o_scr[0, 0:64, 0:8, W:WX], o_at[:], reads=[o_at.res])


_NC_CACHE = {}
_RUNNER = [None]
_REMAP = [None, None]


_STOP = [None]


def _get_nc(with_sample):
    key = (with_sample, _STOP[0])
    if key not in _NC_CACHE:
        _NC_CACHE[key] = build(with_sample, _STOP[0])
    return _NC_CACHE[key]


def kernel(x_prompt, x_sample, cache_k, cache_v, state_pool, state_conv, page_table,
           meta_tokens, norm_mix_g, w_in, sb_bias, pool_w, pool_scale, w_out, norm_ffn_g,
           w_up, conv_w, conv_b, w_down, norm_final_g, _with_sample=True):
    f32 = np.float32
    B = x_prompt.shape[0]
    nc = _get_nc(_with_sample)
    in_maps = []
    ck2 = cv2 = None
    if _with_sample:
        ck2 = np.ascontiguousarray(cache_k, dtype=f32).reshape(-1, 512)
        cv2 = np.ascontiguousarray(cache_v, dtype=f32).reshape(-1, 512)
    for c in range(8):
        b, g = c // 2, c % 2
        xa = np.zeros((TP, D), f32)
        xa[:N_META] = meta_tokens
        xa[N_META:T] = x_prompt[b]
        xsl = np.zeros((NS, WR, D), f32)
        qp = np.zeros((1, NS * W), f32)
        for k in range(NS):
            s = slot_start(g, k)
            lo, hi = s - HALO, s + W
            a, e = max(lo, 0), min(hi, T)
            xsl[k, a - lo:e - lo] = xa[a:e]
            qp[0, k * W:(k + 1) * W] = np.arange(s, s + W, dtype=f32)
        m = {
            "xall": xa, "xslot": xsl, "qpos": qp,
            "w_in": np.asarray(w_in, f32), "w_out": np.asarray(w_out, f32), "w_up": np.asarray(w_up, f32),
            "w_down": np.asarray(w_down, f32),
            "norm_mix_g": np.asarray(norm_mix_g, f32).reshape(D, 1),
            "norm_ffn_g": np.asarray(norm_ffn_g, f32).reshape(D, 1),
            "norm_final_g": np.asarray(norm_final_g, f32).reshape(1, D),
            "sb_bias": np.asarray(sb_bias, f32).reshape(1, 8),
            "pool_w": np.asarray(pool_w, f32), "pool_scale": np.asarray(pool_scale, f32),
            "conv_w": np.asarray(conv_w, f32), "conv_b": np.asarray(conv_b, f32).reshape(1, 2 * DFF),
        }
        if _with_sample:
            sl = slice(N_SEQ * c, N_SEQ * (c + 1))
            m.update({
                "x_sample": np.asarray(x_sample[sl], f32).reshape(N_SEQ, D),
                "cache_k": (_REMAP[0](c, ck2) if _REMAP[0] else ck2), "cache_v": (_REMAP[0](c, cv2) if _REMAP[0] else cv2),
                "state_pool": np.asarray(state_pool[sl], f32).reshape(N_SEQ * 15, 512),
                "state_conv": np.asarray(state_conv[sl], f32).reshape(N_SEQ * 2, 2 * DFF),
                "page_table": (_REMAP[1](c) if _REMAP[1] else np.asarray(page_table[sl], np.int32).reshape(1, N_SEQ * NPG)),
            })
        in_maps.append(m)
    if _RUNNER[0] is not None:
        res = _RUNNER[0](nc, in_maps)
    else:
        res = run_bass_kernel_spmd(nc, in_maps, core_ids=list(range(8))).results

    y_prompt = np.zeros((B, 4096, D), f32)
    k_prompt = np.zeros((B, T, 8, 64), f32)
    v_prompt = np.zeros((B, T, 8, 64), f32)
    pool_prompt = np.zeros((B, 15, 512), f32)
    conv_prompt = np.zeros((B, 2, 2 * DFF), f32)
    for c in range(8):
        b, g = c // 2, c % 2
        r = res[c]
        for k in range(NS):
            r0 = OWN * TILES[g][k]
            nv = min(OWN, 4096 - r0)
            y_prompt[b, r0:r0 + nv] = r["y_slot"][k, 2:2 + nv]
        if g == 0:
            k_prompt[b] = r["k_all"][:T].reshape(T, 8, 64)
            v_prompt[b] = r["v_all"][:T].reshape(T, 8, 64)
        else:
            pool_prompt[b] = r["pool_last"]
            conv_prompt[b] = r["conv_last"]
    if not _with_sample:
        return (y_prompt, None, k_prompt, v_prompt, pool_prompt, conv_prompt, None, None, None, None)
    DB = x_sample.shape[0]
    y_sample = np.zeros((DB, 1, D), f32)
    k_sample = np.zeros((DB, 1, 8, 64), f32)
    v_sample = np.zeros((DB, 1, 8, 64), f32)
    pool_sample = np.zeros((DB, 15, 512), f32)
    conv_sample = np.zeros((DB, 2, 2 * DFF), f32)
    for c in range(8):
        r = res[c]
        sl = slice(N_SEQ * c, N_SEQ * (c + 1))
        y_sample[sl, 0] = r["y_sample"]
        k_sample[sl, 0] = r["k_sample"].reshape(N_SEQ, 8, 64)
        v_sample[sl, 0] = r["v_sample"].reshape(N_SEQ, 8, 64)
        pool_sample[sl] = r["pool_sample"]
        conv_sample[sl] = r["conv_sample"]
    return (y_prompt, y_sample, k_prompt, v_prompt, pool_prompt, conv_prompt,
            k_sample, v_sample, pool_sample, conv_sample)
```

```python
import numpy as np
from contextlib import ExitStack
import concourse.bass as bass
import concourse.mybir as mybir
from concourse.bass_utils import run_bass_kernel_spmd

F32 = mybir.dt.float32
BF16 = mybir.dt.bfloat16
I32 = mybir.dt.int32
AF = mybir.ActivationFunctionType
ALU = mybir.AluOpType
AX = mybir.AxisListType

D = 1024
T = 4112
NBLK = 33
TP = NBLK * 128
N_META = 16
NS = 5
W = 412
HALO = 16
WR = W + HALO
OWN = 410
DFF = 2816
NFC = DFF // 128
EPS = 1e-6
NEG = -30000.0
QT = [(0, 128), (128, 128), (256, 128), (384, 28)]
N_SEQ = 4
NPG = 64
SAME_ENGINE_SYNC = True


TILES = ((0, 3, 4, 7, 8), (1, 2, 5, 6, 9))
LAST_S = 14 + OWN * 9
PP0 = 4097 - LAST_S + HALO
CP0 = 4110 - LAST_S


def slot_start(g, k):
    return 14 + OWN * TILES[g][k]


def n_kb_for_slot(k):
    last = 14 + OWN * (2 * k + 1) + W - 1
    return min(NBLK, last // 128 + 1)


def kb_needs_mask(k, kb):
    return not (128 * kb + 127 < 14 + OWN * 2 * k)


class SemObj:
    def __init__(self, sem):
        self.sem = sem
        self.cnt = 0


class Eng(SemObj):
    def __init__(self, sem, h, name):
        super().__init__(sem)
        self.h = h
        self.name = name
        self.waited = {}


class Res:
    __slots__ = ("w", "r", "excl")

    def __init__(self, excl=False):
        self.w = None
        self.r = {}
        self.excl = excl


class Prog:
    def __init__(self, nc, es):
        self.nc = nc
        self.es = es
        mk = lambda n: es.enter_context(nc.semaphore(n))
        self.pe = Eng(mk("s_pe"), nc.tensor, "pe")
        self.act = Eng(mk("s_act"), nc.scalar, "act")
        self.dve = Eng(mk("s_dve"), nc.vector, "dve")
        self.pool = Eng(mk("s_pool"), nc.gpsimd, "pool")
        self.sp = Eng(mk("s_sp"), nc.sync, "sp")
        self.dsems = {}
        for q in ("sp", "pool", "act"):
            self.dsems[q] = [SemObj(mk(f"d_{q}{i}")) for i in range(12)]
        self.dnext = {"sp": 0, "pool": 0, "act": 0}

    def _wait(self, eng, toks):
        for tk in toks:
            so, v = tk[0], tk[1]
            if so is eng and (not SAME_ENGINE_SYNC or eng is self.pe or len(tk) == 3):
                continue
            if eng.waited.get(so, 0) < v:
                eng.h.wait_ge(so.sem, v)
                eng.waited[so] = v

    @staticmethod
    def _deps(reads, writes):
        toks = []
        for r in reads:
            if r.w is not None:
                toks.append(r.w)
            if r.excl:
                toks.extend(r.r.items())
        for w in writes:
            if w.w is not None:
                toks.append(w.w)
            toks.extend((so, v, "war") for so, v in w.r.items())
        return toks

    @staticmethod
    def _record(tok, reads, writes):
        so, v = tok
        for r in reads:
            if r.r.get(so, 0) < v:
                r.r[so] = v
        for w in writes:
            w.w = tok
            w.r = {}

    def op(self, eng, fn, reads=(), writes=(), inc=True):
        self._wait(eng, self._deps(reads, writes))
        ins = fn()
        if inc:
            ins.then_inc(eng.sem, 1)
            eng.cnt += 1
            tok = (eng, eng.cnt)
        else:
            tok = (eng, eng.cnt + 1)
        self._record(tok, reads, writes)
        return ins

    def dma(self, q, out, in_, reads=(), writes=(), fn=None, **kw):
        eng = {"sp": self.sp, "pool": self.pool, "act": self.act}[q]
        lst = self.dsems[q]
        d = lst[self.dnext[q] % len(lst)]
        self.dnext[q] += 1
        toks = self._deps(reads, writes)
        if d.cnt > 0:
            toks.append((d, d.cnt))
        self._wait(eng, toks)
        if fn is not None:
            fn().then_inc(d.sem, 16)
        else:
            eng.h.dma_start(out=out, in_=in_, **kw).then_inc(d.sem, 16)
        d.cnt += 16
        self._record((d, d.cnt), reads, writes)

    def final_wait(self):
        toks = []
        for q in self.dsems:
            for d in self.dsems[q]:
                if d.cnt:
                    toks.append((d, d.cnt))
        for e in (self.pe, self.act, self.dve, self.pool):
            if e.cnt:
                toks.append((e, e.cnt))
        self._wait(self.sp, toks)


class Tl:
    def __init__(self, t, excl=False):
        self.t = t
        self.res = Res(excl)

    def __getitem__(self, k):
        return self.t[k]


WX = W + 4
NPOOL = [2560]
SAMPLE_STEPS_PER_BLOCK = 9
import os
DBG = int(os.environ.get('KDBG', '99'))


def build(with_sample=True, stop_after=None):
    nc = bass.Bass("TRN2", target_bir_lowering=False)
    dr = lambda name, shape, dt=F32, kind="ExternalInput": nc.dram_tensor(name, shape, dt, kind=kind).ap()
    xall = dr("xall", [TP, D])
    xslot = dr("xslot", [NS, WR, D])
    qpos_d = dr("qpos", [1, NS * W])
    w_in_d = dr("w_in", [D, 2048])
    w_out_d = dr("w_out", [D, D])
    w_up_d = dr("w_up", [D, 2 * DFF])
    w_down_d = dr("w_down", [DFF, D])
    g_mix_d = dr("norm_mix_g", [D, 1])
    g_ffn_d = dr("norm_ffn_g", [D, 1])
    g_fin_d = dr("norm_final_g", [1, D])
    sbb_d = dr("sb_bias", [1, 8])
    pool_w_d = dr("pool_w", [4, 128, 128])
    pool_s_d = dr("pool_scale", [4, 128])
    conv_w_d = dr("conv_w", [3, 2 * DFF])
    conv_b_d = dr("conv_b", [1, 2 * DFF])
    y_o = dr("y_slot", [NS, W, D], kind="ExternalOutput")
    k_o = dr("k_all", [TP, 512], kind="ExternalOutput")
    v_o = dr("v_all", [TP, 512], kind="ExternalOutput")
    pp_o = dr("pool_last", [15, 512], kind="ExternalOutput")
    cp_o = dr("conv_last", [2, 2 * DFF], kind="ExternalOutput")
    o_scr = nc.dram_tensor("o_scr", [NS, 128, 12, WX], BF16).ap()
    wup_bf = nc.dram_tensor("wup_bf", [NFC, 2, 128, 8, 128], BF16).ap()
    wdn_bf = nc.dram_tensor("wdn_bf", [NFC, 128, D], BF16).ap()
    if with_sample:
        xsam = dr("x_sample", [N_SEQ, D])
        ck_d = dr("cache_k", [NPOOL[0] * 128, 512])
        cv_d = dr("cache_v", [NPOOL[0] * 128, 512])
        stp_d = dr("state_pool", [N_SEQ * 15, 512])
        stc_d = dr("state_conv", [N_SEQ * 2, 2 * DFF])
        pt_d = dr("page_table", [1, N_SEQ * NPG], I32)
        ys_o = dr("y_sample", [N_SEQ, D], kind="ExternalOutput")
        ks_o = dr("k_sample", [N_SEQ, 512], kind="ExternalOutput")
        vs_o = dr("v_sample", [N_SEQ, 512], kind="ExternalOutput")
        ps_o = dr("pool_sample", [N_SEQ, 15, 512], kind="ExternalOutput")
        cs_o = dr("conv_sample", [N_SEQ, 2, 2 * DFF], kind="ExternalOutput")
        q_scr = nc.dram_tensor("q_scr", [N_SEQ, 512], F32).ap()

    es = ExitStack()
    with es:
        P = Prog(nc, es)
        pe, act, dve, pool = P.pe, P.act, P.dve, P.pool

        def sb(name, shape, dt=F32, st=es):
            return Tl(st.enter_context(nc.sbuf_tensor(name, shape, dt)))

        def ps(name, shape, dt=F32):
            return Tl(es.enter_context(nc.psum_tensor(name, shape, dt)), excl=True)

        pT = ps("pT", [128, 1024], BF16)
        pG = [ps(f"pG{i}", [128, 512], F32) for i in range(5)]
        pO = [ps(f"pO{i}", [128, 512], F32) for i in range(2)]
        gi = [0]

        def next_pg():
            t = pG[gi[0] % len(pG)]
            gi[0] += 1
            return t

        ident = sb("ident", [128, 128], BF16)
        identf = sb("identf", [128, 128], F32)
        tri = sb("tri", [128, 128], BF16)
        negones = sb("negones", [128, 128], BF16)
        onesf = sb("onesf", [128, 128], F32)
        negf = sb("negf", [128, 128], F32)
        kpos = sb("kpos", [128, NBLK], F32)
        mhalf = sb("mhalf", [128, 1], F32)
        P.op(pool, lambda: nc.gpsimd.memset(onesf[:], 1.0), writes=[onesf.res])
        P.op(pool, lambda: nc.gpsimd.memset(negf[:], -1.0), writes=[negf.res])
        P.op(pool, lambda: nc.gpsimd.memset(mhalf[:], -0.5), writes=[mhalf.res])
        P.op(pool, lambda: nc.gpsimd.memset(negones[:], -1.0), writes=[negones.res])
        P.op(pool, lambda: nc.gpsimd.affine_select(out=identf[:], in_=onesf[:], pattern=[[-1, 128]],
                                                    compare_op=ALU.is_equal, fill=0.0, base=0, channel_multiplier=1),
             reads=[onesf.res], writes=[identf.res])
        P.op(pool, lambda: nc.gpsimd.tensor_copy(out=ident[:], in_=identf[:]), reads=[identf.res], writes=[ident.res])
        P.op(pool, lambda: nc.gpsimd.affine_select(out=tri[:], in_=negf[:], pattern=[[-1, 128]],
                                                    compare_op=ALU.is_ge, fill=0.0, base=0, channel_multiplier=1),
             reads=[negf.res], writes=[tri.res])
        P.op(pool, lambda: nc.gpsimd.iota(kpos[:], pattern=[[128, NBLK]], base=0, channel_multiplier=1,
                                          allow_small_or_imprecise_dtypes=True), writes=[kpos.res])

        bias_t = sb("bias_t", [128, 8])
        P.dma("sp", bias_t[:], sbb_d.partition_broadcast(128), writes=[bias_t.res])
        gfin = sb("gfin", [128, D])
        P.dma("sp", gfin[:], g_fin_d.partition_broadcast(128), writes=[gfin.res])
        gmix = sb("gmix", [128, 8])
        P.dma("sp", gmix[:], g_mix_d.rearrange("(c p) o -> p (c o)", p=128), writes=[gmix.res],
              allow_slow_non_contiguous=True)
        gffn = sb("gffn", [128, 8])
        P.dma("sp", gffn[:], g_ffn_d.rearrange("(c p) o -> p (c o)", p=128), writes=[gffn.res],
              allow_slow_non_contiguous=True)
        gmix_bc = sb("gmix_bc", [128, 8, 128])
        gffn_bc = sb("gffn_bc", [128, 8, 128])
        for dc in range(8):
            P.op(pool, lambda dc=dc: nc.gpsimd.tensor_scalar(out=gmix_bc[:, dc, :], in0=onesf[:], scalar1=gmix[:, dc:dc + 1],
                                                             scalar2=None, op0=ALU.mult),
                 reads=[onesf.res, gmix.res], writes=[gmix_bc.res])
            P.op(pool, lambda dc=dc: nc.gpsimd.tensor_scalar(out=gffn_bc[:, dc, :], in0=onesf[:], scalar1=gffn[:, dc:dc + 1],
                                                             scalar2=None, op0=ALU.mult),
                 reads=[onesf.res, gffn.res], writes=[gffn_bc.res])
        qpos_bc = sb("qpos_bc", [128, NS * W])
        P.dma("sp", qpos_bc[:], qpos_d.partition_broadcast(128), writes=[qpos_bc.res])
        pscale = sb("pscale", [128, 4])
        P.dma("sp", pscale[:], pool_s_d.rearrange("g c -> c g"), writes=[pscale.res], allow_slow_non_contiguous=True)
        cw = sb("cw", [128, 3, 44])
        cb = sb("cb", [128, 44])
        for q4 in range(4):
            cs_ = slice(q4 * 1408, (q4 + 1) * 1408)
            for i3 in range(3):
                P.dma("sp", cw[:, i3, q4 * 11:(q4 + 1) * 11], conv_w_d[i3:i3 + 1, cs_].rearrange("o (c p) -> p (o c)", p=128),
                      writes=[cw.res], allow_slow_non_contiguous=True)
            P.dma("sp", cb[:, q4 * 11:(q4 + 1) * 11], conv_b_d[:, cs_].rearrange("o (c p) -> p (o c)", p=128),
                  writes=[cb.res], allow_slow_non_contiguous=True)
        pw_bf = sb("pw_bf", [128, 4, 128], BF16)
        P.dma("pool", pw_bf[:], pool_w_d.rearrange("g c e -> c g e"), writes=[pw_bf.res])

        junk = [sb(f"junk{i}", [128, D], BF16) for i in range(2)]
        ssq = [sb(f"ssq{i}", [128, 1]) for i in range(4)]
        xn = [sb(f"xn{i}", [128, D], BF16) for i in range(2)]
        cnt = {"x": 0, "ss": 0, "xn": 0, "hT": 0, "j": 0}

        def rms_rows(x_ap_fn, x_res, rows, out_bf_tl):
            jk = junk[cnt["j"] % 2]; cnt["j"] += 1
            ss = ssq[cnt["ss"] % 4]; cnt["ss"] += 1
            P.op(act, lambda: nc.scalar.activation(out=jk[0:rows, :], in_=x_ap_fn(), func=AF.Square,
                                                   accum_out=ss[0:rows, :]),
                 reads=[x_res], writes=[jk.res, ss.res])
            P.op(dve, lambda: nc.vector.tensor_scalar(out=ss[0:rows, :], in0=ss[0:rows, :], scalar1=1.0 / D,
                                                      scalar2=EPS, op0=ALU.mult, op1=ALU.add),
                 reads=[ss.res], writes=[ss.res])
            P.op(pool, lambda: nc.gpsimd.tensor_tensor(out=ss[0:rows, :], in0=ss[0:rows, :], in1=mhalf[0:rows, :],
                                                       op=ALU.pow),
                 reads=[ss.res, mhalf.res], writes=[ss.res])
            if out_bf_tl is not None:
                P.op(dve, lambda: nc.vector.tensor_scalar(out=out_bf_tl[0:rows, :], in0=x_ap_fn(),
                                                          scalar1=ss[0:rows, 0:1], scalar2=None, op0=ALU.mult),
                     reads=[x_res, ss.res], writes=[out_bf_tl.res])
            return ss

        def transpose_rows(xn_tl, rows, dst_ap_fn, dst_res, g_bc):
            for dc in range(8):
                P.op(pe, lambda dc=dc: nc.tensor.transpose(out=pT[:, dc * 128:dc * 128 + rows],
                                                           in_=xn_tl[0:rows, dc * 128:(dc + 1) * 128],
                                                           identity=ident[0:rows, 0:rows]),
                     reads=[xn_tl.res, ident.res], writes=[pT.res], inc=(dc == 7))
            P.op(dve, lambda: nc.vector.tensor_tensor(
                out=dst_ap_fn(), in0=pT[:].rearrange("p (c t) -> p c t", c=8)[:, :, 0:rows],
                in1=g_bc[:, :, 0:rows], op=ALU.mult),
                reads=[pT.res, g_bc.res], writes=[dst_res])

        def barrier():
            toks = [(e, e.cnt) for e in (pe, act, dve, pool) if e.cnt]
            for q in P.dsems:
                toks += [(d, d.cnt) for d in P.dsems[q] if d.cnt]
            for e in (pe, act, dve, pool, P.sp):
                P._wait(e, toks)

        hT_sam = sb("hT_sam", [128, 8, N_SEQ], BF16) if with_sample else None
        if stop_after == "consts":
            P.final_wait()
            return nc

        esAB = ExitStack()
        KT = sb("KT", [128, 4, TP], BF16, esAB)
        Vb = sb("Vb", [128, NBLK, 512], BF16, esAB)
        kt_res = [Res() for _ in range(NBLK)]
        v_res = [Res() for _ in range(NBLK)]
        w_qu = sb("w_qu", [128, 8, 1024], BF16, esAB)
        sam_state = {}
        if with_sample:
            sam_state.update(
                idxi=sb("idxi", [128, N_SEQ * NPG], I32, esAB), bdiag=sb("bdiag", [8, 512], BF16, esAB),
                ones8=sb("ones8", [8, 1], BF16, esAB), A8all=sb("A8all", [128, N_SEQ, 512], BF16, esAB))
        with ExitStack() as esA:
            w_kv = sb("w_kv", [128, 8, 1024], BF16, esA)
            P.dma("pool", w_kv[:], w_in_d[:, 512:1536].rearrange("(c p) n -> p c n", p=128), writes=[w_kv.res])
            P.dma("pool", w_qu[:, :, 0:512], w_in_d[:, 0:512].rearrange("(c p) n -> p c n", p=128), writes=[w_qu.res])
            P.dma("pool", w_qu[:, :, 512:1024], w_in_d[:, 1536:2048].rearrange("(c p) n -> p c n", p=128), writes=[w_qu.res])
            xt = [sb(f"xt{i}", [128, D], F32, esA) for i in range(2)]
            hT = [sb(f"hT{i}", [128, 8, 128], BF16, esA) for i in range(3)]
            ko = [sb(f"ko{i}", [128, 512], F32, esA) for i in range(2)]
            vo = [sb(f"vo{i}", [128, 512], F32, esA) for i in range(2)]
            nblocks = NBLK + (1 if with_sample else 0)
            if stop_after and stop_after.startswith('A') and len(stop_after) > 1:
                nblocks = int(stop_after[1:])
            order = list(range(nblocks))
            if with_sample and nblocks == NBLK + 1:
                order = [NBLK] + list(range(NBLK))
            esS = ExitStack()
            sgen = None
            for tb in order:
                if sgen is not None:
                    for _ in range(SAMPLE_STEPS_PER_BLOCK):
                        if next(sgen, "done") == "done":
                            break
                sam = tb == NBLK
                rows = N_SEQ if sam else 128
                x_tl = xt[tb % 2]
                if sam:
                    P.dma("sp", x_tl[0:rows, :], xsam[:, :], writes=[x_tl.res])
                else:
                    P.dma("sp", x_tl[:], xall[tb * 128:(tb + 1) * 128, :], writes=[x_tl.res])
                if DBG < 2:
                    continue
                xn_tl = xn[cnt["xn"] % 2]; cnt["xn"] += 1
                rms_rows(lambda x_tl=x_tl, rows=rows: x_tl[0:rows, :], x_tl.res, rows, xn_tl)
                if DBG < 3:
                    continue
                if sam:
                    h_ap = lambda: hT_sam[:]
                    h_res = hT_sam.res
                    h_rd = lambda dc: hT_sam[:, dc, :]
                else:
                    h_tl = hT[tb % 3]
                    h_ap = lambda h_tl=h_tl: h_tl[:]
                    h_res = h_tl.res
                    h_rd = lambda dc, h_tl=h_tl: h_tl[:, dc, :]
                transpose_rows(xn_tl, rows, h_ap, h_res, gmix_bc)
                if DBG < 4:
                    continue
                if not sam:
                    pk = next_pg()
                    for j in range(4):
                        for dc in range(8):
                            P.op(pe, lambda j=j, dc=dc, pk=pk: nc.tensor.matmul(
                                pk[:, j * 128:(j + 1) * 128], lhsT=w_kv[:, dc, j * 128:(j + 1) * 128],
                                rhs=h_rd(dc), start=(dc == 0), stop=(dc == 7)),
                                reads=[w_kv.res, h_res], writes=[pk.res], inc=(j == 3 and dc == 7))
                    P.op(act, lambda pk=pk, tb=tb: nc.scalar.copy(
                        out=KT[:, :, tb * 128:(tb + 1) * 128], in_=pk[:].rearrange("p (j t) -> p j t", j=4)),
                        reads=[pk.res], writes=[kt_res[tb]])
                if DBG < 5:
                    continue
                pk2 = next_pg()
                for dc in range(8):
                    P.op(pe, lambda dc=dc, pk2=pk2: nc.tensor.matmul(
                        pk2[0:rows, :], lhsT=h_rd(dc), rhs=w_kv[:, dc, 0:512], start=(dc == 0), stop=(dc == 7)),
                        reads=[w_kv.res, h_res], writes=[pk2.res], inc=(dc == 7))
                ko_tl = ko[tb % 2]
                P.op(act, lambda pk2=pk2, ko_tl=ko_tl: nc.scalar.copy(out=ko_tl[0:rows, :], in_=pk2[0:rows, :]),
                     reads=[pk2.res], writes=[ko_tl.res])
                P.dma("act", (ks_o[:, :] if sam else k_o[tb * 128:(tb + 1) * 128, :]), ko_tl[0:rows, :], reads=[ko_tl.res])
                if DBG < 6:
                    continue
                pv = next_pg()
                for dc in range(8):
                    P.op(pe, lambda dc=dc, pv=pv: nc.tensor.matmul(
                        pv[0:rows, :], lhsT=h_rd(dc), rhs=w_kv[:, dc, 512:1024], start=(dc == 0), stop=(dc == 7)),
                        reads=[w_kv.res, h_res], writes=[pv.res], inc=(dc == 7))
                vo_tl = vo[tb % 2]
                P.op(act, lambda pv=pv, vo_tl=vo_tl: nc.scalar.copy(out=vo_tl[0:rows, :], in_=pv[0:rows, :]),
                     reads=[pv.res], writes=[vo_tl.res])
                if not sam:
                    P.op(dve, lambda vo_tl=vo_tl, tb=tb: nc.vector.tensor_copy(out=Vb[:, tb, :], in_=vo_tl[:]),
                         reads=[vo_tl.res], writes=[v_res[tb]])
                P.dma("act", (vs_o[:, :] if sam else v_o[tb * 128:(tb + 1) * 128, :]), vo_tl[0:rows, :], reads=[vo_tl.res])
                if sam:
                    L_ = dict(locals())
                    L_["esB"] = esS
                    L_["sam_state"] = sam_state
                    sgen = sample_mixer(nc, P, L_)
            if sgen is not None:
                for _ in sgen:
                    pass
            barrier()
            esS.close()
        if stop_after and stop_after.startswith("A"):
            P.final_wait()
            esAB.close()
            return nc

        with ExitStack() as esB:
            o_at = sb("o_at", [64, 8, WX], BF16, esB)
            o_pl = sb("o_pl", [128, 4, WX], BF16, esB)
            P.op(pool, lambda: nc.gpsimd.memset(o_at[:], 0.0), writes=[o_at.res])
            P.op(pool, lambda: nc.gpsimd.memset(o_pl[:], 0.0), writes=[o_pl.res])
            xs1 = [sb(f"xs1_{i}", [128, D], F32, esB) for i in range(2)]
            hTs = sb("hTs", [128, 8, WR], BF16, esB)
            qT = sb("qT", [128, 4, W], BF16, esB)
            uT = sb("uT", [128, 4, WR], F32, esB)
            lv = [sb(f"lv{i}", [128, WR], F32, esB) for i in range(2)]
            icnt = sb("icnt", [128, W], F32, esB)
            dpool = [sb(f"dpool{i}", [128, W], BF16, esB) for i in range(2)]
            NMK = 9
            masks = sb("masks", [128, NMK, W], BF16, esB)
            mask_res = [Res() for _ in range(NMK)]
            e_t = [sb(f"e_t{i}", [128, W], F32, esB) for i in range(3)]
            sp_t = [sb(f"sp_t{i}", [128, W], BF16, esB) for i in range(3)]
            a_t = [sb(f"a_t{i}", [128, W], BF16, esB) for i in range(3)]
            w_t = [sb(f"w_t{i}", [128, W], F32, esB) for i in range(3)]
            acc32 = sb("acc32", [128, W], F32, esB)
            accbf = [sb(f"accbf{i}", [128, W], BF16, esB) for i in range(3)]


            for i in range(NFC):
                for a in range(2):
                    c0_ = a * DFF + i * 128
                    P.dma("pool", wup_bf[i, a], w_up_d[:, c0_:c0_ + 128].rearrange("(c p) f -> p c f", p=128))
                P.dma("pool", wdn_bf[i], w_down_d[i * 128:(i + 1) * 128, :])
            vgen = None
            if with_sample:
                L_ = dict(locals())
                vgen = sample_vpass(nc, P, L_)
            for k in range(NS):
                for (r0, n) in [(0, HALO)] + [(HALO + c0, n) for (c0, n) in QT]:
                    xt_ = xs1[cnt["x"] % 2]; cnt["x"] += 1
                    P.dma("sp", xt_[0:n, :], xslot[k, r0:r0 + n, :], writes=[xt_.res])
                    xn_tl = xn[cnt["xn"] % 2]; cnt["xn"] += 1
                    rms_rows(lambda xt_=xt_, n=n: xt_[0:n, :], xt_.res, n, xn_tl)
                    transpose_rows(xn_tl, n, lambda r0=r0, n=n: hTs[:, :, r0:r0 + n], hTs.res, gmix_bc)
                for j in range(4):
                    pq = next_pg()
                    for dc in range(8):
                        P.op(pe, lambda j=j, dc=dc, pq=pq: nc.tensor.matmul(
                            pq[:, 0:W], lhsT=w_qu[:, dc, j * 128:(j + 1) * 128], rhs=hTs[:, dc, HALO:WR],
                            start=(dc == 0), stop=(dc == 7)),
                            reads=[w_qu.res, hTs.res], writes=[pq.res], inc=(dc == 7))
                    P.op(dve, lambda j=j, pq=pq: nc.vector.tensor_scalar(
                        out=qT[:, j, :], in0=pq[:, 0:W], scalar1=0.125, scalar2=None, op0=ALU.mult),
                        reads=[pq.res], writes=[qT.res])
                for g in range(4):
                    pu = next_pg()
                    for dc in range(8):
                        P.op(pe, lambda g=g, dc=dc, pu=pu: nc.tensor.matmul(
                            pu[:, 0:WR], lhsT=w_qu[:, dc, 512 + g * 128:512 + (g + 1) * 128], rhs=hTs[:, dc, :],
                            start=(dc == 0), stop=(dc == 7)),
                            reads=[w_qu.res, hTs.res], writes=[pu.res], inc=(dc == 7))
                    P.op(act, lambda g=g, pu=pu: nc.scalar.copy(out=uT[:, g, :], in_=pu[:, 0:WR]),
                         reads=[pu.res], writes=[uT.res])
                if k == NS - 1:
                    for g in range(4):
                        P.dma("sp", pp_o[:, g * 128:(g + 1) * 128].rearrange("r c -> c r"), uT[:, g, PP0:PP0 + 15],
                              reads=[uT.res], allow_slow_non_contiguous=True)
                for g in range(4):
                    wg = 2 << g
                    cur, cur_res = (lambda g=g: uT[:, g, :]), uT.res
                    lo = 0
                    for lvl in range(g + 1):
                        sh = 1 << lvl
                        dst = lv[lvl % 2]
                        P.op(dve, lambda cur=cur, dst=dst, sh=sh, lo=lo: nc.vector.tensor_tensor(
                            out=dst[:, lo + sh:WR], in0=cur()[:, lo + sh:WR], in1=cur()[:, lo:WR - sh], op=ALU.add),
                            reads=[cur_res], writes=[dst.res])
                        cur, cur_res = (lambda dst=dst: dst[:]), dst.res
                        lo += sh
                    P.op(dve, lambda wg=wg, k=k: nc.vector.tensor_scalar(
                        out=icnt[:], in0=qpos_bc[:, k * W:(k + 1) * W], scalar1=1.0, scalar2=float(wg),
                        op0=ALU.add, op1=ALU.min), reads=[qpos_bc.res], writes=[icnt.res])
                    P.op(dve, lambda: nc.vector.reciprocal(out=icnt[:], in_=icnt[:]), reads=[icnt.res], writes=[icnt.res])
                    P.op(dve, lambda cur=cur: nc.vector.tensor_tensor(
                        out=icnt[:], in0=cur()[:, HALO:WR], in1=icnt[:], op=ALU.mult),
                        reads=[cur_res, icnt.res], writes=[icnt.res])
                    dp = dpool[g % 2]
                    P.op(dve, lambda g=g, dp=dp: nc.vector.tensor_tensor(
                        out=dp[:], in0=icnt[:], in1=uT[:, g, HALO:WR], op=ALU.subtract),
                        reads=[icnt.res, uT.res], writes=[dp.res])
                    pp = next_pg()
                    P.op(pe, lambda g=g, dp=dp, pp=pp: nc.tensor.matmul(
                        pp[:, 0:W], lhsT=pw_bf[:, g, :], rhs=dp[:], start=True, stop=True),
                        reads=[pw_bf.res, dp.res], writes=[pp.res])
                    P.op(dve, lambda g=g, pp=pp: nc.vector.tensor_scalar(
                        out=o_pl[:, g, 0:W], in0=pp[:, 0:W], scalar1=pscale[:, g:g + 1], scalar2=None, op0=ALU.mult),
                        reads=[pp.res, pscale.res], writes=[o_pl.res])
                nkb = n_kb_for_slot(k)
                mkb = [kb for kb in range(nkb) if kb_needs_mask(k, kb)]
                assert len(mkb) <= NMK, len(mkb)
                midx = {kb: i for i, kb in enumerate(mkb)}
                for kb in mkb:
                    i = midx[kb]
                    P.op(dve, lambda kb=kb, i=i, k=k: nc.vector.tensor_scalar(
                        out=masks[:, i, :], in0=qpos_bc[:, k * W:(k + 1) * W], scalar1=kpos[:, kb:kb + 1],
                        scalar2=NEG, op0=ALU.is_le, op1=ALU.mult),
                        reads=[qpos_bc.res, kpos.res], writes=[mask_res[i]])
                ucount = [0]
                for h in range(8):
                    attn_head(nc, P, h, nkb, midx, ucount, locals())
                P.dma("pool", o_scr[k, 0:64, 0:8, 0:W], o_at[:, :, 0:W], reads=[o_at.res])
                P.dma("pool", o_scr[k, :, 8:12, 0:W], o_pl[:, :, 0:W], reads=[o_pl.res])
            if vgen is not None:
                for _ in vgen:
                    pass
            barrier()
        esAB.close()
        if stop_after == "B1":
            P.final_wait()
            return nc

        with ExitStack() as esC:
            w_oa = sb("w_oa", [64, 8, D], BF16, esC)
            P.dma("pool", w_oa[:], w_out_d[0:512, :].rearrange("(h p) n -> p h n", p=64), writes=[w_oa.res])
            w_op = sb("w_op", [128, 4, D], BF16, esC)
            P.dma("pool", w_op[:], w_out_d[512:1024, :].rearrange("(g p) n -> p g n", p=128), writes=[w_op.res])
            wdn = sb("wdn", [128, NFC, D], BF16, esC)
            P.dma("sp", wdn[:], wdn_bf.rearrange("i p n -> p i n"), writes=[wdn.res])
            xs = sb("xs2", [128, 5, D], F32, esC)
            o_at = sb("o_at2", [64, 8, WX], BF16, esC)
            o_pl = sb("o_pl2", [128, 4, WX], BF16, esC)
            h2T = sb("h2T", [128, 8, WX], BF16, esC)
            cv_t = [sb(f"cv_t{i}", [128, WX], F32, esC) for i in range(6)]
            sg_t = [sb(f"sg_t{i}", [128, WX], F32, esC) for i in range(3)]
            actT = sb("actT", [128, NFC, WX], BF16, esC)
            upl = sb("upl", [128, 2, 44], F32, esC)
            wup_t = [sb(f"wup{i}", [128, 2, 8, 128], BF16, esC) for i in range(4)]
            P.op(dve, lambda: nc.vector.memset(actT[:], 0.0), writes=[actT.res])
            if with_sample:
                upl_s = sb("upl_s", [128, N_SEQ, 44], F32, esC)
                sc8 = [sb(f"sc8_{i}", [8, 128], F32, esC) for i in range(2)]
                pF = pO[1]

            for k in range(NS):
                wk = WX if (k == 0 and with_sample) else W
                tiles = list(QT) + ([(W, N_SEQ)] if (k == 0 and with_sample) else [])
                P.dma("sp", o_at[:], o_scr[k, 0:64, 0:8, :], writes=[o_at.res])
                P.dma("sp", o_pl[:], o_scr[k, :, 8:12, :], writes=[o_pl.res])
                for ti, (c0, n) in enumerate(tiles):
                    if ti < 4:
                        P.dma("sp", xs[0:n, ti, :], xslot[k, HALO + c0:HALO + c0 + n, :], writes=[xs.res])
                    else:
                        P.dma("sp", xs[0:n, ti, :], xsam[:, :], writes=[xs.res])
                for ti, (c0, n) in enumerate(tiles):
                    for half in range(2):
                        pw = next_pg()
                        for h in range(8):
                            P.op(pe, lambda h=h, pw=pw: nc.tensor.matmul(
                                pw[0:n, :], lhsT=o_at[:, h, c0:c0 + n], rhs=w_oa[:, h, half * 512:(half + 1) * 512],
                                start=(h == 0), stop=False),
                                reads=[o_at.res, w_oa.res], writes=[pw.res], inc=False)
                        for g in range(4):
                            P.op(pe, lambda g=g, pw=pw: nc.tensor.matmul(
                                pw[0:n, :], lhsT=o_pl[:, g, c0:c0 + n], rhs=w_op[:, g, half * 512:(half + 1) * 512],
                                start=False, stop=(g == 3)),
                                reads=[o_pl.res, w_op.res], writes=[pw.res], inc=(g == 3))
                        P.op(dve, lambda pw=pw, half=half, ti=ti, n=n: nc.vector.tensor_tensor(
                            out=xs[0:n, ti, half * 512:(half + 1) * 512], in0=pw[0:n, :],
                            in1=xs[0:n, ti, half * 512:(half + 1) * 512], op=ALU.add),
                            reads=[pw.res, xs.res], writes=[xs.res])
                    xn_tl = xn[cnt["xn"] % 2]; cnt["xn"] += 1
                    rms_rows(lambda ti=ti, n=n: xs[0:n, ti, :], xs.res, n, xn_tl)
                    transpose_rows(xn_tl, n, lambda c0=c0, n=n: h2T[:, :, c0:c0 + n], h2T.res, gffn_bc)
                pend = None

                def emit_mult(sg, cv, i, wk):
                    P.op(dve, lambda: nc.vector.tensor_tensor(
                        out=actT[:, i, 2:wk], in0=sg[:, 2:wk], in1=cv[:, 2:wk], op=ALU.mult),
                        reads=[sg.res, cv.res], writes=[actT.res])

                for i in range(NFC):
                    wt = wup_t[i % 4]
                    for a in range(2):
                        P.dma("sp", wt[:, a], wup_bf[i, a], writes=[wt.res])
                    pus = []
                    for a in range(2):
                        pu = next_pg()
                        for dc in range(8):
                            P.op(pe, lambda a=a, dc=dc, pu=pu, wt=wt: nc.tensor.matmul(
                                pu[:, 0:wk], lhsT=wt[:, a, dc, :], rhs=h2T[:, dc, 0:wk],
                                start=(dc == 0), stop=(dc == 7)),
                                reads=[wt.res, h2T.res], writes=[pu.res], inc=(dc == 7))
                        pus.append(pu)
                    cvs = [cv_t[(2 * i + a) % 6] for a in range(2)]
                    for a in range(2):
                        fc = i + a * NFC
                        P.op(act, lambda pu=pus[a], cv=cvs[a], fc=fc: nc.scalar.activation(
                            out=cv[:, 2:W], in_=pu[:, 0:W - 2], func=AF.Identity, scale=cw[:, 0, fc:fc + 1],
                            bias=cb[:, fc:fc + 1]), reads=[pus[a].res, cw.res, cb.res], writes=[cvs[a].res])
                    if pend is not None:
                        emit_mult(*pend)
                        pend = None
                    for tap in (1, 2):
                        for a in range(2):
                            fc = i + a * NFC
                            P.op(dve, lambda pu=pus[a], cv=cvs[a], fc=fc, tap=tap: nc.vector.scalar_tensor_tensor(
                                out=cv[:, 2:W], in0=pu[:, tap:W - 2 + tap], scalar=cw[:, tap, fc:fc + 1], in1=cv[:, 2:W],
                                op0=ALU.mult, op1=ALU.add), reads=[pus[a].res, cw.res, cvs[a].res], writes=[cvs[a].res])
                    for a in range(2):
                        fc = i + a * NFC
                        pu, cv = pus[a], cvs[a]
                        if k == NS - 1:
                            P.op(dve, lambda pu=pu, fc=fc: nc.vector.tensor_copy(out=upl[:, :, fc], in_=pu[:, CP0:CP0 + 2]),
                                 reads=[pu.res], writes=[upl.res])
                        if k == 0 and with_sample:
                            s8 = sc8[(2 * i + a) % 2]
                            P.dma("sp", s8[:], stc_d[:, fc * 128:(fc + 1) * 128], writes=[s8.res])
                            P.op(pe, lambda s8=s8: nc.tensor.transpose(out=pF[:, 0:8], in_=s8[0:8, :],
                                                                       identity=identf[0:8, 0:8]),
                                 reads=[s8.res, identf.res], writes=[pF.res])
                            P.op(dve, lambda pu=pu, fc=fc: nc.vector.tensor_copy(out=upl_s[:, :, fc], in_=pu[:, W:WX]),
                                 reads=[pu.res], writes=[upl_s.res])
                            P.op(dve, lambda pu=pu, cv=cv, fc=fc: nc.vector.tensor_scalar(
                                out=cv[:, W:WX], in0=pu[:, W:WX], scalar1=cw[:, 2, fc:fc + 1], scalar2=cb[:, fc:fc + 1],
                                op0=ALU.mult, op1=ALU.add), reads=[pu.res, cw.res, cb.res], writes=[cv.res])
                            for r in (0, 1):
                                P.op(dve, lambda cv=cv, fc=fc, r=r: nc.vector.scalar_tensor_tensor(
                                    out=cv[:, W:WX], in0=pF[:, 0:8].rearrange("p (s r) -> p r s", r=2)[:, r, :],
                                    scalar=cw[:, r, fc:fc + 1], in1=cv[:, W:WX], op0=ALU.mult, op1=ALU.add),
                                    reads=[pF.res, cw.res, cv.res], writes=[cv.res])
                    sg = sg_t[i % 3]
                    P.op(act, lambda sg=sg, cv=cvs[0]: nc.scalar.activation(out=sg[:, 2:wk], in_=cv[:, 2:wk], func=AF.Silu),
                         reads=[cvs[0].res], writes=[sg.res])
                    pend = (sg, cvs[1], i, wk)
                if pend is not None:
                    emit_mult(*pend)
                    pend = None
                for ti, (c0, n) in enumerate(tiles):
                    for half in range(2):
                        pd = next_pg()
                        for i in range(NFC):
                            P.op(pe, lambda i=i, pd=pd: nc.tensor.matmul(
                                pd[0:n, :], lhsT=actT[:, i, c0:c0 + n], rhs=wdn[:, i, half * 512:(half + 1) * 512],
                                start=(i == 0), stop=(i == NFC - 1)),
                                reads=[actT.res, wdn.res], writes=[pd.res], inc=(i == NFC - 1))
                        P.op(dve, lambda pd=pd, half=half, ti=ti, n=n: nc.vector.tensor_tensor(
                            out=xs[0:n, ti, half * 512:(half + 1) * 512], in0=pd[0:n, :],
                            in1=xs[0:n, ti, half * 512:(half + 1) * 512], op=ALU.add),
                            reads=[pd.res, xs.res], writes=[xs.res])
                    ss = rms_rows(lambda ti=ti, n=n: xs[0:n, ti, :], xs.res, n, None)
                    P.op(dve, lambda ss=ss, ti=ti, n=n: nc.vector.scalar_tensor_tensor(
                        out=xs[0:n, ti, :], in0=xs[0:n, ti, :], scalar=ss[0:n, 0:1], in1=gfin[0:n, :],
                        op0=ALU.mult, op1=ALU.mult),
                        reads=[xs.res, ss.res, gfin.res], writes=[xs.res])
                    if ti < 4:
                        P.dma("pool", y_o[k, c0:c0 + n, :], xs[0:n, ti, :], reads=[xs.res])
                    else:
                        P.dma("pool", ys_o[:, :], xs[0:n, ti, :], reads=[xs.res])
            for r in range(2):
                for q4 in range(4):
                    P.dma("sp", cp_o[r:r + 1, q4 * 1408:(q4 + 1) * 1408].rearrange("r (c p) -> p (r c)", p=128),
                          upl[:, r, q4 * 11:(q4 + 1) * 11], reads=[upl.res], allow_slow_non_contiguous=True)
            if with_sample:
                for s in range(N_SEQ):
                    for q4 in range(4):
                        P.dma("sp", cs_o[s, 1:2, q4 * 1408:(q4 + 1) * 1408].rearrange("r (c p) -> p (r c)", p=128),
                              upl_s[:, s, q4 * 11:(q4 + 1) * 11], reads=[upl_s.res], allow_slow_non_contiguous=True)
                    P.dma("sp", cs_o[s, 0:1, :], stc_d[2 * s + 1:2 * s + 2, :])
            P.final_wait()
    return nc


def attn_head(nc, P, h, nkb, midx, ucount, L):
    pe, act, dve, pool = P.pe, P.act, P.dve, P.pool
    KT, Vb, qT, kt_res, v_res = L["KT"], L["Vb"], L["qT"], L["kt_res"], L["v_res"]
    masks, mask_res, ident, tri, negones = L["masks"], L["mask_res"], L["ident"], L["tri"], L["negones"]
    e_t, sp_t, a_t, acc32, accbf, w_t = L["e_t"], L["sp_t"], L["a_t"], L["acc32"], L["accbf"], L["w_t"]
    bias_t, o_at, pO, next_pg = L["bias_t"], L["o_at"], L["pO"], L["next_pg"]
    j, hb = h // 2, (h % 2) * 64
    po = pO[0]
    kbs = list(range(nkb - 1, -1, -1))
    n = len(kbs)
    st_ = {}
    u0 = ucount[0]

    def s1(kb):
        p = next_pg()
        need_m = kb in midx
        P.op(pe, lambda: nc.tensor.matmul(
            p[:, 0:W], lhsT=KT[hb:hb + 64, j, kb * 128:(kb + 1) * 128], rhs=qT[hb:hb + 64, j, :],
            start=True, stop=not need_m),
            reads=[kt_res[kb], qT.res], writes=[p.res], inc=not need_m)
        if need_m:
            P.op(pe, lambda: nc.tensor.matmul(
                p[:, 0:W], lhsT=ident[:], rhs=masks[:, midx[kb], :], start=False, stop=True),
                reads=[ident.res, mask_res[midx[kb]]], writes=[p.res])
        st_[kb] = {"p": p}

    def s2a(kb, u):
        p = st_[kb]["p"]
        e = e_t[u % 3]
        P.op(act, lambda: nc.scalar.activation(out=e[:], in_=p[:, 0:W], func=AF.Exp,
                                               bias=bias_t[:, h:h + 1], scale=1.0),
             reads=[p.res, bias_t.res], writes=[e.res])
        st_[kb]["e"] = e

    def s2b(kb, u):
        e = st_[kb]["e"]
        s = sp_t[u % 3]
        P.op(act, lambda: nc.scalar.activation(out=s[:], in_=e[:], func=AF.Ln, bias=1.0, scale=1.0),
             reads=[e.res], writes=[s.res])
        st_[kb]["s"] = s

    def s3(kb, u, first):
        s = st_[kb]["s"]
        pc = next_pg()
        st_[kb]["pc"] = pc
        P.op(pe, lambda: nc.tensor.matmul(pc[:, 0:W], lhsT=tri[:], rhs=s[:], start=True, stop=first),
             reads=[tri.res, s.res], writes=[pc.res], inc=first)
        if not first:
            ab = accbf[(u - 1) % 3]
            P.op(pe, lambda: nc.tensor.matmul(pc[:, 0:W], lhsT=negones[:], rhs=ab[:], start=False, stop=True),
                 reads=[negones.res, ab.res], writes=[pc.res])
        abn = accbf[u % 3]
        if first:
            P.op(dve, lambda: nc.vector.tensor_copy(out=abn[:], in_=s[:]), reads=[s.res], writes=[abn.res])
        else:
            abo = accbf[(u - 1) % 3]
            P.op(dve, lambda: nc.vector.tensor_tensor(out=abn[:], in0=abo[:], in1=s[:], op=ALU.add),
                 reads=[abo.res, s.res], writes=[abn.res])

    def s4(kb, u):
        pc = st_[kb]["pc"]; e = st_[kb]["e"]
        w = w_t[u % 3]
        a = a_t[u % 3]
        P.op(act, lambda: nc.scalar.activation(out=w[:], in_=pc[:, 0:W], func=AF.Exp),
             reads=[pc.res], writes=[w.res])
        P.op(dve, lambda: nc.vector.tensor_tensor(out=a[:], in0=e[:], in1=w[:], op=ALU.mult),
             reads=[e.res, w.res], writes=[a.res])
        st_[kb]["a"] = a

    def s5(kb, first, last):
        a = st_[kb]["a"]
        P.op(pe, lambda: nc.tensor.matmul(po[0:64, 0:W], lhsT=Vb[:, kb, h * 64:(h + 1) * 64], rhs=a[:],
                                          start=first, stop=last),
             reads=[v_res[kb], a.res], writes=[po.res], inc=last)
        del st_[kb]

    vgen = L.get("vgen")
    s1(kbs[0])
    for i in range(n + 1):
        if i + 1 < n:
            s1(kbs[i + 1])
        if i < n:
            s2a(kbs[i], u0 + i)
        if i >= 1:
            s4(kbs[i - 1], u0 + i - 1)
        if i < n:
            s2b(kbs[i], u0 + i)
            s3(kbs[i], u0 + i, i == 0)
        if i >= 1:
            s5(kbs[i - 1], i - 1 == 0, i - 1 == n - 1)
        if vgen is not None and i % 2 == 1:
            next(vgen, None)
    ucount[0] += n
    P.op(dve, lambda: nc.vector.tensor_copy(out=o_at[:, h, 0:W], in_=po[0:64, 0:W]),
         reads=[po.res], writes=[o_at.res])


def sample_mixer(nc, P, L):
    pe, act, dve, pool = P.pe, P.act, P.dve, P.pool
    sb, esB, next_pg, sam_state = L["sb"], L["esB"], L["next_pg"], L["sam_state"]
    hT_sam, w_qu, pw_bf, pscale, bias_t = L["hT_sam"], L["w_qu"], L["pw_bf"], L["pscale"], L["bias_t"]
    tri, negones, identf, pO, o_scr = L["tri"], L["negones"], L["identf"], L["pO"], L["o_scr"]
    o_pl = sb("o_spl", [128, 4, N_SEQ], BF16, esB)
    ck_d, cv_d, stp_d, pt_d, ps_o, q_scr = L["ck_d"], L["cv_d"], L["stp_d"], L["pt_d"], L["ps_o"], L["q_scr"]
    NP = N_SEQ * NPG
    pti = sb("pti", [128, NP], I32, esB)
    ptf = sb("ptf", [128, NP], F32, esB)
    iop = sb("iop", [128, 1], F32, esB)
    idxi = sam_state["idxi"]
    P.dma("sp", pti[:], pt_d.partition_broadcast(128), writes=[pti.res])
    P.op(pool, lambda: nc.gpsimd.iota(iop[:], pattern=[[0, 1]], base=0, channel_multiplier=1,
                                      allow_small_or_imprecise_dtypes=True), writes=[iop.res])
    P.op(pool, lambda: nc.gpsimd.tensor_copy(out=ptf[:], in_=pti[:]), reads=[pti.res], writes=[ptf.res])
    P.op(pool, lambda: nc.gpsimd.tensor_scalar(out=ptf[:], in0=ptf[:], scalar1=128.0, scalar2=iop[:, 0:1],
                                               op0=ALU.mult, op1=ALU.add), reads=[ptf.res, iop.res], writes=[ptf.res])
    P.op(pool, lambda: nc.gpsimd.tensor_copy(out=idxi[:], in_=ptf[:]), reads=[ptf.res], writes=[idxi.res])
    bd0 = sb("bd0", [8, 512], F32, esB)
    bd1 = sb("bd1", [8, 512], F32, esB)
    bdiag = sam_state["bdiag"]
    ones8 = sam_state["ones8"]
    P.op(pool, lambda: nc.gpsimd.memset(bd0[:], 1.0), writes=[bd0.res])
    P.op(pool, lambda: nc.gpsimd.memset(ones8[:], 1.0), writes=[ones8.res])
    P.op(pool, lambda: nc.gpsimd.affine_select(out=bd1[:], in_=bd0[:], pattern=[[1, 512]], compare_op=ALU.is_ge,
                                                fill=0.0, base=0, channel_multiplier=-64),
         reads=[bd0.res], writes=[bd1.res])
    P.op(pool, lambda: nc.gpsimd.affine_select(out=bd0[:], in_=bd1[:], pattern=[[-1, 512]], compare_op=ALU.is_ge,
                                                fill=0.0, base=63, channel_multiplier=64),
         reads=[bd1.res], writes=[bd0.res])
    P.op(pool, lambda: nc.gpsimd.tensor_copy(out=bdiag[:], in_=bd0[:]), reads=[bd0.res], writes=[bdiag.res])
    qscr_res = Res()
    pq = next_pg()
    for dc in range(8):
        P.op(pe, lambda dc=dc: nc.tensor.matmul(pq[0:N_SEQ, :], lhsT=hT_sam[:, dc, :], rhs=w_qu[:, dc, 0:512],
                                                start=(dc == 0), stop=(dc == 7)),
             reads=[hT_sam.res, w_qu.res], writes=[pq.res], inc=(dc == 7))
    q_tok = sb("q_tok", [N_SEQ, 512], F32, esB)
    P.op(dve, lambda: nc.vector.tensor_scalar(out=q_tok[:], in0=pq[0:N_SEQ, :], scalar1=0.125, scalar2=None, op0=ALU.mult),
         reads=[pq.res], writes=[q_tok.res])
    P.dma("sp", q_scr[:, :], q_tok[:], reads=[q_tok.res], writes=[qscr_res])
    pu = next_pg()
    for dc in range(8):
        P.op(pe, lambda dc=dc: nc.tensor.matmul(pu[0:N_SEQ, :], lhsT=hT_sam[:, dc, :], rhs=w_qu[:, dc, 512:1024],
                                                start=(dc == 0), stop=(dc == 7)),
             reads=[hT_sam.res, w_qu.res], writes=[pu.res], inc=(dc == 7))
    u_tok = sb("u_tok", [N_SEQ, 512], F32, esB)
    P.op(act, lambda: nc.scalar.copy(out=u_tok[:], in_=pu[0:N_SEQ, :]), reads=[pu.res], writes=[u_tok.res])
    P.dma("sp", ps_o[:, 14, :], u_tok[:], reads=[u_tok.res])
    for s in range(N_SEQ):
        P.dma("sp", ps_o[s, 0:14, :], stp_d[s * 15 + 1:s * 15 + 15, :])
    uTs = sb("uTs", [128, 4, N_SEQ], F32, esB)
    for g in range(4):
        pu2 = next_pg()
        for dc in range(8):
            P.op(pe, lambda g=g, dc=dc, pu2=pu2: nc.tensor.matmul(
                pu2[:, 0:N_SEQ], lhsT=w_qu[:, dc, 512 + g * 128:512 + (g + 1) * 128], rhs=hT_sam[:, dc, :],
                start=(dc == 0), stop=(dc == 7)),
                reads=[hT_sam.res, w_qu.res], writes=[pu2.res], inc=(dc == 7))
        P.op(act, lambda g=g, pu2=pu2: nc.scalar.copy(out=uTs[:, g, :], in_=pu2[:, 0:N_SEQ]),
             reads=[pu2.res], writes=[uTs.res])
    st60 = sb("st60", [N_SEQ * 15, 512], F32, esB)
    P.dma("sp", st60[:], stp_d[:, :], writes=[st60.res])
    stT = sb("stT", [128, 4, N_SEQ * 15], F32, esB)
    pst = next_pg()
    for g in range(4):
        P.op(pe, lambda g=g: nc.tensor.transpose(out=pst[:, g * 60:(g + 1) * 60], in_=st60[0:60, g * 128:(g + 1) * 128],
                                                 identity=identf[0:60, 0:60]),
             reads=[st60.res, identf.res], writes=[pst.res], inc=(g == 3))
    P.op(act, lambda: nc.scalar.copy(out=stT[:], in_=pst[:, 0:240].rearrange("p (g c) -> p g c", g=4)),
         reads=[pst.res], writes=[stT.res])
    ssum = sb("ssum", [128, N_SEQ], F32, esB)
    d_bf = [sb(f"d_bf{i}", [128, N_SEQ], BF16, esB) for i in range(2)]
    for g in range(4):
        wg = 2 << g
        nr = wg - 1
        P.op(dve, lambda g=g, nr=nr: nc.vector.tensor_reduce(
            out=ssum[:], in_=stT[:, g, :].rearrange("p (s r) -> p s r", r=15)[:, :, 15 - nr:15], axis=AX.X, op=ALU.add),
            reads=[stT.res], writes=[ssum.res])
        P.op(dve, lambda g=g: nc.vector.tensor_tensor(out=ssum[:], in0=ssum[:], in1=uTs[:, g, :], op=ALU.add),
             reads=[ssum.res, uTs.res], writes=[ssum.res])
        db = d_bf[g % 2]
        P.op(dve, lambda g=g, wg=wg, db=db: nc.vector.scalar_tensor_tensor(
            out=db[:], in0=ssum[:], scalar=1.0 / wg, in1=uTs[:, g, :], op0=ALU.mult, op1=ALU.subtract),
            reads=[ssum.res, uTs.res], writes=[db.res])
        pp = next_pg()
        P.op(pe, lambda g=g, db=db, pp=pp: nc.tensor.matmul(pp[:, 0:N_SEQ], lhsT=pw_bf[:, g, :], rhs=db[:],
                                                            start=True, stop=True),
             reads=[pw_bf.res, db.res], writes=[pp.res])
        P.op(dve, lambda g=g, pp=pp: nc.vector.tensor_scalar(
            out=o_pl[:, g, :], in0=pp[:, 0:N_SEQ], scalar1=pscale[:, g:g + 1], scalar2=None, op0=ALU.mult),
            reads=[pp.res, pscale.res], writes=[o_pl.res])
    P.dma("sp", o_scr[0, :, 8:12, W:WX], o_pl[:], reads=[o_pl.res])
    yield
    LAG = 3
    qb = [sb(f"qb{i}", [128, 512], F32, esB) for i in range(2)]
    Kp = [sb(f"Kp{i}", [128, 512], F32, esB) for i in range(5)]
    tmp = [sb(f"ktmp{i}", [128, 512], F32, esB) for i in range(4)]
    Z = sb("Zs", [128, 512], F32, esB)
    e8 = sb("e8", [128, 512], F32, esB)
    sp8 = sb("sp8", [128, 512], BF16, esB)
    S0 = sb("S0", [128, 512], F32, esB)
    Sa = sb("Sa", [128, 512], F32, esB)
    Sb_ = sb("Sb", [128, 512], F32, esB)
    A8all = sam_state["A8all"]
    hv = lambda t: t[:].rearrange("p (g h) -> p h g", h=8)
    for s in range(N_SEQ):
        q_t = qb[s % 2]
        P.dma("sp", q_t[:], q_scr[s:s + 1, :].partition_broadcast(128), reads=[qscr_res], writes=[q_t.res])
        zres = [Res() for _ in range(NPG)]
        for zr in zres:
            zr.w = Z.res.w
            zr.r = dict(Z.res.r)
        for step in range(NPG + LAG + 1):
            if step < NPG:
                col = s * NPG + step
                kp = Kp[step % 5]
                P.dma("pool", None, None, reads=[idxi.res], writes=[kp.res],
                      fn=lambda kp=kp, col=col: nc.gpsimd.indirect_dma_start(
                          out=kp[:], out_offset=None, in_=ck_d,
                          in_offset=bass.IndirectOffsetOnAxis(ap=idxi[:, col:col + 1], axis=0)))
            if LAG <= step < NPG + LAG:
                pg = step - LAG
                kp = Kp[pg % 5]
                tm = tmp[pg % 4]
                P.op(dve, lambda tm=tm, kp=kp: nc.vector.tensor_tensor(out=tm[:], in0=kp[:], in1=q_t[:], op=ALU.mult),
                     reads=[kp.res, q_t.res], writes=[tm.res])
            if step >= LAG + 1:
                pg = step - LAG - 1
                tm = tmp[pg % 4]
                P.op(dve, lambda tm=tm, pg=pg: nc.vector.tensor_reduce(
                    out=Z[:, pg * 8:(pg + 1) * 8], in_=tm[:].rearrange("p (h d) -> p h d", h=8), axis=AX.X, op=ALU.add),
                    reads=[tm.res], writes=[zres[pg]])
            yield
        Z.res.w = zres[NPG - 1].w
        Z.res.r = {}
        for h in range(8):
            P.op(act, lambda h=h: nc.scalar.activation(out=hv(e8)[:, h, :], in_=hv(Z)[:, h, :], func=AF.Exp,
                                                       bias=bias_t[:, h:h + 1], scale=1.0),
                 reads=[Z.res, bias_t.res], writes=[e8.res])
        P.op(act, lambda: nc.scalar.activation(out=sp8[:], in_=e8[:], func=AF.Ln, bias=1.0, scale=1.0),
             reads=[e8.res], writes=[sp8.res])
        pc = pO[0]
        P.op(pe, lambda: nc.tensor.matmul(pc[:, :], lhsT=tri[:], rhs=sp8[:], start=True, stop=True),
             reads=[tri.res, sp8.res], writes=[pc.res])
        pt_ = pO[1]
        P.op(pe, lambda: nc.tensor.matmul(pt_[:, :], lhsT=negones[:], rhs=sp8[:], start=True, stop=True),
             reads=[negones.res, sp8.res], writes=[pt_.res])
        P.op(act, lambda: nc.scalar.copy(out=S0[:], in_=pt_[:, :]), reads=[pt_.res], writes=[S0.res])
        cur = S0
        for i, dd in enumerate((1, 2, 4, 8, 16, 32)):
            nxt = Sa if i % 2 == 0 else Sb_
            n0 = 512 - 8 * dd
            P.op(dve, lambda cur=cur, nxt=nxt, n0=n0, dd=dd: nc.vector.tensor_tensor(
                out=nxt[:, 0:n0], in0=cur[:, 0:n0], in1=cur[:, 8 * dd:512], op=ALU.add),
                reads=[cur.res], writes=[nxt.res])
            P.op(dve, lambda cur=cur, nxt=nxt, n0=n0: nc.vector.tensor_copy(out=nxt[:, n0:512], in_=cur[:, n0:512]),
                 reads=[cur.res, nxt.res], writes=[nxt.res])
            cur = nxt
        P.op(dve, lambda cur=cur: nc.vector.tensor_tensor(out=cur[:], in0=cur[:], in1=S0[:], op=ALU.subtract),
             reads=[cur.res, S0.res], writes=[cur.res])
        P.op(dve, lambda cur=cur: nc.vector.tensor_tensor(out=cur[:], in0=cur[:], in1=Z[:], op=ALU.add),
             reads=[cur.res, Z.res], writes=[cur.res])
        P.op(dve, lambda cur=cur: nc.vector.tensor_tensor(out=cur[:], in0=cur[:], in1=pc[:, :], op=ALU.add),
             reads=[cur.res, pc.res], writes=[cur.res])
        for h in range(8):
            P.op(act, lambda h=h, cur=cur: nc.scalar.activation(out=A8all[:, s, :].rearrange("p (g h) -> p h g", h=8)[:, h, :], in_=hv(cur)[:, h, :], func=AF.Exp,
                                                                bias=bias_t[:, h:h + 1], scale=1.0),
                 reads=[cur.res, bias_t.res], writes=[A8all.res])
        yield


def sample_vpass(nc, P, L):
    pe, act, dve, pool = P.pe, P.act, P.dve, P.pool
    sb, esB, pO, o_scr, cv_d = L["sb"], L["esB"], L["pO"], L["o_scr"], L["cv_d"]
    st = L["sam_state"]
    A8all, idxi, bdiag, ones8 = st["A8all"], st["idxi"], st["bdiag"], st["ones8"]
    LAG = 2
    NB = 4
    Vp = [sb(f"Vp{i}", [128, 512], BF16, esB) for i in range(NB)]
    m8 = sb("m8", [8, 512], BF16, esB)
    o_at = sb("o_sat", [64, 8, N_SEQ], BF16, esB)
    pov = pO[1]
    for s in range(N_SEQ):
        for step in range(NPG + LAG):
            if step < NPG:
                col = s * NPG + step
                vp = Vp[step % NB]
                P.dma("pool", None, None, reads=[idxi.res], writes=[vp.res],
                      fn=lambda vp=vp, col=col: nc.gpsimd.indirect_dma_start(
                          out=vp[:], out_offset=None, in_=cv_d,
                          in_offset=bass.IndirectOffsetOnAxis(ap=idxi[:, col:col + 1], axis=0)))
            if step >= LAG:
                pg = step - LAG
                vp = Vp[pg % NB]
                P.op(pe, lambda vp=vp, pg=pg, s=s: nc.tensor.matmul(pov[0:8, :], lhsT=A8all[:, s, pg * 8:(pg + 1) * 8], rhs=vp[:],
                                                                    start=(pg == 0), stop=(pg == NPG - 1)),
                     reads=[A8all.res, vp.res], writes=[pov.res], inc=True)
            yield
        P.op(dve, lambda: nc.vector.tensor_tensor(out=m8[:], in0=pov[0:8, :], in1=bdiag[:], op=ALU.mult),
             reads=[pov.res, bdiag.res], writes=[m8.res])
        for h in range(8):
            P.op(pe, lambda h=h: nc.tensor.matmul(pov[0:64, h:h + 1], lhsT=m8[0:8, h * 64:(h + 1) * 64], rhs=ones8[0:8, 0:1],
                                                  start=True, stop=True),
                 reads=[m8.res, ones8.res], writes=[pov.res], inc=(h == 7))
        P.op(dve, lambda s=s: nc.vector.tensor_copy(out=o_at[:, :, s], in_=pov[0:64, 0:8]),
             reads=[pov.res], writes=[o_at.res])
        yield
    P.dma("sp", o_scr[0, 0:64, 0:8, W:WX], o_at[:], reads=[o_at.res])


_NC_CACHE = {}
_RUNNER = [None]
_REMAP = [None, None]


_STOP = [None]


def _get_nc(with_sample):
    key = (with_sample, _STOP[0])
    if key not in _NC_CACHE:
        _NC_CACHE[key] = build(with_sample, _STOP[0])
    return _NC_CACHE[key]


def kernel(x_prompt, x_sample, cache_k, cache_v, state_pool, state_conv, page_table,
           meta_tokens, norm_mix_g, w_in, sb_bias, pool_w, pool_scale, w_out, norm_ffn_g,
           w_up, conv_w, conv_b, w_down, norm_final_g, _with_sample=True):
    f32 = np.float32
    B = x_prompt.shape[0]
    nc = _get_nc(_with_sample)
    in_maps = []
    ck2 = cv2 = None
    if _with_sample:
        ck2 = np.ascontiguousarray(cache_k, dtype=f32).reshape(-1, 512)
        cv2 = np.ascontiguousarray(cache_v, dtype=f32).reshape(-1, 512)
    for c in range(8):
        b, g = c // 2, c % 2
        xa = np.zeros((TP, D), f32)
        xa[:N_META] = meta_tokens
        xa[N_META:T] = x_prompt[b]
        xsl = np.zeros((NS, WR, D), f32)
        qp = np.zeros((1, NS * W), f32)
        for k in range(NS):
            s = slot_start(g, k)
            lo, hi = s - HALO, s + W
            a, e = max(lo, 0), min(hi, T)
            xsl[k, a - lo:e - lo] = xa[a:e]
            qp[0, k * W:(k + 1) * W] = np.arange(s, s + W, dtype=f32)
        m = {
            "xall": xa, "xslot": xsl, "qpos": qp,
            "w_in": np.asarray(w_in, f32), "w_out": np.asarray(w_out, f32), "w_up": np.asarray(w_up, f32),
            "w_down": np.asarray(w_down, f32),
            "norm_mix_g": np.asarray(norm_mix_g, f32).reshape(D, 1),
            "norm_ffn_g": np.asarray(norm_ffn_g, f32).reshape(D, 1),
            "norm_final_g": np.asarray(norm_final_g, f32).reshape(1, D),
            "sb_bias": np.asarray(sb_bias, f32).reshape(1, 8),
            "pool_w": np.asarray(pool_w, f32), "pool_scale": np.asarray(pool_scale, f32),
            "conv_w": np.asarray(conv_w, f32), "conv_b": np.asarray(conv_b, f32).reshape(1, 2 * DFF),
        }
        if _with_sample:
            sl = slice(N_SEQ * c, N_SEQ * (c + 1))
            m.update({
                "x_sample": np.asarray(x_sample[sl], f32).reshape(N_SEQ, D),
                "cache_k": (_REMAP[0](c, ck2) if _REMAP[0] else ck2), "cache_v": (_REMAP[0](c, cv2) if _REMAP[0] else cv2),
                "state_pool": np.asarray(state_pool[sl], f32).reshape(N_SEQ * 15, 512),
                "state_conv": np.asarray(state_conv[sl], f32).reshape(N_SEQ * 2, 2 * DFF),
                "page_table": (_REMAP[1](c) if _REMAP[1] else np.asarray(page_table[sl], np.int32).reshape(1, N_SEQ * NPG)),
            })
        in_maps.append(m)
    if _RUNNER[0] is not None:
        res = _RUNNER[0](nc, in_maps)
    else:
        res = run_bass_kernel_spmd(nc, in_maps, core_ids=list(range(8))).results

    y_prompt = np.zeros((B, 4096, D), f32)
    k_prompt = np.zeros((B, T, 8, 64), f32)
    v_prompt = np.zeros((B, T, 8, 64), f32)
    pool_prompt = np.zeros((B, 15, 512), f32)
    conv_prompt = np.zeros((B, 2, 2 * DFF), f32)
    for c in range(8):
        b, g = c // 2, c % 2
        r = res[c]
        for k in range(NS):
            r0 = OWN * TILES[g][k]
            nv = min(OWN, 4096 - r0)
            y_prompt[b, r0:r0 + nv] = r["y_slot"][k, 2:2 + nv]
        if g == 0:
            k_prompt[b] = r["k_all"][:T].reshape(T, 8, 64)
            v_prompt[b] = r["v_all"][:T].reshape(T, 8, 64)
        else:
            pool_prompt[b] = r["pool_last"]
            conv_prompt[b] = r["conv_last"]
    if not _with_sample:
        return (y_prompt, None, k_prompt, v_prompt, pool_prompt, conv_prompt, None, None, None, None)
    DB = x_sample.shape[0]
    y_sample = np.zeros((DB, 1, D), f32)
    k_sample = np.zeros((DB, 1, 8, 64), f32)
    v_sample = np.zeros((DB, 1, 8, 64), f32)
    pool_sample = np.zeros((DB, 15, 512), f32)
    conv_sample = np.zeros((DB, 2, 2 * DFF), f32)
    for c in range(8):
        r = res[c]
        sl = slice(N_SEQ * c, N_SEQ * (c + 1))
        y_sample[sl, 0] = r["y_sample"]
        k_sample[sl, 0] = r["k_sample"].reshape(N_SEQ, 8, 64)
        v_sample[sl, 0] = r["v_sample"].reshape(N_SEQ, 8, 64)
        pool_sample[sl] = r["pool_sample"]
        conv_sample[sl] = r["conv_sample"]
    return (y_prompt, y_sample, k_prompt, v_prompt, pool_prompt, conv_prompt,
            k_sample, v_sample, pool_sample, conv_sample)
```

```python
import numpy as np
from contextlib import ExitStack
import concourse.bass as bass
import concourse.mybir as mybir
from concourse.bass_utils import run_bass_kernel_spmd

F32 = mybir.dt.float32
BF16 = mybir.dt.bfloat16
I32 = mybir.dt.int32
AF = mybir.ActivationFunctionType
ALU = mybir.AluOpType
AX = mybir.AxisListType

D = 1024
T = 4112
NBLK = 33
TP = NBLK * 128
N_META = 16
NS = 5
W = 412
HALO = 16
WR = W + HALO
OWN = 410
DFF = 2816
NFC = DFF // 128
EPS = 1e-6
NEG = -30000.0
QT = [(0, 128), (128, 128), (256, 128), (384, 28)]
N_SEQ = 4
NPG = 64
SAME_ENGINE_SYNC = True


TILES = ((0, 3, 4, 7, 8), (1, 2, 5, 6, 9))
LAST_S = 14 + OWN * 9
PP0 = 4097 - LAST_S + HALO
CP0 = 4110 - LAST_S


def slot_start(g, k):
    return 14 + OWN * TILES[g][k]


def n_kb_for_slot(k):
    last = 14 + OWN * (2 * k + 1) + W - 1
    return min(NBLK, last // 128 + 1)


def kb_needs_mask(k, kb):
    return not (128 * kb + 127 < 14 + OWN * 2 * k)


class SemObj:
    def __init__(self, sem):
        self.sem = sem
        self.cnt = 0


class Eng(SemObj):
    def __init__(self, sem, h, name):
        super().__init__(sem)
        self.h = h
        self.name = name
        self.waited = {}


class Res:
    __slots__ = ("w", "r", "excl")

    def __init__(self, excl=False):
        self.w = None
        self.r = {}
        self.excl = excl


class Prog:
    def __init__(self, nc, es):
        self.nc = nc
        self.es = es
        mk = lambda n: es.enter_context(nc.semaphore(n))
        self.pe = Eng(mk("s_pe"), nc.tensor, "pe")
        self.act = Eng(mk("s_act"), nc.scalar, "act")
        self.dve = Eng(mk("s_dve"), nc.vector, "dve")
        self.pool = Eng(mk("s_pool"), nc.gpsimd, "pool")
        self.sp = Eng(mk("s_sp"), nc.sync, "sp")
        self.dsems = {}
        for q in ("sp", "pool", "act"):
            self.dsems[q] = [SemObj(mk(f"d_{q}{i}")) for i in range(12)]
        self.dnext = {"sp": 0, "pool": 0, "act": 0}

    def _wait(self, eng, toks, is_dma=False):
        for tk in toks:
            so, v = tk[0], tk[1]
            if so is eng and not is_dma and (not SAME_ENGINE_SYNC or eng is self.pe):
                continue
            if eng.waited.get(so, 0) < v:
                eng.h.wait_ge(so.sem, v)
                eng.waited[so] = v

    @staticmethod
    def _deps(reads, writes):
        toks = []
        for r in reads:
            if r.w is not None:
                toks.append(r.w)
            if r.excl:
                toks.extend(r.r.items())
        for w in writes:
            if w.w is not None:
                toks.append(w.w)
            toks.extend((so, v, "war") for so, v in w.r.items())
        return toks

    @staticmethod
    def _record(tok, reads, writes):
        so, v = tok
        for r in reads:
            if r.r.get(so, 0) < v:
                r.r[so] = v
        for w in writes:
            w.w = tok
            w.r = {}

    def op(self, eng, fn, reads=(), writes=(), inc=True):
        self._wait(eng, self._deps(reads, writes))
        ins = fn()
        if inc:
            ins.then_inc(eng.sem, 1)
            eng.cnt += 1
            tok = (eng, eng.cnt)
        else:
            tok = (eng, eng.cnt + 1)
        self._record(tok, reads, writes)
        return ins

    def dma(self, q, out, in_, reads=(), writes=(), fn=None, **kw):
        eng = {"sp": self.sp, "pool": self.pool, "act": self.act}[q]
        lst = self.dsems[q]
        d = lst[self.dnext[q] % len(lst)]
        self.dnext[q] += 1
        toks = self._deps(reads, writes)
        if d.cnt > 0:
            toks.append((d, d.cnt))
        self._wait(eng, toks, is_dma=True)
        if fn is not None:
            fn().then_inc(d.sem, 16)
        else:
            eng.h.dma_start(out=out, in_=in_, **kw).then_inc(d.sem, 16)
        d.cnt += 16
        self._record((d, d.cnt), reads, writes)

    def final_wait(self):
        toks = []
        for q in self.dsems:
            for d in self.dsems[q]:
                if d.cnt:
                    toks.append((d, d.cnt))
        for e in (self.pe, self.act, self.dve, self.pool):
            if e.cnt:
                toks.append((e, e.cnt))
        self._wait(self.sp, toks)


class Tl:
    def __init__(self, t, excl=False):
        self.t = t
        self.res = Res(excl)

    def __getitem__(self, k):
        return self.t[k]


WX = W + 4
NPOOL = [2560]
SAMPLE_STEPS_PER_BLOCK = 9
import os
DBG = int(os.environ.get('KDBG', '99'))


def build(with_sample=True, stop_after=None):
    nc = bass.Bass("TRN2", target_bir_lowering=False)
    dr = lambda name, shape, dt=F32, kind="ExternalInput": nc.dram_tensor(name, shape, dt, kind=kind).ap()
    xall = dr("xall", [TP, D])
    xslot = dr("xslot", [NS, WR, D])
    qpos_d = dr("qpos", [1, NS * W])
    w_in_d = dr("w_in", [D, 2048])
    w_out_d = dr("w_out", [D, D])
    w_up_d = dr("w_up", [D, 2 * DFF])
    w_down_d = dr("w_down", [DFF, D])
    g_mix_d = dr("norm_mix_g", [D, 1])
    g_ffn_d = dr("norm_ffn_g", [D, 1])
    g_fin_d = dr("norm_final_g", [1, D])
    sbb_d = dr("sb_bias", [1, 8])
    pool_w_d = dr("pool_w", [4, 128, 128])
    pool_s_d = dr("pool_scale", [4, 128])
    conv_w_d = dr("conv_w", [3, 2 * DFF])
    conv_b_d = dr("conv_b", [1, 2 * DFF])
    y_o = dr("y_slot", [NS, W, D], kind="ExternalOutput")
    k_o = dr("k_all", [TP, 512], kind="ExternalOutput")
    v_o = dr("v_all", [TP, 512], kind="ExternalOutput")
    pp_o = dr("pool_last", [15, 512], kind="ExternalOutput")
    cp_o = dr("conv_last", [2, 2 * DFF], kind="ExternalOutput")
    o_scr = nc.dram_tensor("o_scr", [NS, 128, 12, WX], BF16).ap()
    wup_bf = nc.dram_tensor("wup_bf", [NFC, 2, 128, 8, 128], BF16).ap()
    wdn_bf = nc.dram_tensor("wdn_bf", [NFC, 128, D], BF16).ap()
    if with_sample:
        xsam = dr("x_sample", [N_SEQ, D])
        ck_d = dr("cache_k", [NPOOL[0] * 128, 512])
        cv_d = dr("cache_v", [NPOOL[0] * 128, 512])
        stp_d = dr("state_pool", [N_SEQ * 15, 512])
        stc_d = dr("state_conv", [N_SEQ * 2, 2 * DFF])
        pt_d = dr("page_table", [1, N_SEQ * NPG], I32)
        ys_o = dr("y_sample", [N_SEQ, D], kind="ExternalOutput")
        ks_o = dr("k_sample", [N_SEQ, 512], kind="ExternalOutput")
        vs_o = dr("v_sample", [N_SEQ, 512], kind="ExternalOutput")
        ps_o = dr("pool_sample", [N_SEQ, 15, 512], kind="ExternalOutput")
        cs_o = dr("conv_sample", [N_SEQ, 2, 2 * DFF], kind="ExternalOutput")
        q_scr = nc.dram_tensor("q_scr", [N_SEQ, 512], F32).ap()

    es = ExitStack()
    with es:
        P = Prog(nc, es)
        pe, act, dve, pool = P.pe, P.act, P.dve, P.pool

        def sb(name, shape, dt=F32, st=es):
            return Tl(st.enter_context(nc.sbuf_tensor(name, shape, dt)))

        def ps(name, shape, dt=F32):
            return Tl(es.enter_context(nc.psum_tensor(name, shape, dt)), excl=True)

        pT = ps("pT", [128, 1024], BF16)
        pG = [ps(f"pG{i}", [128, 512], F32) for i in range(5)]
        pO = [ps(f"pO{i}", [128, 512], F32) for i in range(2)]
        gi = [0]

        def next_pg():
            t = pG[gi[0] % len(pG)]
            gi[0] += 1
            return t

        ident = sb("ident", [128, 128], BF16)
        identf = sb("identf", [128, 128], F32)
        tri = sb("tri", [128, 128], BF16)
        negones = sb("negones", [128, 128], BF16)
        onesf = sb("onesf", [128, 128], F32)
        negf = sb("negf", [128, 128], F32)
        kpos = sb("kpos", [128, NBLK], F32)
        mhalf = sb("mhalf", [128, 1], F32)
        P.op(pool, lambda: nc.gpsimd.memset(onesf[:], 1.0), writes=[onesf.res])
        P.op(pool, lambda: nc.gpsimd.memset(negf[:], -1.0), writes=[negf.res])
        P.op(pool, lambda: nc.gpsimd.memset(mhalf[:], -0.5), writes=[mhalf.res])
        P.op(pool, lambda: nc.gpsimd.memset(negones[:], -1.0), writes=[negones.res])
        P.op(pool, lambda: nc.gpsimd.affine_select(out=identf[:], in_=onesf[:], pattern=[[-1, 128]],
                                                    compare_op=ALU.is_equal, fill=0.0, base=0, channel_multiplier=1),
             reads=[onesf.res], writes=[identf.res])
        P.op(pool, lambda: nc.gpsimd.tensor_copy(out=ident[:], in_=identf[:]), reads=[identf.res], writes=[ident.res])
        P.op(pool, lambda: nc.gpsimd.affine_select(out=tri[:], in_=negf[:], pattern=[[-1, 128]],
                                                    compare_op=ALU.is_ge, fill=0.0, base=0, channel_multiplier=1),
             reads=[negf.res], writes=[tri.res])
        P.op(pool, lambda: nc.gpsimd.iota(kpos[:], pattern=[[128, NBLK]], base=0, channel_multiplier=1,
                                          allow_small_or_imprecise_dtypes=True), writes=[kpos.res])

        bias_t = sb("bias_t", [128, 8])
        P.dma("sp", bias_t[:], sbb_d.partition_broadcast(128), writes=[bias_t.res])
        gfin = sb("gfin", [128, D])
        P.dma("sp", gfin[:], g_fin_d.partition_broadcast(128), writes=[gfin.res])
        gmix = sb("gmix", [128, 8])
        P.dma("sp", gmix[:], g_mix_d.rearrange("(c p) o -> p (c o)", p=128), writes=[gmix.res],
              allow_slow_non_contiguous=True)
        gffn = sb("gffn", [128, 8])
        P.dma("sp", gffn[:], g_ffn_d.rearrange("(c p) o -> p (c o)", p=128), writes=[gffn.res],
              allow_slow_non_contiguous=True)
        gmix_bc = sb("gmix_bc", [128, 8, 128])
        gffn_bc = sb("gffn_bc", [128, 8, 128])
        for dc in range(8):
            P.op(pool, lambda dc=dc: nc.gpsimd.tensor_scalar(out=gmix_bc[:, dc, :], in0=onesf[:], scalar1=gmix[:, dc:dc + 1],
                                                             scalar2=None, op0=ALU.mult),
                 reads=[onesf.res, gmix.res], writes=[gmix_bc.res])
            P.op(pool, lambda dc=dc: nc.gpsimd.tensor_scalar(out=gffn_bc[:, dc, :], in0=onesf[:], scalar1=gffn[:, dc:dc + 1],
                                                             scalar2=None, op0=ALU.mult),
                 reads=[onesf.res, gffn.res], writes=[gffn_bc.res])
        qpos_bc = sb("qpos_bc", [128, NS * W])
        P.dma("sp", qpos_bc[:], qpos_d.partition_broadcast(128), writes=[qpos_bc.res])
        pscale = sb("pscale", [128, 4])
        P.dma("sp", pscale[:], pool_s_d.rearrange("g c -> c g"), writes=[pscale.res], allow_slow_non_contiguous=True)
        cw = sb("cw", [128, 3, 44])
        cb = sb("cb", [128, 44])
        for q4 in range(4):
            cs_ = slice(q4 * 1408, (q4 + 1) * 1408)
            for i3 in range(3):
                P.dma("sp", cw[:, i3, q4 * 11:(q4 + 1) * 11], conv_w_d[i3:i3 + 1, cs_].rearrange("o (c p) -> p (o c)", p=128),
                      writes=[cw.res], allow_slow_non_contiguous=True)
            P.dma("sp", cb[:, q4 * 11:(q4 + 1) * 11], conv_b_d[:, cs_].rearrange("o (c p) -> p (o c)", p=128),
                  writes=[cb.res], allow_slow_non_contiguous=True)
        pw_bf = sb("pw_bf", [128, 4, 128], BF16)
        P.dma("pool", pw_bf[:], pool_w_d.rearrange("g c e -> c g e"), writes=[pw_bf.res])

        junk = [sb(f"junk{i}", [128, D], BF16) for i in range(2)]
        ssq = [sb(f"ssq{i}", [128, 1]) for i in range(4)]
        xn = [sb(f"xn{i}", [128, D], BF16) for i in range(2)]
        cnt = {"x": 0, "ss": 0, "xn": 0, "hT": 0, "j": 0}

        def rms_rows(x_ap_fn, x_res, rows, out_bf_tl):
            jk = junk[cnt["j"] % 2]; cnt["j"] += 1
            ss = ssq[cnt["ss"] % 4]; cnt["ss"] += 1
            P.op(act, lambda: nc.scalar.activation(out=jk[0:rows, :], in_=x_ap_fn(), func=AF.Square,
                                                   accum_out=ss[0:rows, :]),
                 reads=[x_res], writes=[jk.res, ss.res])
            P.op(dve, lambda: nc.vector.tensor_scalar(out=ss[0:rows, :], in0=ss[0:rows, :], scalar1=1.0 / D,
                                                      scalar2=EPS, op0=ALU.mult, op1=ALU.add),
                 reads=[ss.res], writes=[ss.res])
            P.op(pool, lambda: nc.gpsimd.tensor_tensor(out=ss[0:rows, :], in0=ss[0:rows, :], in1=mhalf[0:rows, :],
                                                       op=ALU.pow),
                 reads=[ss.res, mhalf.res], writes=[ss.res])
            if out_bf_tl is not None:
                P.op(dve, lambda: nc.vector.tensor_scalar(out=out_bf_tl[0:rows, :], in0=x_ap_fn(),
                                                          scalar1=ss[0:rows, 0:1], scalar2=None, op0=ALU.mult),
                     reads=[x_res, ss.res], writes=[out_bf_tl.res])
            return ss

        def transpose_rows(xn_tl, rows, dst_ap_fn, dst_res, g_bc):
            for dc in range(8):
                P.op(pe, lambda dc=dc: nc.tensor.transpose(out=pT[:, dc * 128:dc * 128 + rows],
                                                           in_=xn_tl[0:rows, dc * 128:(dc + 1) * 128],
                                                           identity=ident[0:rows, 0:rows]),
                     reads=[xn_tl.res, ident.res], writes=[pT.res], inc=(dc == 7))
            P.op(dve, lambda: nc.vector.tensor_tensor(
                out=dst_ap_fn(), in0=pT[:].rearrange("p (c t) -> p c t", c=8)[:, :, 0:rows],
                in1=g_bc[:, :, 0:rows], op=ALU.mult),
                reads=[pT.res, g_bc.res], writes=[dst_res])

        def barrier():
            toks = [(e, e.cnt) for e in (pe, act, dve, pool) if e.cnt]
            for q in P.dsems:
                toks += [(d, d.cnt) for d in P.dsems[q] if d.cnt]
            for e in (pe, act, dve, pool, P.sp):
                P._wait(e, toks)

        hT_sam = sb("hT_sam", [128, 8, N_SEQ], BF16) if with_sample else None
        if stop_after == "consts":
            P.final_wait()
            return nc

        esAB = ExitStack()
        KT = sb("KT", [128, 4, TP], BF16, esAB)
        Vb = sb("Vb", [128, NBLK, 512], BF16, esAB)
        kt_res = [Res() for _ in range(NBLK)]
        v_res = [Res() for _ in range(NBLK)]
        w_qu = sb("w_qu", [128, 8, 1024], BF16, esAB)
        sam_state = {}
        if with_sample:
            sam_state.update(
                idxi=sb("idxi", [128, N_SEQ * NPG], I32, esAB), bdiag=sb("bdiag", [8, 512], BF16, esAB),
                ones8=sb("ones8", [8, 1], BF16, esAB), A8all=sb("A8all", [128, N_SEQ, 512], BF16, esAB))
        with ExitStack() as esA:
            w_kv = sb("w_kv", [128, 8, 1024], BF16, esA)
            P.dma("pool", w_kv[:], w_in_d[:, 512:1536].rearrange("(c p) n -> p c n", p=128), writes=[w_kv.res])
            P.dma("pool", w_qu[:, :, 0:512], w_in_d[:, 0:512].rearrange("(c p) n -> p c n", p=128), writes=[w_qu.res])
            P.dma("pool", w_qu[:, :, 512:1024], w_in_d[:, 1536:2048].rearrange("(c p) n -> p c n", p=128), writes=[w_qu.res])
            xt = [sb(f"xt{i}", [128, D], F32, esA) for i in range(2)]
            hT = [sb(f"hT{i}", [128, 8, 128], BF16, esA) for i in range(3)]
            ko = [sb(f"ko{i}", [128, 512], F32, esA) for i in range(2)]
            vo = [sb(f"vo{i}", [128, 512], F32, esA) for i in range(2)]
            nblocks = NBLK + (1 if with_sample else 0)
            if stop_after and stop_after.startswith('A') and len(stop_after) > 1:
                nblocks = int(stop_after[1:])
            order = list(range(nblocks))
            if with_sample and nblocks == NBLK + 1:
                order = [NBLK] + list(range(NBLK))
            esS = ExitStack()
            sgen = None
            for tb in order:
                if sgen is not None:
                    for _ in range(SAMPLE_STEPS_PER_BLOCK):
                        if next(sgen, "done") == "done":
                            break
                sam = tb == NBLK
                rows = N_SEQ if sam else 128
                x_tl = xt[tb % 2]
                if sam:
                    P.dma("sp", x_tl[0:rows, :], xsam[:, :], writes=[x_tl.res])
                else:
                    P.dma("sp", x_tl[:], xall[tb * 128:(tb + 1) * 128, :], writes=[x_tl.res])
                if DBG < 2:
                    continue
                xn_tl = xn[cnt["xn"] % 2]; cnt["xn"] += 1
                rms_rows(lambda x_tl=x_tl, rows=rows: x_tl[0:rows, :], x_tl.res, rows, xn_tl)
                if DBG < 3:
                    continue
                if sam:
                    h_ap = lambda: hT_sam[:]
                    h_res = hT_sam.res
                    h_rd = lambda dc: hT_sam[:, dc, :]
                else:
                    h_tl = hT[tb % 3]
                    h_ap = lambda h_tl=h_tl: h_tl[:]
                    h_res = h_tl.res
                    h_rd = lambda dc, h_tl=h_tl: h_tl[:, dc, :]
                transpose_rows(xn_tl, rows, h_ap, h_res, gmix_bc)
                if DBG < 4:
                    continue
                if not sam:
                    pk = next_pg()
                    for j in range(4):
                        for dc in range(8):
                            P.op(pe, lambda j=j, dc=dc, pk=pk: nc.tensor.matmul(
                                pk[:, j * 128:(j + 1) * 128], lhsT=w_kv[:, dc, j * 128:(j + 1) * 128],
                                rhs=h_rd(dc), start=(dc == 0), stop=(dc == 7)),
                                reads=[w_kv.res, h_res], writes=[pk.res], inc=(j == 3 and dc == 7))
                    P.op(act, lambda pk=pk, tb=tb: nc.scalar.copy(
                        out=KT[:, :, tb * 128:(tb + 1) * 128], in_=pk[:].rearrange("p (j t) -> p j t", j=4)),
                        reads=[pk.res], writes=[kt_res[tb]])
                if DBG < 5:
                    continue
                pk2 = next_pg()
                for dc in range(8):
                    P.op(pe, lambda dc=dc, pk2=pk2: nc.tensor.matmul(
                        pk2[0:rows, :], lhsT=h_rd(dc), rhs=w_kv[:, dc, 0:512], start=(dc == 0), stop=(dc == 7)),
                        reads=[w_kv.res, h_res], writes=[pk2.res], inc=(dc == 7))
                ko_tl = ko[tb % 2]
                P.op(act, lambda pk2=pk2, ko_tl=ko_tl: nc.scalar.copy(out=ko_tl[0:rows, :], in_=pk2[0:rows, :]),
                     reads=[pk2.res], writes=[ko_tl.res])
                P.dma("act", (ks_o[:, :] if sam else k_o[tb * 128:(tb + 1) * 128, :]), ko_tl[0:rows, :], reads=[ko_tl.res])
                if DBG < 6:
                    continue
                pv = next_pg()
                for dc in range(8):
                    P.op(pe, lambda dc=dc, pv=pv: nc.tensor.matmul(
                        pv[0:rows, :], lhsT=h_rd(dc), rhs=w_kv[:, dc, 512:1024], start=(dc == 0), stop=(dc == 7)),
                        reads=[w_kv.res, h_res], writes=[pv.res], inc=(dc == 7))
                vo_tl = vo[tb % 2]
                P.op(act, lambda pv=pv, vo_tl=vo_tl: nc.scalar.copy(out=vo_tl[0:rows, :], in_=pv[0:rows, :]),
                     reads=[pv.res], writes=[vo_tl.res])
                if not sam:
                    P.op(act, lambda pv=pv, tb=tb: nc.scalar.copy(out=Vb[:, tb, :], in_=pv[:]),
                         reads=[pv.res], writes=[v_res[tb]])
                P.dma("act", (vs_o[:, :] if sam else v_o[tb * 128:(tb + 1) * 128, :]), vo_tl[0:rows, :], reads=[vo_tl.res])
                if sam:
                    L_ = dict(locals())
                    L_["esB"] = esS
                    L_["sam_state"] = sam_state
                    sgen = sample_mixer(nc, P, L_)
            if sgen is not None:
                for _ in sgen:
                    pass
            barrier()
            esS.close()
        if stop_after and stop_after.startswith("A"):
            P.final_wait()
            esAB.close()
            return nc

        with ExitStack() as esB:
            o_at = sb("o_at", [64, 8, WX], BF16, esB)
            o_pl = sb("o_pl", [128, 4, WX], BF16, esB)
            P.op(pool, lambda: nc.gpsimd.memset(o_at[:], 0.0), writes=[o_at.res])
            P.op(pool, lambda: nc.gpsimd.memset(o_pl[:], 0.0), writes=[o_pl.res])
            xs1 = [sb(f"xs1_{i}", [128, D], F32, esB) for i in range(2)]
            hTs = sb("hTs", [128, 8, WR], BF16, esB)
            qT = sb("qT", [128, 4, W], BF16, esB)
            uT = sb("uT", [128, 4, WR], F32, esB)
            lv = [sb(f"lv{i}", [128, WR], F32, esB) for i in range(2)]
            icnt = sb("icnt", [128, W], F32, esB)
            dpool = [sb(f"dpool{i}", [128, W], BF16, esB) for i in range(2)]
            NMK = 9
            masks = sb("masks", [128, NMK, W], BF16, esB)
            mask_res = [Res() for _ in range(NMK)]
            e_t = [sb(f"e_t{i}", [128, W], F32, esB) for i in range(3)]
            sp_t = [sb(f"sp_t{i}", [128, W], BF16, esB) for i in range(3)]
            a_t = [sb(f"a_t{i}", [128, W], BF16, esB) for i in range(3)]
            w_t = [sb(f"w_t{i}", [128, W], F32, esB) for i in range(3)]
            acc32 = sb("acc32", [128, W], F32, esB)
            accbf = [sb(f"accbf{i}", [128, W], BF16, esB) for i in range(3)]


            for i in range(NFC):
                for a in range(2):
                    c0_ = a * DFF + i * 128
                    P.dma("pool", wup_bf[i, a], w_up_d[:, c0_:c0_ + 128].rearrange("(c p) f -> p c f", p=128))
                P.dma("pool", wdn_bf[i], w_down_d[i * 128:(i + 1) * 128, :])
            vgen = None
            if with_sample:
                L_ = dict(locals())
                vgen = sample_vpass(nc, P, L_)
            for k in range(NS):
                for (r0, n) in [(0, HALO)] + [(HALO + c0, n) for (c0, n) in QT]:
                    xt_ = xs1[cnt["x"] % 2]; cnt["x"] += 1
                    P.dma("sp", xt_[0:n, :], xslot[k, r0:r0 + n, :], writes=[xt_.res])
                    xn_tl = xn[cnt["xn"] % 2]; cnt["xn"] += 1
                    rms_rows(lambda xt_=xt_, n=n: xt_[0:n, :], xt_.res, n, xn_tl)
                    transpose_rows(xn_tl, n, lambda r0=r0, n=n: hTs[:, :, r0:r0 + n], hTs.res, gmix_bc)
                for j in range(4):
                    pq = next_pg()
                    for dc in range(8):
                        P.op(pe, lambda j=j, dc=dc, pq=pq: nc.tensor.matmul(
                            pq[:, 0:W], lhsT=w_qu[:, dc, j * 128:(j + 1) * 128], rhs=hTs[:, dc, HALO:WR],
                            start=(dc == 0), stop=(dc == 7)),
                            reads=[w_qu.res, hTs.res], writes=[pq.res], inc=(dc == 7))
                    P.op(dve, lambda j=j, pq=pq: nc.vector.tensor_scalar(
                        out=qT[:, j, :], in0=pq[:, 0:W], scalar1=0.125, scalar2=None, op0=ALU.mult),
                        reads=[pq.res], writes=[qT.res])
                for g in range(4):
                    pu = next_pg()
                    for dc in range(8):
                        P.op(pe, lambda g=g, dc=dc, pu=pu: nc.tensor.matmul(
                            pu[:, 0:WR], lhsT=w_qu[:, dc, 512 + g * 128:512 + (g + 1) * 128], rhs=hTs[:, dc, :],
                            start=(dc == 0), stop=(dc == 7)),
                            reads=[w_qu.res, hTs.res], writes=[pu.res], inc=(dc == 7))
                    P.op(act, lambda g=g, pu=pu: nc.scalar.copy(out=uT[:, g, :], in_=pu[:, 0:WR]),
                         reads=[pu.res], writes=[uT.res])
                if k == NS - 1:
                    for g in range(4):
                        P.dma("sp", pp_o[:, g * 128:(g + 1) * 128].rearrange("r c -> c r"), uT[:, g, PP0:PP0 + 15],
                              reads=[uT.res], allow_slow_non_contiguous=True)
                for g in range(4):
                    wg = 2 << g
                    cur, cur_res = (lambda g=g: uT[:, g, :]), uT.res
                    lo = 0
                    for lvl in range(g + 1):
                        sh = 1 << lvl
                        dst = lv[lvl % 2]
                        P.op(dve, lambda cur=cur, dst=dst, sh=sh, lo=lo: nc.vector.tensor_tensor(
                            out=dst[:, lo + sh:WR], in0=cur()[:, lo + sh:WR], in1=cur()[:, lo:WR - sh], op=ALU.add),
                            reads=[cur_res], writes=[dst.res])
                        cur, cur_res = (lambda dst=dst: dst[:]), dst.res
                        lo += sh
                    P.op(dve, lambda wg=wg, k=k: nc.vector.tensor_scalar(
                        out=icnt[:], in0=qpos_bc[:, k * W:(k + 1) * W], scalar1=1.0, scalar2=float(wg),
                        op0=ALU.add, op1=ALU.min), reads=[qpos_bc.res], writes=[icnt.res])
                    P.op(dve, lambda: nc.vector.reciprocal(out=icnt[:], in_=icnt[:]), reads=[icnt.res], writes=[icnt.res])
                    P.op(dve, lambda cur=cur: nc.vector.tensor_tensor(
                        out=icnt[:], in0=cur()[:, HALO:WR], in1=icnt[:], op=ALU.mult),
                        reads=[cur_res, icnt.res], writes=[icnt.res])
                    dp = dpool[g % 2]
                    P.op(dve, lambda g=g, dp=dp: nc.vector.tensor_tensor(
                        out=dp[:], in0=icnt[:], in1=uT[:, g, HALO:WR], op=ALU.subtract),
                        reads=[icnt.res, uT.res], writes=[dp.res])
                    pp = next_pg()
                    P.op(pe, lambda g=g, dp=dp, pp=pp: nc.tensor.matmul(
                        pp[:, 0:W], lhsT=pw_bf[:, g, :], rhs=dp[:], start=True, stop=True),
                        reads=[pw_bf.res, dp.res], writes=[pp.res])
                    P.op(dve, lambda g=g, pp=pp: nc.vector.tensor_scalar(
                        out=o_pl[:, g, 0:W], in0=pp[:, 0:W], scalar1=pscale[:, g:g + 1], scalar2=None, op0=ALU.mult),
                        reads=[pp.res, pscale.res], writes=[o_pl.res])
                nkb = n_kb_for_slot(k)
                mkb = [kb for kb in range(nkb) if kb_needs_mask(k, kb)]
                assert len(mkb) <= NMK, len(mkb)
                midx = {kb: i for i, kb in enumerate(mkb)}
                for kb in mkb:
                    i = midx[kb]
                    P.op(dve, lambda kb=kb, i=i, k=k: nc.vector.tensor_scalar(
                        out=masks[:, i, :], in0=qpos_bc[:, k * W:(k + 1) * W], scalar1=kpos[:, kb:kb + 1],
                        scalar2=NEG, op0=ALU.is_le, op1=ALU.mult),
                        reads=[qpos_bc.res, kpos.res], writes=[mask_res[i]])
                ucount = [0]
                for h in range(8):
                    attn_head(nc, P, h, nkb, midx, ucount, locals())
                P.dma("pool", o_scr[k, 0:64, 0:8, 0:W], o_at[:, :, 0:W], reads=[o_at.res])
                P.dma("pool", o_scr[k, :, 8:12, 0:W], o_pl[:, :, 0:W], reads=[o_pl.res])
            if vgen is not None:
                for _ in vgen:
                    pass
            barrier()
        esAB.close()
        if stop_after == "B1":
            P.final_wait()
            return nc

        with ExitStack() as esC:
            pG.append(pO[0])
            w_oa = sb("w_oa", [64, 8, D], BF16, esC)
            P.dma("pool", w_oa[:], w_out_d[0:512, :].rearrange("(h p) n -> p h n", p=64), writes=[w_oa.res])
            w_op = sb("w_op", [128, 4, D], BF16, esC)
            P.dma("pool", w_op[:], w_out_d[512:1024, :].rearrange("(g p) n -> p g n", p=128), writes=[w_op.res])
            wdn = sb("wdn", [128, NFC, D], BF16, esC)
            P.dma("sp", wdn[:], wdn_bf.rearrange("i p n -> p i n"), writes=[wdn.res])
            xs = sb("xs2", [128, 5, D], F32, esC)
            o_at = sb("o_at2", [64, 8, WX], BF16, esC)
            o_pl = sb("o_pl2", [128, 4, WX], BF16, esC)
            h2T = sb("h2T", [128, 8, WX], BF16, esC)
            cv_t = [sb(f"cv_t{i}", [128, WX], F32, esC) for i in range(6)]
            sg_t = [sb(f"sg_t{i}", [128, WX], F32, esC) for i in range(3)]
            actT = sb("actT", [128, NFC, WX], BF16, esC)
            upl = sb("upl", [128, 2, 44], F32, esC)
            wup_t = [sb(f"wup{i}", [128, 2, 8, 128], BF16, esC) for i in range(4)]
            P.op(dve, lambda: nc.vector.memset(actT[:], 0.0), writes=[actT.res])
            if with_sample:
                upl_s = sb("upl_s", [128, N_SEQ, 44], F32, esC)
                sc8 = [sb(f"sc8_{i}", [8, 128], F32, esC) for i in range(2)]
                pF = pO[1]

            for k in range(NS):
                wk = WX if (k == 0 and with_sample) else W
                tiles = list(QT) + ([(W, N_SEQ)] if (k == 0 and with_sample) else [])
                P.dma("sp", o_at[:], o_scr[k, 0:64, 0:8, :], writes=[o_at.res])
                P.dma("sp", o_pl[:], o_scr[k, :, 8:12, :], writes=[o_pl.res])
                for ti, (c0, n) in enumerate(tiles):
                    if ti < 4:
                        P.dma("sp", xs[0:n, ti, :], xslot[k, HALO + c0:HALO + c0 + n, :], writes=[xs.res])
                    else:
                        P.dma("sp", xs[0:n, ti, :], xsam[:, :], writes=[xs.res])
                for ti, (c0, n) in enumerate(tiles):
                    for half in range(2):
                        pw = next_pg()
                        for h in range(8):
                            P.op(pe, lambda h=h, pw=pw: nc.tensor.matmul(
                                pw[0:n, :], lhsT=o_at[:, h, c0:c0 + n], rhs=w_oa[:, h, half * 512:(half + 1) * 512],
                                start=(h == 0), stop=False),
                                reads=[o_at.res, w_oa.res], writes=[pw.res], inc=False)
                        for g in range(4):
                            P.op(pe, lambda g=g, pw=pw: nc.tensor.matmul(
                                pw[0:n, :], lhsT=o_pl[:, g, c0:c0 + n], rhs=w_op[:, g, half * 512:(half + 1) * 512],
                                start=False, stop=(g == 3)),
                                reads=[o_pl.res, w_op.res], writes=[pw.res], inc=(g == 3))
                        P.op(dve, lambda pw=pw, half=half, ti=ti, n=n: nc.vector.tensor_tensor(
                            out=xs[0:n, ti, half * 512:(half + 1) * 512], in0=pw[0:n, :],
                            in1=xs[0:n, ti, half * 512:(half + 1) * 512], op=ALU.add),
                            reads=[pw.res, xs.res], writes=[xs.res])
                    xn_tl = xn[cnt["xn"] % 2]; cnt["xn"] += 1
                    rms_rows(lambda ti=ti, n=n: xs[0:n, ti, :], xs.res, n, xn_tl)
                    transpose_rows(xn_tl, n, lambda c0=c0, n=n: h2T[:, :, c0:c0 + n], h2T.res, gffn_bc)
                pend = None

                def emit_mult(sg, cv, i, wk):
                    P.op(dve, lambda: nc.vector.tensor_tensor(
                        out=actT[:, i, 2:wk], in0=sg[:, 2:wk], in1=cv[:, 2:wk], op=ALU.mult),
                        reads=[sg.res, cv.res], writes=[actT.res])

                for i in range(NFC):
                    wt = wup_t[i % 4]
                    for a in range(2):
                        P.dma("sp", wt[:, a], wup_bf[i, a], writes=[wt.res])
                    pus = []
                    for a in range(2):
                        pu = next_pg()
                        for dc in range(8):
                            P.op(pe, lambda a=a, dc=dc, pu=pu, wt=wt: nc.tensor.matmul(
                                pu[:, 0:wk], lhsT=wt[:, a, dc, :], rhs=h2T[:, dc, 0:wk],
                                start=(dc == 0), stop=(dc == 7)),
                                reads=[wt.res, h2T.res], writes=[pu.res], inc=(dc == 7))
                        pus.append(pu)
                    cvs = [cv_t[(2 * i + a) % 6] for a in range(2)]
                    for a in range(2):
                        fc = i + a * NFC
                        P.op(act, lambda pu=pus[a], cv=cvs[a], fc=fc: nc.scalar.activation(
                            out=cv[:, 2:W], in_=pu[:, 0:W - 2], func=AF.Identity, scale=cw[:, 0, fc:fc + 1],
                            bias=cb[:, fc:fc + 1]), reads=[pus[a].res, cw.res, cb.res], writes=[cvs[a].res])
                    if pend is not None:
                        emit_mult(*pend)
                        pend = None
                    for tap in (1, 2):
                        for a in range(2):
                            fc = i + a * NFC
                            P.op(dve, lambda pu=pus[a], cv=cvs[a], fc=fc, tap=tap: nc.vector.scalar_tensor_tensor(
                                out=cv[:, 2:W], in0=pu[:, tap:W - 2 + tap], scalar=cw[:, tap, fc:fc + 1], in1=cv[:, 2:W],
                                op0=ALU.mult, op1=ALU.add), reads=[pus[a].res, cw.res, cvs[a].res], writes=[cvs[a].res])
                    for a in range(2):
                        fc = i + a * NFC
                        pu, cv = pus[a], cvs[a]
                        if k == NS - 1:
                            P.op(dve, lambda pu=pu, fc=fc: nc.vector.tensor_copy(out=upl[:, :, fc], in_=pu[:, CP0:CP0 + 2]),
                                 reads=[pu.res], writes=[upl.res])
                        if k == 0 and with_sample:
                            s8 = sc8[(2 * i + a) % 2]
                            P.dma("sp", s8[:], stc_d[:, fc * 128:(fc + 1) * 128], writes=[s8.res])
                            P.op(pe, lambda s8=s8: nc.tensor.transpose(out=pF[:, 0:8], in_=s8[0:8, :],
                                                                       identity=identf[0:8, 0:8]),
                                 reads=[s8.res, identf.res], writes=[pF.res])
                            P.op(dve, lambda pu=pu, fc=fc: nc.vector.tensor_copy(out=upl_s[:, :, fc], in_=pu[:, W:WX]),
                                 reads=[pu.res], writes=[upl_s.res])
                            P.op(dve, lambda pu=pu, cv=cv, fc=fc: nc.vector.tensor_scalar(
                                out=cv[:, W:WX], in0=pu[:, W:WX], scalar1=cw[:, 2, fc:fc + 1], scalar2=cb[:, fc:fc + 1],
                                op0=ALU.mult, op1=ALU.add), reads=[pu.res, cw.res, cb.res], writes=[cv.res])
                            for r in (0, 1):
                                P.op(dve, lambda cv=cv, fc=fc, r=r: nc.vector.scalar_tensor_tensor(
                                    out=cv[:, W:WX], in0=pF[:, 0:8].rearrange("p (s r) -> p r s", r=2)[:, r, :],
                                    scalar=cw[:, r, fc:fc + 1], in1=cv[:, W:WX], op0=ALU.mult, op1=ALU.add),
                                    reads=[pF.res, cw.res, cv.res], writes=[cv.res])
                    sg = sg_t[i % 3]
                    P.op(act, lambda sg=sg, cv=cvs[0]: nc.scalar.activation(out=sg[:, 2:wk], in_=cv[:, 2:wk], func=AF.Silu),
                         reads=[cvs[0].res], writes=[sg.res])
                    pend = (sg, cvs[1], i, wk)
                if pend is not None:
                    emit_mult(*pend)
                    pend = None
                for ti, (c0, n) in enumerate(tiles):
                    for half in range(2):
                        pd = next_pg()
                        for i in range(NFC):
                            P.op(pe, lambda i=i, pd=pd: nc.tensor.matmul(
                                pd[0:n, :], lhsT=actT[:, i, c0:c0 + n], rhs=wdn[:, i, half * 512:(half + 1) * 512],
                                start=(i == 0), stop=(i == NFC - 1)),
                                reads=[actT.res, wdn.res], writes=[pd.res], inc=(i == NFC - 1))
                        P.op(dve, lambda pd=pd, half=half, ti=ti, n=n: nc.vector.tensor_tensor(
                            out=xs[0:n, ti, half * 512:(half + 1) * 512], in0=pd[0:n, :],
                            in1=xs[0:n, ti, half * 512:(half + 1) * 512], op=ALU.add),
                            reads=[pd.res, xs.res], writes=[xs.res])
                    ss = rms_rows(lambda ti=ti, n=n: xs[0:n, ti, :], xs.res, n, None)
                    P.op(dve, lambda ss=ss, ti=ti, n=n: nc.vector.scalar_tensor_tensor(
                        out=xs[0:n, ti, :], in0=xs[0:n, ti, :], scalar=ss[0:n, 0:1], in1=gfin[0:n, :],
                        op0=ALU.mult, op1=ALU.mult),
                        reads=[xs.res, ss.res, gfin.res], writes=[xs.res])
                    if ti < 4:
                        P.dma("pool", y_o[k, c0:c0 + n, :], xs[0:n, ti, :], reads=[xs.res])
                    else:
                        P.dma("pool", ys_o[:, :], xs[0:n, ti, :], reads=[xs.res])
            for r in range(2):
                for q4 in range(4):
                    P.dma("sp", cp_o[r:r + 1, q4 * 1408:(q4 + 1) * 1408].rearrange("r (c p) -> p (r c)", p=128),
                          upl[:, r, q4 * 11:(q4 + 1) * 11], reads=[upl.res], allow_slow_non_contiguous=True)
            if with_sample:
                for s in range(N_SEQ):
                    for q4 in range(4):
                        P.dma("sp", cs_o[s, 1:2, q4 * 1408:(q4 + 1) * 1408].rearrange("r (c p) -> p (r c)", p=128),
                              upl_s[:, s, q4 * 11:(q4 + 1) * 11], reads=[upl_s.res], allow_slow_non_contiguous=True)
                    P.dma("sp", cs_o[s, 0:1, :], stc_d[2 * s + 1:2 * s + 2, :])
            P.final_wait()
    return nc


def attn_head(nc, P, h, nkb, midx, ucount, L):
    pe, act, dve, pool = P.pe, P.act, P.dve, P.pool
    KT, Vb, qT, kt_res, v_res = L["KT"], L["Vb"], L["qT"], L["kt_res"], L["v_res"]
    masks, mask_res, ident, tri, negones = L["masks"], L["mask_res"], L["ident"], L["tri"], L["negones"]
    e_t, sp_t, a_t, acc32, accbf, w_t = L["e_t"], L["sp_t"], L["a_t"], L["acc32"], L["accbf"], L["w_t"]
    bias_t, o_at, pO, next_pg = L["bias_t"], L["o_at"], L["pO"], L["next_pg"]
    j, hb = h // 2, (h % 2) * 64
    po = pO[0]
    kbs = list(range(nkb - 1, -1, -1))
    n = len(kbs)
    st_ = {}
    u0 = ucount[0]

    def s1(kb):
        p = next_pg()
        need_m = kb in midx
        P.op(pe, lambda: nc.tensor.matmul(
            p[:, 0:W], lhsT=KT[hb:hb + 64, j, kb * 128:(kb + 1) * 128], rhs=qT[hb:hb + 64, j, :],
            start=True, stop=not need_m),
            reads=[kt_res[kb], qT.res], writes=[p.res], inc=not need_m)
        if need_m:
            P.op(pe, lambda: nc.tensor.matmul(
                p[:, 0:W], lhsT=ident[:], rhs=masks[:, midx[kb], :], start=False, stop=True),
                reads=[ident.res, mask_res[midx[kb]]], writes=[p.res])
        st_[kb] = {"p": p}

    def s2a(kb, u):
        p = st_[kb]["p"]
        e = e_t[u % 3]
        P.op(act, lambda: nc.scalar.activation(out=e[:], in_=p[:, 0:W], func=AF.Exp,
                                               bias=bias_t[:, h:h + 1], scale=1.0),
             reads=[p.res, bias_t.res], writes=[e.res])
        st_[kb]["e"] = e

    def s2b(kb, u):
        e = st_[kb]["e"]
        s = sp_t[u % 3]
        P.op(act, lambda: nc.scalar.activation(out=s[:], in_=e[:], func=AF.Ln, bias=1.0, scale=1.0),
             reads=[e.res], writes=[s.res])
        st_[kb]["s"] = s

    def s3(kb, u, first):
        s = st_[kb]["s"]
        pc = next_pg()
        st_[kb]["pc"] = pc
        P.op(pe, lambda: nc.tensor.matmul(pc[:, 0:W], lhsT=tri[:], rhs=s[:], start=True, stop=first),
             reads=[tri.res, s.res], writes=[pc.res], inc=first)
        if not first:
            ab = accbf[(u - 1) % 3]
            P.op(pe, lambda: nc.tensor.matmul(pc[:, 0:W], lhsT=negones[:], rhs=ab[:], start=False, stop=True),
                 reads=[negones.res, ab.res], writes=[pc.res])
        abn = accbf[u % 3]
        if first:
            P.op(dve, lambda: nc.vector.tensor_copy(out=abn[:], in_=s[:]), reads=[s.res], writes=[abn.res])
        else:
            abo = accbf[(u - 1) % 3]
            P.op(dve, lambda: nc.vector.tensor_tensor(out=abn[:], in0=abo[:], in1=s[:], op=ALU.add),
                 reads=[abo.res, s.res], writes=[abn.res])

    def s4(kb, u):
        pc = st_[kb]["pc"]; e = st_[kb]["e"]
        w = w_t[u % 3]
        a = a_t[u % 3]
        P.op(act, lambda: nc.scalar.activation(out=w[:], in_=pc[:, 0:W], func=AF.Exp),
             reads=[pc.res], writes=[w.res])
        P.op(dve, lambda: nc.vector.tensor_tensor(out=a[:], in0=e[:], in1=w[:], op=ALU.mult),
             reads=[e.res, w.res], writes=[a.res])
        st_[kb]["a"] = a

    def s5(kb, first, last):
        a = st_[kb]["a"]
        P.op(pe, lambda: nc.tensor.matmul(po[0:64, 0:W], lhsT=Vb[:, kb, h * 64:(h + 1) * 64], rhs=a[:],
                                          start=first, stop=last),
             reads=[v_res[kb], a.res], writes=[po.res], inc=last)
        del st_[kb]

    vgen = L.get("vgen")
    s1(kbs[0])
    for i in range(n + 1):
        if i + 1 < n:
            s1(kbs[i + 1])
        if i < n:
            s2a(kbs[i], u0 + i)
        if i >= 1:
            s4(kbs[i - 1], u0 + i - 1)
        if i < n:
            s2b(kbs[i], u0 + i)
            s3(kbs[i], u0 + i, i == 0)
        if i >= 1:
            s5(kbs[i - 1], i - 1 == 0, i - 1 == n - 1)
        if vgen is not None and i % 2 == 1:
            next(vgen, None)
    ucount[0] += n
    P.op(dve, lambda: nc.vector.tensor_copy(out=o_at[:, h, 0:W], in_=po[0:64, 0:W]),
         reads=[po.res], writes=[o_at.res])


def sample_mixer(nc, P, L):
    pe, act, dve, pool = P.pe, P.act, P.dve, P.pool
    sb, esB, next_pg, sam_state = L["sb"], L["esB"], L["next_pg"], L["sam_state"]
    hT_sam, w_qu, pw_bf, pscale, bias_t = L["hT_sam"], L["w_qu"], L["pw_bf"], L["pscale"], L["bias_t"]
    tri, negones, identf, pO, o_scr = L["tri"], L["negones"], L["identf"], L["pO"], L["o_scr"]
    o_pl = sb("o_spl", [128, 4, N_SEQ], BF16, esB)
    ck_d, cv_d, stp_d, pt_d, ps_o, q_scr = L["ck_d"], L["cv_d"], L["stp_d"], L["pt_d"], L["ps_o"], L["q_scr"]
    NP = N_SEQ * NPG
    pti = sb("pti", [128, NP], I32, esB)
    ptf = sb("ptf", [128, NP], F32, esB)
    iop = sb("iop", [128, 1], F32, esB)
    idxi = sam_state["idxi"]
    P.dma("sp", pti[:], pt_d.partition_broadcast(128), writes=[pti.res])
    P.op(pool, lambda: nc.gpsimd.iota(iop[:], pattern=[[0, 1]], base=0, channel_multiplier=1,
                                      allow_small_or_imprecise_dtypes=True), writes=[iop.res])
    P.op(pool, lambda: nc.gpsimd.tensor_copy(out=ptf[:], in_=pti[:]), reads=[pti.res], writes=[ptf.res])
    P.op(pool, lambda: nc.gpsimd.tensor_scalar(out=ptf[:], in0=ptf[:], scalar1=128.0, scalar2=iop[:, 0:1],
                                               op0=ALU.mult, op1=ALU.add), reads=[ptf.res, iop.res], writes=[ptf.res])
    P.op(pool, lambda: nc.gpsimd.tensor_copy(out=idxi[:], in_=ptf[:]), reads=[ptf.res], writes=[idxi.res])
    bd0 = sb("bd0", [8, 512], F32, esB)
    bd1 = sb("bd1", [8, 512], F32, esB)
    bdiag = sam_state["bdiag"]
    ones8 = sam_state["ones8"]
    P.op(pool, lambda: nc.gpsimd.memset(bd0[:], 1.0), writes=[bd0.res])
    P.op(pool, lambda: nc.gpsimd.memset(ones8[:], 1.0), writes=[ones8.res])
    P.op(pool, lambda: nc.gpsimd.affine_select(out=bd1[:], in_=bd0[:], pattern=[[1, 512]], compare_op=ALU.is_ge,
                                                fill=0.0, base=0, channel_multiplier=-64),
         reads=[bd0.res], writes=[bd1.res])
    P.op(pool, lambda: nc.gpsimd.affine_select(out=bd0[:], in_=bd1[:], pattern=[[-1, 512]], compare_op=ALU.is_ge,
                                                fill=0.0, base=63, channel_multiplier=64),
         reads=[bd1.res], writes=[bd0.res])
    P.op(pool, lambda: nc.gpsimd.tensor_copy(out=bdiag[:], in_=bd0[:]), reads=[bd0.res], writes=[bdiag.res])
    qscr_res = Res()
    pq = next_pg()
    for dc in range(8):
        P.op(pe, lambda dc=dc: nc.tensor.matmul(pq[0:N_SEQ, :], lhsT=hT_sam[:, dc, :], rhs=w_qu[:, dc, 0:512],
                                                start=(dc == 0), stop=(dc == 7)),
             reads=[hT_sam.res, w_qu.res], writes=[pq.res], inc=(dc == 7))
    q_tok = sb("q_tok", [N_SEQ, 512], F32, esB)
    P.op(dve, lambda: nc.vector.tensor_scalar(out=q_tok[:], in0=pq[0:N_SEQ, :], scalar1=0.125, scalar2=None, op0=ALU.mult),
         reads=[pq.res], writes=[q_tok.res])
    P.dma("sp", q_scr[:, :], q_tok[:], reads=[q_tok.res], writes=[qscr_res])
    pu = next_pg()
    for dc in range(8):
        P.op(pe, lambda dc=dc: nc.tensor.matmul(pu[0:N_SEQ, :], lhsT=hT_sam[:, dc, :], rhs=w_qu[:, dc, 512:1024],
                                                start=(dc == 0), stop=(dc == 7)),
             reads=[hT_sam.res, w_qu.res], writes=[pu.res], inc=(dc == 7))
    u_tok = sb("u_tok", [N_SEQ, 512], F32, esB)
    P.op(act, lambda: nc.scalar.copy(out=u_tok[:], in_=pu[0:N_SEQ, :]), reads=[pu.res], writes=[u_tok.res])
    P.dma("sp", ps_o[:, 14, :], u_tok[:], reads=[u_tok.res])
    for s in range(N_SEQ):
        P.dma("sp", ps_o[s, 0:14, :], stp_d[s * 15 + 1:s * 15 + 15, :])
    uTs = sb("uTs", [128, 4, N_SEQ], F32, esB)
    for g in range(4):
        pu2 = next_pg()
        for dc in range(8):
            P.op(pe, lambda g=g, dc=dc, pu2=pu2: nc.tensor.matmul(
                pu2[:, 0:N_SEQ], lhsT=w_qu[:, dc, 512 + g * 128:512 + (g + 1) * 128], rhs=hT_sam[:, dc, :],
                start=(dc == 0), stop=(dc == 7)),
                reads=[hT_sam.res, w_qu.res], writes=[pu2.res], inc=(dc == 7))
        P.op(act, lambda g=g, pu2=pu2: nc.scalar.copy(out=uTs[:, g, :], in_=pu2[:, 0:N_SEQ]),
             reads=[pu2.res], writes=[uTs.res])
    st60 = sb("st60", [N_SEQ * 15, 512], F32, esB)
    P.dma("sp", st60[:], stp_d[:, :], writes=[st60.res])
    stT = sb("stT", [128, 4, N_SEQ * 15], F32, esB)
    pst = next_pg()
    for g in range(4):
        P.op(pe, lambda g=g: nc.tensor.transpose(out=pst[:, g * 60:(g + 1) * 60], in_=st60[0:60, g * 128:(g + 1) * 128],
                                                 identity=identf[0:60, 0:60]),
             reads=[st60.res, identf.res], writes=[pst.res], inc=(g == 3))
    P.op(act, lambda: nc.scalar.copy(out=stT[:], in_=pst[:, 0:240].rearrange("p (g c) -> p g c", g=4)),
         reads=[pst.res], writes=[stT.res])
    ssum = sb("ssum", [128, N_SEQ], F32, esB)
    d_bf = [sb(f"d_bf{i}", [128, N_SEQ], BF16, esB) for i in range(2)]
    for g in range(4):
        wg = 2 << g
        nr = wg - 1
        P.op(dve, lambda g=g, nr=nr: nc.vector.tensor_reduce(
            out=ssum[:], in_=stT[:, g, :].rearrange("p (s r) -> p s r", r=15)[:, :, 15 - nr:15], axis=AX.X, op=ALU.add),
            reads=[stT.res], writes=[ssum.res])
        P.op(dve, lambda g=g: nc.vector.tensor_tensor(out=ssum[:], in0=ssum[:], in1=uTs[:, g, :], op=ALU.add),
             reads=[ssum.res, uTs.res], writes=[ssum.res])
        db = d_bf[g % 2]
        P.op(dve, lambda g=g, wg=wg, db=db: nc.vector.scalar_tensor_tensor(
            out=db[:], in0=ssum[:], scalar=1.0 / wg, in1=uTs[:, g, :], op0=ALU.mult, op1=ALU.subtract),
            reads=[ssum.res, uTs.res], writes=[db.res])
        pp = next_pg()
        P.op(pe, lambda g=g, db=db, pp=pp: nc.tensor.matmul(pp[:, 0:N_SEQ], lhsT=pw_bf[:, g, :], rhs=db[:],
                                                            start=True, stop=True),
             reads=[pw_bf.res, db.res], writes=[pp.res])
        P.op(dve, lambda g=g, pp=pp: nc.vector.tensor_scalar(
            out=o_pl[:, g, :], in0=pp[:, 0:N_SEQ], scalar1=pscale[:, g:g + 1], scalar2=None, op0=ALU.mult),
            reads=[pp.res, pscale.res], writes=[o_pl.res])
    P.dma("sp", o_scr[0, :, 8:12, W:WX], o_pl[:], reads=[o_pl.res])
    yield
    LAG = 3
    qb = [sb(f"qb{i}", [128, 512], F32, esB) for i in range(2)]
    Kp = [sb(f"Kp{i}", [128, 512], F32, esB) for i in range(5)]
    tmp = [sb(f"ktmp{i}", [128, 512], F32, esB) for i in range(4)]
    Z = sb("Zs", [128, 512], F32, esB)
    e8 = sb("e8", [128, 512], F32, esB)
    sp8 = sb("sp8", [128, 512], BF16, esB)
    S0 = sb("S0", [128, 512], F32, esB)
    Sa = sb("Sa", [128, 512], F32, esB)
    Sb_ = sb("Sb", [128, 512], F32, esB)
    A8all = sam_state["A8all"]
    hv = lambda t: t[:].rearrange("p (g h) -> p h g", h=8)
    for s in range(N_SEQ):
        q_t = qb[s % 2]
        P.dma("sp", q_t[:], q_scr[s:s + 1, :].partition_broadcast(128), reads=[qscr_res], writes=[q_t.res])
        zres = [Res() for _ in range(NPG)]
        for zr in zres:
            zr.w = Z.res.w
            zr.r = dict(Z.res.r)
        for step in range(NPG + LAG + 1):
            if step < NPG:
                col = s * NPG + step
                kp = Kp[step % 5]
                P.dma("pool", None, None, reads=[idxi.res], writes=[kp.res],
                      fn=lambda kp=kp, col=col: nc.gpsimd.indirect_dma_start(
                          out=kp[:], out_offset=None, in_=ck_d,
                          in_offset=bass.IndirectOffsetOnAxis(ap=idxi[:, col:col + 1], axis=0)))
            if LAG <= step < NPG + LAG:
                pg = step - LAG
                kp = Kp[pg % 5]
                tm = tmp[pg % 4]
                eng = pool if pg % 3 == 2 else dve
                P.op(eng, lambda eng=eng, tm=tm, kp=kp: eng.h.tensor_tensor(out=tm[:], in0=kp[:], in1=q_t[:], op=ALU.mult),
                     reads=[kp.res, q_t.res], writes=[tm.res])
            if step >= LAG + 1:
                pg = step - LAG - 1
                tm = tmp[pg % 4]
                eng = dve
                P.op(eng, lambda eng=eng, tm=tm, pg=pg: eng.h.tensor_reduce(
                    out=Z[:, pg * 8:(pg + 1) * 8], in_=tm[:].rearrange("p (h d) -> p h d", h=8), axis=AX.X, op=ALU.add),
                    reads=[tm.res], writes=[zres[pg]])
            yield
        Z.res.w = zres[NPG - 1].w
        Z.res.r = {}
        for h in range(8):
            P.op(act, lambda h=h: nc.scalar.activation(out=hv(e8)[:, h, :], in_=hv(Z)[:, h, :], func=AF.Exp,
                                                       bias=bias_t[:, h:h + 1], scale=1.0),
                 reads=[Z.res, bias_t.res], writes=[e8.res])
        P.op(act, lambda: nc.scalar.activation(out=sp8[:], in_=e8[:], func=AF.Ln, bias=1.0, scale=1.0),
             reads=[e8.res], writes=[sp8.res])
        pc = pO[0]
        P.op(pe, lambda: nc.tensor.matmul(pc[:, :], lhsT=tri[:], rhs=sp8[:], start=True, stop=True),
             reads=[tri.res, sp8.res], writes=[pc.res])
        pt_ = pO[1]
        P.op(pe, lambda: nc.tensor.matmul(pt_[:, :], lhsT=negones[:], rhs=sp8[:], start=True, stop=True),
             reads=[negones.res, sp8.res], writes=[pt_.res])
        P.op(act, lambda: nc.scalar.copy(out=S0[:], in_=pt_[:, :]), reads=[pt_.res], writes=[S0.res])
        cur = S0
        for i, dd in enumerate((1, 2, 4, 8, 16, 32)):
            nxt = Sa if i % 2 == 0 else Sb_
            n0 = 512 - 8 * dd
            P.op(dve, lambda cur=cur, nxt=nxt, n0=n0, dd=dd: nc.vector.tensor_tensor(
                out=nxt[:, 0:n0], in0=cur[:, 0:n0], in1=cur[:, 8 * dd:512], op=ALU.add),
                reads=[cur.res], writes=[nxt.res])
            P.op(dve, lambda cur=cur, nxt=nxt, n0=n0: nc.vector.tensor_copy(out=nxt[:, n0:512], in_=cur[:, n0:512]),
                 reads=[cur.res, nxt.res], writes=[nxt.res])
            cur = nxt
        P.op(dve, lambda cur=cur: nc.vector.tensor_tensor(out=cur[:], in0=cur[:], in1=S0[:], op=ALU.subtract),
             reads=[cur.res, S0.res], writes=[cur.res])
        P.op(dve, lambda cur=cur: nc.vector.tensor_tensor(out=cur[:], in0=cur[:], in1=Z[:], op=ALU.add),
             reads=[cur.res, Z.res], writes=[cur.res])
        P.op(dve, lambda cur=cur: nc.vector.tensor_tensor(out=cur[:], in0=cur[:], in1=pc[:, :], op=ALU.add),
             reads=[cur.res, pc.res], writes=[cur.res])
        for h in range(8):
            P.op(act, lambda h=h, cur=cur: nc.scalar.activation(out=A8all[:, s, :].rearrange("p (g h) -> p h g", h=8)[:, h, :], in_=hv(cur)[:, h, :], func=AF.Exp,
                                                                bias=bias_t[:, h:h + 1], scale=1.0),
                 reads=[cur.res, bias_t.res], writes=[A8all.res])
        yield


def sample_vpass(nc, P, L):
    pe, act, dve, pool = P.pe, P.act, P.dve, P.pool
    sb, esB, pO, o_scr, cv_d = L["sb"], L["esB"], L["pO"], L["o_scr"], L["cv_d"]
    st = L["sam_state"]
    A8all, idxi, bdiag, ones8 = st["A8all"], st["idxi"], st["bdiag"], st["ones8"]
    LAG = 2
    NB = 4
    Vp = [sb(f"Vp{i}", [128, 512], BF16, esB) for i in range(NB)]
    m8 = sb("m8", [8, 512], BF16, esB)
    o_at = sb("o_sat", [64, 8, N_SEQ], BF16, esB)
    pov = pO[1]
    for s in range(N_SEQ):
        for step in range(NPG + LAG):
            if step < NPG:
                col = s * NPG + step
                vp = Vp[step % NB]
                P.dma("pool", None, None, reads=[idxi.res], writes=[vp.res],
                      fn=lambda vp=vp, col=col: nc.gpsimd.indirect_dma_start(
                          out=vp[:], out_offset=None, in_=cv_d,
                          in_offset=bass.IndirectOffsetOnAxis(ap=idxi[:, col:col + 1], axis=0)))
            if step >= LAG:
                pg = step - LAG
                vp = Vp[pg % NB]
                P.op(pe, lambda vp=vp, pg=pg, s=s: nc.tensor.matmul(pov[0:8, :], lhsT=A8all[:, s, pg * 8:(pg + 1) * 8], rhs=vp[:],
                                                                    start=(pg == 0), stop=(pg == NPG - 1)),
                     reads=[A8all.res, vp.res], writes=[pov.res], inc=True)
            yield
        P.op(dve, lambda: nc.vector.tensor_tensor(out=m8[:], in0=pov[0:8, :], in1=bdiag[:], op=ALU.mult),
             reads=[pov.res, bdiag.res], writes=[m8.res])
        for h in range(8):
            P.op(pe, lambda h=h: nc.tensor.matmul(pov[0:64, h:h + 1], lhsT=m8[0:8, h * 64:(h + 1) * 64], rhs=ones8[0:8, 0:1],
                                                  start=True, stop=True),
                 reads=[m8.res, ones8.res], writes=[pov.res], inc=(h == 7))
        P.op(dve, lambda s=s: nc.vector.tensor_copy(out=o_at[:, :, s], in_=pov[0:64, 0:8]),
             reads=[pov.res], writes=[o_at.res])
        yield
    P.dma("sp", o_scr[0, 0:64, 0:8, W:WX], o_at[:], reads=[o_at.res])


_NC_CACHE = {}
_RUNNER = [None]
_REMAP = [None, None]


_STOP = [None]


def _get_nc(with_sample):
    key = (with_sample, _STOP[0])
    if key not in _NC_CACHE:
        _NC_CACHE[key] = build(with_sample, _STOP[0])
    return _NC_CACHE[key]


def kernel(x_prompt, x_sample, cache_k, cache_v, state_pool, state_conv, page_table,
           meta_tokens, norm_mix_g, w_in, sb_bias, pool_w, pool_scale, w_out, norm_ffn_g,
           w_up, conv_w, conv_b, w_down, norm_final_g, _with_sample=True):
    f32 = np.float32
    B = x_prompt.shape[0]
    nc = _get_nc(_with_sample)
    in_maps = []
    ck2 = cv2 = None
    if _with_sample:
        ck2 = np.ascontiguousarray(cache_k, dtype=f32).reshape(-1, 512)
        cv2 = np.ascontiguousarray(cache_v, dtype=f32).reshape(-1, 512)
    for c in range(8):
        b, g = c // 2, c % 2
        xa = np.zeros((TP, D), f32)
        xa[:N_META] = meta_tokens
        xa[N_META:T] = x_prompt[b]
        xsl = np.zeros((NS, WR, D), f32)
        qp = np.zeros((1, NS * W), f32)
        for k in range(NS):
            s = slot_start(g, k)
            lo, hi = s - HALO, s + W
            a, e = max(lo, 0), min(hi, T)
            xsl[k, a - lo:e - lo] = xa[a:e]
            qp[0, k * W:(k + 1) * W] = np.arange(s, s + W, dtype=f32)
        m = {
            "xall": xa, "xslot": xsl, "qpos": qp,
            "w_in": np.asarray(w_in, f32), "w_out": np.asarray(w_out, f32), "w_up": np.asarray(w_up, f32),
            "w_down": np.asarray(w_down, f32),
            "norm_mix_g": np.asarray(norm_mix_g, f32).reshape(D, 1),
            "norm_ffn_g": np.asarray(norm_ffn_g, f32).reshape(D, 1),
            "norm_final_g": np.asarray(norm_final_g, f32).reshape(1, D),
            "sb_bias": np.asarray(sb_bias, f32).reshape(1, 8),
            "pool_w": np.asarray(pool_w, f32), "pool_scale": np.asarray(pool_scale, f32),
            "conv_w": np.asarray(conv_w, f32), "conv_b": np.asarray(conv_b, f32).reshape(1, 2 * DFF),
        }
        if _with_sample:
            sl = slice(N_SEQ * c, N_SEQ * (c + 1))
            m.update({
                "x_sample": np.asarray(x_sample[sl], f32).reshape(N_SEQ, D),
                "cache_k": (_REMAP[0](c, ck2) if _REMAP[0] else ck2), "cache_v": (_REMAP[0](c, cv2) if _REMAP[0] else cv2),
                "state_pool": np.asarray(state_pool[sl], f32).reshape(N_SEQ * 15, 512),
                "state_conv": np.asarray(state_conv[sl], f32).reshape(N_SEQ * 2, 2 * DFF),
                "page_table": (_REMAP[1](c) if _REMAP[1] else np.asarray(page_table[sl], np.int32).reshape(1, N_SEQ * NPG)),
            })
        in_maps.append(m)
    if _RUNNER[0] is not None:
        res = _RUNNER[0](nc, in_maps)
    else:
        res = run_bass_kernel_spmd(nc, in_maps, core_ids=list(range(8))).results

    y_prompt = np.zeros((B, 4096, D), f32)
    k_prompt = np.zeros((B, T, 8, 64), f32)
    v_prompt = np.zeros((B, T, 8, 64), f32)
    pool_prompt = np.zeros((B, 15, 512), f32)
    conv_prompt = np.zeros((B, 2, 2 * DFF), f32)
    for c in range(8):
        b, g = c // 2, c % 2
        r = res[c]
        for k in range(NS):
            r0 = OWN * TILES[g][k]
            nv = min(OWN, 4096 - r0)
            y_prompt[b, r0:r0 + nv] = r["y_slot"][k, 2:2 + nv]
        if g == 0:
            k_prompt[b] = r["k_all"][:T].reshape(T, 8, 64)
            v_prompt[b] = r["v_all"][:T].reshape(T, 8, 64)
        else:
            pool_prompt[b] = r["pool_last"]
            conv_prompt[b] = r["conv_last"]
    if not _with_sample:
        return (y_prompt, None, k_prompt, v_prompt, pool_prompt, conv_prompt, None, None, None, None)
    DB = x_sample.shape[0]
    y_sample = np.zeros((DB, 1, D), f32)
    k_sample = np.zeros((DB, 1, 8, 64), f32)
    v_sample = np.zeros((DB, 1, 8, 64), f32)
    pool_sample = np.zeros((DB, 15, 512), f32)
    conv_sample = np.zeros((DB, 2, 2 * DFF), f32)
    for c in range(8):
        r = res[c]
        sl = slice(N_SEQ * c, N_SEQ * (c + 1))
        y_sample[sl, 0] = r["y_sample"]
        k_sample[sl, 0] = r["k_sample"].reshape(N_SEQ, 8, 64)
        v_sample[sl, 0] = r["v_sample"].reshape(N_SEQ, 8, 64)
        pool_sample[sl] = r["pool_sample"]
        conv_sample[sl] = r["conv_sample"]
    return (y_prompt, y_sample, k_prompt, v_prompt, pool_prompt, conv_prompt,
            k_sample, v_sample, pool_sample, conv_sample)
```
